# Optimizing a Trainium2 kernel written in Bass

```python
import jax, jax.numpy as jnp
from jax import lax
import numpy as np


D_MODEL = 1024
BATCH = 16
SEQ = 4096
DEPTH = 1

MIX_WIDTH = D_MODEL
GLA_HEADS = 4
GLA_WIDTH = MIX_WIDTH // 2
GLA_DV = GLA_WIDTH // GLA_HEADS
GLA_DK = GLA_DV // 2
GLA_QK = GLA_HEADS * GLA_DK
GLA_GATE_RANK = 16
GLA_GATE_NORM = 16.0
GDN_HEADS = 4
GDN_WIDTH = MIX_WIDTH - GLA_WIDTH
GDN_DK = GDN_WIDTH // GDN_HEADS
GDN_DV = GDN_DK
CONV_WIDTH = 4
CHUNK = 64
LN_EPS = 1e-5
RMS_EPS = 1e-6
ALPHA = (2.0 * DEPTH) ** 0.25
BETA_INIT = (8.0 * DEPTH) ** -0.25

IN_SIZES = (
    GLA_QK,
    GLA_QK,
    GLA_WIDTH,
    GLA_GATE_RANK,
    GLA_WIDTH,
    3 * GDN_WIDTH,
    GDN_HEADS,
    GDN_HEADS,
    GDN_WIDTH,
)
IN_COLS = sum(IN_SIZES)

kernel_name = 'hymba_gla_gdn_deepnorm_adaln'


def _split_cols(t, sizes):
    idx = []
    acc = 0
    for s in sizes[:-1]:
        acc += s
        idx.append(acc)
    return jnp.split(t, idx, axis=-1)


def layer_norm(u, w, b):
    u32 = u.astype(jnp.float32)
    mu = jnp.mean(u32, axis=-1, keepdims=True)
    var = jnp.mean(jnp.square(u32 - mu), axis=-1, keepdims=True)
    return ((u32 - mu) * lax.rsqrt(var + LN_EPS) * w + b).astype(u.dtype)


def rms_norm(u, w):
    u32 = u.astype(jnp.float32)
    return (u32 * lax.rsqrt(jnp.mean(jnp.square(u32), axis=-1, keepdims=True) + RMS_EPS) * w).astype(u.dtype)


def l2_norm(u):
    u32 = u.astype(jnp.float32)
    return (u32 * lax.rsqrt(jnp.sum(jnp.square(u32), axis=-1, keepdims=True) + RMS_EPS)).astype(u.dtype)


def causal_depthwise_conv(u, w):
    K, C = w.shape
    return lax.conv_general_dilated(u, w[:, None, :].astype(u.dtype), window_strides=(1,),
                                    padding=((K - 1, 0),), dimension_numbers=('NWC', 'WIO', 'NWC'),
                                    feature_group_count=C)


def _to_chunks(t, n_chunks):
    B, T, H = t.shape[:3]
    t = t.reshape((B, n_chunks, CHUNK, H) + t.shape[3:])
    return jnp.moveaxis(t, 3, 1)


def _from_chunks(t):
    B, H, N, C, d = t.shape
    return jnp.moveaxis(t, 1, 3).reshape(B, N * C, H, d)


def gla_chunked(q, k, v, g):
    out_dtype = v.dtype
    B, T, H, dk = q.shape
    dv = v.shape[-1]
    N = T // CHUNK
    q, k, v, g = (_to_chunks(t.astype(jnp.float32), N) for t in (q, k, v, g))
    b = jnp.cumsum(g, axis=3)
    b_ref = b[:, :, :, CHUNK // 2 - 1:CHUNK // 2, :]
    causal = jnp.tril(jnp.ones((CHUNK, CHUNK), dtype=bool))
    att = jnp.einsum('bhnid,bhnjd->bhnij', q * jnp.exp(b - b_ref), k * jnp.exp(b_ref - b))
    att = jnp.where(causal, att, 0.0)
    o_intra = jnp.einsum('bhnij,bhnjv->bhniv', att, v)
    b_last = b[:, :, :, -1, :]
    upd = jnp.einsum('bhncd,bhncv->bhndv', k * jnp.exp(b_last[:, :, :, None, :] - b), v)
    decay = jnp.exp(b_last)

    def step(S, inp):
        d_n, u_n = inp
        return d_n[..., None] * S + u_n, S

    S0 = jnp.zeros((B, H, dk, dv), jnp.float32)
    _, S_prev = lax.scan(step, S0, (jnp.moveaxis(decay, 2, 0), jnp.moveaxis(upd, 2, 0)))
    S_prev = jnp.moveaxis(S_prev, 0, 2)
    o_inter = jnp.einsum('bhncd,bhndv->bhncv', q * jnp.exp(b), S_prev)
    return _from_chunks(o_intra + o_inter).astype(out_dtype)


def gated_delta_chunked(q, k, v, g, beta):
    out_dtype = v.dtype
    B, T, H, dk = q.shape
    dv = v.shape[-1]
    N = T // CHUNK
    q, k, v = (_to_chunks(t.astype(jnp.float32), N) for t in (q, k, v))
    g, beta = (_to_chunks(t.astype(jnp.float32), N) for t in (g, beta))
    d = jnp.cumsum(g, axis=-1)
    causal = jnp.tril(jnp.ones((CHUNK, CHUNK), dtype=bool))
    strict = jnp.tril(jnp.ones((CHUNK, CHUNK), dtype=bool), k=-1)
    L = jnp.exp(jnp.where(causal, d[..., :, None] - d[..., None, :], -jnp.inf))
    k_beta = k * beta[..., None]
    A = jnp.where(strict, jnp.einsum('bhnid,bhnjd->bhnij', k_beta, k) * L, 0.0)
    eye = jnp.eye(CHUNK, dtype=jnp.float32)
    rhs = jnp.concatenate([v * beta[..., None], k_beta * jnp.exp(d)[..., None]], axis=-1)
    sol = lax.linalg.triangular_solve(A + eye, rhs, left_side=True, lower=True, unit_diagonal=True)
    u, w = sol[..., :dv], sol[..., dv:]
    qk = jnp.where(causal, jnp.einsum('bhnid,bhnjd->bhnij', q, k) * L, 0.0)
    q_dec = q * jnp.exp(d)[..., None]
    k_dec = k * jnp.exp(d[..., -1:] - d)[..., None]
    chunk_decay = jnp.exp(d[..., -1])

    def step(S, inp):
        qk_n, q_n, k_n, u_n, w_n, dec_n = inp
        v_new = u_n - jnp.einsum('bhcd,bhdv->bhcv', w_n, S)
        o_n = jnp.einsum('bhcd,bhdv->bhcv', q_n, S) + jnp.einsum('bhij,bhjv->bhiv', qk_n, v_new)
        S = dec_n[..., None, None] * S + jnp.einsum('bhcd,bhcv->bhdv', k_n, v_new)
        return S, o_n

    xs = tuple(jnp.moveaxis(t, 2, 0) for t in (qk, q_dec, k_dec, u, w, chunk_decay))
    S0 = jnp.zeros((B, H, dk, dv), jnp.float32)
    _, o = lax.scan(step, S0, xs)
    return _from_chunks(jnp.moveaxis(o, 0, 2)).astype(out_dtype)


def hybrid_layer(x, c, w_ada, b_ada, w_in, gla_w_gate_up, gla_b_gate, gla_norm_w,
                 gdn_conv_w, gdn_a_log, gdn_dt_bias, gdn_norm_w, w_out, ln_w, ln_b):
    B, T, _ = x.shape
    mod = (c @ w_ada + b_ada)[:, None, :]
    shift, scale, gate = jnp.split(mod, 3, axis=-1)
    h = x * (1.0 + scale) + shift
    proj = h @ w_in
    (gla_q, gla_k, gla_v, gla_lr, gla_og, gdn_qkv, gdn_a, gdn_b, gdn_og) = _split_cols(proj, IN_SIZES)

    q = gla_q.reshape(B, T, GLA_HEADS, GLA_DK) * (GLA_DK ** -0.5)
    k = gla_k.reshape(B, T, GLA_HEADS, GLA_DK)
    v = gla_v.reshape(B, T, GLA_HEADS, GLA_DV)
    z = (gla_lr @ gla_w_gate_up + gla_b_gate).astype(jnp.float32)
    g = (jax.nn.log_sigmoid(z) / GLA_GATE_NORM).reshape(B, T, GLA_HEADS, GLA_DK)
    o_a = rms_norm(gla_chunked(q, k, v, g), gla_norm_w)
    y_a = o_a.reshape(B, T, GLA_WIDTH) * jax.nn.silu(gla_og)

    qkv = jax.nn.silu(causal_depthwise_conv(gdn_qkv, gdn_conv_w))
    q, k, v = jnp.split(qkv, 3, axis=-1)
    q = l2_norm(q.reshape(B, T, GDN_HEADS, GDN_DK)) * (GDN_DK ** -0.5)
    k = l2_norm(k.reshape(B, T, GDN_HEADS, GDN_DK))
    v = v.reshape(B, T, GDN_HEADS, GDN_DV)
    g = -jnp.exp(gdn_a_log.astype(jnp.float32)) * jax.nn.softplus((gdn_a + gdn_dt_bias).astype(jnp.float32))
    beta = jax.nn.sigmoid(gdn_b.astype(jnp.float32))
    o_b = rms_norm(gated_delta_chunked(q, k, v, g, beta), gdn_norm_w)
    y_b = o_b.reshape(B, T, GDN_WIDTH) * jax.nn.silu(gdn_og)

    y = jnp.concatenate([y_a, y_b], axis=-1) @ w_out
    return layer_norm(ALPHA * x + (1.0 + gate) * y, ln_w, ln_b)


def setup_inputs(seed: int = 0) -> dict:
    key = jax.random.key(seed)
    ks = jax.random.split(key, 16)
    f32 = jnp.float32
    x = jax.random.normal(ks[0], (BATCH, SEQ, D_MODEL), f32)
    c = jax.random.normal(ks[1], (BATCH, D_MODEL), f32)
    w_ada = jax.random.normal(ks[2], (DEPTH, D_MODEL, 3 * D_MODEL), f32) * (0.1 * D_MODEL ** -0.5)
    b_ada = jax.random.normal(ks[3], (DEPTH, 3 * D_MODEL), f32) * 0.01
    col_scale = jnp.concatenate([
        jnp.ones((2 * GLA_QK,), f32),
        jnp.full((GLA_WIDTH,), BETA_INIT, f32),
        jnp.ones((GLA_GATE_RANK + GLA_WIDTH + 2 * GDN_WIDTH,), f32),
        jnp.full((GDN_WIDTH,), BETA_INIT, f32),
        jnp.ones((2 * GDN_HEADS + GDN_WIDTH,), f32),
    ])
    w_in = jax.random.normal(ks[4], (DEPTH, D_MODEL, IN_COLS), f32) * (D_MODEL ** -0.5) * col_scale
    gla_w_gate_up = jax.random.normal(ks[5], (DEPTH, GLA_GATE_RANK, GLA_QK), f32) * (GLA_GATE_RANK ** -0.5)
    gla_b_gate = jax.random.normal(ks[6], (DEPTH, GLA_QK), f32) * 0.1
    gla_norm_w = 1.0 + 0.01 * jax.random.normal(ks[7], (DEPTH, GLA_DV), f32)
    gdn_conv_w = jax.random.normal(ks[8], (DEPTH, CONV_WIDTH, 3 * GDN_WIDTH), f32) * (CONV_WIDTH ** -0.5)
    gdn_a_log = jnp.log(jax.random.uniform(ks[9], (DEPTH, GDN_HEADS), f32, 1.0, 16.0))
    dt = jnp.exp(jax.random.uniform(ks[10], (DEPTH, GDN_HEADS), f32, np.log(1e-3), np.log(1e-1)))
    gdn_dt_bias = dt + jnp.log(-jnp.expm1(-dt))
    gdn_norm_w = 1.0 + 0.01 * jax.random.normal(ks[11], (DEPTH, GDN_DV), f32)
    w_out = jax.random.normal(ks[12], (DEPTH, MIX_WIDTH, D_MODEL), f32) * (MIX_WIDTH ** -0.5) * BETA_INIT
    ln_w = 1.0 + 0.01 * jax.random.normal(ks[13], (DEPTH, D_MODEL), f32)
    ln_b = 0.01 * jax.random.normal(ks[14], (DEPTH, D_MODEL), f32)
    return {'x': x, 'c': c, 'w_ada': w_ada, 'b_ada': b_ada, 'w_in': w_in,
            'gla_w_gate_up': gla_w_gate_up, 'gla_b_gate': gla_b_gate, 'gla_norm_w': gla_norm_w,
            'gdn_conv_w': gdn_conv_w, 'gdn_a_log': gdn_a_log, 'gdn_dt_bias': gdn_dt_bias,
            'gdn_norm_w': gdn_norm_w, 'w_out': w_out, 'ln_w': ln_w, 'ln_b': ln_b}


def reference(x, c, w_ada, b_ada, w_in, gla_w_gate_up, gla_b_gate, gla_norm_w,
              gdn_conv_w, gdn_a_log, gdn_dt_bias, gdn_norm_w, w_out, ln_w, ln_b):
    for layer in range(DEPTH):
        x = hybrid_layer(x, c, w_ada[layer], b_ada[layer], w_in[layer], gla_w_gate_up[layer],
                         gla_b_gate[layer], gla_norm_w[layer], gdn_conv_w[layer], gdn_a_log[layer],
                         gdn_dt_bias[layer], gdn_norm_w[layer], w_out[layer], ln_w[layer], ln_b[layer])
    return x
```

```python
import bisect
import contextlib
import os
import numpy as np
import concourse.bass as bass
import concourse.mybir as mybir
from concourse.bass_utils import run_bass_kernel_spmd

F32 = mybir.dt.float32
BF16 = mybir.dt.bfloat16
AF = mybir.ActivationFunctionType
ALU = mybir.AluOpType

ENGINES = ("tensor", "vector", "scalar", "gpsimd", "sync")
D = 1024
NCOL = 3608
ALPHA = 2.0 ** 0.25
NEG = -30000.0


class Sched:
    def __init__(self, nc):
        self.nc = nc
        self.ops = []
        self.last_writer = {}
        self.readers = {}
        self.dma_groups = {}

    def _add(self, eng, fn, reads, writes, dma_group=None):
        idx = len(self.ops)
        deps = set()
        for r in reads:
            w = self.last_writer.get(r)
            if w is not None:
                deps.add((w, "raw"))
        for r in writes:
            w = self.last_writer.get(r)
            if w is not None:
                deps.add((w, "waw"))
            for rd in self.readers.get(r, ()):
                deps.add((rd, "war"))
        for r in reads:
            self.readers.setdefault(r, []).append(idx)
        for r in writes:
            self.last_writer[r] = idx
            self.readers[r] = []
        op = dict(eng=eng, fn=fn, deps=deps, dma_group=dma_group, signal=False, seq=None)
        if dma_group is not None:
            c = self.dma_groups.get(dma_group, 0) + 1
            self.dma_groups[dma_group] = c
            op["seq"] = c
            op["signal"] = True
        self.ops.append(op)
        return idx

    def op(self, eng, fn, reads=(), writes=()):
        return self._add(eng, fn, tuple(reads), tuple(writes))

    def dma(self, eng, fn, reads=(), writes=(), group=None):
        return self._add(eng, fn, tuple(reads), tuple(writes), dma_group=group)

    def emit(self, final_wait_groups=()):
        nc = self.nc
        ops = self.ops
        for i, op in enumerate(ops):
            nd = set()
            for (p, kind) in op["deps"]:
                pop = ops[p]
                if pop["dma_group"] is None and op["dma_group"] is None and pop["eng"] == op["eng"]:
                    if op["eng"] == "tensor" or kind != "raw":
                        continue
                nd.add(p)
            best = {}
            keep = set()
            for p in nd:
                pop = ops[p]
                if pop["dma_group"] is not None:
                    keep.add(p)
                elif p > best.get(pop["eng"], -1):
                    best[pop["eng"]] = p
            keep.update(best.values())
            op["ndeps"] = keep
            for p in keep:
                ops[p]["signal"] = True
        cnt = {e: 0 for e in ENGINES}
        for op in ops:
            if op["dma_group"] is None and op["signal"]:
                cnt[op["eng"]] += 1
                op["seq"] = cnt[op["eng"]]
        stack = contextlib.ExitStack()
        esem = {e: stack.enter_context(nc.semaphore("s_" + e)) for e in ENGINES if cnt[e] > 0}
        dsem = {g: stack.enter_context(nc.semaphore("d_%d" % k)) for k, g in enumerate(self.dma_groups)}
        per_eng = {e: [] for e in ENGINES}
        grp_idx = {}
        for i, op in enumerate(ops):
            per_eng[op["eng"]].append(i)
            if op["dma_group"] is not None:
                grp_idx.setdefault(op["dma_group"], []).append(i)

        def run_engine(ename, eobj):
            waited = {}
            for i in per_eng[ename]:
                op = ops[i]
                need = {}
                for p in op["ndeps"]:
                    pop = ops[p]
                    if pop["dma_group"] is not None:
                        key = ("d", pop["dma_group"])
                        val = 16 * bisect.bisect_left(grp_idx[pop["dma_group"]], i)
                    else:
                        key = ("e", pop["eng"])
                        val = pop["seq"]
                    if val > need.get(key, 0):
                        need[key] = val
                for key, val in need.items():
                    if waited.get(key, 0) >= val:
                        continue
                    waited[key] = val
                    sem = dsem[key[1]] if key[0] == "d" else esem[key[1]]
                    eobj.wait_ge(sem, val)
                ins = op["fn"](eobj)
                if op["dma_group"] is not None:
                    ins.then_inc(dsem[op["dma_group"]], 16)
                elif op["signal"]:
                    ins.then_inc(esem[ename], 1)
            if ename == "sync":
                for g in final_wait_groups:
                    eobj.wait_ge(dsem[g], 16 * self.dma_groups[g])

        with stack:
            with nc.Block() as block:
                for ename in ENGINES:
                    getattr(block, ename)(lambda eobj, ename=ename: run_engine(ename, eobj))
        return cnt


def host_consts():
    j = np.arange(128)[:, None]
    i = np.arange(128)[None, :]
    f = lambda m: m.astype(np.float32)
    cf = np.zeros((128, 9, 128), np.float32)
    cf[:, 0] = np.eye(128)
    cf[:, 1] = 1.0
    cf[:, 2] = 0.0
    cf[:64, 2, 0] = 1.0
    cf[64:, 2, 1] = 1.0
    cf[:, 4] = -(1.0 / 16.0) * f(j > i)
    cf[:, 5] = f(j <= i)
    cf[:, 6] = f(j > i)
    cf[:, 7] = np.where(i >= j, 0.0, NEG)
    cf[:, 8] = np.where(i > j, 0.0, NEG)
    uref = np.zeros((128, 130), np.float32)
    uref[:, :128] = -(1.0 / 16.0) * (f(j <= i) - f(j <= 63))
    uref[:, 128] = -(1.0 / 16.0) * f(j[:, 0] <= 63)
    uref[:, 129] = -(1.0 / 16.0)
    bm = np.zeros((128, 8, 128), np.float32)
    bm[:, 0] = np.eye(128)
    bm[:, 1] = f(j <= i)
    bm[:, 2] = f((j // 16) == (i // 16))
    for n, s in ((3, 16), (5, 32), (7, 64)):
        low = f(((j // s) % 2 == 1) & ((i // s) == (j // s) - 1))
        if n < 7:
            bm[:, n] = low
            bm[:, n + 1] = low.T
        else:
            bm[:, n] = low.T
    sel = np.zeros((2, 2, 128), np.float32)
    sel[0, 0] = 1.0
    sel[1, 1] = 1.0
    return cf, uref, bm, sel


def interleave(gens):
    gens = list(gens)
    while gens:
        for g in list(gens):
            try:
                next(g)
            except StopIteration:
                gens.remove(g)
        yield


def run(gen):
    for _ in gen:
        pass


PYIELD = int(os.environ.get("K_PYIELD", "2"))
ORDER = os.environ.get("K_ORDER", "bmp")


def build(n_units, taps=False, stage=99):
    T = n_units * 128
    nc = bass.Bass("TRN2", target_bir_lowering=False)
    din = lambda name, shape: nc.dram_tensor(name, shape, F32, kind="ExternalInput").ap()
    x_d = din("x", [2, T, D])
    cT_d = din("cT", [128, 8, 2])
    wada_d = din("w_ada", [D, 3 * D])
    bada_d = din("b_ada", [1, 3 * D])
    win_d = din("w_in", [D, NCOL])
    wg_d = din("wg_aug", [17, 256])
    glanw_d = din("gla_nw", [128, 1])
    gdnnw_d = din("gdn_nw", [128, 1])
    convw_d = din("convw", [128, 48])
    alog_d = din("alog_b", [128, 4])
    dtb_d = din("dtb_b", [128, 4])
    wout_d = din("w_out", [D, D])
    lnw_d = din("ln_w", [1, D])
    lnb_d = din("ln_b", [1, D])
    cf_d = din("cf", [128, 9, 128])
    uref_d = din("uref", [128, 130])
    bm_d = din("bm", [128, 8, 128])
    sel_d = din("sel", [2, 2, 128])
    out_d = nc.dram_tensor("out", [2, T, D], F32, kind="ExternalOutput").ap()

    es = contextlib.ExitStack()
    with es:
        def sb(name, shape, dt=BF16):
            return es.enter_context(nc.sbuf_tensor("s_" + name, shape, dt))
        s = Sched(nc)
        banks = [es.enter_context(nc.psum_tensor("ps%d" % i, [128, 512], F32)) for i in range(7)]
        psb = es.enter_context(nc.psum_tensor("psb", [128, 1024], BF16))

        def mkbanks(ids):
            ctr = [0]

            def nb():
                b = ids[ctr[0] % len(ids)]
                ctr[0] += 1
                return banks[b], "ps%d" % b
            return nb
        nb_setup = mkbanks([0, 1, 2, 3, 4, 5, 6])
        nb_p = mkbanks([0])
        nb_gla = mkbanks([1])
        nb_gdn = mkbanks([2, 3])
        nb_back = mkbanks([4, 5, 6])

        cf = sb("cf", [128, 9, 128], F32)
        uref = sb("uref", [128, 130], F32)
        R = sb("R", [128, D], F32)
        RN = sb("RN", [128, 8, 128], F32)
        bmf = R[:].rearrange("p (a b) -> p a b", a=8)
        sel = sb("sel", [2, 2, 128], F32)
        s.dma("sync", lambda e: e.dma_start(out=cf[:], in_=cf_d), writes=["cf"], group="c0")
        s.dma("sync", lambda e: e.dma_start(out=uref[:], in_=uref_d), writes=["uref"], group="c0")
        s.dma("sync", lambda e: e.dma_start(out=bmf, in_=bm_d), writes=["R"], group="c0")
        s.dma("sync", lambda e: e.dma_start(out=sel[:], in_=sel_d), writes=["sel"], group="c0")
        ident_f, ones_f = cf[:, 0, :], cf[:, 1, :]
        mgt16, uincl, mgt, mt_incl, mt_strict = cf[:, 4, :], cf[:, 5, :], cf[:, 6, :], cf[:, 7, :], cf[:, 8, :]
        bmrep = sb("bmrep", [128, 2, 4, 128])
        bm1 = sb("bm1", [128, 8, 128])
        for n, m in enumerate((0, 2)):
            s.op("vector", lambda e, m=m, n=n: e.tensor_copy(out=bmrep[:, n, :, :], in_=bmf[:, m, :].unsqueeze(1).to_broadcast([128, 4, 128])),
                 reads=["R"], writes=["bmr"])
        s.op("vector", lambda e: e.tensor_copy(out=bm1[:], in_=bmf), reads=["R"], writes=["bmr"])
        bcm = lambda m: bm1[:, m, :].unsqueeze(1).to_broadcast([128, 4, 128])
        identr_b, bd16 = bmrep[:, 0], bmrep[:, 1]
        masku = bcm(1)
        ident_b = bm1[:, 0, :]
        ones_b = sb("ones_b", [128, 128])
        s.op("vector", lambda e: e.tensor_copy(out=ones_b[:], in_=cf[:, 1, :]), reads=["cf"], writes=["ones_b"])

        small = {}
        for name, d, shape in (("glanw", glanw_d, [128, 1]), ("gdnnw", gdnnw_d, [128, 1]), ("convw", convw_d, [128, 48]),
                               ("alog", alog_d, [128, 4]), ("dtb", dtb_d, [128, 4]), ("cT", cT_d, [128, 8, 2])):
            t = sb(name, shape, F32)
            small[name] = t
            s.dma("sync", lambda e, t=t, d=d: e.dma_start(out=t[:], in_=d), writes=[name], group="c1")
        lnw_b = sb("lnw_b", [128, D], F32)
        lnb_b = sb("lnb_b", [128, D], F32)
        s.dma("sync", lambda e: e.dma_start(out=lnw_b[:], in_=lnw_d.partition_broadcast(128)), writes=["lnw_b"], group="c1")
        s.dma("sync", lambda e: e.dma_start(out=lnb_b[:], in_=lnb_d.partition_broadcast(128)), writes=["lnb_b"], group="c1")
        wgb = sb("wgb", [17, 256])
        s.dma("sync", lambda e: e.dma_start(out=RN[0:17, 0:2, :].rearrange("p a b -> p (a b)"), in_=wg_d), writes=["RN"], group="RN")
        s.op("vector", lambda e: e.tensor_copy(out=wgb[:], in_=RN[0:17, 0:2, :].rearrange("p a b -> p (a b)")), reads=["RN"], writes=["wgb"])
        nega = sb("nega", [128, 4], F32)
        s.op("scalar", lambda e: e.activation(out=nega[:], in_=small["alog"][:], func=AF.Exp), reads=["alog"], writes=["nega"])
        s.op("vector", lambda e: e.tensor_scalar(out=nega[:], in0=nega[:], scalar1=-1.0, scalar2=None, op0=ALU.mult), reads=["nega"], writes=["nega"])
        diag = sb("diag", [128, 48, 128])
        for t in range(48):
            s.op("vector" if t % 2 else "gpsimd",
                 lambda e, t=t: e.tensor_scalar(out=diag[:, t, :], in0=ident_f, scalar1=small["convw"][:, t:t + 1], scalar2=None, op0=ALU.mult),
                 reads=["cf", "convw"], writes=["diag"])

        stgs = [(R[:, :], "R"), (RN[:].rearrange("p a b -> p (a b)"), "RN")]
        Wb = sb("Wb", [128, 8, NCOL])
        Woutb = sb("Woutb", [128, 8, D])
        Sg = [sb("Sg%d" % i, [128, 4, 128], F32) for i in range(2)]
        modp = sb("modp", [128, 32], F32)
        g1b = [sb("g1b%d" % i, [128, D], F32) for i in range(2)]
        mch = Sg[0][0:2, :, :].rearrange("p h c -> p (h c)")
        bch = Sg[1][0:2, :, :].rearrange("p h c -> p (h c)")
        modbanks = [(banks[i], "ps%d" % i) for i in range(6)]
        mpb, mpkey = banks[6], "ps6"
        nst = 0
        for kt in range(8):
            for ch in range(3):
                st, key = stgs[nst % 2]
                nst += 1
                s.dma("sync", lambda e, st=st, kt=kt, ch=ch: e.dma_start(out=st[:, 0:1024], in_=wada_d[kt * 128:(kt + 1) * 128, ch * 1024:(ch + 1) * 1024]), writes=[key], group=key)
                for cgl in range(2):
                    bk, bkey = modbanks[ch * 2 + cgl]
                    s.op("tensor", lambda e, bk=bk, st=st, kt=kt, cgl=cgl: e.matmul(out=bk[0:2, 0:512], lhsT=small["cT"][:, kt, :], rhs=st[:, cgl * 512:(cgl + 1) * 512],
                                                                                 start=(kt == 0), stop=(kt == 7)), reads=[key, "cT"], writes=[bkey])
        for cg in range(6):
            bk, bkey = modbanks[cg]
            s.dma("sync", lambda e, cg=cg: e.dma_start(out=bch, in_=bada_d[:, cg * 512:(cg + 1) * 512].partition_broadcast(2)), writes=["Sg1"], group="bch")
            s.op("vector", lambda e, bk=bk: e.tensor_tensor(out=mch, in0=bk[0:2, 0:512], in1=bch, op=ALU.add), reads=[bkey, "Sg1"], writes=["Sg0"])
            if cg >= 2:
                s.op("vector", lambda e: e.tensor_scalar(out=mch, in0=mch, scalar1=1.0, scalar2=None, op0=ALU.add), reads=["Sg0"], writes=["Sg0"])
            if cg < 4:
                for j in range(4):
                    jn = cg * 4 + j
                    s.op("tensor", lambda e, jn=jn, j=j: e.matmul(out=mpb[:, jn * 2:(jn + 1) * 2], lhsT=mch[:, j * 128:(j + 1) * 128], rhs=cf[0:2, 0, 0:2], start=True, stop=True),
                         reads=["Sg0", "cf"], writes=[mpkey])
            else:
                half = cg - 4
                for sq in range(2):
                    b2, b2key = modbanks[sq]
                    s.op("tensor", lambda e, b2=b2, sq=sq: e.matmul(out=b2[:, 0:512], lhsT=sel[0:2, sq, :], rhs=mch, start=True, stop=True), reads=["Sg0", "sel"], writes=[b2key])
                    s.op("vector", lambda e, b2=b2, sq=sq, half=half: e.tensor_copy(out=g1b[sq][:, half * 512:(half + 1) * 512], in_=b2[:, 0:512]), reads=[b2key], writes=["g1b%d" % sq])
        s.op("vector", lambda e: e.tensor_copy(out=modp[:], in_=mpb[:, 0:32]), reads=[mpkey], writes=["modp"])
        for kt in range(8):
            for ch in range(4):
                c0 = ch * 1024
                w = min(1024, NCOL - c0)
                st, key = stgs[nst % 2]
                nst += 1
                s.dma("sync", lambda e, st=st, kt=kt, c0=c0, w=w: e.dma_start(out=st[:, 0:w], in_=win_d[kt * 128:(kt + 1) * 128, c0:c0 + w]), writes=[key], group=key)
                s.op("vector" if nst % 2 else "gpsimd", lambda e, st=st, kt=kt, c0=c0, w=w: e.tensor_copy(out=Wb[:, kt, c0:c0 + w], in_=st[:, 0:w]), reads=[key], writes=["Wb"])
        for kt in range(8):
            st, key = stgs[nst % 2]
            nst += 1
            s.dma("sync", lambda e, st=st, kt=kt: e.dma_start(out=st[:, 0:D], in_=wout_d[kt * 128:(kt + 1) * 128, :]), writes=[key], group=key)
            s.op("vector" if nst % 2 else "gpsimd", lambda e, st=st, kt=kt: e.tensor_copy(out=Woutb[:, kt, :], in_=st[:, 0:D]), reads=[key], writes=["Woutb"])

        def two(name, shape, dt=BF16):
            return [sb("%s_%d" % (name, i), shape, dt) for i in range(2)]
        xt = [sb("xt_%d" % i, [128, D], F32) for i in range(3)]
        SG = [sb("SG_%d" % i, [128, 8, 128]) for i in range(3)]
        OA = two("OA", [128, 4, 128])
        Am = two("Am", [128, 4, 128]); AT = two("AT", [128, 4, 128])
        KW = two("KW", [128, 4, 128]); KD = two("KD", [128, 4, 128]); VB = two("VB", [128, 4, 128])
        QKT = two("QKT", [128, 4, 128]); QDEC = two("QDEC", [128, 4, 128])
        EX = two("EX", [128, 12], F32)
        hT = sb("hT", [128, 8, 128])
        FM = two("FM", [128, 4, 128])
        U = [sb("U%d" % i, [128, 12, 131]) for i in range(2)]
        lraug = two("lraug", [32, 128])
        KVt = two("KVt", [128, 768])
        spt = sb("spt", [128, 256], F32)
        E1 = sb("E1", [128, 2, 128]); E2 = sb("E2", [128, 2, 128])
        ER = sb("ER", [128, 2, 2], F32)
        QT = sb("QT", [128, 2, 128]); KT = sb("KT", [128, 2, 128])
        EK = sb("EK", [128, 256]); KP = sb("KP", [128, 256])
        ATT = sb("ATT", [128, 4, 128])
        SPb = sb("SPb", [128, 4, 128])
        KTz = sb("KTz", [128, 4, 128])
        Sgla = [sb("Sgla%d" % i, [128, 2, 128], F32) for i in range(2)]
        QKV = sb("QKV", [128, 12, 128])
        SQ = sb("SQ", [128, 8, 128])
        RNb = SQ
        QKN = sb("QKN", [128, 8, 128])
        T4 = sb("T4", [128, 4], F32); G4 = sb("G4", [128, 4], F32); AB = two("AB", [128, 8], F32)
        TB = sb("TB", [128, 4], F32); BETA = sb("BETA", [128, 4], F32); LNT = sb("LNT", [128, 4], F32)
        D4 = sb("D4", [128, 4], F32); ND4 = sb("ND4", [128, 4], F32); D4B = sb("D4B", [128, 4], F32); KWs = sb("KWs", [128, 4], F32)
        DG = RN[:, 4:8, :]; DG2 = RN[:, 0:4, :]
        ED = sb("ED", [128, 4, 128]); E1g = sb("E1g", [128, 4, 128]); E2g = sb("E2g", [128, 4, 128])
        INV = sb("INV", [128, 8, 4, 128])
        Pp = [INV[:, 0], INV[:, 1]]; PTp = [INV[:, 2], INV[:, 3]]; Yp = [INV[:, 4], INV[:, 5]]; Zp = [INV[:, 6], INV[:, 7]]
        X1 = sb("X1", [128, 4, 128]); X2 = sb("X2", [128, 4, 128])
        UU = sb("UU", [128, 4, 128], F32)
        WT = X2; VN = X1; OB = sb("OB", [128, 4, 128])
        Sgb = [sb("Sgb%d" % i, [128, 4, 128]) for i in range(2)]
        v8 = lambda a, b: INV[:, a:b].rearrange("p a h c -> p (a h) c")
        SQO, SQOk = v8(0, 2), ["Pp0", "Pp1"]
        TO, TOk = v8(2, 4), ["PTp0", "PTp1"]
        YT, YTk = v8(4, 6), ["Yp0", "Yp1"]
        RNOb, RNObk = v8(6, 8), ["Zp0", "Zp1"]
        RNO = R[:].rearrange("p (a b) -> p a b", a=8)
        JUNK = X1
        ST = sb("ST", [128, 8], F32)
        DBG = None
        if taps:
            try:
                DBG = sb("DBG", [128, 512], F32)
            except AssertionError:
                DBG = None

        for i in range(2):
            s.op("gpsimd", lambda e, i=i: e.memset(Sgla[i][:], 0.0), writes=["Sgla%d" % i])
            s.op("gpsimd", lambda e, i=i: e.memset(Sg[i][:], 0.0), reads=["modp", "g1b0", "g1b1"], writes=["Sg%d" % i])
            s.op("gpsimd", lambda e, i=i: e.memset(Sgb[i][:], 0.0), writes=["Sgb%d" % i])
            s.op("gpsimd", lambda e, i=i: e.memset(U[i][:], 0.0), writes=["U%d" % i])
        for i in range(2):
            s.op("gpsimd", lambda e, i=i: e.memset(lraug[i][:], 1.0), writes=["lraug%d" % i])

        f3 = lambda ap: ap.rearrange("p (h c) -> p h c", h=4)
        bc = lambda ap: ap.unsqueeze(2).to_broadcast([128, 4, 128])

        def tt(eng, out, a, b, op, reads, writes):
            s.op(eng, lambda e: e.tensor_tensor(out=out, in0=a, in1=b, op=op), reads=reads, writes=writes)

        def act(out, in_, func, reads, writes, **kw):
            s.op("scalar", lambda e: e.activation(out=out, in_=in_, func=func, **kw), reads=reads, writes=writes)

        dbg_d = {}

        def dbg(sq, m, name, ap, keys):
            if not taps or DBG is None:
                return
            shp = list(ap.shape)
            P = shp[0]
            w = int(np.prod(shp[1:]))
            if name not in dbg_d:
                dbg_d[name] = nc.dram_tensor("dbg_" + name, [2, n_units, 128, 512], F32, kind="ExternalOutput").ap()
            dst = DBG[0:P, 0:w]
            if len(shp) == 3:
                dst = dst.rearrange("p (h c) -> p h c", h=shp[1])
            s.op("vector", lambda e: e.tensor_copy(out=dst, in_=ap), reads=list(keys), writes=["DBG"])
            s.dma("sync", lambda e: e.dma_start(out=dbg_d[name][sq, m, 0:P, 0:w], in_=DBG[0:P, 0:w]), reads=["DBG"], group="tap")

        def pstage(sq, m, u):
            p, p3 = u % 2, u % 3
            xs, xk = xt[p3], "xt%d" % p3
            pk = lambda n: "%s%d" % (n, p)
            Uk = "U%d" % sq
            s.dma("sync", lambda e: e.dma_start(out=xs[:], in_=x_d[sq, m * 128:(m + 1) * 128, :]), writes=[xk], group=xk)
            for g in range(2):
                bk, bkey = nb_p()
                for c in range(4):
                    kt = g * 4 + c
                    s.op("tensor", lambda e, bk=bk, c=c, kt=kt: e.transpose(out=bk[:, c * 128:(c + 1) * 128], in_=xs[:, kt * 128:(kt + 1) * 128], identity=ident_f),
                         reads=[xk, "cf"], writes=[bkey])
                for c in range(4):
                    kt = g * 4 + c
                    s.op("vector", lambda e, bk=bk, c=c, kt=kt: e.tensor_scalar(out=hT[:, kt, :], in0=bk[:, c * 128:(c + 1) * 128],
                                                                             scalar1=modp[:, 16 + kt * 2 + sq:17 + kt * 2 + sq], scalar2=modp[:, kt * 2 + sq:kt * 2 + sq + 1],
                                                                             op0=ALU.mult, op1=ALU.add), reads=[bkey, "modp"], writes=["hT"])
                yield
            fm_cols = [0, 128, 256, 384] + [1040 + 128 * i for i in range(4)] + [3096 + 128 * i for i in range(4)] + [1552 + 128 * i for i in range(12)]
            for g in range(6):
                bk, bkey = nb_p()
                for c in range(4):
                    c0 = fm_cols[g * 4 + c]
                    for kt in range(8):
                        s.op("tensor", lambda e, bk=bk, c=c, c0=c0, kt=kt: e.matmul(out=bk[:, c * 128:(c + 1) * 128], lhsT=Wb[:, kt, c0:c0 + 128], rhs=hT[:, kt, :],
                                                                                  start=(kt == 0), stop=(kt == 7)), reads=["Wb", "hT"], writes=[bkey])
                    if (c + 1) % PYIELD == 0:
                        yield
                if g == 0:
                    act(FM[p][:], f3(bk[:, :]), AF.Copy, [bkey], [pk("FM")])
                elif g in (1, 2):
                    act(SG[p3][:, (g - 1) * 4:g * 4, :], f3(bk[:, :]), AF.Silu, [bkey], ["SG%d" % p3])
                else:
                    s.op("vector", lambda e, bk=bk, g=g: e.tensor_copy(out=U[sq][:, (g - 3) * 4:(g - 2) * 4, 3:131], in_=f3(bk[:, :])), reads=[bkey], writes=[Uk])
            abk, abkey = nb_p()
            for kt in range(8):
                s.op("tensor", lambda e, kt=kt: e.matmul(out=abk[0:16, 0:128], lhsT=Wb[:, kt, 1024:1040], rhs=hT[:, kt, :], start=(kt == 0), stop=(kt == 7)),
                     reads=["Wb", "hT"], writes=[abkey])
            for kt in range(8):
                s.op("tensor", lambda e, kt=kt: e.matmul(out=abk[:, 128:136], lhsT=hT[:, kt, :], rhs=Wb[:, kt, 3088:3096], start=(kt == 0), stop=(kt == 7)),
                     reads=["Wb", "hT"], writes=[abkey])
            s.op("vector", lambda e: e.tensor_copy(out=lraug[p][0:16, :], in_=abk[0:16, 0:128]), reads=[abkey], writes=[pk("lraug")])
            s.op("vector", lambda e: e.tensor_copy(out=AB[p][:], in_=abk[:, 128:136]), reads=[abkey], writes=[pk("AB")])
            yield
            for part, (c0, w) in enumerate(((256, 512), (768, 256))):
                bk, bkey = nb_p()
                for kt in range(8):
                    s.op("tensor", lambda e, bk=bk, kt=kt, c0=c0, w=w: e.matmul(out=bk[:, 0:w], lhsT=hT[:, kt, :], rhs=Wb[:, kt, c0:c0 + w], start=(kt == 0), stop=(kt == 7)),
                         reads=["Wb", "hT"], writes=[bkey])
                s.op("vector", lambda e, bk=bk, c0=c0, w=w: e.tensor_copy(out=KVt[p][:, c0 - 256:c0 - 256 + w], in_=bk[:, 0:w]), reads=[bkey], writes=[pk("KVt")])
                yield

        def mid(sq, m, u):
            yield from interleave([gla(sq, m, u % 2), gdnpre(sq, m, u % 2)])

        def gla(sq, m, p):
            pk = lambda n: "%s%d" % (n, p)
            Slk = "Sgla%d" % sq
            bk, bkey = nb_gla()
            s.op("tensor", lambda e, bk=bk: e.matmul(out=bk[:, 0:256], lhsT=lraug[p][0:17, :], rhs=wgb[0:17, :], start=True, stop=True), reads=[pk("lraug"), "wgb"], writes=[bkey])
            act(spt[:], bk[:, 0:256], AF.Exp, [bkey], ["spt"], scale=-1.0)
            act(spt[:], spt[:], AF.Ln, ["spt"], ["spt"], bias=1.0)
            yield
            pbk, pbkey = nb_gla()
            for hp in range(2):
                s.op("tensor", lambda e, hp=hp: e.matmul(out=pbk[:, hp * 130:(hp + 1) * 130], lhsT=spt[:, hp * 128:(hp + 1) * 128], rhs=uref[:, :], start=True, stop=True),
                     reads=["spt", "uref"], writes=[pbkey])
            pv = pbk[:, 0:260].rearrange("p (h c) -> p h c", h=2)
            act(E1[:], pv[:, :, 0:128], AF.Exp, [pbkey], ["E1"], bias=float(np.log(0.125)))
            act(E2[:], pv[:, :, 0:128], AF.Exp, [pbkey], ["E2"], scale=-1.0)
            act(ER[:], pv[:, :, 128:130], AF.Exp, [pbkey], ["ER"])
            yield
            tt("vector", QT[:], FM[p][:, 0:2, :], E1[:], ALU.mult, [pk("FM"), "E1"], ["QT"])
            tt("vector", KT[:], FM[p][:, 2:4, :], E2[:], ALU.mult, [pk("FM"), "E2"], ["KT"])
            bk, bkey = nb_gla()
            s.op("tensor", lambda e, bk=bk: e.matmul(out=bk[:, 0:256], lhsT=mgt16, rhs=spt[:, :], start=True, stop=True), reads=["spt", "cf"], writes=[bkey])
            act(EK[:], bk[:, 0:256], AF.Exp, [bkey], ["EK"])
            tt("gpsimd", KP[:], KVt[p][:, 0:256], EK[:], ALU.mult, [pk("KVt"), "EK"], ["KP"])
            yield
            for h in range(4):
                hp, par = h // 2, h % 2
                s.op("vector", lambda e, h=h, hp=hp, par=par: e.tensor_scalar(out=KTz[:, h, :], in0=KT[:, hp, :], scalar1=cf[:, 2, par:par + 1], scalar2=None, op0=ALU.mult),
                     reads=["KT", "cf"], writes=["KTz"])
            yield
            bk, bkey = nb_gla()
            for h in range(4):
                hp, par = h // 2, h % 2
                s.op("tensor", lambda e, bk=bk, h=h, hp=hp, par=par: e.matmul(out=bk[:, h * 128:(h + 1) * 128], lhsT=KTz[:, h, :], rhs=QT[:, hp, :],
                                                                           start=True, stop=True), reads=["KTz", "QT"], writes=[bkey])
            tt("vector", ATT[:], f3(bk[:, :]), masku, ALU.mult, [bkey, "bmr"], ["ATT"])
            for h in range(4):
                hp, par = h // 2, h % 2
                s.op("vector", lambda e, h=h, hp=hp, par=par: e.tensor_scalar(out=SPb[:, h, :], in0=Sgla[sq][:, hp, :], scalar1=ER[:, hp, 0:1], scalar2=cf[:, 2, par:par + 1],
                                                                           op0=ALU.mult, op1=ALU.mult), reads=[Slk, "ER", "cf"], writes=["SPb"])
            yield
            oak, oakey = nb_gla()
            for h in range(4):
                hp, par = h // 2, h % 2
                s.op("tensor", lambda e, h=h, hp=hp, par=par: e.matmul(out=oak[:, h * 128:(h + 1) * 128], lhsT=SPb[:, h, :], rhs=QT[:, hp, :],
                                                                      start=True, stop=False), reads=["SPb", "QT"], writes=[oakey])
                s.op("tensor", lambda e, h=h: e.matmul(out=oak[:, h * 128:(h + 1) * 128], lhsT=KVt[p][:, 256 + h * 128:384 + h * 128], rhs=ATT[:, h, :], start=False, stop=True),
                     reads=[pk("KVt"), "ATT"], writes=[oakey])
            act(OA[p][:], f3(oak[:, :]), AF.Copy, [oakey], [pk("OA")])
            dbg(sq, m, "oa", OA[p][:], [pk("OA")])
            yield
            bk, bkey = nb_gla()
            for h in range(4):
                hp, par = h // 2, h % 2
                s.op("tensor", lambda e, bk=bk, h=h, hp=hp, par=par: e.matmul(out=bk[:, h * 128:(h + 1) * 128], lhsT=KP[:, hp * 128:(hp + 1) * 128],
                                                                           rhs=KVt[p][:, 256 + h * 128:384 + h * 128], start=True, stop=True), reads=["KP", pk("KVt")], writes=[bkey])
            for h in range(4):
                hp, par = h // 2, h % 2
                ps_ = slice(par * 64, (par + 1) * 64)
                s.op("vector", lambda e, bk=bk, h=h, hp=hp, ps_=ps_: e.scalar_tensor_tensor(out=Sgla[sq][ps_, hp, :], in0=Sgla[sq][ps_, hp, :], scalar=ER[ps_, hp, 1:2], in1=bk[ps_, h * 128:(h + 1) * 128],
                                                                                     op0=ALU.mult, op1=ALU.add), reads=[Slk, "ER", bkey], writes=[Slk])
            yield

        def gdnpre(sq, m, p):
            pk = lambda n: "%s%d" % (n, p)
            Uk = "U%d" % sq
            for g3 in range(3):
                bk, bkey = nb_gdn()
                for c in range(4):
                    t = g3 * 4 + c
                    for k in range(4):
                        s.op("tensor", lambda e, bk=bk, c=c, t=t, k=k: e.matmul(out=bk[:, c * 128:(c + 1) * 128], lhsT=diag[:, t * 4 + k, :], rhs=U[sq][:, t, k:k + 128],
                                                                              start=(k == 0), stop=(k == 3)), reads=["diag", Uk], writes=[bkey])
                act(QKV[:, g3 * 4:(g3 + 1) * 4, :], f3(bk[:, :]), AF.Silu, [bkey], ["QKV"])
                yield
            s.op("gpsimd", lambda e: e.tensor_copy(out=U[sq][:, :, 0:3], in_=U[sq][:, :, 128:131]), reads=[Uk], writes=[Uk])
            tt("gpsimd", SQ[:], QKV[:, 0:8, :], QKV[:, 0:8, :], ALU.mult, ["QKV"], ["SQ"])
            tt("vector", T4[:], AB[p][:, 0:4], small["dtb"][:], ALU.add, [pk("AB"), "dtb"], ["T4"])
            act(T4[:], T4[:], AF.Exp, ["T4"], ["T4"])
            act(T4[:], T4[:], AF.Ln, ["T4"], ["T4"], bias=1.0)
            tt("vector", G4[:], T4[:], nega[:], ALU.mult, ["T4", "nega"], ["G4"])
            act(TB[:], AB[p][:, 4:8], AF.Exp, [pk("AB")], ["TB"], scale=-1.0)
            s.op("vector", lambda e: e.tensor_scalar(out=TB[:], in0=TB[:], scalar1=1.0, scalar2=None, op0=ALU.add), reads=["TB"], writes=["TB"])
            s.op("vector", lambda e: e.reciprocal(out=BETA[:], in_=TB[:]), reads=["TB"], writes=["BETA"])
            act(LNT[:], TB[:], AF.Ln, ["TB"], ["LNT"])
            yield
            for g in range(2):
                bk, bkey = nb_gdn()
                for c in range(4):
                    s.op("tensor", lambda e, bk=bk, c=c, g=g: e.matmul(out=bk[:, c * 128:(c + 1) * 128], lhsT=ones_b[:, :], rhs=SQ[:, g * 4 + c, :], start=True, stop=True),
                         reads=["ones_b", "SQ"], writes=[bkey])
                act(RN[:, g * 4:(g + 1) * 4, :], f3(bk[:, :]), AF.Ln, [bkey], ["RN"], bias=1e-6)
                act(RNb[:, g * 4:(g + 1) * 4, :], RN[:, g * 4:(g + 1) * 4, :], AF.Exp, ["RN"], ["SQ"], scale=-0.5, bias=(float(np.log(128.0 ** -0.5)) if g == 0 else 0.0))
                yield
            tt("vector", QKN[:], QKV[:, 0:8, :], RNb[:], ALU.mult, ["QKV", "SQ"], ["QKN"])
            dbk, dbkey = nb_gdn()
            for n, lm in enumerate((uincl, mgt, ones_f)):
                s.op("tensor", lambda e, n=n, lm=lm: e.matmul(out=dbk[:, n * 4:(n + 1) * 4], lhsT=lm, rhs=G4[:, :], start=True, stop=True), reads=["cf", "G4"], writes=[dbkey])
            act(EX[p][:], dbk[:, 0:12], AF.Exp, [dbkey], [pk("EX")])
            s.op("vector", lambda e: e.tensor_copy(out=D4[:], in_=dbk[:, 0:4]), reads=[dbkey], writes=["D4"])
            s.op("vector", lambda e: e.tensor_scalar(out=ND4[:], in0=dbk[:, 0:4], scalar1=-1.0, scalar2=None, op0=ALU.mult), reads=[dbkey], writes=["ND4"])
            tt("vector", D4B[:], D4[:], LNT[:], ALU.subtract, ["D4", "LNT"], ["D4B"])
            tt("vector", KWs[:], BETA[:], EX[p][:, 0:4], ALU.mult, ["BETA", pk("EX")], ["KWs"])
            yield
            for t in range(4):
                s.op("tensor", lambda e, t=t: e.transpose(out=psb[:, t * 128:(t + 1) * 128], in_=QKN[:, 4 + t, :], identity=ident_b), reads=["QKN", "bmr"], writes=["psb"])
            for t in range(4):
                s.op("tensor", lambda e, t=t: e.transpose(out=psb[:, 512 + t * 128:512 + (t + 1) * 128], in_=QKV[:, 8 + t, :], identity=ident_b), reads=["QKV", "bmr"], writes=["psb"])
            tt("vector", KW[p][:], f3(psb[:, 0:512]), bc(KWs[:]), ALU.mult, ["psb", "KWs"], [pk("KW")])
            tt("vector", KD[p][:], f3(psb[:, 0:512]), bc(EX[p][:, 4:8]), ALU.mult, ["psb", pk("EX")], [pk("KD")])
            tt("vector", VB[p][:], f3(psb[:, 512:1024]), bc(BETA[:]), ALU.mult, ["psb", "BETA"], [pk("VB")])
            yield
            tt("gpsimd", DG, identr_b, bc(D4[:]), ALU.mult, ["bmr", "D4"], ["RN"])
            tt("gpsimd", DG2, identr_b, bc(D4B[:]), ALU.mult, ["bmr", "D4B"], ["RN"])
            dgf = DG.rearrange("p h c -> p (h c)")
            dg2f = DG2.rearrange("p h c -> p (h c)")
            bk, bkey = nb_gdn()
            s.op("tensor", lambda e, bk=bk: e.matmul(out=bk[:, :], lhsT=ones_f, rhs=dgf, start=True, stop=True), reads=["cf", "RN"], writes=[bkey])
            act(ED[:], f3(bk[:, :]), AF.Exp, [bkey], ["ED"])
            yield
            for (src, srck, msk, dst, dstk) in ((dgf, "RN", mt_incl, E1g, "E1g"), (dg2f, "RN", mt_strict, E2g, "E2g")):
                bk, bkey = nb_gdn()
                for h in range(4):
                    s.op("tensor", lambda e, bk=bk, src=src, h=h: e.matmul(out=bk[:, h * 128:(h + 1) * 128], lhsT=ones_f, rhs=src[:, h * 128:(h + 1) * 128], start=True, stop=False),
                         reads=["cf", srck], writes=[bkey])
                    s.op("tensor", lambda e, bk=bk, h=h, msk=msk: e.matmul(out=bk[:, h * 128:(h + 1) * 128], lhsT=ident_f, rhs=msk, start=False, stop=True), reads=["cf"], writes=[bkey])
                for h in range(4):
                    act(dst[:, h, :], bk[:, h * 128:(h + 1) * 128], AF.Exp, [bkey, "ND4"], [dstk], bias=ND4[:, h:h + 1])
                yield
            tt("gpsimd", QDEC[p][:], QKN[:, 0:4, :], ED[:], ALU.mult, ["QKN", "ED"], [pk("QDEC")])
            gk, gkey = nb_gdn()
            for h in range(4):
                s.op("tensor", lambda e, h=h: e.matmul(out=gk[:, h * 128:(h + 1) * 128], lhsT=QKN[:, 4 + h, :], rhs=QKN[:, 4 + h, :], start=True, stop=True), reads=["QKN"], writes=[gkey])
            tt("vector", AT[p][:], f3(gk[:, :]), E2g[:], ALU.mult, [gkey, "E2g"], [pk("AT")])
            qk, qkey = nb_gdn()
            for h in range(4):
                s.op("tensor", lambda e, h=h: e.matmul(out=qk[:, h * 128:(h + 1) * 128], lhsT=QKN[:, 4 + h, :], rhs=QKN[:, h, :], start=True, stop=True), reads=["QKN"], writes=[qkey])
            tt("vector", QKT[p][:], f3(qk[:, :]), E1g[:], ALU.mult, [qkey, "E1g"], [pk("QKT")])
            yield
            for h in range(4):
                s.op("tensor", lambda e, h=h: e.transpose(out=psb[:, h * 128:(h + 1) * 128], in_=AT[p][:, h, :], identity=ident_b), reads=[pk("AT"), "bmr"], writes=["psb"])
            act(Am[p][:], f3(psb[:, 0:512]), AF.Copy, ["psb"], [pk("Am")])
            yield

        def back(sq, m, u):
            p, p3 = u % 2, u % 3
            xs, xk = xt[p3], "xt%d" % p3
            pk = lambda n: "%s%d" % (n, p)
            Sgk, Sgbk = "Sg%d" % sq, "Sgb%d" % sq
            A_, Ak, AT_, ATk = Am[p], pk("Am"), AT[p], pk("AT")

            def mm4(L, Lk, Rt, Rk):
                bk, bkey = nb_back()
                for h in range(4):
                    s.op("tensor", lambda e, bk=bk, h=h: e.matmul(out=bk[:, h * 128:(h + 1) * 128], lhsT=L[:, h, :], rhs=Rt[:, h, :], start=True, stop=True),
                         reads=[Lk, Rk], writes=[bkey])
                return bk, bkey
            tt("gpsimd", Pp[0], A_[:], bd16, ALU.mult, [Ak, "bmr"], ["Pp0"])
            tt("gpsimd", PTp[0], AT_[:], bd16, ALU.mult, [ATk, "bmr"], ["PTp0"])
            tt("gpsimd", Yp[0], identr_b, Pp[0], ALU.subtract, ["bmr", "Pp0"], ["Yp0"])
            tt("gpsimd", Zp[0], identr_b, PTp[0], ALU.subtract, ["bmr", "PTp0"], ["Zp0"])
            yield
            cur = 0
            yz = 0
            for lvl in range(3):
                nx = 1 - cur
                b1, k1 = mm4(PTp[cur], "PTp%d" % cur, Pp[cur], "Pp%d" % cur)
                b2, k2 = mm4(Pp[cur], "Pp%d" % cur, PTp[cur], "PTp%d" % cur)
                act(Pp[nx], f3(b1[:, :]), AF.Copy, [k1], ["Pp%d" % nx])
                s.op("vector", lambda e, b2=b2, nx=nx: e.tensor_copy(out=PTp[nx], in_=f3(b2[:, :])), reads=[k2], writes=["PTp%d" % nx])
                yield
                b3, k3 = mm4(PTp[nx], "PTp%d" % nx, Yp[yz], "Yp%d" % yz)
                b4, k4 = mm4(Pp[nx], "Pp%d" % nx, Zp[yz], "Zp%d" % yz)
                tt("vector", Yp[1 - yz], f3(b3[:, :]), Yp[yz], ALU.add, [k3, "Yp%d" % yz], ["Yp%d" % (1 - yz)])
                tt("vector", Zp[1 - yz], f3(b4[:, :]), Zp[yz], ALU.add, [k4, "Zp%d" % yz], ["Zp%d" % (1 - yz)])
                yield
                cur = nx
                yz = 1 - yz
            for (mi, both) in ((3, True), (5, True), (7, False)):
                Yc, Zc, Yk, Zk = Yp[yz], Zp[yz], "Yp%d" % yz, "Zp%d" % yz
                Yn, Zn, Ynk, Znk = Yp[1 - yz], Zp[1 - yz], "Yp%d" % (1 - yz), "Zp%d" % (1 - yz)
                mT = bcm(mi + 1) if both else bcm(mi)
                b1, k1 = mm4(A_, Ak, Zc, Zk)
                if both:
                    b3, k3 = mm4(AT_, ATk, Yc, Yk)
                tt("vector", X1[:], f3(b1[:, :]), mT, ALU.mult, [k1, "bmr"], ["X1"])
                if both:
                    tt("vector", X2[:], f3(b3[:, :]), bcm(mi), ALU.mult, [k3, "bmr"], ["X2"])
                yield
                b2, k2 = mm4(Yc, Yk, X1, "X1")
                if both:
                    b4, k4 = mm4(Zc, Zk, X2, "X2")
                tt("vector", Zn, Zc, f3(b2[:, :]), ALU.subtract, [Zk, k2], [Znk])
                if both:
                    tt("vector", Yn, Yc, f3(b4[:, :]), ALU.subtract, [Yk, k4], [Ynk])
                yield
                yz = 1 - yz
            TT, TTk = Zp[yz], "Zp%d" % yz
            dbg(sq, m, "TT", TT, [TTk])
            b1, k1 = mm4(TT, TTk, VB[p], pk("VB"))
            b2, k2 = mm4(KW[p], pk("KW"), TT, TTk)
            act(UU[:], f3(b1[:, :]), AF.Copy, [k1], ["UU"])
            act(WT[:], f3(b2[:, :]), AF.Copy, [k2], ["X2"])
            yield
            b3, k3 = mm4(WT, "X2", Sgb[sq], Sgbk)
            tt("vector", VN[:], UU[:], f3(b3[:, :]), ALU.subtract, ["UU", k3], ["X1"])
            yield
            obk, obkey = nb_back()
            for h in range(4):
                s.op("tensor", lambda e, h=h: e.matmul(out=obk[:, h * 128:(h + 1) * 128], lhsT=Sgb[sq][:, h, :], rhs=QDEC[p][:, h, :], start=True, stop=False), reads=[Sgbk, pk("QDEC")], writes=[obkey])
                s.op("tensor", lambda e, h=h: e.matmul(out=obk[:, h * 128:(h + 1) * 128], lhsT=VN[:, h, :], rhs=QKT[p][:, h, :], start=False, stop=True), reads=["X1", pk("QKT")], writes=[obkey])
            act(OB[:], f3(obk[:, :]), AF.Copy, [obkey], ["OB"])
            b4, k4 = mm4(KD[p], pk("KD"), VN, "X1")
            for h in range(4):
                s.op("vector", lambda e, h=h: e.scalar_tensor_tensor(out=Sg[sq][:, h, :], in0=Sg[sq][:, h, :], scalar=EX[p][:, 8 + h:9 + h], in1=b4[:, h * 128:(h + 1) * 128],
                                                                     op0=ALU.mult, op1=ALU.add), reads=[Sgk, pk("EX"), k4], writes=[Sgk])
            s.op("gpsimd", lambda e: e.tensor_copy(out=Sgb[sq][:], in_=Sg[sq][:]), reads=[Sgk], writes=[Sgbk])
            dbg(sq, m, "ob", OB[:], ["OB"])
            yield
            for g, (osb, okey) in enumerate(((OA[p], pk("OA")), (OB, "OB"))):
                sl = slice(g * 4, (g + 1) * 4)
                tt("gpsimd", SQO[:, sl, :], osb[:], osb[:], ALU.mult, [okey], SQOk)
                bk, bkey = nb_back()
                for c in range(4):
                    s.op("tensor", lambda e, bk=bk, c=c, g=g: e.matmul(out=bk[:, c * 128:(c + 1) * 128], lhsT=ones_b[:, :], rhs=SQO[:, g * 4 + c, :], start=True, stop=True),
                         reads=["ones_b"] + SQOk, writes=[bkey])
                act(RNO[:, sl, :], f3(bk[:, :]), AF.Ln, [bkey], ["R"], bias=128e-6)
                act(RNOb[:, sl, :], RNO[:, sl, :], AF.Exp, ["R"], RNObk, scale=-0.5, bias=float(0.5 * np.log(128.0)))
                tt("gpsimd", TO[:, sl, :], osb[:], RNOb[:, sl, :], ALU.mult, [okey] + RNObk, TOk)
                nw = small["glanw"] if g == 0 else small["gdnnw"]
                s.op("vector", lambda e, sl=sl, nw=nw: e.scalar_tensor_tensor(out=YT[:, sl, :], in0=TO[:, sl, :], scalar=nw[:, 0:1], in1=SG[p3][:, sl, :],
                                                                           op0=ALU.mult, op1=ALU.mult), reads=TOk + ["SG%d" % p3, "glanw", "gdnnw"], writes=YTk)
                yield
            for half in range(2):
                bk, bkey = nb_back()
                for kt in range(8):
                    s.op("tensor", lambda e, bk=bk, kt=kt, half=half: e.matmul(out=bk[:, :], lhsT=YT[:, kt, :], rhs=Woutb[:, kt, half * 512:(half + 1) * 512], start=(kt == 0), stop=(kt == 7)),
                         reads=YTk + ["Woutb"], writes=[bkey])
                tt("vector", R[:, half * 512:(half + 1) * 512], bk[:, :], g1b[sq][:, half * 512:(half + 1) * 512], ALU.mult, [bkey, "g1b%d" % sq], ["R"])
                yield
            s.op("vector", lambda e: e.scalar_tensor_tensor(out=R[:], in0=xs[:], scalar=ALPHA, in1=R[:], op0=ALU.mult, op1=ALU.add), reads=[xk, "R"], writes=["R"])
            s.op("gpsimd", lambda e: e.memset(ST[:], 0.0), writes=["ST"])
            jv = lambda t: t[:].rearrange("p h c -> p (h c)")
            for hf in range(2):
                act(jv(X1), R[:, hf * 512:(hf + 1) * 512], AF.Copy, ["R", "ST"], ["X1", "ST"], accum_out=ST[:, hf:hf + 1])
                act(jv(X1), R[:, hf * 512:(hf + 1) * 512], AF.Square, ["R", "ST"], ["X1", "ST"], accum_out=ST[:, 2 + hf:3 + hf])
            yield
            tt("vector", ST[:, 4:5], ST[:, 0:1], ST[:, 1:2], ALU.add, ["ST"], ["ST"])
            tt("vector", ST[:, 5:6], ST[:, 2:3], ST[:, 3:4], ALU.add, ["ST"], ["ST"])
            s.op("vector", lambda e: e.tensor_scalar(out=ST[:, 4:5], in0=ST[:, 4:5], scalar1=1.0 / D, scalar2=None, op0=ALU.mult), reads=["ST"], writes=["ST"])
            tt("vector", ST[:, 6:7], ST[:, 4:5], ST[:, 4:5], ALU.mult, ["ST"], ["ST"])
            s.op("vector", lambda e: e.scalar_tensor_tensor(out=ST[:, 7:8], in0=ST[:, 5:6], scalar=1.0 / D, in1=ST[:, 6:7], op0=ALU.mult, op1=ALU.subtract), reads=["ST"], writes=["ST"])
            act(ST[:, 7:8], ST[:, 7:8], AF.Ln, ["ST"], ["ST"], bias=1e-5)
            act(ST[:, 7:8], ST[:, 7:8], AF.Exp, ["ST"], ["ST"], scale=-0.5)
            s.op("vector", lambda e: e.scalar_tensor_tensor(out=ST[:, 6:7], in0=ST[:, 4:5], scalar=-1.0, in1=ST[:, 7:8], op0=ALU.mult, op1=ALU.mult), reads=["ST"], writes=["ST"])
            act(R[:], R[:], AF.Identity, ["R", "ST"], ["R"], scale=ST[:, 7:8], bias=ST[:, 6:7])
            tt("vector", R[:], R[:], lnw_b[:], ALU.mult, ["R", "lnw_b"], ["R"])
            tt("gpsimd", R[:], R[:], lnb_b[:], ALU.add, ["R", "lnb_b"], ["R"])
            s.dma("sync", lambda e: e.dma_start(out=out_d[sq, m * 128:(m + 1) * 128, :], in_=R[:]), reads=["R"], group="out")
            yield

        units = [(sq, m) for m in range(n_units) for sq in range(2)]
        NUu = len(units)
        for it in range(NUu + 2):
            gd = {}
            if 0 <= it - 2 < NUu:
                gd["b"] = back(units[it - 2][0], units[it - 2][1], it - 2)
            if 0 <= it - 1 < NUu:
                gd["m"] = mid(units[it - 1][0], units[it - 1][1], it - 1)
            if it < NUu:
                gd["p"] = pstage(units[it][0], units[it][1], it)
            run(interleave([gd[k] for k in ORDER if k in gd]))
        fw = ["out"] + (["tap"] if (taps and "tap" in s.dma_groups) else [])
        cnt = s.emit(final_wait_groups=fw)
        print("ops", len(s.ops), "signals", cnt)
    return nc


def make_inputs_for_core(core, inputs, consts, n_units):
    T = n_units * 128
    cfa, urefa, bma, sela = consts
    f = lambda a: np.ascontiguousarray(np.asarray(a, dtype=np.float32))
    xs = f(inputs["x"][2 * core:2 * core + 2, :T])
    c2 = f(inputs["c"][2 * core:2 * core + 2])
    cT = np.ascontiguousarray(c2.reshape(2, 8, 128).transpose(2, 1, 0))
    convw = f(inputs["gdn_conv_w"][0])
    convw_l = np.ascontiguousarray(convw.reshape(4, 12, 128).transpose(2, 1, 0).reshape(128, 48))
    return dict(
        x=xs, cT=cT, w_ada=f(inputs["w_ada"][0]), b_ada=f(inputs["b_ada"]), w_in=f(inputs["w_in"][0]),
        wg_aug=np.ascontiguousarray(np.concatenate([f(inputs["gla_w_gate_up"][0]), f(inputs["gla_b_gate"])], axis=0)),
        gla_nw=f(inputs["gla_norm_w"][0]).reshape(128, 1), gdn_nw=f(inputs["gdn_norm_w"][0]).reshape(128, 1),
        convw=convw_l,
        alog_b=np.ascontiguousarray(np.broadcast_to(f(inputs["gdn_a_log"]), (128, 4))),
        dtb_b=np.ascontiguousarray(np.broadcast_to(f(inputs["gdn_dt_bias"]), (128, 4))),
        w_out=f(inputs["w_out"][0]), ln_w=f(inputs["ln_w"]), ln_b=f(inputs["ln_b"]),
        cf=cfa, uref=urefa, bm=bma, sel=sela)


def kernel(**inputs):
    n_units = 32
    consts = host_consts()
    nc = build(n_units)
    in_maps = [make_inputs_for_core(c, inputs, consts, n_units) for c in range(8)]
    res = run_bass_kernel_spmd(nc, in_maps, core_ids=list(range(8)))
    out = np.concatenate([r["out"] for r in res.results], axis=0)
    return out.astype(np.float32)
```

```python
import bisect
import contextlib
import os
import numpy as np
import concourse.bass as bass
import concourse.mybir as mybir
from concourse.bass_utils import run_bass_kernel_spmd

F32 = mybir.dt.float32
BF16 = mybir.dt.bfloat16
AF = mybir.ActivationFunctionType
ALU = mybir.AluOpType

ENGINES = ("tensor", "vector", "scalar", "gpsimd", "sync")
D = 1024
NCOL = 3608
ALPHA = 2.0 ** 0.25
NEG = -30000.0


class Sched:
    def __init__(self, nc):
        self.nc = nc
        self.ops = []
        self.last_writer = {}
        self.readers = {}
        self.dma_groups = {}

    def _add(self, eng, fn, reads, writes, dma_group=None):
        idx = len(self.ops)
        deps = set()
        for r in reads:
            w = self.last_writer.get(r)
            if w is not None:
                deps.add((w, "raw"))
        for r in writes:
            w = self.last_writer.get(r)
            if w is not None:
                deps.add((w, "waw"))
            for rd in self.readers.get(r, ()):
                deps.add((rd, "war"))
        for r in reads:
            self.readers.setdefault(r, []).append(idx)
        for r in writes:
            self.last_writer[r] = idx
            self.readers[r] = []
        op = dict(eng=eng, fn=fn, deps=deps, dma_group=dma_group, signal=False, seq=None)
        if dma_group is not None:
            c = self.dma_groups.get(dma_group, 0) + 1
            self.dma_groups[dma_group] = c
            op["seq"] = c
            op["signal"] = True
        self.ops.append(op)
        return idx

    def op(self, eng, fn, reads=(), writes=()):
        return self._add(eng, fn, tuple(reads), tuple(writes))

    def dma(self, eng, fn, reads=(), writes=(), group=None):
        return self._add(eng, fn, tuple(reads), tuple(writes), dma_group=group)

    def emit(self, final_wait_groups=()):
        nc = self.nc
        ops = self.ops
        for i, op in enumerate(ops):
            nd = set()
            for (p, kind) in op["deps"]:
                pop = ops[p]
                if pop["dma_group"] is None and op["dma_group"] is None and pop["eng"] == op["eng"]:
                    if op["eng"] == "tensor" or kind != "raw":
                        continue
                nd.add(p)
            best = {}
            keep = set()
            for p in nd:
                pop = ops[p]
                if pop["dma_group"] is not None:
                    keep.add(p)
                elif p > best.get(pop["eng"], -1):
                    best[pop["eng"]] = p
            keep.update(best.values())
            op["ndeps"] = keep
            for p in keep:
                ops[p]["signal"] = True
        cnt = {e: 0 for e in ENGINES}
        for op in ops:
            if op["dma_group"] is None and op["signal"]:
                cnt[op["eng"]] += 1
                op["seq"] = cnt[op["eng"]]
        stack = contextlib.ExitStack()
        esem = {e: stack.enter_context(nc.semaphore("s_" + e)) for e in ENGINES if cnt[e] > 0}
        dsem = {g: stack.enter_context(nc.semaphore("d_%d" % k)) for k, g in enumerate(self.dma_groups)}
        per_eng = {e: [] for e in ENGINES}
        grp_idx = {}
        for i, op in enumerate(ops):
            per_eng[op["eng"]].append(i)
            if op["dma_group"] is not None:
                grp_idx.setdefault(op["dma_group"], []).append(i)

        def run_engine(ename, eobj):
            waited = {}
            for i in per_eng[ename]:
                op = ops[i]
                need = {}
                for p in op["ndeps"]:
                    pop = ops[p]
                    if pop["dma_group"] is not None:
                        key = ("d", pop["dma_group"])
                        val = 16 * bisect.bisect_left(grp_idx[pop["dma_group"]], i)
                    else:
                        key = ("e", pop["eng"])
                        val = pop["seq"]
                    if val > need.get(key, 0):
                        need[key] = val
                for key, val in need.items():
                    if waited.get(key, 0) >= val:
                        continue
                    waited[key] = val
                    sem = dsem[key[1]] if key[0] == "d" else esem[key[1]]
                    eobj.wait_ge(sem, val)
                ins = op["fn"](eobj)
                if op["dma_group"] is not None:
                    ins.then_inc(dsem[op["dma_group"]], 16)
                elif op["signal"]:
                    ins.then_inc(esem[ename], 1)
            if ename == "sync":
                for g in final_wait_groups:
                    eobj.wait_ge(dsem[g], 16 * self.dma_groups[g])

        with stack:
            with nc.Block() as block:
                for ename in ENGINES:
                    getattr(block, ename)(lambda eobj, ename=ename: run_engine(ename, eobj))
        return cnt


def host_consts():
    j = np.arange(128)[:, None]
    i = np.arange(128)[None, :]
    f = lambda m: m.astype(np.float32)
    cf = np.zeros((128, 9, 128), np.float32)
    cf[:, 0] = np.eye(128)
    cf[:, 1] = 1.0
    cf[:, 2] = 0.0
    cf[:64, 2, 0] = 1.0
    cf[64:, 2, 1] = 1.0
    cf[:, 4] = -(1.0 / 16.0) * f(j > i)
    cf[:, 5] = f(j <= i)
    cf[:, 6] = f(j > i)
    cf[:, 7] = np.where(i >= j, 0.0, NEG)
    cf[:, 8] = np.where(i > j, 0.0, NEG)
    uref = np.zeros((128, 130), np.float32)
    uref[:, :128] = -(1.0 / 16.0) * (f(j <= i) - f(j <= 63))
    uref[:, 128] = -(1.0 / 16.0) * f(j[:, 0] <= 63)
    uref[:, 129] = -(1.0 / 16.0)
    bm = np.zeros((128, 8, 128), np.float32)
    bm[:, 0] = np.eye(128)
    bm[:, 1] = f(j <= i)
    bm[:, 2] = f((j // 16) == (i // 16))
    for n, s in ((3, 16), (5, 32), (7, 64)):
        low = f(((j // s) % 2 == 1) & ((i // s) == (j // s) - 1))
        if n < 7:
            bm[:, n] = low
            bm[:, n + 1] = low.T
        else:
            bm[:, n] = low.T
    sel = np.zeros((2, 2, 128), np.float32)
    sel[0, 0] = 1.0
    sel[1, 1] = 1.0
    return cf, uref, bm, sel


def interleave(gens):
    gens = list(gens)
    while gens:
        for g in list(gens):
            if g not in gens:
                continue
            try:
                next(g)
            except StopIteration:
                gens = [x for x in gens if x is not g]
        yield


def run(gen):
    for _ in gen:
        pass


PYIELD = int(os.environ.get("K_PYIELD", "2"))
ORDER = os.environ.get("K_ORDER", "bmp")
PDELAY = int(os.environ.get("K_PDELAY", "0"))
BANKSETS = {"a": ([0], [1], [2, 3], [4, 5, 6]), "b": ([0, 1], [2], [3, 4], [5, 6])}[os.environ.get("K_BANKS", "b")]


def build(n_units, taps=False, stage=99):
    T = n_units * 128
    nc = bass.Bass("TRN2", target_bir_lowering=False)
    din = lambda name, shape: nc.dram_tensor(name, shape, F32, kind="ExternalInput").ap()
    x_d = din("x", [2, T, D])
    cT_d = din("cT", [128, 8, 2])
    wada_d = din("w_ada", [D, 3 * D])
    bada_d = din("b_ada", [1, 3 * D])
    win_d = din("w_in", [D, NCOL])
    wg_d = din("wg_aug", [17, 256])
    glanw_d = din("gla_nw", [128, 1])
    gdnnw_d = din("gdn_nw", [128, 1])
    convw_d = din("convw", [128, 48])
    alog_d = din("alog_b", [128, 4])
    dtb_d = din("dtb_b", [128, 4])
    wout_d = din("w_out", [D, D])
    lnw_d = din("ln_w", [1, D])
    lnb_d = din("ln_b", [1, D])
    cf_d = din("cf", [128, 9, 128])
    uref_d = din("uref", [128, 130])
    bm_d = din("bm", [128, 8, 128])
    sel_d = din("sel", [2, 2, 128])
    out_d = nc.dram_tensor("out", [2, T, D], F32, kind="ExternalOutput").ap()

    es = contextlib.ExitStack()
    with es:
        def sb(name, shape, dt=BF16):
            return es.enter_context(nc.sbuf_tensor("s_" + name, shape, dt))
        s = Sched(nc)
        banks = [es.enter_context(nc.psum_tensor("ps%d" % i, [128, 512], F32)) for i in range(7)]
        psb = es.enter_context(nc.psum_tensor("psb", [128, 1024], BF16))

        def mkbanks(ids):
            ctr = [0]

            def nb():
                b = ids[ctr[0] % len(ids)]
                ctr[0] += 1
                return banks[b], "ps%d" % b
            return nb
        nb_setup = mkbanks([0, 1, 2, 3, 4, 5, 6])
        _bs = BANKSETS
        nb_p = mkbanks(_bs[0])
        nb_gla = mkbanks(_bs[1])
        nb_gdn = mkbanks(_bs[2])
        nb_back = mkbanks(_bs[3])

        cf = sb("cf", [128, 9, 128], F32)
        uref = sb("uref", [128, 130], F32)
        R = sb("R", [128, D], F32)
        RN = sb("RN", [128, 8, 128], F32)
        bmf = R[:].rearrange("p (a b) -> p a b", a=8)
        sel = sb("sel", [2, 2, 128], F32)
        s.dma("sync", lambda e: e.dma_start(out=cf[:], in_=cf_d), writes=["cf"], group="c0")
        s.dma("sync", lambda e: e.dma_start(out=uref[:], in_=uref_d), writes=["uref"], group="c0")
        s.dma("sync", lambda e: e.dma_start(out=bmf, in_=bm_d), writes=["R"], group="c0")
        s.dma("sync", lambda e: e.dma_start(out=sel[:], in_=sel_d), writes=["sel"], group="c0")
        ident_f, ones_f = cf[:, 0, :], cf[:, 1, :]
        mgt16, uincl, mgt, mt_incl, mt_strict = cf[:, 4, :], cf[:, 5, :], cf[:, 6, :], cf[:, 7, :], cf[:, 8, :]
        bmrep = sb("bmrep", [128, 2, 4, 128])
        bm1 = sb("bm1", [128, 8, 128])
        for n, m in enumerate((0, 2)):
            s.op("vector", lambda e, m=m, n=n: e.tensor_copy(out=bmrep[:, n, :, :], in_=bmf[:, m, :].unsqueeze(1).to_broadcast([128, 4, 128])),
                 reads=["R"], writes=["bmr"])
        s.op("vector", lambda e: e.tensor_copy(out=bm1[:], in_=bmf), reads=["R"], writes=["bmr"])
        bcm = lambda m: bm1[:, m, :].unsqueeze(1).to_broadcast([128, 4, 128])
        identr_b, bd16 = bmrep[:, 0], bmrep[:, 1]
        masku = bcm(1)
        ident_b = bm1[:, 0, :]
        ones_b = sb("ones_b", [128, 128])
        s.op("vector", lambda e: e.tensor_copy(out=ones_b[:], in_=cf[:, 1, :]), reads=["cf"], writes=["ones_b"])

        small = {}
        for name, d, shape in (("glanw", glanw_d, [128, 1]), ("gdnnw", gdnnw_d, [128, 1]), ("convw", convw_d, [128, 48]),
                               ("alog", alog_d, [128, 4]), ("dtb", dtb_d, [128, 4]), ("cT", cT_d, [128, 8, 2])):
            t = sb(name, shape, F32)
            small[name] = t
            s.dma("sync", lambda e, t=t, d=d: e.dma_start(out=t[:], in_=d), writes=[name], group="c1")
        lnw_b = sb("lnw_b", [128, D], F32)
        lnb_b = sb("lnb_b", [128, D], F32)
        s.dma("sync", lambda e: e.dma_start(out=lnw_b[:], in_=lnw_d.partition_broadcast(128)), writes=["lnw_b"], group="c1")
        s.dma("sync", lambda e: e.dma_start(out=lnb_b[:], in_=lnb_d.partition_broadcast(128)), writes=["lnb_b"], group="c1")
        wgb = sb("wgb", [17, 256])
        s.dma("sync", lambda e: e.dma_start(out=RN[0:17, 0:2, :].rearrange("p a b -> p (a b)"), in_=wg_d), writes=["RN"], group="RN")
        s.op("vector", lambda e: e.tensor_copy(out=wgb[:], in_=RN[0:17, 0:2, :].rearrange("p a b -> p (a b)")), reads=["RN"], writes=["wgb"])
        nega = sb("nega", [128, 4], F32)
        s.op("scalar", lambda e: e.activation(out=nega[:], in_=small["alog"][:], func=AF.Exp), reads=["alog"], writes=["nega"])
        s.op("vector", lambda e: e.tensor_scalar(out=nega[:], in0=nega[:], scalar1=-1.0, scalar2=None, op0=ALU.mult), reads=["nega"], writes=["nega"])
        diag = sb("diag", [128, 48, 128])
        for t in range(48):
            s.op("vector" if t % 2 else "gpsimd",
                 lambda e, t=t: e.tensor_scalar(out=diag[:, t, :], in0=ident_f, scalar1=small["convw"][:, t:t + 1], scalar2=None, op0=ALU.mult),
                 reads=["cf", "convw"], writes=["diag"])

        stgs = [(R[:, :], "R"), (RN[:].rearrange("p a b -> p (a b)"), "RN")]
        Wb = sb("Wb", [128, 8, NCOL])
        Woutb = sb("Woutb", [128, 8, D])
        Sg = [sb("Sg%d" % i, [128, 4, 128], F32) for i in range(2)]
        modp = sb("modp", [128, 32], F32)
        g1b = [sb("g1b%d" % i, [128, D], F32) for i in range(2)]
        mch = Sg[0][0:2, :, :].rearrange("p h c -> p (h c)")
        bch = Sg[1][0:2, :, :].rearrange("p h c -> p (h c)")
        modbanks = [(banks[i], "ps%d" % i) for i in range(6)]
        mpb, mpkey = banks[6], "ps6"
        nst = 0
        for kt in range(8):
            for ch in range(3):
                st, key = stgs[nst % 2]
                nst += 1
                s.dma("sync", lambda e, st=st, kt=kt, ch=ch: e.dma_start(out=st[:, 0:1024], in_=wada_d[kt * 128:(kt + 1) * 128, ch * 1024:(ch + 1) * 1024]), writes=[key], group=key)
                for cgl in range(2):
                    bk, bkey = modbanks[ch * 2 + cgl]
                    s.op("tensor", lambda e, bk=bk, st=st, kt=kt, cgl=cgl: e.matmul(out=bk[0:2, 0:512], lhsT=small["cT"][:, kt, :], rhs=st[:, cgl * 512:(cgl + 1) * 512],
                                                                                 start=(kt == 0), stop=(kt == 7)), reads=[key, "cT"], writes=[bkey])
        for cg in range(6):
            bk, bkey = modbanks[cg]
            s.dma("sync", lambda e, cg=cg: e.dma_start(out=bch, in_=bada_d[:, cg * 512:(cg + 1) * 512].partition_broadcast(2)), writes=["Sg1"], group="bch")
            s.op("vector", lambda e, bk=bk: e.tensor_tensor(out=mch, in0=bk[0:2, 0:512], in1=bch, op=ALU.add), reads=[bkey, "Sg1"], writes=["Sg0"])
            if cg >= 2:
                s.op("vector", lambda e: e.tensor_scalar(out=mch, in0=mch, scalar1=1.0, scalar2=None, op0=ALU.add), reads=["Sg0"], writes=["Sg0"])
            if cg < 4:
                for j in range(4):
                    jn = cg * 4 + j
                    s.op("tensor", lambda e, jn=jn, j=j: e.matmul(out=mpb[:, jn * 2:(jn + 1) * 2], lhsT=mch[:, j * 128:(j + 1) * 128], rhs=cf[0:2, 0, 0:2], start=True, stop=True),
                         reads=["Sg0", "cf"], writes=[mpkey])
            else:
                half = cg - 4
                for sq in range(2):
                    b2, b2key = modbanks[sq]
                    s.op("tensor", lambda e, b2=b2, sq=sq: e.matmul(out=b2[:, 0:512], lhsT=sel[0:2, sq, :], rhs=mch, start=True, stop=True), reads=["Sg0", "sel"], writes=[b2key])
                    s.op("vector", lambda e, b2=b2, sq=sq, half=half: e.tensor_copy(out=g1b[sq][:, half * 512:(half + 1) * 512], in_=b2[:, 0:512]), reads=[b2key], writes=["g1b%d" % sq])
        s.op("vector", lambda e: e.tensor_copy(out=modp[:], in_=mpb[:, 0:32]), reads=[mpkey], writes=["modp"])
        for kt in range(8):
            for ch in range(4):
                c0 = ch * 1024
                w = min(1024, NCOL - c0)
                st, key = stgs[nst % 2]
                nst += 1
                s.dma("sync", lambda e, st=st, kt=kt, c0=c0, w=w: e.dma_start(out=st[:, 0:w], in_=win_d[kt * 128:(kt + 1) * 128, c0:c0 + w]), writes=[key], group=key)
                s.op("vector" if nst % 2 else "gpsimd", lambda e, st=st, kt=kt, c0=c0, w=w: e.tensor_copy(out=Wb[:, kt, c0:c0 + w], in_=st[:, 0:w]), reads=[key], writes=["Wb"])
        for kt in range(8):
            st, key = stgs[nst % 2]
            nst += 1
            s.dma("sync", lambda e, st=st, kt=kt: e.dma_start(out=st[:, 0:D], in_=wout_d[kt * 128:(kt + 1) * 128, :]), writes=[key], group=key)
            s.op("vector" if nst % 2 else "gpsimd", lambda e, st=st, kt=kt: e.tensor_copy(out=Woutb[:, kt, :], in_=st[:, 0:D]), reads=[key], writes=["Woutb"])

        def two(name, shape, dt=BF16):
            return [sb("%s_%d" % (name, i), shape, dt) for i in range(2)]
        xt = [sb("xt_%d" % i, [128, D], F32) for i in range(3)]
        SG = [sb("SG_%d" % i, [128, 8, 128]) for i in range(3)]
        OA = two("OA", [128, 4, 128])
        Am = two("Am", [128, 4, 128]); AT = two("AT", [128, 4, 128])
        KW = two("KW", [128, 4, 128]); KD = two("KD", [128, 4, 128]); VB = two("VB", [128, 4, 128])
        QKT = two("QKT", [128, 4, 128]); QDEC = two("QDEC", [128, 4, 128])
        EX = two("EX", [128, 12], F32)
        hT = sb("hT", [128, 8, 128])
        FM = two("FM", [128, 4, 128])
        U = [sb("U%d" % i, [128, 12, 131]) for i in range(2)]
        lraug = two("lraug", [32, 128])
        KVt = two("KVt", [128, 768])
        spt = sb("spt", [128, 256], F32)
        E1 = sb("E1", [128, 2, 128]); E2 = sb("E2", [128, 2, 128])
        ER = sb("ER", [128, 2, 2], F32)
        QT = sb("QT", [128, 2, 128]); KT = sb("KT", [128, 2, 128])
        EK = sb("EK", [128, 256]); KP = sb("KP", [128, 256])
        ATT = sb("ATT", [128, 4, 128])
        SPb = sb("SPb", [128, 4, 128])
        KTz = sb("KTz", [128, 4, 128])
        Sgla = [sb("Sgla%d" % i, [128, 2, 128], F32) for i in range(2)]
        QKV = sb("QKV", [128, 12, 128])
        SQ = sb("SQ", [128, 8, 128])
        RNb = SQ
        QKN = sb("QKN", [128, 8, 128])
        T4 = sb("T4", [128, 4], F32); G4 = sb("G4", [128, 4], F32); AB = two("AB", [128, 8], F32)
        TB = sb("TB", [128, 4], F32); BETA = sb("BETA", [128, 4], F32); LNT = sb("LNT", [128, 4], F32)
        D4 = sb("D4", [128, 4], F32); ND4 = sb("ND4", [128, 4], F32); D4B = sb("D4B", [128, 4], F32); KWs = sb("KWs", [128, 4], F32)
        DG = RN[:, 4:8, :]; DG2 = RN[:, 0:4, :]
        ED = sb("ED", [128, 4, 128]); E1g = sb("E1g", [128, 4, 128]); E2g = sb("E2g", [128, 4, 128])
        INV = sb("INV", [128, 8, 4, 128])
        Pp = [INV[:, 0], INV[:, 1]]; PTp = [INV[:, 2], INV[:, 3]]; Yp = [INV[:, 4], INV[:, 5]]; Zp = [INV[:, 6], INV[:, 7]]
        X1 = sb("X1", [128, 4, 128]); X2 = sb("X2", [128, 4, 128])
        UU = sb("UU", [128, 4, 128], F32)
        WT = X2; VN = X1; OB = sb("OB", [128, 4, 128])
        Sgb = [sb("Sgb%d" % i, [128, 4, 128]) for i in range(2)]
        v8 = lambda a, b: INV[:, a:b].rearrange("p a h c -> p (a h) c")
        SQO, SQOk = v8(0, 2), ["Pp0", "Pp1"]
        TO, TOk = v8(2, 4), ["PTp0", "PTp1"]
        YT, YTk = v8(4, 6), ["Yp0", "Yp1"]
        RNOb, RNObk = v8(6, 8), ["Zp0", "Zp1"]
        RNO = R[:].rearrange("p (a b) -> p a b", a=8)
        JUNK = X1
        ST = sb("ST", [128, 8], F32)
        DBG = None
        if taps:
            try:
                DBG = sb("DBG", [128, 512], F32)
            except AssertionError:
                DBG = None

        for i in range(2):
            s.op("gpsimd", lambda e, i=i: e.memset(Sgla[i][:], 0.0), writes=["Sgla%d" % i])
            s.op("gpsimd", lambda e, i=i: e.memset(Sg[i][:], 0.0), reads=["modp", "g1b0", "g1b1"], writes=["Sg%d" % i])
            s.op("gpsimd", lambda e, i=i: e.memset(Sgb[i][:], 0.0), writes=["Sgb%d" % i])
            s.op("gpsimd", lambda e, i=i: e.memset(U[i][:], 0.0), writes=["U%d" % i])
        for i in range(2):
            s.op("gpsimd", lambda e, i=i: e.memset(lraug[i][:], 1.0), writes=["lraug%d" % i])

        f3 = lambda ap: ap.rearrange("p (h c) -> p h c", h=4)
        bc = lambda ap: ap.unsqueeze(2).to_broadcast([128, 4, 128])

        def tt(eng, out, a, b, op, reads, writes):
            s.op(eng, lambda e: e.tensor_tensor(out=out, in0=a, in1=b, op=op), reads=reads, writes=writes)

        def act(out, in_, func, reads, writes, **kw):
            s.op("scalar", lambda e: e.activation(out=out, in_=in_, func=func, **kw), reads=reads, writes=writes)

        units = [(sq, m) for m in range(n_units) for sq in range(2)]
        dbg_d = {}

        def dbg(sq, m, name, ap, keys):
            if not taps or DBG is None:
                return
            shp = list(ap.shape)
            P = shp[0]
            w = int(np.prod(shp[1:]))
            if name not in dbg_d:
                dbg_d[name] = nc.dram_tensor("dbg_" + name, [2, n_units, 128, 512], F32, kind="ExternalOutput").ap()
            dst = DBG[0:P, 0:w]
            if len(shp) == 3:
                dst = dst.rearrange("p (h c) -> p h c", h=shp[1])
            s.op("vector", lambda e: e.tensor_copy(out=dst, in_=ap), reads=list(keys), writes=["DBG"])
            s.dma("sync", lambda e: e.dma_start(out=dbg_d[name][sq, m, 0:P, 0:w], in_=DBG[0:P, 0:w]), reads=["DBG"], group="tap")

        def load_x(u):
            sq_, m_ = units[u]
            p3_ = u % 3
            s.dma("sync", lambda e: e.dma_start(out=xt[p3_][:], in_=x_d[sq_, m_ * 128:(m_ + 1) * 128, :]), writes=["xt%d" % p3_], group="xt%d" % p3_)

        def pstage(sq, m, u):
            p, p3 = u % 2, u % 3
            xs, xk = xt[p3], "xt%d" % p3
            pk = lambda n: "%s%d" % (n, p)
            Uk = "U%d" % sq
            for _ in range(PDELAY):
                yield
            for g in range(2):
                bk, bkey = nb_p()
                for c in range(4):
                    kt = g * 4 + c
                    s.op("tensor", lambda e, bk=bk, c=c, kt=kt: e.transpose(out=bk[:, c * 128:(c + 1) * 128], in_=xs[:, kt * 128:(kt + 1) * 128], identity=ident_f),
                         reads=[xk, "cf"], writes=[bkey])
                for c in range(4):
                    kt = g * 4 + c
                    s.op("vector", lambda e, bk=bk, c=c, kt=kt: e.tensor_scalar(out=hT[:, kt, :], in0=bk[:, c * 128:(c + 1) * 128],
                                                                             scalar1=modp[:, 16 + kt * 2 + sq:17 + kt * 2 + sq], scalar2=modp[:, kt * 2 + sq:kt * 2 + sq + 1],
                                                                             op0=ALU.mult, op1=ALU.add), reads=[bkey, "modp"], writes=["hT"])
                yield
            fm_cols = [0, 128, 256, 384] + [1040 + 128 * i for i in range(4)] + [3096 + 128 * i for i in range(4)] + [1552 + 128 * i for i in range(12)]
            for g in range(6):
                bk, bkey = nb_p()
                for c in range(4):
                    c0 = fm_cols[g * 4 + c]
                    for kt in range(8):
                        s.op("tensor", lambda e, bk=bk, c=c, c0=c0, kt=kt: e.matmul(out=bk[:, c * 128:(c + 1) * 128], lhsT=Wb[:, kt, c0:c0 + 128], rhs=hT[:, kt, :],
                                                                                  start=(kt == 0), stop=(kt == 7)), reads=["Wb", "hT"], writes=[bkey])
                    if (c + 1) % PYIELD == 0:
                        yield
                if g == 0:
                    act(FM[p][:], f3(bk[:, :]), AF.Copy, [bkey], [pk("FM")])
                elif g in (1, 2):
                    act(SG[p3][:, (g - 1) * 4:g * 4, :], f3(bk[:, :]), AF.Silu, [bkey], ["SG%d" % p3])
                else:
                    s.op("vector", lambda e, bk=bk, g=g: e.tensor_copy(out=U[sq][:, (g - 3) * 4:(g - 2) * 4, 3:131], in_=f3(bk[:, :])), reads=[bkey], writes=[Uk])
            abk, abkey = nb_p()
            for kt in range(8):
                s.op("tensor", lambda e, kt=kt: e.matmul(out=abk[0:16, 0:128], lhsT=Wb[:, kt, 1024:1040], rhs=hT[:, kt, :], start=(kt == 0), stop=(kt == 7)),
                     reads=["Wb", "hT"], writes=[abkey])
            for kt in range(8):
                s.op("tensor", lambda e, kt=kt: e.matmul(out=abk[:, 128:136], lhsT=hT[:, kt, :], rhs=Wb[:, kt, 3088:3096], start=(kt == 0), stop=(kt == 7)),
                     reads=["Wb", "hT"], writes=[abkey])
            s.op("vector", lambda e: e.tensor_copy(out=lraug[p][0:16, :], in_=abk[0:16, 0:128]), reads=[abkey], writes=[pk("lraug")])
            s.op("vector", lambda e: e.tensor_copy(out=AB[p][:], in_=abk[:, 128:136]), reads=[abkey], writes=[pk("AB")])
            yield
            for part, (c0, w) in enumerate(((256, 512), (768, 256))):
                bk, bkey = nb_p()
                for kt in range(8):
                    s.op("tensor", lambda e, bk=bk, kt=kt, c0=c0, w=w: e.matmul(out=bk[:, 0:w], lhsT=hT[:, kt, :], rhs=Wb[:, kt, c0:c0 + w], start=(kt == 0), stop=(kt == 7)),
                         reads=["Wb", "hT"], writes=[bkey])
                s.op("vector", lambda e, bk=bk, c0=c0, w=w: e.tensor_copy(out=KVt[p][:, c0 - 256:c0 - 256 + w], in_=bk[:, 0:w]), reads=[bkey], writes=[pk("KVt")])
                yield

        def mid(sq, m, u):
            yield from interleave([gla(sq, m, u % 2), gdnpre(sq, m, u % 2)])

        def gla(sq, m, p):
            pk = lambda n: "%s%d" % (n, p)
            Slk = "Sgla%d" % sq
            bk, bkey = nb_gla()
            s.op("tensor", lambda e, bk=bk: e.matmul(out=bk[:, 0:256], lhsT=lraug[p][0:17, :], rhs=wgb[0:17, :], start=True, stop=True), reads=[pk("lraug"), "wgb"], writes=[bkey])
            act(spt[:], bk[:, 0:256], AF.Exp, [bkey], ["spt"], scale=-1.0)
            act(spt[:], spt[:], AF.Ln, ["spt"], ["spt"], bias=1.0)
            yield
            pbk, pbkey = nb_gla()
            for hp in range(2):
                s.op("tensor", lambda e, hp=hp: e.matmul(out=pbk[:, hp * 130:(hp + 1) * 130], lhsT=spt[:, hp * 128:(hp + 1) * 128], rhs=uref[:, :], start=True, stop=True),
                     reads=["spt", "uref"], writes=[pbkey])
            pv = pbk[:, 0:260].rearrange("p (h c) -> p h c", h=2)
            act(E1[:], pv[:, :, 0:128], AF.Exp, [pbkey], ["E1"], bias=float(np.log(0.125)))
            act(E2[:], pv[:, :, 0:128], AF.Exp, [pbkey], ["E2"], scale=-1.0)
            act(ER[:], pv[:, :, 128:130], AF.Exp, [pbkey], ["ER"])
            yield
            tt("vector", QT[:], FM[p][:, 0:2, :], E1[:], ALU.mult, [pk("FM"), "E1"], ["QT"])
            tt("vector", KT[:], FM[p][:, 2:4, :], E2[:], ALU.mult, [pk("FM"), "E2"], ["KT"])
            bk, bkey = nb_gla()
            s.op("tensor", lambda e, bk=bk: e.matmul(out=bk[:, 0:256], lhsT=mgt16, rhs=spt[:, :], start=True, stop=True), reads=["spt", "cf"], writes=[bkey])
            act(EK[:], bk[:, 0:256], AF.Exp, [bkey], ["EK"])
            tt("gpsimd", KP[:], KVt[p][:, 0:256], EK[:], ALU.mult, [pk("KVt"), "EK"], ["KP"])
            yield
            for h in range(4):
                hp, par = h // 2, h % 2
                s.op("vector", lambda e, h=h, hp=hp, par=par: e.tensor_scalar(out=KTz[:, h, :], in0=KT[:, hp, :], scalar1=cf[:, 2, par:par + 1], scalar2=None, op0=ALU.mult),
                     reads=["KT", "cf"], writes=["KTz"])
            yield
            bk, bkey = nb_gla()
            for h in range(4):
                hp, par = h // 2, h % 2
                s.op("tensor", lambda e, bk=bk, h=h, hp=hp, par=par: e.matmul(out=bk[:, h * 128:(h + 1) * 128], lhsT=KTz[:, h, :], rhs=QT[:, hp, :],
                                                                           start=True, stop=True), reads=["KTz", "QT"], writes=[bkey])
            tt("vector", ATT[:], f3(bk[:, :]), masku, ALU.mult, [bkey, "bmr"], ["ATT"])
            for h in range(4):
                hp, par = h // 2, h % 2
                s.op("vector", lambda e, h=h, hp=hp, par=par: e.tensor_scalar(out=SPb[:, h, :], in0=Sgla[sq][:, hp, :], scalar1=ER[:, hp, 0:1], scalar2=cf[:, 2, par:par + 1],
                                                                           op0=ALU.mult, op1=ALU.mult), reads=[Slk, "ER", "cf"], writes=["SPb"])
            yield
            oak, oakey = nb_gla()
            for h in range(4):
                hp, par = h // 2, h % 2
                s.op("tensor", lambda e, h=h, hp=hp, par=par: e.matmul(out=oak[:, h * 128:(h + 1) * 128], lhsT=SPb[:, h, :], rhs=QT[:, hp, :],
                                                                      start=True, stop=False), reads=["SPb", "QT"], writes=[oakey])
                s.op("tensor", lambda e, h=h: e.matmul(out=oak[:, h * 128:(h + 1) * 128], lhsT=KVt[p][:, 256 + h * 128:384 + h * 128], rhs=ATT[:, h, :], start=False, stop=True),
                     reads=[pk("KVt"), "ATT"], writes=[oakey])
            act(OA[p][:], f3(oak[:, :]), AF.Copy, [oakey], [pk("OA")])
            dbg(sq, m, "oa", OA[p][:], [pk("OA")])
            yield
            bk, bkey = nb_gla()
            for h in range(4):
                hp, par = h // 2, h % 2
                s.op("tensor", lambda e, bk=bk, h=h, hp=hp, par=par: e.matmul(out=bk[:, h * 128:(h + 1) * 128], lhsT=KP[:, hp * 128:(hp + 1) * 128],
                                                                           rhs=KVt[p][:, 256 + h * 128:384 + h * 128], start=True, stop=True), reads=["KP", pk("KVt")], writes=[bkey])
            for h in range(4):
                hp, par = h // 2, h % 2
                ps_ = slice(par * 64, (par + 1) * 64)
                s.op("vector", lambda e, bk=bk, h=h, hp=hp, ps_=ps_: e.scalar_tensor_tensor(out=Sgla[sq][ps_, hp, :], in0=Sgla[sq][ps_, hp, :], scalar=ER[ps_, hp, 1:2], in1=bk[ps_, h * 128:(h + 1) * 128],
                                                                                     op0=ALU.mult, op1=ALU.add), reads=[Slk, "ER", bkey], writes=[Slk])
            yield

        def gdnpre(sq, m, p):
            pk = lambda n: "%s%d" % (n, p)
            Uk = "U%d" % sq
            for g3 in range(3):
                bk, bkey = nb_gdn()
                for c in range(4):
                    t = g3 * 4 + c
                    for k in range(4):
                        s.op("tensor", lambda e, bk=bk, c=c, t=t, k=k: e.matmul(out=bk[:, c * 128:(c + 1) * 128], lhsT=diag[:, t * 4 + k, :], rhs=U[sq][:, t, k:k + 128],
                                                                              start=(k == 0), stop=(k == 3)), reads=["diag", Uk], writes=[bkey])
                act(QKV[:, g3 * 4:(g3 + 1) * 4, :], f3(bk[:, :]), AF.Silu, [bkey], ["QKV"])
                yield
            s.op("gpsimd", lambda e: e.tensor_copy(out=U[sq][:, :, 0:3], in_=U[sq][:, :, 128:131]), reads=[Uk], writes=[Uk])
            tt("gpsimd", SQ[:], QKV[:, 0:8, :], QKV[:, 0:8, :], ALU.mult, ["QKV"], ["SQ"])
            tt("vector", T4[:], AB[p][:, 0:4], small["dtb"][:], ALU.add, [pk("AB"), "dtb"], ["T4"])
            act(T4[:], T4[:], AF.Exp, ["T4"], ["T4"])
            act(T4[:], T4[:], AF.Ln, ["T4"], ["T4"], bias=1.0)
            tt("vector", G4[:], T4[:], nega[:], ALU.mult, ["T4", "nega"], ["G4"])
            act(TB[:], AB[p][:, 4:8], AF.Exp, [pk("AB")], ["TB"], scale=-1.0)
            s.op("vector", lambda e: e.tensor_scalar(out=TB[:], in0=TB[:], scalar1=1.0, scalar2=None, op0=ALU.add), reads=["TB"], writes=["TB"])
            s.op("vector", lambda e: e.reciprocal(out=BETA[:], in_=TB[:]), reads=["TB"], writes=["BETA"])
            act(LNT[:], TB[:], AF.Ln, ["TB"], ["LNT"])
            yield
            for g in range(2):
                bk, bkey = nb_gdn()
                for c in range(4):
                    s.op("tensor", lambda e, bk=bk, c=c, g=g: e.matmul(out=bk[:, c * 128:(c + 1) * 128], lhsT=ones_b[:, :], rhs=SQ[:, g * 4 + c, :], start=True, stop=True),
                         reads=["ones_b", "SQ"], writes=[bkey])
                act(RN[:, g * 4:(g + 1) * 4, :], f3(bk[:, :]), AF.Ln, [bkey], ["RN"], bias=1e-6)
                act(RNb[:, g * 4:(g + 1) * 4, :], RN[:, g * 4:(g + 1) * 4, :], AF.Exp, ["RN"], ["SQ"], scale=-0.5, bias=(float(np.log(128.0 ** -0.5)) if g == 0 else 0.0))
                yield
            tt("vector", QKN[:], QKV[:, 0:8, :], RNb[:], ALU.mult, ["QKV", "SQ"], ["QKN"])
            dbk, dbkey = nb_gdn()
            for n, lm in enumerate((uincl, mgt, ones_f)):
                s.op("tensor", lambda e, n=n, lm=lm: e.matmul(out=dbk[:, n * 4:(n + 1) * 4], lhsT=lm, rhs=G4[:, :], start=True, stop=True), reads=["cf", "G4"], writes=[dbkey])
            act(EX[p][:], dbk[:, 0:12], AF.Exp, [dbkey], [pk("EX")])
            s.op("vector", lambda e: e.tensor_copy(out=D4[:], in_=dbk[:, 0:4]), reads=[dbkey], writes=["D4"])
            s.op("vector", lambda e: e.tensor_scalar(out=ND4[:], in0=dbk[:, 0:4], scalar1=-1.0, scalar2=None, op0=ALU.mult), reads=[dbkey], writes=["ND4"])
            tt("vector", D4B[:], D4[:], LNT[:], ALU.subtract, ["D4", "LNT"], ["D4B"])
            tt("vector", KWs[:], BETA[:], EX[p][:, 0:4], ALU.mult, ["BETA", pk("EX")], ["KWs"])
            yield
            for t in range(4):
                s.op("tensor", lambda e, t=t: e.transpose(out=psb[:, t * 128:(t + 1) * 128], in_=QKN[:, 4 + t, :], identity=ident_b), reads=["QKN", "bmr"], writes=["psb"])
            for t in range(4):
                s.op("tensor", lambda e, t=t: e.transpose(out=psb[:, 512 + t * 128:512 + (t + 1) * 128], in_=QKV[:, 8 + t, :], identity=ident_b), reads=["QKV", "bmr"], writes=["psb"])
            tt("vector", KW[p][:], f3(psb[:, 0:512]), bc(KWs[:]), ALU.mult, ["psb", "KWs"], [pk("KW")])
            tt("vector", KD[p][:], f3(psb[:, 0:512]), bc(EX[p][:, 4:8]), ALU.mult, ["psb", pk("EX")], [pk("KD")])
            tt("vector", VB[p][:], f3(psb[:, 512:1024]), bc(BETA[:]), ALU.mult, ["psb", "BETA"], [pk("VB")])
            yield
            tt("gpsimd", DG, identr_b, bc(D4[:]), ALU.mult, ["bmr", "D4"], ["RN"])
            tt("gpsimd", DG2, identr_b, bc(D4B[:]), ALU.mult, ["bmr", "D4B"], ["RN"])
            dgf = DG.rearrange("p h c -> p (h c)")
            dg2f = DG2.rearrange("p h c -> p (h c)")
            bk, bkey = nb_gdn()
            s.op("tensor", lambda e, bk=bk: e.matmul(out=bk[:, :], lhsT=ones_f, rhs=dgf, start=True, stop=True), reads=["cf", "RN"], writes=[bkey])
            act(ED[:], f3(bk[:, :]), AF.Exp, [bkey], ["ED"])
            yield
            for (src, srck, msk, dst, dstk) in ((dgf, "RN", mt_incl, E1g, "E1g"), (dg2f, "RN", mt_strict, E2g, "E2g")):
                bk, bkey = nb_gdn()
                for h in range(4):
                    s.op("tensor", lambda e, bk=bk, src=src, h=h: e.matmul(out=bk[:, h * 128:(h + 1) * 128], lhsT=ones_f, rhs=src[:, h * 128:(h + 1) * 128], start=True, stop=False),
                         reads=["cf", srck], writes=[bkey])
                    s.op("tensor", lambda e, bk=bk, h=h, msk=msk: e.matmul(out=bk[:, h * 128:(h + 1) * 128], lhsT=ident_f, rhs=msk, start=False, stop=True), reads=["cf"], writes=[bkey])
                for h in range(4):
                    act(dst[:, h, :], bk[:, h * 128:(h + 1) * 128], AF.Exp, [bkey, "ND4"], [dstk], bias=ND4[:, h:h + 1])
                yield
            tt("gpsimd", QDEC[p][:], QKN[:, 0:4, :], ED[:], ALU.mult, ["QKN", "ED"], [pk("QDEC")])
            gk, gkey = nb_gdn()
            for h in range(4):
                s.op("tensor", lambda e, h=h: e.matmul(out=gk[:, h * 128:(h + 1) * 128], lhsT=QKN[:, 4 + h, :], rhs=QKN[:, 4 + h, :], start=True, stop=True), reads=["QKN"], writes=[gkey])
            tt("vector", AT[p][:], f3(gk[:, :]), E2g[:], ALU.mult, [gkey, "E2g"], [pk("AT")])
            qk, qkey = nb_gdn()
            for h in range(4):
                s.op("tensor", lambda e, h=h: e.matmul(out=qk[:, h * 128:(h + 1) * 128], lhsT=QKN[:, 4 + h, :], rhs=QKN[:, h, :], start=True, stop=True), reads=["QKN"], writes=[qkey])
            tt("vector", QKT[p][:], f3(qk[:, :]), E1g[:], ALU.mult, [qkey, "E1g"], [pk("QKT")])
            yield
            for h in range(4):
                s.op("tensor", lambda e, h=h: e.transpose(out=psb[:, h * 128:(h + 1) * 128], in_=AT[p][:, h, :], identity=ident_b), reads=[pk("AT"), "bmr"], writes=["psb"])
            act(Am[p][:], f3(psb[:, 0:512]), AF.Copy, ["psb"], [pk("Am")])
            yield

        def back(sq, m, u):
            p, p3 = u % 2, u % 3
            xs, xk = xt[p3], "xt%d" % p3
            pk = lambda n: "%s%d" % (n, p)
            Sgk, Sgbk = "Sg%d" % sq, "Sgb%d" % sq
            A_, Ak, AT_, ATk = Am[p], pk("Am"), AT[p], pk("AT")

            def mm4(L, Lk, Rt, Rk):
                bk, bkey = nb_back()
                for h in range(4):
                    s.op("tensor", lambda e, bk=bk, h=h: e.matmul(out=bk[:, h * 128:(h + 1) * 128], lhsT=L[:, h, :], rhs=Rt[:, h, :], start=True, stop=True),
                         reads=[Lk, Rk], writes=[bkey])
                return bk, bkey
            tt("gpsimd", Pp[0], A_[:], bd16, ALU.mult, [Ak, "bmr"], ["Pp0"])
            tt("gpsimd", PTp[0], AT_[:], bd16, ALU.mult, [ATk, "bmr"], ["PTp0"])
            tt("gpsimd", Yp[0], identr_b, Pp[0], ALU.subtract, ["bmr", "Pp0"], ["Yp0"])
            tt("gpsimd", Zp[0], identr_b, PTp[0], ALU.subtract, ["bmr", "PTp0"], ["Zp0"])
            yield
            act(R[:], xs[:], AF.Copy, [xk], ["R"], scale=ALPHA)
            if u + 3 < len(units):
                load_x(u + 3)
            cur = 0
            yz = 0
            for lvl in range(3):
                nx = 1 - cur
                b1, k1 = mm4(PTp[cur], "PTp%d" % cur, Pp[cur], "Pp%d" % cur)
                b2, k2 = mm4(Pp[cur], "Pp%d" % cur, PTp[cur], "PTp%d" % cur)
                act(Pp[nx], f3(b1[:, :]), AF.Copy, [k1], ["Pp%d" % nx])
                s.op("vector", lambda e, b2=b2, nx=nx: e.tensor_copy(out=PTp[nx], in_=f3(b2[:, :])), reads=[k2], writes=["PTp%d" % nx])
                yield
                b3, k3 = mm4(PTp[nx], "PTp%d" % nx, Yp[yz], "Yp%d" % yz)
                b4, k4 = mm4(Pp[nx], "Pp%d" % nx, Zp[yz], "Zp%d" % yz)
                tt("vector", Yp[1 - yz], f3(b3[:, :]), Yp[yz], ALU.add, [k3, "Yp%d" % yz], ["Yp%d" % (1 - yz)])
                tt("vector", Zp[1 - yz], f3(b4[:, :]), Zp[yz], ALU.add, [k4, "Zp%d" % yz], ["Zp%d" % (1 - yz)])
                yield
                cur = nx
                yz = 1 - yz
            for (mi, both) in ((3, True), (5, True), (7, False)):
                Yc, Zc, Yk, Zk = Yp[yz], Zp[yz], "Yp%d" % yz, "Zp%d" % yz
                Yn, Zn, Ynk, Znk = Yp[1 - yz], Zp[1 - yz], "Yp%d" % (1 - yz), "Zp%d" % (1 - yz)
                mT = bcm(mi + 1) if both else bcm(mi)
                b1, k1 = mm4(A_, Ak, Zc, Zk)
                if both:
                    b3, k3 = mm4(AT_, ATk, Yc, Yk)
                tt("vector", X1[:], f3(b1[:, :]), mT, ALU.mult, [k1, "bmr"], ["X1"])
                if both:
                    tt("vector", X2[:], f3(b3[:, :]), bcm(mi), ALU.mult, [k3, "bmr"], ["X2"])
                yield
                b2, k2 = mm4(Yc, Yk, X1, "X1")
                if both:
                    b4, k4 = mm4(Zc, Zk, X2, "X2")
                tt("vector", Zn, Zc, f3(b2[:, :]), ALU.subtract, [Zk, k2], [Znk])
                if both:
                    tt("vector", Yn, Yc, f3(b4[:, :]), ALU.subtract, [Yk, k4], [Ynk])
                yield
                yz = 1 - yz
            TT, TTk = Zp[yz], "Zp%d" % yz
            dbg(sq, m, "TT", TT, [TTk])
            b1, k1 = mm4(TT, TTk, VB[p], pk("VB"))
            b2, k2 = mm4(KW[p], pk("KW"), TT, TTk)
            act(UU[:], f3(b1[:, :]), AF.Copy, [k1], ["UU"])
            act(WT[:], f3(b2[:, :]), AF.Copy, [k2], ["X2"])
            yield
            b3, k3 = mm4(WT, "X2", Sgb[sq], Sgbk)
            tt("vector", VN[:], UU[:], f3(b3[:, :]), ALU.subtract, ["UU", k3], ["X1"])
            yield
            obk, obkey = nb_back()
            for h in range(4):
                s.op("tensor", lambda e, h=h: e.matmul(out=obk[:, h * 128:(h + 1) * 128], lhsT=Sgb[sq][:, h, :], rhs=QDEC[p][:, h, :], start=True, stop=False), reads=[Sgbk, pk("QDEC")], writes=[obkey])
                s.op("tensor", lambda e, h=h: e.matmul(out=obk[:, h * 128:(h + 1) * 128], lhsT=VN[:, h, :], rhs=QKT[p][:, h, :], start=False, stop=True), reads=["X1", pk("QKT")], writes=[obkey])
            act(OB[:], f3(obk[:, :]), AF.Copy, [obkey], ["OB"])
            b4, k4 = mm4(KD[p], pk("KD"), VN, "X1")
            for h in range(4):
                s.op("vector", lambda e, h=h: e.scalar_tensor_tensor(out=Sg[sq][:, h, :], in0=Sg[sq][:, h, :], scalar=EX[p][:, 8 + h:9 + h], in1=b4[:, h * 128:(h + 1) * 128],
                                                                     op0=ALU.mult, op1=ALU.add), reads=[Sgk, pk("EX"), k4], writes=[Sgk])
            s.op("gpsimd", lambda e: e.tensor_copy(out=Sgb[sq][:], in_=Sg[sq][:]), reads=[Sgk], writes=[Sgbk])
            dbg(sq, m, "ob", OB[:], ["OB"])
            yield
            for g, (osb, okey) in enumerate(((OA[p], pk("OA")), (OB, "OB"))):
                sl = slice(g * 4, (g + 1) * 4)
                tt("gpsimd", SQO[:, sl, :], osb[:], osb[:], ALU.mult, [okey], SQOk)
                bk, bkey = nb_back()
                for c in range(4):
                    s.op("tensor", lambda e, bk=bk, c=c, g=g: e.matmul(out=bk[:, c * 128:(c + 1) * 128], lhsT=ones_b[:, :], rhs=SQO[:, g * 4 + c, :], start=True, stop=True),
                         reads=["ones_b"] + SQOk, writes=[bkey])
                act(UU[:], f3(bk[:, :]), AF.Ln, [bkey], ["UU"], bias=128e-6)
                act(RNOb[:, sl, :], UU[:], AF.Exp, ["UU"], RNObk, scale=-0.5, bias=float(0.5 * np.log(128.0)))
                tt("gpsimd", TO[:, sl, :], osb[:], RNOb[:, sl, :], ALU.mult, [okey] + RNObk, TOk)
                nw = small["glanw"] if g == 0 else small["gdnnw"]
                s.op("vector", lambda e, sl=sl, nw=nw: e.scalar_tensor_tensor(out=YT[:, sl, :], in0=TO[:, sl, :], scalar=nw[:, 0:1], in1=SG[p3][:, sl, :],
                                                                           op0=ALU.mult, op1=ALU.mult), reads=TOk + ["SG%d" % p3, "glanw", "gdnnw"], writes=YTk)
                yield
            for half in range(2):
                bk, bkey = nb_back()
                for kt in range(8):
                    s.op("tensor", lambda e, bk=bk, kt=kt, half=half: e.matmul(out=bk[:, :], lhsT=YT[:, kt, :], rhs=Woutb[:, kt, half * 512:(half + 1) * 512], start=(kt == 0), stop=(kt == 7)),
                         reads=YTk + ["Woutb"], writes=[bkey])
                uuf = UU[:].rearrange("p h c -> p (h c)")
                tt("vector", uuf, bk[:, :], g1b[sq][:, half * 512:(half + 1) * 512], ALU.mult, [bkey, "g1b%d" % sq], ["UU"])
                tt("vector", R[:, half * 512:(half + 1) * 512], R[:, half * 512:(half + 1) * 512], uuf, ALU.add, ["R", "UU"], ["R"])
                yield
            s.op("gpsimd", lambda e: e.memset(ST[:], 0.0), writes=["ST"])
            jv = lambda t: t[:].rearrange("p h c -> p (h c)")
            for hf in range(2):
                act(jv(X1), R[:, hf * 512:(hf + 1) * 512], AF.Copy, ["R", "ST"], ["X1", "ST"], accum_out=ST[:, hf:hf + 1])
                act(jv(X1), R[:, hf * 512:(hf + 1) * 512], AF.Square, ["R", "ST"], ["X1", "ST"], accum_out=ST[:, 2 + hf:3 + hf])
            yield
            tt("vector", ST[:, 4:5], ST[:, 0:1], ST[:, 1:2], ALU.add, ["ST"], ["ST"])
            tt("vector", ST[:, 5:6], ST[:, 2:3], ST[:, 3:4], ALU.add, ["ST"], ["ST"])
            s.op("vector", lambda e: e.tensor_scalar(out=ST[:, 4:5], in0=ST[:, 4:5], scalar1=1.0 / D, scalar2=None, op0=ALU.mult), reads=["ST"], writes=["ST"])
            tt("vector", ST[:, 6:7], ST[:, 4:5], ST[:, 4:5], ALU.mult, ["ST"], ["ST"])
            s.op("vector", lambda e: e.scalar_tensor_tensor(out=ST[:, 7:8], in0=ST[:, 5:6], scalar=1.0 / D, in1=ST[:, 6:7], op0=ALU.mult, op1=ALU.subtract), reads=["ST"], writes=["ST"])
            act(ST[:, 7:8], ST[:, 7:8], AF.Ln, ["ST"], ["ST"], bias=1e-5)
            act(ST[:, 7:8], ST[:, 7:8], AF.Exp, ["ST"], ["ST"], scale=-0.5)
            s.op("vector", lambda e: e.scalar_tensor_tensor(out=ST[:, 6:7], in0=ST[:, 4:5], scalar=-1.0, in1=ST[:, 7:8], op0=ALU.mult, op1=ALU.mult), reads=["ST"], writes=["ST"])
            act(R[:], R[:], AF.Identity, ["R", "ST"], ["R"], scale=ST[:, 7:8], bias=ST[:, 6:7])
            tt("vector", R[:], R[:], lnw_b[:], ALU.mult, ["R", "lnw_b"], ["R"])
            tt("gpsimd", R[:], R[:], lnb_b[:], ALU.add, ["R", "lnb_b"], ["R"])
            s.dma("sync", lambda e: e.dma_start(out=out_d[sq, m * 128:(m + 1) * 128, :], in_=R[:]), reads=["R"], group="out")
            yield

        NUu = len(units)
        for u0 in range(min(3, NUu)):
            load_x(u0)
        for it in range(NUu + 2):
            gd = {}
            if 0 <= it - 2 < NUu:
                gd["b"] = back(units[it - 2][0], units[it - 2][1], it - 2)
            if 0 <= it - 1 < NUu:
                gd["m"] = mid(units[it - 1][0], units[it - 1][1], it - 1)
            if it < NUu:
                gd["p"] = pstage(units[it][0], units[it][1], it)
            run(interleave([gd[k] for k in ORDER if k in gd]))
        fw = ["out"] + (["tap"] if (taps and "tap" in s.dma_groups) else [])
        cnt = s.emit(final_wait_groups=fw)
        print("ops", len(s.ops), "signals", cnt)
    return nc


def make_inputs_for_core(core, inputs, consts, n_units):
    T = n_units * 128
    cfa, urefa, bma, sela = consts
    f = lambda a: np.ascontiguousarray(np.asarray(a, dtype=np.float32))
    xs = f(inputs["x"][2 * core:2 * core + 2, :T])
    c2 = f(inputs["c"][2 * core:2 * core + 2])
    cT = np.ascontiguousarray(c2.reshape(2, 8, 128).transpose(2, 1, 0))
    convw = f(inputs["gdn_conv_w"][0])
    convw_l = np.ascontiguousarray(convw.reshape(4, 12, 128).transpose(2, 1, 0).reshape(128, 48))
    return dict(
        x=xs, cT=cT, w_ada=f(inputs["w_ada"][0]), b_ada=f(inputs["b_ada"]), w_in=f(inputs["w_in"][0]),
        wg_aug=np.ascontiguousarray(np.concatenate([f(inputs["gla_w_gate_up"][0]), f(inputs["gla_b_gate"])], axis=0)),
        gla_nw=f(inputs["gla_norm_w"][0]).reshape(128, 1), gdn_nw=f(inputs["gdn_norm_w"][0]).reshape(128, 1),
        convw=convw_l,
        alog_b=np.ascontiguousarray(np.broadcast_to(f(inputs["gdn_a_log"]), (128, 4))),
        dtb_b=np.ascontiguousarray(np.broadcast_to(f(inputs["gdn_dt_bias"]), (128, 4))),
        w_out=f(inputs["w_out"][0]), ln_w=f(inputs["ln_w"]), ln_b=f(inputs["ln_b"]),
        cf=cfa, uref=urefa, bm=bma, sel=sela)


def kernel(**inputs):
    n_units = 32
    consts = host_consts()
    nc = build(n_units)
    in_maps = [make_inputs_for_core(c, inputs, consts, n_units) for c in range(8)]
    res = run_bass_kernel_spmd(nc, in_maps, core_ids=list(range(8)))
    out = np.concatenate([r["out"] for r in res.results], axis=0)
    return out.astype(np.float32)
```

```python
import bisect
import contextlib
import os
import numpy as np
import concourse.bass as bass
import concourse.mybir as mybir
from concourse.bass_utils import run_bass_kernel_spmd

F32 = mybir.dt.float32
BF16 = mybir.dt.bfloat16
AF = mybir.ActivationFunctionType
ALU = mybir.AluOpType

ENGINES = ("tensor", "vector", "scalar", "gpsimd", "sync")
D = 1024
NCOL = 3608
ALPHA = 2.0 ** 0.25
NEG = -30000.0


class Sched:
    def __init__(self, nc):
        self.nc = nc
        self.ops = []
        self.last_writer = {}
        self.readers = {}
        self.dma_groups = {}

    def _add(self, eng, fn, reads, writes, dma_group=None):
        idx = len(self.ops)
        deps = set()
        for r in reads:
            w = self.last_writer.get(r)
            if w is not None:
                deps.add((w, "raw"))
        for r in writes:
            w = self.last_writer.get(r)
            if w is not None:
                deps.add((w, "waw"))
            for rd in self.readers.get(r, ()):
                deps.add((rd, "war"))
        for r in reads:
            self.readers.setdefault(r, []).append(idx)
        for r in writes:
            self.last_writer[r] = idx
            self.readers[r] = []
        op = dict(eng=eng, fn=fn, deps=deps, dma_group=dma_group, signal=False, seq=None)
        if dma_group is not None:
            c = self.dma_groups.get(dma_group, 0) + 1
            self.dma_groups[dma_group] = c
            op["seq"] = c
            op["signal"] = True
        self.ops.append(op)
        return idx

    def op(self, eng, fn, reads=(), writes=()):
        return self._add(eng, fn, tuple(reads), tuple(writes))

    def dma(self, eng, fn, reads=(), writes=(), group=None):
        return self._add(eng, fn, tuple(reads), tuple(writes), dma_group=group)

    def emit(self, final_wait_groups=()):
        nc = self.nc
        ops = self.ops
        for i, op in enumerate(ops):
            nd = set()
            for (p, kind) in op["deps"]:
                pop = ops[p]
                if pop["dma_group"] is None and op["dma_group"] is None and pop["eng"] == op["eng"]:
                    if op["eng"] == "tensor" or kind != "raw":
                        continue
                nd.add(p)
            best = {}
            keep = set()
            for p in nd:
                pop = ops[p]
                if pop["dma_group"] is not None:
                    keep.add(p)
                elif p > best.get(pop["eng"], -1):
                    best[pop["eng"]] = p
            keep.update(best.values())
            op["ndeps"] = keep
            for p in keep:
                ops[p]["signal"] = True
        cnt = {e: 0 for e in ENGINES}
        for op in ops:
            if op["dma_group"] is None and op["signal"]:
                cnt[op["eng"]] += 1
                op["seq"] = cnt[op["eng"]]
        stack = contextlib.ExitStack()
        esem = {e: stack.enter_context(nc.semaphore("s_" + e)) for e in ENGINES if cnt[e] > 0}
        dsem = {g: stack.enter_context(nc.semaphore("d_%d" % k)) for k, g in enumerate(self.dma_groups)}
        per_eng = {e: [] for e in ENGINES}
        grp_idx = {}
        for i, op in enumerate(ops):
            per_eng[op["eng"]].append(i)
            if op["dma_group"] is not None:
                grp_idx.setdefault(op["dma_group"], []).append(i)

        def run_engine(ename, eobj):
            waited = {}
            for i in per_eng[ename]:
                op = ops[i]
                need = {}
                for p in op["ndeps"]:
                    pop = ops[p]
                    if pop["dma_group"] is not None:
                        key = ("d", pop["dma_group"])
                        val = 16 * bisect.bisect_left(grp_idx[pop["dma_group"]], i)
                    else:
                        key = ("e", pop["eng"])
                        val = pop["seq"]
                    if val > need.get(key, 0):
                        need[key] = val
                for key, val in need.items():
                    if waited.get(key, 0) >= val:
                        continue
                    waited[key] = val
                    sem = dsem[key[1]] if key[0] == "d" else esem[key[1]]
                    eobj.wait_ge(sem, val)
                ins = op["fn"](eobj)
                if op["dma_group"] is not None:
                    ins.then_inc(dsem[op["dma_group"]], 16)
                elif op["signal"]:
                    ins.then_inc(esem[ename], 1)
            if ename == "sync":
                for g in final_wait_groups:
                    eobj.wait_ge(dsem[g], 16 * self.dma_groups[g])

        with stack:
            with nc.Block() as block:
                for ename in ENGINES:
                    getattr(block, ename)(lambda eobj, ename=ename: run_engine(ename, eobj))
        return cnt


def host_consts():
    j = np.arange(128)[:, None]
    i = np.arange(128)[None, :]
    f = lambda m: m.astype(np.float32)
    cf = np.zeros((128, 9, 128), np.float32)
    cf[:, 0] = np.eye(128)
    cf[:, 1] = 1.0
    cf[:, 2] = 0.0
    cf[:64, 2, 0] = 1.0
    cf[64:, 2, 1] = 1.0
    cf[:, 4] = -(1.0 / 16.0) * f(j > i)
    cf[:, 5] = f(j <= i)
    cf[:, 6] = f(j > i)
    cf[:, 7] = np.where(i >= j, 0.0, NEG)
    cf[:, 8] = np.where(i > j, 0.0, NEG)
    uref = np.zeros((128, 130), np.float32)
    uref[:, :128] = -(1.0 / 16.0) * (f(j <= i) - f(j <= 63))
    uref[:, 128] = -(1.0 / 16.0) * f(j[:, 0] <= 63)
    uref[:, 129] = -(1.0 / 16.0)
    bm = np.zeros((128, 8, 128), np.float32)
    bm[:, 0] = np.eye(128)
    bm[:, 1] = f(j <= i)
    bm[:, 2] = f((j // 16) == (i // 16))
    for n, s in ((3, 16), (5, 32), (7, 64)):
        low = f(((j // s) % 2 == 1) & ((i // s) == (j // s) - 1))
        if n < 7:
            bm[:, n] = low
            bm[:, n + 1] = low.T
        else:
            bm[:, n] = low.T
    sel = np.zeros((2, 2, 128), np.float32)
    sel[0, 0] = 1.0
    sel[1, 1] = 1.0
    return cf, uref, bm, sel


def interleave(gens):
    gens = list(gens)
    while gens:
        for g in list(gens):
            if g not in gens:
                continue
            try:
                next(g)
            except StopIteration:
                gens = [x for x in gens if x is not g]
        yield


def run(gen):
    for _ in gen:
        pass


PYIELD = int(os.environ.get("K_PYIELD", "1"))
ORDER = os.environ.get("K_ORDER", "bpbmp")
PDELAY = int(os.environ.get("K_PDELAY", "0"))
BANKSETS = {"a": ([0], [1], [2, 3], [4, 5, 6]), "b": ([0, 1], [2], [3, 4], [5, 6])}[os.environ.get("K_BANKS", "b")]


def build(n_units, taps=False, stage=99):
    T = n_units * 128
    nc = bass.Bass("TRN2", target_bir_lowering=False)
    din = lambda name, shape: nc.dram_tensor(name, shape, F32, kind="ExternalInput").ap()
    x_d = din("x", [2, T, D])
    cT_d = din("cT", [128, 8, 2])
    wada_d = din("w_ada", [D, 3 * D])
    bada_d = din("b_ada", [1, 3 * D])
    win_d = din("w_in", [D, NCOL])
    wg_d = din("wg_aug", [17, 256])
    glanw_d = din("gla_nw", [128, 1])
    gdnnw_d = din("gdn_nw", [128, 1])
    convw_d = din("convw", [128, 48])
    alog_d = din("alog_b", [128, 4])
    dtb_d = din("dtb_b", [128, 4])
    wout_d = din("w_out", [D, D])
    lnw_d = din("ln_w", [1, D])
    lnb_d = din("ln_b", [1, D])
    cf_d = din("cf", [128, 9, 128])
    uref_d = din("uref", [128, 130])
    bm_d = din("bm", [128, 8, 128])
    sel_d = din("sel", [2, 2, 128])
    out_d = nc.dram_tensor("out", [2, T, D], F32, kind="ExternalOutput").ap()

    es = contextlib.ExitStack()
    with es:
        def sb(name, shape, dt=BF16):
            return es.enter_context(nc.sbuf_tensor("s_" + name, shape, dt))
        s = Sched(nc)
        banks = [es.enter_context(nc.psum_tensor("ps%d" % i, [128, 512], F32)) for i in range(7)]
        psb = es.enter_context(nc.psum_tensor("psb", [128, 1024], BF16))

        def mkbanks(ids):
            ctr = [0]

            def nb():
                b = ids[ctr[0] % len(ids)]
                ctr[0] += 1
                return banks[b], "ps%d" % b
            return nb
        nb_setup = mkbanks([0, 1, 2, 3, 4, 5, 6])
        _bs = BANKSETS
        nb_p = mkbanks(_bs[0])
        nb_gla = mkbanks(_bs[1])
        nb_gdn = mkbanks(_bs[2])
        nb_back = mkbanks(_bs[3])

        cf = sb("cf", [128, 9, 128], F32)
        uref = sb("uref", [128, 130], F32)
        R = sb("R", [128, D], F32)
        RN = sb("RN", [128, 8, 128], F32)
        bmf = R[:].rearrange("p (a b) -> p a b", a=8)
        UU = sb("UU", [128, 4, 128], F32)
        sel = UU[0:2, 0:2, :]
        s.dma("sync", lambda e: e.dma_start(out=cf[:], in_=cf_d), writes=["cf"], group="c0")
        s.dma("sync", lambda e: e.dma_start(out=uref[:], in_=uref_d), writes=["uref"], group="c0")
        s.dma("sync", lambda e: e.dma_start(out=bmf, in_=bm_d), writes=["R"], group="c0")
        s.dma("sync", lambda e: e.dma_start(out=sel, in_=sel_d), writes=["UU"], group="c0")
        ident_f, ones_f = cf[:, 0, :], cf[:, 1, :]
        mgt16, uincl, mgt, mt_incl, mt_strict = cf[:, 4, :], cf[:, 5, :], cf[:, 6, :], cf[:, 7, :], cf[:, 8, :]
        bmrep = sb("bmrep", [128, 2, 4, 128])
        bm1 = sb("bm1", [128, 8, 128])
        for n, m in enumerate((0, 2)):
            s.op("vector", lambda e, m=m, n=n: e.tensor_copy(out=bmrep[:, n, :, :], in_=bmf[:, m, :].unsqueeze(1).to_broadcast([128, 4, 128])),
                 reads=["R"], writes=["bmr"])
        s.op("vector", lambda e: e.tensor_copy(out=bm1[:], in_=bmf), reads=["R"], writes=["bmr"])
        bcm = lambda m: bm1[:, m, :].unsqueeze(1).to_broadcast([128, 4, 128])
        identr_b, bd16 = bmrep[:, 0], bmrep[:, 1]
        masku = bcm(1)
        ident_b = bm1[:, 0, :]
        ones_b = sb("ones_b", [128, 128])
        s.op("vector", lambda e: e.tensor_copy(out=ones_b[:], in_=cf[:, 1, :]), reads=["cf"], writes=["ones_b"])

        small = {}
        for name, d, shape in (("glanw", glanw_d, [128, 1]), ("gdnnw", gdnnw_d, [128, 1]), ("convw", convw_d, [128, 48]),
                               ("alog", alog_d, [128, 4]), ("dtb", dtb_d, [128, 4]), ("cT", cT_d, [128, 8, 2])):
            t = sb(name, shape, F32)
            small[name] = t
            s.dma("sync", lambda e, t=t, d=d: e.dma_start(out=t[:], in_=d), writes=[name], group="c1")
        lnw_b = sb("lnw_b", [128, D], F32)
        lnb_b = sb("lnb_b", [128, D], F32)
        s.dma("sync", lambda e: e.dma_start(out=lnw_b[:], in_=lnw_d.partition_broadcast(128)), writes=["lnw_b"], group="c1")
        s.dma("sync", lambda e: e.dma_start(out=lnb_b[:], in_=lnb_d.partition_broadcast(128)), writes=["lnb_b"], group="c1")
        wgb = sb("wgb", [17, 256])
        s.dma("sync", lambda e: e.dma_start(out=RN[0:17, 0:2, :].rearrange("p a b -> p (a b)"), in_=wg_d), writes=["RN"], group="RN")
        s.op("vector", lambda e: e.tensor_copy(out=wgb[:], in_=RN[0:17, 0:2, :].rearrange("p a b -> p (a b)")), reads=["RN"], writes=["wgb"])
        nega = sb("nega", [128, 4], F32)
        s.op("scalar", lambda e: e.activation(out=nega[:], in_=small["alog"][:], func=AF.Exp), reads=["alog"], writes=["nega"])
        s.op("vector", lambda e: e.tensor_scalar(out=nega[:], in0=nega[:], scalar1=-1.0, scalar2=None, op0=ALU.mult), reads=["nega"], writes=["nega"])
        diag = sb("diag", [128, 48, 128])
        for t in range(48):
            s.op("vector" if t % 2 else "gpsimd",
                 lambda e, t=t: e.tensor_scalar(out=diag[:, t, :], in0=ident_f, scalar1=small["convw"][:, t:t + 1], scalar2=None, op0=ALU.mult),
                 reads=["cf", "convw"], writes=["diag"])

        stgs = [(R[:, :], "R"), (RN[:].rearrange("p a b -> p (a b)"), "RN")]
        Wb = sb("Wb", [128, 8, NCOL])
        Woutb = sb("Woutb", [128, 8, D])
        Sg = [sb("Sg%d" % i, [128, 4, 128], F32) for i in range(2)]
        modp = sb("modp", [128, 32], F32)
        g1b = [sb("g1b%d" % i, [128, D], F32) for i in range(2)]
        mch = Sg[0][0:2, :, :].rearrange("p h c -> p (h c)")
        bch = Sg[1][0:2, :, :].rearrange("p h c -> p (h c)")
        modbanks = [(banks[i], "ps%d" % i) for i in range(6)]
        mpb, mpkey = banks[6], "ps6"
        nst = 0
        for kt in range(8):
            for ch in range(3):
                st, key = stgs[nst % 2]
                nst += 1
                s.dma("sync", lambda e, st=st, kt=kt, ch=ch: e.dma_start(out=st[:, 0:1024], in_=wada_d[kt * 128:(kt + 1) * 128, ch * 1024:(ch + 1) * 1024]), writes=[key], group=key)
                for cgl in range(2):
                    bk, bkey = modbanks[ch * 2 + cgl]
                    s.op("tensor", lambda e, bk=bk, st=st, kt=kt, cgl=cgl: e.matmul(out=bk[0:2, 0:512], lhsT=small["cT"][:, kt, :], rhs=st[:, cgl * 512:(cgl + 1) * 512],
                                                                                 start=(kt == 0), stop=(kt == 7)), reads=[key, "cT"], writes=[bkey])
        for cg in range(6):
            bk, bkey = modbanks[cg]
            s.dma("sync", lambda e, cg=cg: e.dma_start(out=bch, in_=bada_d[:, cg * 512:(cg + 1) * 512].partition_broadcast(2)), writes=["Sg1"], group="bch")
            s.op("vector", lambda e, bk=bk: e.tensor_tensor(out=mch, in0=bk[0:2, 0:512], in1=bch, op=ALU.add), reads=[bkey, "Sg1"], writes=["Sg0"])
            if cg >= 2:
                s.op("vector", lambda e: e.tensor_scalar(out=mch, in0=mch, scalar1=1.0, scalar2=None, op0=ALU.add), reads=["Sg0"], writes=["Sg0"])
            if cg < 4:
                for j in range(4):
                    jn = cg * 4 + j
                    s.op("tensor", lambda e, jn=jn, j=j: e.matmul(out=mpb[:, jn * 2:(jn + 1) * 2], lhsT=mch[:, j * 128:(j + 1) * 128], rhs=cf[0:2, 0, 0:2], start=True, stop=True),
                         reads=["Sg0", "cf"], writes=[mpkey])
            else:
                half = cg - 4
                for sq in range(2):
                    b2, b2key = modbanks[sq]
                    s.op("tensor", lambda e, b2=b2, sq=sq: e.matmul(out=b2[:, 0:512], lhsT=sel[:, sq, :], rhs=mch, start=True, stop=True), reads=["Sg0", "UU"], writes=[b2key])
                    s.op("vector", lambda e, b2=b2, sq=sq, half=half: e.tensor_copy(out=g1b[sq][:, half * 512:(half + 1) * 512], in_=b2[:, 0:512]), reads=[b2key], writes=["g1b%d" % sq])
        s.op("vector", lambda e: e.tensor_copy(out=modp[:], in_=mpb[:, 0:32]), reads=[mpkey], writes=["modp"])
        for kt in range(8):
            for ch in range(4):
                c0 = ch * 1024
                w = min(1024, NCOL - c0)
                st, key = stgs[nst % 2]
                nst += 1
                s.dma("sync", lambda e, st=st, kt=kt, c0=c0, w=w: e.dma_start(out=st[:, 0:w], in_=win_d[kt * 128:(kt + 1) * 128, c0:c0 + w]), writes=[key], group=key)
                s.op("vector" if nst % 2 else "gpsimd", lambda e, st=st, kt=kt, c0=c0, w=w: e.tensor_copy(out=Wb[:, kt, c0:c0 + w], in_=st[:, 0:w]), reads=[key], writes=["Wb"])
        for kt in range(8):
            st, key = stgs[nst % 2]
            nst += 1
            s.dma("sync", lambda e, st=st, kt=kt: e.dma_start(out=st[:, 0:D], in_=wout_d[kt * 128:(kt + 1) * 128, :]), writes=[key], group=key)
            s.op("vector" if nst % 2 else "gpsimd", lambda e, st=st, kt=kt: e.tensor_copy(out=Woutb[:, kt, :], in_=st[:, 0:D]), reads=[key], writes=["Woutb"])

        def two(name, shape, dt=BF16):
            return [sb("%s_%d" % (name, i), shape, dt) for i in range(2)]
        xt = [sb("xt_%d" % i, [128, D], F32) for i in range(3)]
        SG = [sb("SG_%d" % i, [128, 8, 128]) for i in range(3)]
        OA = two("OA", [128, 4, 128])
        Am = two("Am", [128, 4, 128]); AT = two("AT", [128, 4, 128])
        KW = two("KW", [128, 4, 128]); KD = two("KD", [128, 4, 128]); VB = two("VB", [128, 4, 128])
        QKT = two("QKT", [128, 4, 128]); QDEC = two("QDEC", [128, 4, 128])
        EX = two("EX", [128, 12], F32)
        hT = sb("hT", [128, 8, 128])
        FM = two("FM", [128, 4, 128])
        U = [sb("U%d" % i, [128, 12, 131]) for i in range(2)]
        lraug = two("lraug", [32, 128])
        KVt = two("KVt", [128, 768])
        spt = sb("spt", [128, 256], F32)
        E1 = sb("E1", [128, 2, 128]); E2 = sb("E2", [128, 2, 128])
        ER = sb("ER", [128, 2, 2], F32)
        QT = sb("QT", [128, 2, 128]); KT = sb("KT", [128, 2, 128])
        EK = sb("EK", [128, 256]); KP = sb("KP", [128, 256])
        ATT = sb("ATT", [128, 4, 128])
        SPb = sb("SPb", [128, 4, 128])
        KTz = sb("KTz", [128, 4, 128])
        Sgla = [sb("Sgla%d" % i, [128, 2, 128], F32) for i in range(2)]
        QKV = sb("QKV", [128, 12, 128])
        SQ = sb("SQ", [128, 8, 128])
        RNb = SQ
        QKN = sb("QKN", [128, 8, 128])
        T4 = sb("T4", [128, 4], F32); G4 = sb("G4", [128, 4], F32); AB = two("AB", [128, 8], F32)
        TB = sb("TB", [128, 4], F32); BETA = sb("BETA", [128, 4], F32); LNT = sb("LNT", [128, 4], F32)
        D4 = sb("D4", [128, 4], F32); ND4 = sb("ND4", [128, 4], F32); D4B = sb("D4B", [128, 4], F32); KWs = sb("KWs", [128, 4], F32)
        DG = RN[:, 4:8, :]; DG2 = RN[:, 0:4, :]
        ED = sb("ED", [128, 4, 128]); E1g = sb("E1g", [128, 4, 128]); E2g = sb("E2g", [128, 4, 128])
        INV = sb("INV", [128, 8, 4, 128])
        Pp = [INV[:, 0], INV[:, 1]]; PTp = [INV[:, 2], INV[:, 3]]; Yp = [INV[:, 4], INV[:, 5]]; Zp = [INV[:, 6], INV[:, 7]]
        X1 = sb("X1", [128, 4, 128]); X2 = sb("X2", [128, 4, 128])
        YTA = sb("YTA", [128, 4, 128])
        WT = X2; VN = X1; OB = sb("OB", [128, 4, 128])
        Sgb = [sb("Sgb%d" % i, [128, 4, 128]) for i in range(2)]
        v8 = lambda a, b: INV[:, a:b].rearrange("p a h c -> p (a h) c")
        SQO, SQOk = v8(0, 2), ["Pp0", "Pp1"]
        TO, TOk = v8(2, 4), ["PTp0", "PTp1"]
        YT, YTk = v8(4, 6), ["Yp0", "Yp1"]
        RNOb, RNObk = v8(6, 8), ["Zp0", "Zp1"]
        RNO = R[:].rearrange("p (a b) -> p a b", a=8)
        JUNK = X1
        ST = sb("ST", [128, 8], F32)
        DBG = None
        if taps:
            try:
                DBG = sb("DBG", [128, 512], F32)
            except AssertionError:
                DBG = None

        for i in range(2):
            s.op("gpsimd", lambda e, i=i: e.memset(Sgla[i][:], 0.0), writes=["Sgla%d" % i])
            s.op("gpsimd", lambda e, i=i: e.memset(Sg[i][:], 0.0), reads=["modp", "g1b0", "g1b1"], writes=["Sg%d" % i])
            s.op("gpsimd", lambda e, i=i: e.memset(Sgb[i][:], 0.0), writes=["Sgb%d" % i])
            s.op("gpsimd", lambda e, i=i: e.memset(U[i][:], 0.0), writes=["U%d" % i])
        for i in range(2):
            s.op("gpsimd", lambda e, i=i: e.memset(lraug[i][:], 1.0), writes=["lraug%d" % i])

        f3 = lambda ap: ap.rearrange("p (h c) -> p h c", h=4)
        bc = lambda ap: ap.unsqueeze(2).to_broadcast([128, 4, 128])

        def tt(eng, out, a, b, op, reads, writes):
            s.op(eng, lambda e: e.tensor_tensor(out=out, in0=a, in1=b, op=op), reads=reads, writes=writes)

        def act(out, in_, func, reads, writes, **kw):
            s.op("scalar", lambda e: e.activation(out=out, in_=in_, func=func, **kw), reads=reads, writes=writes)

        units = [(sq, m) for m in range(n_units) for sq in range(2)]
        dbg_d = {}

        def dbg(sq, m, name, ap, keys):
            if not taps or DBG is None:
                return
            shp = list(ap.shape)
            P = shp[0]
            w = int(np.prod(shp[1:]))
            if name not in dbg_d:
                dbg_d[name] = nc.dram_tensor("dbg_" + name, [2, n_units, 128, 512], F32, kind="ExternalOutput").ap()
            dst = DBG[0:P, 0:w]
            if len(shp) == 3:
                dst = dst.rearrange("p (h c) -> p h c", h=shp[1])
            s.op("vector", lambda e: e.tensor_copy(out=dst, in_=ap), reads=list(keys), writes=["DBG"])
            s.dma("sync", lambda e: e.dma_start(out=dbg_d[name][sq, m, 0:P, 0:w], in_=DBG[0:P, 0:w]), reads=["DBG"], group="tap")

        def load_x(u):
            sq_, m_ = units[u]
            p3_ = u % 3
            s.dma("sync", lambda e: e.dma_start(out=xt[p3_][:], in_=x_d[sq_, m_ * 128:(m_ + 1) * 128, :]), writes=["xt%d" % p3_], group="xt%d" % p3_)

        def pstage(sq, m, u):
            p, p3 = u % 2, u % 3
            xs, xk = xt[p3], "xt%d" % p3
            pk = lambda n: "%s%d" % (n, p)
            Uk = "U%d" % sq
            for _ in range(PDELAY):
                yield
            for g in range(2):
                bk, bkey = nb_p()
                for c in range(4):
                    kt = g * 4 + c
                    s.op("tensor", lambda e, bk=bk, c=c, kt=kt: e.transpose(out=bk[:, c * 128:(c + 1) * 128], in_=xs[:, kt * 128:(kt + 1) * 128], identity=ident_f),
                         reads=[xk, "cf"], writes=[bkey])
                for c in range(4):
                    kt = g * 4 + c
                    s.op("vector", lambda e, bk=bk, c=c, kt=kt: e.tensor_scalar(out=hT[:, kt, :], in0=bk[:, c * 128:(c + 1) * 128],
                                                                             scalar1=modp[:, 16 + kt * 2 + sq:17 + kt * 2 + sq], scalar2=modp[:, kt * 2 + sq:kt * 2 + sq + 1],
                                                                             op0=ALU.mult, op1=ALU.add), reads=[bkey, "modp"], writes=["hT"])
                yield
            fm_cols = [0, 128, 256, 384] + [1040 + 128 * i for i in range(4)] + [3096 + 128 * i for i in range(4)] + [1552 + 128 * i for i in range(12)]
            for g in range(6):
                bk, bkey = nb_p()
                for c in range(4):
                    c0 = fm_cols[g * 4 + c]
                    for kt in range(8):
                        s.op("tensor", lambda e, bk=bk, c=c, c0=c0, kt=kt: e.matmul(out=bk[:, c * 128:(c + 1) * 128], lhsT=Wb[:, kt, c0:c0 + 128], rhs=hT[:, kt, :],
                                                                                  start=(kt == 0), stop=(kt == 7)), reads=["Wb", "hT"], writes=[bkey])
                    if (c + 1) % PYIELD == 0:
                        yield
                if g == 0:
                    act(FM[p][:], f3(bk[:, :]), AF.Copy, [bkey], [pk("FM")])
                elif g in (1, 2):
                    act(SG[p3][:, (g - 1) * 4:g * 4, :], f3(bk[:, :]), AF.Silu, [bkey], ["SG%d" % p3])
                else:
                    s.op("vector", lambda e, bk=bk, g=g: e.tensor_copy(out=U[sq][:, (g - 3) * 4:(g - 2) * 4, 3:131], in_=f3(bk[:, :])), reads=[bkey], writes=[Uk])
            abk, abkey = nb_p()
            for kt in range(8):
                s.op("tensor", lambda e, kt=kt: e.matmul(out=abk[0:16, 0:128], lhsT=Wb[:, kt, 1024:1040], rhs=hT[:, kt, :], start=(kt == 0), stop=(kt == 7)),
                     reads=["Wb", "hT"], writes=[abkey])
            for kt in range(8):
                s.op("tensor", lambda e, kt=kt: e.matmul(out=abk[:, 128:136], lhsT=hT[:, kt, :], rhs=Wb[:, kt, 3088:3096], start=(kt == 0), stop=(kt == 7)),
                     reads=["Wb", "hT"], writes=[abkey])
            s.op("vector", lambda e: e.tensor_copy(out=lraug[p][0:16, :], in_=abk[0:16, 0:128]), reads=[abkey], writes=[pk("lraug")])
            s.op("vector", lambda e: e.tensor_copy(out=AB[p][:], in_=abk[:, 128:136]), reads=[abkey], writes=[pk("AB")])
            yield
            for part, (c0, w) in enumerate(((256, 512), (768, 256))):
                bk, bkey = nb_p()
                for kt in range(8):
                    s.op("tensor", lambda e, bk=bk, kt=kt, c0=c0, w=w: e.matmul(out=bk[:, 0:w], lhsT=hT[:, kt, :], rhs=Wb[:, kt, c0:c0 + w], start=(kt == 0), stop=(kt == 7)),
                         reads=["Wb", "hT"], writes=[bkey])
                s.op("vector", lambda e, bk=bk, c0=c0, w=w: e.tensor_copy(out=KVt[p][:, c0 - 256:c0 - 256 + w], in_=bk[:, 0:w]), reads=[bkey], writes=[pk("KVt")])
                yield

        def mid(sq, m, u):
            yield from interleave([gla(sq, m, u % 2), gdnpre(sq, m, u % 2)])

        def gla(sq, m, p):
            pk = lambda n: "%s%d" % (n, p)
            Slk = "Sgla%d" % sq
            bk, bkey = nb_gla()
            s.op("tensor", lambda e, bk=bk: e.matmul(out=bk[:, 0:256], lhsT=lraug[p][0:17, :], rhs=wgb[0:17, :], start=True, stop=True), reads=[pk("lraug"), "wgb"], writes=[bkey])
            act(spt[:], bk[:, 0:256], AF.Exp, [bkey], ["spt"], scale=-1.0)
            act(spt[:], spt[:], AF.Ln, ["spt"], ["spt"], bias=1.0)
            yield
            pbk, pbkey = nb_gla()
            for hp in range(2):
                s.op("tensor", lambda e, hp=hp: e.matmul(out=pbk[:, hp * 130:(hp + 1) * 130], lhsT=spt[:, hp * 128:(hp + 1) * 128], rhs=uref[:, :], start=True, stop=True),
                     reads=["spt", "uref"], writes=[pbkey])
            pv = pbk[:, 0:260].rearrange("p (h c) -> p h c", h=2)
            act(E1[:], pv[:, :, 0:128], AF.Exp, [pbkey], ["E1"], bias=float(np.log(0.125)))
            act(E2[:], pv[:, :, 0:128], AF.Exp, [pbkey], ["E2"], scale=-1.0)
            act(ER[:], pv[:, :, 128:130], AF.Exp, [pbkey], ["ER"])
            yield
            tt("vector", QT[:], FM[p][:, 0:2, :], E1[:], ALU.mult, [pk("FM"), "E1"], ["QT"])
            tt("vector", KT[:], FM[p][:, 2:4, :], E2[:], ALU.mult, [pk("FM"), "E2"], ["KT"])
            bk, bkey = nb_gla()
            s.op("tensor", lambda e, bk=bk: e.matmul(out=bk[:, 0:256], lhsT=mgt16, rhs=spt[:, :], start=True, stop=True), reads=["spt", "cf"], writes=[bkey])
            act(EK[:], bk[:, 0:256], AF.Exp, [bkey], ["EK"])
            tt("gpsimd", KP[:], KVt[p][:, 0:256], EK[:], ALU.mult, [pk("KVt"), "EK"], ["KP"])
            yield
            for h in range(4):
                hp, par = h // 2, h % 2
                s.op("vector", lambda e, h=h, hp=hp, par=par: e.tensor_scalar(out=KTz[:, h, :], in0=KT[:, hp, :], scalar1=cf[:, 2, par:par + 1], scalar2=None, op0=ALU.mult),
                     reads=["KT", "cf"], writes=["KTz"])
            yield
            bk, bkey = nb_gla()
            for h in range(4):
                hp, par = h // 2, h % 2
                s.op("tensor", lambda e, bk=bk, h=h, hp=hp, par=par: e.matmul(out=bk[:, h * 128:(h + 1) * 128], lhsT=KTz[:, h, :], rhs=QT[:, hp, :],
                                                                           start=True, stop=True), reads=["KTz", "QT"], writes=[bkey])
            tt("vector", ATT[:], f3(bk[:, :]), masku, ALU.mult, [bkey, "bmr"], ["ATT"])
            for h in range(4):
                hp, par = h // 2, h % 2
                s.op("vector", lambda e, h=h, hp=hp, par=par: e.tensor_scalar(out=SPb[:, h, :], in0=Sgla[sq][:, hp, :], scalar1=ER[:, hp, 0:1], scalar2=cf[:, 2, par:par + 1],
                                                                           op0=ALU.mult, op1=ALU.mult), reads=[Slk, "ER", "cf"], writes=["SPb"])
            yield
            oak, oakey = nb_gla()
            for h in range(4):
                hp, par = h // 2, h % 2
                s.op("tensor", lambda e, h=h, hp=hp, par=par: e.matmul(out=oak[:, h * 128:(h + 1) * 128], lhsT=SPb[:, h, :], rhs=QT[:, hp, :],
                                                                      start=True, stop=False), reads=["SPb", "QT"], writes=[oakey])
                s.op("tensor", lambda e, h=h: e.matmul(out=oak[:, h * 128:(h + 1) * 128], lhsT=KVt[p][:, 256 + h * 128:384 + h * 128], rhs=ATT[:, h, :], start=False, stop=True),
                     reads=[pk("KVt"), "ATT"], writes=[oakey])
            act(OA[p][:], f3(oak[:, :]), AF.Copy, [oakey], [pk("OA")])
            dbg(sq, m, "oa", OA[p][:], [pk("OA")])
            yield
            bk, bkey = nb_gla()
            for h in range(4):
                hp, par = h // 2, h % 2
                s.op("tensor", lambda e, bk=bk, h=h, hp=hp, par=par: e.matmul(out=bk[:, h * 128:(h + 1) * 128], lhsT=KP[:, hp * 128:(hp + 1) * 128],
                                                                           rhs=KVt[p][:, 256 + h * 128:384 + h * 128], start=True, stop=True), reads=["KP", pk("KVt")], writes=[bkey])
            for h in range(4):
                hp, par = h // 2, h % 2
                ps_ = slice(par * 64, (par + 1) * 64)
                s.op("vector", lambda e, bk=bk, h=h, hp=hp, ps_=ps_: e.scalar_tensor_tensor(out=Sgla[sq][ps_, hp, :], in0=Sgla[sq][ps_, hp, :], scalar=ER[ps_, hp, 1:2], in1=bk[ps_, h * 128:(h + 1) * 128],
                                                                                     op0=ALU.mult, op1=ALU.add), reads=[Slk, "ER", bkey], writes=[Slk])
            yield

        def gdnpre(sq, m, p):
            pk = lambda n: "%s%d" % (n, p)
            Uk = "U%d" % sq
            for g3 in range(3):
                bk, bkey = nb_gdn()
                for c in range(4):
                    t = g3 * 4 + c
                    for k in range(4):
                        s.op("tensor", lambda e, bk=bk, c=c, t=t, k=k: e.matmul(out=bk[:, c * 128:(c + 1) * 128], lhsT=diag[:, t * 4 + k, :], rhs=U[sq][:, t, k:k + 128],
                                                                              start=(k == 0), stop=(k == 3)), reads=["diag", Uk], writes=[bkey])
                act(QKV[:, g3 * 4:(g3 + 1) * 4, :], f3(bk[:, :]), AF.Silu, [bkey], ["QKV"])
                yield
            s.op("gpsimd", lambda e: e.tensor_copy(out=U[sq][:, :, 0:3], in_=U[sq][:, :, 128:131]), reads=[Uk], writes=[Uk])
            tt("gpsimd", SQ[:], QKV[:, 0:8, :], QKV[:, 0:8, :], ALU.mult, ["QKV"], ["SQ"])
            tt("vector", T4[:], AB[p][:, 0:4], small["dtb"][:], ALU.add, [pk("AB"), "dtb"], ["T4"])
            act(T4[:], T4[:], AF.Exp, ["T4"], ["T4"])
            act(T4[:], T4[:], AF.Ln, ["T4"], ["T4"], bias=1.0)
            tt("vector", G4[:], T4[:], nega[:], ALU.mult, ["T4", "nega"], ["G4"])
            act(TB[:], AB[p][:, 4:8], AF.Exp, [pk("AB")], ["TB"], scale=-1.0)
            s.op("vector", lambda e: e.tensor_scalar(out=TB[:], in0=TB[:], scalar1=1.0, scalar2=None, op0=ALU.add), reads=["TB"], writes=["TB"])
            s.op("vector", lambda e: e.reciprocal(out=BETA[:], in_=TB[:]), reads=["TB"], writes=["BETA"])
            act(LNT[:], TB[:], AF.Ln, ["TB"], ["LNT"])
            yield
            for g in range(2):
                bk, bkey = nb_gdn()
                for c in range(4):
                    s.op("tensor", lambda e, bk=bk, c=c, g=g: e.matmul(out=bk[:, c * 128:(c + 1) * 128], lhsT=ones_b[:, :], rhs=SQ[:, g * 4 + c, :], start=True, stop=True),
                         reads=["ones_b", "SQ"], writes=[bkey])
                act(RN[:, g * 4:(g + 1) * 4, :], f3(bk[:, :]), AF.Ln, [bkey], ["RN"], bias=1e-6)
                act(RNb[:, g * 4:(g + 1) * 4, :], RN[:, g * 4:(g + 1) * 4, :], AF.Exp, ["RN"], ["SQ"], scale=-0.5, bias=(float(np.log(128.0 ** -0.5)) if g == 0 else 0.0))
                yield
            tt("vector", QKN[:], QKV[:, 0:8, :], RNb[:], ALU.mult, ["QKV", "SQ"], ["QKN"])
            dbk, dbkey = nb_gdn()
            for n, lm in enumerate((uincl, mgt, ones_f)):
                s.op("tensor", lambda e, n=n, lm=lm: e.matmul(out=dbk[:, n * 4:(n + 1) * 4], lhsT=lm, rhs=G4[:, :], start=True, stop=True), reads=["cf", "G4"], writes=[dbkey])
            act(EX[p][:], dbk[:, 0:12], AF.Exp, [dbkey], [pk("EX")])
            s.op("vector", lambda e: e.tensor_copy(out=D4[:], in_=dbk[:, 0:4]), reads=[dbkey], writes=["D4"])
            s.op("vector", lambda e: e.tensor_scalar(out=ND4[:], in0=dbk[:, 0:4], scalar1=-1.0, scalar2=None, op0=ALU.mult), reads=[dbkey], writes=["ND4"])
            tt("vector", D4B[:], D4[:], LNT[:], ALU.subtract, ["D4", "LNT"], ["D4B"])
            tt("vector", KWs[:], BETA[:], EX[p][:, 0:4], ALU.mult, ["BETA", pk("EX")], ["KWs"])
            yield
            for t in range(4):
                s.op("tensor", lambda e, t=t: e.transpose(out=psb[:, t * 128:(t + 1) * 128], in_=QKN[:, 4 + t, :], identity=ident_b), reads=["QKN", "bmr"], writes=["psb"])
            for t in range(4):
                s.op("tensor", lambda e, t=t: e.transpose(out=psb[:, 512 + t * 128:512 + (t + 1) * 128], in_=QKV[:, 8 + t, :], identity=ident_b), reads=["QKV", "bmr"], writes=["psb"])
            tt("vector", KW[p][:], f3(psb[:, 0:512]), bc(KWs[:]), ALU.mult, ["psb", "KWs"], [pk("KW")])
            tt("vector", KD[p][:], f3(psb[:, 0:512]), bc(EX[p][:, 4:8]), ALU.mult, ["psb", pk("EX")], [pk("KD")])
            tt("vector", VB[p][:], f3(psb[:, 512:1024]), bc(BETA[:]), ALU.mult, ["psb", "BETA"], [pk("VB")])
            yield
            tt("gpsimd", DG, identr_b, bc(D4[:]), ALU.mult, ["bmr", "D4"], ["RN"])
            tt("gpsimd", DG2, identr_b, bc(D4B[:]), ALU.mult, ["bmr", "D4B"], ["RN"])
            dgf = DG.rearrange("p h c -> p (h c)")
            dg2f = DG2.rearrange("p h c -> p (h c)")
            bk, bkey = nb_gdn()
            s.op("tensor", lambda e, bk=bk: e.matmul(out=bk[:, :], lhsT=ones_f, rhs=dgf, start=True, stop=True), reads=["cf", "RN"], writes=[bkey])
            act(ED[:], f3(bk[:, :]), AF.Exp, [bkey], ["ED"])
            yield
            for (src, srck, msk, dst, dstk) in ((dgf, "RN", mt_incl, E1g, "E1g"), (dg2f, "RN", mt_strict, E2g, "E2g")):
                bk, bkey = nb_gdn()
                for h in range(4):
                    s.op("tensor", lambda e, bk=bk, src=src, h=h: e.matmul(out=bk[:, h * 128:(h + 1) * 128], lhsT=ones_f, rhs=src[:, h * 128:(h + 1) * 128], start=True, stop=False),
                         reads=["cf", srck], writes=[bkey])
                    s.op("tensor", lambda e, bk=bk, h=h, msk=msk: e.matmul(out=bk[:, h * 128:(h + 1) * 128], lhsT=ident_f, rhs=msk, start=False, stop=True), reads=["cf"], writes=[bkey])
                for h in range(4):
                    act(dst[:, h, :], bk[:, h * 128:(h + 1) * 128], AF.Exp, [bkey, "ND4"], [dstk], bias=ND4[:, h:h + 1])
                yield
            tt("gpsimd", QDEC[p][:], QKN[:, 0:4, :], ED[:], ALU.mult, ["QKN", "ED"], [pk("QDEC")])
            gk, gkey = nb_gdn()
            for h in range(4):
                s.op("tensor", lambda e, h=h: e.matmul(out=gk[:, h * 128:(h + 1) * 128], lhsT=QKN[:, 4 + h, :], rhs=QKN[:, 4 + h, :], start=True, stop=True), reads=["QKN"], writes=[gkey])
            tt("vector", AT[p][:], f3(gk[:, :]), E2g[:], ALU.mult, [gkey, "E2g"], [pk("AT")])
            qk, qkey = nb_gdn()
            for h in range(4):
                s.op("tensor", lambda e, h=h: e.matmul(out=qk[:, h * 128:(h + 1) * 128], lhsT=QKN[:, 4 + h, :], rhs=QKN[:, h, :], start=True, stop=True), reads=["QKN"], writes=[qkey])
            tt("vector", QKT[p][:], f3(qk[:, :]), E1g[:], ALU.mult, [qkey, "E1g"], [pk("QKT")])
            yield
            for h in range(4):
                s.op("tensor", lambda e, h=h: e.transpose(out=psb[:, h * 128:(h + 1) * 128], in_=AT[p][:, h, :], identity=ident_b), reads=[pk("AT"), "bmr"], writes=["psb"])
            act(Am[p][:], f3(psb[:, 0:512]), AF.Copy, ["psb"], [pk("Am")])
            yield

        def back(sq, m, u):
            p, p3 = u % 2, u % 3
            xs, xk = xt[p3], "xt%d" % p3
            pk = lambda n: "%s%d" % (n, p)
            Sgk, Sgbk = "Sg%d" % sq, "Sgb%d" % sq
            A_, Ak, AT_, ATk = Am[p], pk("Am"), AT[p], pk("AT")

            def mm4(L, Lk, Rt, Rk):
                bk, bkey = nb_back()
                for h in range(4):
                    s.op("tensor", lambda e, bk=bk, h=h: e.matmul(out=bk[:, h * 128:(h + 1) * 128], lhsT=L[:, h, :], rhs=Rt[:, h, :], start=True, stop=True),
                         reads=[Lk, Rk], writes=[bkey])
                return bk, bkey
            def invchain():
                tt("vector", Pp[0], A_[:], bd16, ALU.mult, [Ak, "bmr"], ["Pp0"])
                tt("gpsimd", PTp[0], AT_[:], bd16, ALU.mult, [ATk, "bmr"], ["PTp0"])
                tt("vector", Yp[0], identr_b, Pp[0], ALU.subtract, ["bmr", "Pp0"], ["Yp0"])
                tt("gpsimd", Zp[0], identr_b, PTp[0], ALU.subtract, ["bmr", "PTp0"], ["Zp0"])
                yield
                act(R[:], xs[:], AF.Copy, [xk], ["R"], scale=ALPHA)
                if u + 3 < len(units):
                    load_x(u + 3)
                cur = 0
                yz = 0
                for lvl in range(3):
                    nx = 1 - cur
                    b1, k1 = mm4(PTp[cur], "PTp%d" % cur, Pp[cur], "Pp%d" % cur)
                    b2, k2 = mm4(Pp[cur], "Pp%d" % cur, PTp[cur], "PTp%d" % cur)
                    act(Pp[nx], f3(b1[:, :]), AF.Copy, [k1], ["Pp%d" % nx])
                    s.op("vector", lambda e, b2=b2, nx=nx: e.tensor_copy(out=PTp[nx], in_=f3(b2[:, :])), reads=[k2], writes=["PTp%d" % nx])
                    yield
                    b3, k3 = mm4(PTp[nx], "PTp%d" % nx, Yp[yz], "Yp%d" % yz)
                    b4, k4 = mm4(Pp[nx], "Pp%d" % nx, Zp[yz], "Zp%d" % yz)
                    tt("vector", Yp[1 - yz], f3(b3[:, :]), Yp[yz], ALU.add, [k3, "Yp%d" % yz], ["Yp%d" % (1 - yz)])
                    tt("vector", Zp[1 - yz], f3(b4[:, :]), Zp[yz], ALU.add, [k4, "Zp%d" % yz], ["Zp%d" % (1 - yz)])
                    yield
                    cur = nx
                    yz = 1 - yz
                for (mi, both) in ((3, True), (5, True), (7, False)):
                    Yc, Zc, Yk, Zk = Yp[yz], Zp[yz], "Yp%d" % yz, "Zp%d" % yz
                    Yn, Zn, Ynk, Znk = Yp[1 - yz], Zp[1 - yz], "Yp%d" % (1 - yz), "Zp%d" % (1 - yz)
                    mT = bcm(mi + 1) if both else bcm(mi)
                    b1, k1 = mm4(A_, Ak, Zc, Zk)
                    if both:
                        b3, k3 = mm4(AT_, ATk, Yc, Yk)
                    tt("vector", X1[:], f3(b1[:, :]), mT, ALU.mult, [k1, "bmr"], ["X1"])
                    if both:
                        tt("vector", X2[:], f3(b3[:, :]), bcm(mi), ALU.mult, [k3, "bmr"], ["X2"])
                    yield
                    b2, k2 = mm4(Yc, Yk, X1, "X1")
                    if both:
                        b4, k4 = mm4(Zc, Zk, X2, "X2")
                    tt("vector", Zn, Zc, f3(b2[:, :]), ALU.subtract, [Zk, k2], [Znk])
                    if both:
                        tt("vector", Yn, Yc, f3(b4[:, :]), ALU.subtract, [Yk, k4], [Ynk])
                    yield
                    yz = 1 - yz
                TT, TTk = Zp[yz], "Zp%d" % yz
                dbg(sq, m, "TT", TT, [TTk])
                b1, k1 = mm4(TT, TTk, VB[p], pk("VB"))
                b2, k2 = mm4(KW[p], pk("KW"), TT, TTk)
                act(UU[:], f3(b1[:, :]), AF.Copy, [k1], ["UU"])
                act(WT[:], f3(b2[:, :]), AF.Copy, [k2], ["X2"])
                yield
                b3, k3 = mm4(WT, "X2", Sgb[sq], Sgbk)
                tt("vector", VN[:], UU[:], f3(b3[:, :]), ALU.subtract, ["UU", k3], ["X1"])
                yield
                obk, obkey = nb_back()
                for h in range(4):
                    s.op("tensor", lambda e, h=h: e.matmul(out=obk[:, h * 128:(h + 1) * 128], lhsT=Sgb[sq][:, h, :], rhs=QDEC[p][:, h, :], start=True, stop=False), reads=[Sgbk, pk("QDEC")], writes=[obkey])
                    s.op("tensor", lambda e, h=h: e.matmul(out=obk[:, h * 128:(h + 1) * 128], lhsT=VN[:, h, :], rhs=QKT[p][:, h, :], start=False, stop=True), reads=["X1", pk("QKT")], writes=[obkey])
                act(OB[:], f3(obk[:, :]), AF.Copy, [obkey], ["OB"])
                b4, k4 = mm4(KD[p], pk("KD"), VN, "X1")
                for h in range(4):
                    s.op("vector", lambda e, h=h: e.scalar_tensor_tensor(out=Sg[sq][:, h, :], in0=Sg[sq][:, h, :], scalar=EX[p][:, 8 + h:9 + h], in1=b4[:, h * 128:(h + 1) * 128],
                                                                         op0=ALU.mult, op1=ALU.add), reads=[Sgk, pk("EX"), k4], writes=[Sgk])
                s.op("gpsimd", lambda e: e.tensor_copy(out=Sgb[sq][:], in_=Sg[sq][:]), reads=[Sgk], writes=[Sgbk])
                dbg(sq, m, "ob", OB[:], ["OB"])
                yield

            def early():
                tt("gpsimd", X1[:], OA[p][:], OA[p][:], ALU.mult, [pk("OA")], ["X1"])
                yield
                bk, bkey = nb_back()
                for c in range(4):
                    s.op("tensor", lambda e, bk=bk, c=c: e.matmul(out=bk[:, c * 128:(c + 1) * 128], lhsT=ones_b[:, :], rhs=X1[:, c, :], start=True, stop=True),
                         reads=["ones_b", "X1"], writes=[bkey])
                act(UU[:], f3(bk[:, :]), AF.Ln, [bkey], ["UU"], bias=128e-6)
                yield
                act(X2[:], UU[:], AF.Exp, ["UU"], ["X2"], scale=-0.5, bias=float(0.5 * np.log(128.0)))
                tt("gpsimd", X1[:], OA[p][:], X2[:], ALU.mult, [pk("OA"), "X2"], ["X1"])
                yield
                s.op("vector", lambda e: e.scalar_tensor_tensor(out=YTA[:], in0=X1[:], scalar=small["glanw"][:, 0:1], in1=SG[p3][:, 0:4, :],
                                                                op0=ALU.mult, op1=ALU.mult), reads=["X1", "SG%d" % p3, "glanw"], writes=["YTA"])
                yield

            yield from interleave([invchain(), early()])
            for g, (osb, okey) in ((1, (OB, "OB")),):
                sl = slice(g * 4, (g + 1) * 4)
                tt("vector", SQO[:, sl, :], osb[:], osb[:], ALU.mult, [okey], SQOk)
                bk, bkey = nb_back()
                for c in range(4):
                    s.op("tensor", lambda e, bk=bk, c=c, g=g: e.matmul(out=bk[:, c * 128:(c + 1) * 128], lhsT=ones_b[:, :], rhs=SQO[:, g * 4 + c, :], start=True, stop=True),
                         reads=["ones_b"] + SQOk, writes=[bkey])
                act(UU[:], f3(bk[:, :]), AF.Ln, [bkey], ["UU"], bias=128e-6)
                act(RNOb[:, sl, :], UU[:], AF.Exp, ["UU"], RNObk, scale=-0.5, bias=float(0.5 * np.log(128.0)))
                tt("vector", TO[:, sl, :], osb[:], RNOb[:, sl, :], ALU.mult, [okey] + RNObk, TOk)
                nw = small["glanw"] if g == 0 else small["gdnnw"]
                s.op("vector", lambda e, sl=sl, nw=nw: e.scalar_tensor_tensor(out=YT[:, sl, :], in0=TO[:, sl, :], scalar=nw[:, 0:1], in1=SG[p3][:, sl, :],
                                                                           op0=ALU.mult, op1=ALU.mult), reads=TOk + ["SG%d" % p3, "glanw", "gdnnw"], writes=YTk)
                yield
            for half in range(2):
                bk, bkey = nb_back()
                for kt in range(8):
                    ylhs = YTA[:, kt, :] if kt < 4 else YT[:, kt, :]
                    s.op("tensor", lambda e, bk=bk, kt=kt, half=half, ylhs=ylhs: e.matmul(out=bk[:, :], lhsT=ylhs, rhs=Woutb[:, kt, half * 512:(half + 1) * 512], start=(kt == 0), stop=(kt == 7)),
                         reads=YTk + ["YTA", "Woutb"], writes=[bkey])
                uuf = UU[:].rearrange("p h c -> p (h c)")
                tt("vector", uuf, bk[:, :], g1b[sq][:, half * 512:(half + 1) * 512], ALU.mult, [bkey, "g1b%d" % sq], ["UU"])
                tt("vector", R[:, half * 512:(half + 1) * 512], R[:, half * 512:(half + 1) * 512], uuf, ALU.add, ["R", "UU"], ["R"])
                yield
            s.op("gpsimd", lambda e: e.memset(ST[:], 0.0), writes=["ST"])
            jv = lambda t: t[:].rearrange("p h c -> p (h c)")
            for hf in range(2):
                act(jv(X1), R[:, hf * 512:(hf + 1) * 512], AF.Copy, ["R", "ST"], ["X1", "ST"], accum_out=ST[:, hf:hf + 1])
                act(jv(X1), R[:, hf * 512:(hf + 1) * 512], AF.Square, ["R", "ST"], ["X1", "ST"], accum_out=ST[:, 2 + hf:3 + hf])
            yield
            tt("vector", ST[:, 4:5], ST[:, 0:1], ST[:, 1:2], ALU.add, ["ST"], ["ST"])
            tt("vector", ST[:, 5:6], ST[:, 2:3], ST[:, 3:4], ALU.add, ["ST"], ["ST"])
            s.op("vector", lambda e: e.tensor_scalar(out=ST[:, 4:5], in0=ST[:, 4:5], scalar1=1.0 / D, scalar2=None, op0=ALU.mult), reads=["ST"], writes=["ST"])
            tt("vector", ST[:, 6:7], ST[:, 4:5], ST[:, 4:5], ALU.mult, ["ST"], ["ST"])
            s.op("vector", lambda e: e.scalar_tensor_tensor(out=ST[:, 7:8], in0=ST[:, 5:6], scalar=1.0 / D, in1=ST[:, 6:7], op0=ALU.mult, op1=ALU.subtract), reads=["ST"], writes=["ST"])
            act(ST[:, 7:8], ST[:, 7:8], AF.Ln, ["ST"], ["ST"], bias=1e-5)
            act(ST[:, 7:8], ST[:, 7:8], AF.Exp, ["ST"], ["ST"], scale=-0.5)
            s.op("vector", lambda e: e.scalar_tensor_tensor(out=ST[:, 6:7], in0=ST[:, 4:5], scalar=-1.0, in1=ST[:, 7:8], op0=ALU.mult, op1=ALU.mult), reads=["ST"], writes=["ST"])
            act(R[:], R[:], AF.Identity, ["R", "ST"], ["R"], scale=ST[:, 7:8], bias=ST[:, 6:7])
            tt("vector", R[:], R[:], lnw_b[:], ALU.mult, ["R", "lnw_b"], ["R"])
            tt("gpsimd", R[:], R[:], lnb_b[:], ALU.add, ["R", "lnb_b"], ["R"])
            s.dma("sync", lambda e: e.dma_start(out=out_d[sq, m * 128:(m + 1) * 128, :], in_=R[:]), reads=["R"], group="out")
            yield

        NUu = len(units)
        for u0 in range(min(3, NUu)):
            load_x(u0)
        for it in range(NUu + 2):
            gd = {}
            if 0 <= it - 2 < NUu:
                gd["b"] = back(units[it - 2][0], units[it - 2][1], it - 2)
            if 0 <= it - 1 < NUu:
                gd["m"] = mid(units[it - 1][0], units[it - 1][1], it - 1)
            if it < NUu:
                gd["p"] = pstage(units[it][0], units[it][1], it)
            run(interleave([gd[k] for k in ORDER if k in gd]))
        fw = ["out"] + (["tap"] if (taps and "tap" in s.dma_groups) else [])
        cnt = s.emit(final_wait_groups=fw)
        print("ops", len(s.ops), "signals", cnt)
    return nc


def make_inputs_for_core(core, inputs, consts, n_units):
    T = n_units * 128
    cfa, urefa, bma, sela = consts
    f = lambda a: np.ascontiguousarray(np.asarray(a, dtype=np.float32))
    xs = f(inputs["x"][2 * core:2 * core + 2, :T])
    c2 = f(inputs["c"][2 * core:2 * core + 2])
    cT = np.ascontiguousarray(c2.reshape(2, 8, 128).transpose(2, 1, 0))
    convw = f(inputs["gdn_conv_w"][0])
    convw_l = np.ascontiguousarray(convw.reshape(4, 12, 128).transpose(2, 1, 0).reshape(128, 48))
    return dict(
        x=xs, cT=cT, w_ada=f(inputs["w_ada"][0]), b_ada=f(inputs["b_ada"]), w_in=f(inputs["w_in"][0]),
        wg_aug=np.ascontiguousarray(np.concatenate([f(inputs["gla_w_gate_up"][0]), f(inputs["gla_b_gate"])], axis=0)),
        gla_nw=f(inputs["gla_norm_w"][0]).reshape(128, 1), gdn_nw=f(inputs["gdn_norm_w"][0]).reshape(128, 1),
        convw=convw_l,
        alog_b=np.ascontiguousarray(np.broadcast_to(f(inputs["gdn_a_log"]), (128, 4))),
        dtb_b=np.ascontiguousarray(np.broadcast_to(f(inputs["gdn_dt_bias"]), (128, 4))),
        w_out=f(inputs["w_out"][0]), ln_w=f(inputs["ln_w"]), ln_b=f(inputs["ln_b"]),
        cf=cfa, uref=urefa, bm=bma, sel=sela)


def kernel(**inputs):
    n_units = 32
    consts = host_consts()
    nc = build(n_units)
    in_maps = [make_inputs_for_core(c, inputs, consts, n_units) for c in range(8)]
    res = run_bass_kernel_spmd(nc, in_maps, core_ids=list(range(8)))
    out = np.concatenate([r["out"] for r in res.results], axis=0)
    return out.astype(np.float32)
```

```python
import bisect
import contextlib
import os
import numpy as np
import concourse.bass as bass
import concourse.mybir as mybir
from concourse.bass_utils import run_bass_kernel_spmd

F32 = mybir.dt.float32
BF16 = mybir.dt.bfloat16
AF = mybir.ActivationFunctionType
ALU = mybir.AluOpType

ENGINES = ("tensor", "vector", "scalar", "gpsimd", "sync")
D = 1024
NCOL = 3608
ALPHA = 2.0 ** 0.25
NEG = -30000.0


class Sched:
    def __init__(self, nc):
        self.nc = nc
        self.ops = []
        self.last_writer = {}
        self.readers = {}
        self.dma_groups = {}

    def _add(self, eng, fn, reads, writes, dma_group=None):
        idx = len(self.ops)
        deps = set()
        for r in reads:
            w = self.last_writer.get(r)
            if w is not None:
                deps.add((w, "raw"))
        for r in writes:
            w = self.last_writer.get(r)
            if w is not None:
                deps.add((w, "waw"))
            for rd in self.readers.get(r, ()):
                deps.add((rd, "war"))
        for r in reads:
            self.readers.setdefault(r, []).append(idx)
        for r in writes:
            self.last_writer[r] = idx
            self.readers[r] = []
        op = dict(eng=eng, fn=fn, deps=deps, dma_group=dma_group, signal=False, seq=None)
        if dma_group is not None:
            c = self.dma_groups.get(dma_group, 0) + 1
            self.dma_groups[dma_group] = c
            op["seq"] = c
            op["signal"] = True
        self.ops.append(op)
        return idx

    def op(self, eng, fn, reads=(), writes=()):
        return self._add(eng, fn, tuple(reads), tuple(writes))

    def dma(self, eng, fn, reads=(), writes=(), group=None):
        return self._add(eng, fn, tuple(reads), tuple(writes), dma_group=group)

    def emit(self, final_wait_groups=()):
        nc = self.nc
        ops = self.ops
        for i, op in enumerate(ops):
            nd = set()
            for (p, kind) in op["deps"]:
                pop = ops[p]
                if pop["dma_group"] is None and op["dma_group"] is None and pop["eng"] == op["eng"]:
                    if op["eng"] == "tensor" or kind != "raw":
                        continue
                nd.add(p)
            best = {}
            keep = set()
            for p in nd:
                pop = ops[p]
                if pop["dma_group"] is not None:
                    keep.add(p)
                elif p > best.get(pop["eng"], -1):
                    best[pop["eng"]] = p
            keep.update(best.values())
            op["ndeps"] = keep
            for p in keep:
                ops[p]["signal"] = True
        cnt = {e: 0 for e in ENGINES}
        for op in ops:
            if op["dma_group"] is None and op["signal"]:
                cnt[op["eng"]] += 1
                op["seq"] = cnt[op["eng"]]
        stack = contextlib.ExitStack()
        esem = {e: stack.enter_context(nc.semaphore("s_" + e)) for e in ENGINES if cnt[e] > 0}
        dsem = {g: stack.enter_context(nc.semaphore("d_%d" % k)) for k, g in enumerate(self.dma_groups)}
        per_eng = {e: [] for e in ENGINES}
        grp_idx = {}
        for i, op in enumerate(ops):
            per_eng[op["eng"]].append(i)
            if op["dma_group"] is not None:
                grp_idx.setdefault(op["dma_group"], []).append(i)

        def run_engine(ename, eobj):
            waited = {}
            for i in per_eng[ename]:
                op = ops[i]
                need = {}
                for p in op["ndeps"]:
                    pop = ops[p]
                    if pop["dma_group"] is not None:
                        key = ("d", pop["dma_group"])
                        val = 16 * bisect.bisect_left(grp_idx[pop["dma_group"]], i)
                    else:
                        key = ("e", pop["eng"])
                        val = pop["seq"]
                    if val > need.get(key, 0):
                        need[key] = val
                for key, val in need.items():
                    if waited.get(key, 0) >= val:
                        continue
                    waited[key] = val
                    sem = dsem[key[1]] if key[0] == "d" else esem[key[1]]
                    eobj.wait_ge(sem, val)
                ins = op["fn"](eobj)
                if op["dma_group"] is not None:
                    ins.then_inc(dsem[op["dma_group"]], 16)
                elif op["signal"]:
                    ins.then_inc(esem[ename], 1)
            if ename == "sync":
                for g in final_wait_groups:
                    eobj.wait_ge(dsem[g], 16 * self.dma_groups[g])

        with stack:
            with nc.Block() as block:
                for ename in ENGINES:
                    getattr(block, ename)(lambda eobj, ename=ename: run_engine(ename, eobj))
        return cnt


def host_consts():
    j = np.arange(128)[:, None]
    i = np.arange(128)[None, :]
    f = lambda m: m.astype(np.float32)
    cf = np.zeros((128, 9, 128), np.float32)
    cf[:, 0] = np.eye(128)
    cf[:, 1] = 1.0
    cf[:, 2] = 0.0
    cf[:64, 2, 0] = 1.0
    cf[64:, 2, 1] = 1.0
    cf[:, 3] = -1.0
    cf[:, 4] = -(1.0 / 16.0) * f(j > i)
    cf[:, 5] = f(j <= i)
    cf[:, 6] = f(j > i)
    cf[:, 7] = np.where(i >= j, 0.0, NEG)
    cf[:, 8] = np.where(i > j, 0.0, NEG)
    uref = np.zeros((128, 130), np.float32)
    uref[:, :128] = -(1.0 / 16.0) * (f(j <= i) - f(j <= 63))
    uref[:, 128] = -(1.0 / 16.0) * f(j[:, 0] <= 63)
    uref[:, 129] = -(1.0 / 16.0)
    bm = np.zeros((128, 8, 128), np.float32)
    bm[:, 0] = np.eye(128)
    bm[:, 1] = f(j <= i)
    bm[:, 2] = f((j // 16) == (i // 16))
    for n, s in ((3, 16), (5, 32), (7, 64)):
        low = f(((j // s) % 2 == 1) & ((i // s) == (j // s) - 1))
        if n < 7:
            bm[:, n] = low
            bm[:, n + 1] = low.T
        else:
            bm[:, n] = low.T
    sel = np.zeros((2, 2, 128), np.float32)
    sel[0, 0] = 1.0
    sel[1, 1] = 1.0
    return cf, uref, bm, sel


def interleave(gens):
    gens = list(gens)
    while gens:
        for g in list(gens):
            if g not in gens:
                continue
            try:
                next(g)
            except StopIteration:
                gens = [x for x in gens if x is not g]
        yield


def run(gen):
    for _ in gen:
        pass


PYIELD = int(os.environ.get("K_PYIELD", "1"))
ORDER = os.environ.get("K_ORDER", "bpbmp")
PDELAY = int(os.environ.get("K_PDELAY", "0"))
BANKSETS = {"a": ([0], [1], [2, 3], [4, 5, 6]), "b": ([0, 1], [2], [3, 4], [5, 6]), "c": ([0, 1], [2], [3], [4, 5, 6]), "d": ([0, 1, 2], [3], [4], [5, 6])}[os.environ.get("K_BANKS", "b")]


def build(n_units, taps=False, stage=99):
    T = n_units * 128
    nc = bass.Bass("TRN2", target_bir_lowering=False)
    din = lambda name, shape: nc.dram_tensor(name, shape, F32, kind="ExternalInput").ap()
    x_d = din("x", [2, T, D])
    cT_d = din("cT", [128, 8, 2])
    wada_d = din("w_ada", [D, 3 * D])
    bada_d = din("b_ada", [1, 3 * D])
    win_d = din("w_in", [D, NCOL])
    wg_d = din("wg_aug", [17, 256])
    glanw_d = din("gla_nw", [128, 1])
    gdnnw_d = din("gdn_nw", [128, 1])
    convw_d = din("convw", [128, 48])
    alog_d = din("alog_b", [128, 4])
    dtb_d = din("dtb_b", [128, 4])
    wout_d = din("w_out", [D, D])
    lnw_d = din("ln_w", [1, D])
    lnb_d = din("ln_b", [1, D])
    cf_d = din("cf", [128, 9, 128])
    uref_d = din("uref", [128, 130])
    bm_d = din("bm", [128, 8, 128])
    sel_d = din("sel", [2, 2, 128])
    out_d = nc.dram_tensor("out", [2, T, D], F32, kind="ExternalOutput").ap()

    es = contextlib.ExitStack()
    with es:
        def sb(name, shape, dt=BF16):
            return es.enter_context(nc.sbuf_tensor("s_" + name, shape, dt))
        s = Sched(nc)
        banks = [es.enter_context(nc.psum_tensor("ps%d" % i, [128, 512], F32)) for i in range(7)]
        psb = es.enter_context(nc.psum_tensor("psb", [128, 1024], BF16))

        def mkbanks(ids):
            ctr = [0]

            def nb():
                b = ids[ctr[0] % len(ids)]
                ctr[0] += 1
                return banks[b], "ps%d" % b
            return nb
        nb_setup = mkbanks([0, 1, 2, 3, 4, 5, 6])
        _bs = BANKSETS
        nb_p = mkbanks(_bs[0])
        nb_gla = mkbanks(_bs[1])
        nb_gdn = mkbanks(_bs[2])
        nb_back = mkbanks(_bs[3])

        cf = sb("cf", [128, 9, 128], F32)
        uref = sb("uref", [128, 130], F32)
        R = sb("R", [128, D], F32)
        RN = sb("RN", [128, 8, 128], F32)
        bmf = R[:].rearrange("p (a b) -> p a b", a=8)
        UU = sb("UU", [128, 4, 128], F32)
        sel = UU[0:2, 0:2, :]
        s.dma("sync", lambda e: e.dma_start(out=cf[:], in_=cf_d), writes=["cf"], group="c0")
        s.dma("sync", lambda e: e.dma_start(out=uref[:], in_=uref_d), writes=["uref"], group="c0")
        s.dma("sync", lambda e: e.dma_start(out=bmf, in_=bm_d), writes=["R"], group="c0")
        s.dma("sync", lambda e: e.dma_start(out=sel, in_=sel_d), writes=["UU"], group="c0")
        ident_f, ones_f = cf[:, 0, :], cf[:, 1, :]
        mgt16, uincl, mgt, mt_incl, mt_strict = cf[:, 4, :], cf[:, 5, :], cf[:, 6, :], cf[:, 7, :], cf[:, 8, :]
        bmrep = sb("bmrep", [128, 2, 4, 128])
        bm1 = sb("bm1", [128, 8, 128])
        for n, m in enumerate((0, 2)):
            s.op("vector", lambda e, m=m, n=n: e.tensor_copy(out=bmrep[:, n, :, :], in_=bmf[:, m, :].unsqueeze(1).to_broadcast([128, 4, 128])),
                 reads=["R"], writes=["bmr"])
        s.op("vector", lambda e: e.tensor_copy(out=bm1[:], in_=bmf), reads=["R"], writes=["bmr"])
        bcm = lambda m: bm1[:, m, :].unsqueeze(1).to_broadcast([128, 4, 128])
        identr_b, bd16 = bmrep[:, 0], bmrep[:, 1]
        masku = bcm(1)
        ident_b = bm1[:, 0, :]
        ones_b = sb("ones_b", [128, 128])
        s.op("vector", lambda e: e.tensor_copy(out=ones_b[:], in_=cf[:, 1, :]), reads=["cf"], writes=["ones_b"])

        small = {}
        for name, d, shape in (("glanw", glanw_d, [128, 1]), ("gdnnw", gdnnw_d, [128, 1]), ("convw", convw_d, [128, 48]),
                               ("alog", alog_d, [128, 4]), ("dtb", dtb_d, [128, 4]), ("cT", cT_d, [128, 8, 2])):
            t = sb(name, shape, F32)
            small[name] = t
            s.dma("sync", lambda e, t=t, d=d: e.dma_start(out=t[:], in_=d), writes=[name], group="c1")
        lnw_b = sb("lnw_b", [128, D], F32)
        lnb_b = sb("lnb_b", [128, D], F32)
        s.dma("sync", lambda e: e.dma_start(out=lnw_b[:], in_=lnw_d.partition_broadcast(128)), writes=["lnw_b"], group="c1")
        s.dma("sync", lambda e: e.dma_start(out=lnb_b[:], in_=lnb_d.partition_broadcast(128)), writes=["lnb_b"], group="c1")
        wgb = sb("wgb", [17, 256])
        s.dma("sync", lambda e: e.dma_start(out=RN[0:17, 0:2, :].rearrange("p a b -> p (a b)"), in_=wg_d), writes=["RN"], group="RN")
        s.op("vector", lambda e: e.tensor_copy(out=wgb[:], in_=RN[0:17, 0:2, :].rearrange("p a b -> p (a b)")), reads=["RN"], writes=["wgb"])
        nega = sb("nega", [128, 4], F32)
        s.op("scalar", lambda e: e.activation(out=nega[:], in_=small["alog"][:], func=AF.Exp), reads=["alog"], writes=["nega"])
        s.op("vector", lambda e: e.tensor_scalar(out=nega[:], in0=nega[:], scalar1=-1.0, scalar2=None, op0=ALU.mult), reads=["nega"], writes=["nega"])
        diag = sb("diag", [128, 48, 128])
        for t in range(48):
            s.op("vector" if t % 2 else "gpsimd",
                 lambda e, t=t: e.tensor_scalar(out=diag[:, t, :], in0=ident_f, scalar1=small["convw"][:, t:t + 1], scalar2=None, op0=ALU.mult),
                 reads=["cf", "convw"], writes=["diag"])

        stgs = [(R[:, :], "R"), (RN[:].rearrange("p a b -> p (a b)"), "RN")]
        Wb = sb("Wb", [128, 8, NCOL])
        Woutb = sb("Woutb", [128, 8, D])
        Sg = [sb("Sg%d" % i, [128, 4, 128], F32) for i in range(2)]
        modp = sb("modp", [128, 32], F32)
        g1b = [sb("g1b%d" % i, [128, D], F32) for i in range(2)]
        mch = Sg[0][0:2, :, :].rearrange("p h c -> p (h c)")
        bch = Sg[1][0:2, :, :].rearrange("p h c -> p (h c)")
        modbanks = [(banks[i], "ps%d" % i) for i in range(6)]
        mpb, mpkey = banks[6], "ps6"
        nst = 0
        for kt in range(8):
            for ch in range(3):
                st, key = stgs[nst % 2]
                nst += 1
                s.dma("sync", lambda e, st=st, kt=kt, ch=ch: e.dma_start(out=st[:, 0:1024], in_=wada_d[kt * 128:(kt + 1) * 128, ch * 1024:(ch + 1) * 1024]), writes=[key], group=key)
                for cgl in range(2):
                    bk, bkey = modbanks[ch * 2 + cgl]
                    s.op("tensor", lambda e, bk=bk, st=st, kt=kt, cgl=cgl: e.matmul(out=bk[0:2, 0:512], lhsT=small["cT"][:, kt, :], rhs=st[:, cgl * 512:(cgl + 1) * 512],
                                                                                 start=(kt == 0), stop=(kt == 7)), reads=[key, "cT"], writes=[bkey])
        for cg in range(6):
            bk, bkey = modbanks[cg]
            s.dma("sync", lambda e, cg=cg: e.dma_start(out=bch, in_=bada_d[:, cg * 512:(cg + 1) * 512].partition_broadcast(2)), writes=["Sg1"], group="bch")
            s.op("vector", lambda e, bk=bk: e.tensor_tensor(out=mch, in0=bk[0:2, 0:512], in1=bch, op=ALU.add), reads=[bkey, "Sg1"], writes=["Sg0"])
            if cg >= 2:
                s.op("vector", lambda e: e.tensor_scalar(out=mch, in0=mch, scalar1=1.0, scalar2=None, op0=ALU.add), reads=["Sg0"], writes=["Sg0"])
            if cg < 4:
                for j in range(4):
                    jn = cg * 4 + j
                    s.op("tensor", lambda e, jn=jn, j=j: e.matmul(out=mpb[:, jn * 2:(jn + 1) * 2], lhsT=mch[:, j * 128:(j + 1) * 128], rhs=cf[0:2, 0, 0:2], start=True, stop=True),
                         reads=["Sg0", "cf"], writes=[mpkey])
            else:
                half = cg - 4
                for sq in range(2):
                    b2, b2key = modbanks[sq]
                    s.op("tensor", lambda e, b2=b2, sq=sq: e.matmul(out=b2[:, 0:512], lhsT=sel[:, sq, :], rhs=mch, start=True, stop=True), reads=["Sg0", "UU"], writes=[b2key])
                    s.op("vector", lambda e, b2=b2, sq=sq, half=half: e.tensor_copy(out=g1b[sq][:, half * 512:(half + 1) * 512], in_=b2[:, 0:512]), reads=[b2key], writes=["g1b%d" % sq])
        s.op("vector", lambda e: e.tensor_copy(out=modp[:], in_=mpb[:, 0:32]), reads=[mpkey], writes=["modp"])
        for kt in range(8):
            for ch in range(4):
                c0 = ch * 1024
                w = min(1024, NCOL - c0)
                st, key = stgs[nst % 2]
                nst += 1
                s.dma("sync", lambda e, st=st, kt=kt, c0=c0, w=w: e.dma_start(out=st[:, 0:w], in_=win_d[kt * 128:(kt + 1) * 128, c0:c0 + w]), writes=[key], group=key)
                s.op("vector" if nst % 2 else "gpsimd", lambda e, st=st, kt=kt, c0=c0, w=w: e.tensor_copy(out=Wb[:, kt, c0:c0 + w], in_=st[:, 0:w]), reads=[key], writes=["Wb"])
        for kt in range(8):
            st, key = stgs[nst % 2]
            nst += 1
            s.dma("sync", lambda e, st=st, kt=kt: e.dma_start(out=st[:, 0:D], in_=wout_d[kt * 128:(kt + 1) * 128, :]), writes=[key], group=key)
            s.op("vector" if nst % 2 else "gpsimd", lambda e, st=st, kt=kt: e.tensor_copy(out=Woutb[:, kt, :], in_=st[:, 0:D]), reads=[key], writes=["Woutb"])

        def two(name, shape, dt=BF16):
            return [sb("%s_%d" % (name, i), shape, dt) for i in range(2)]
        xt = [sb("xt_%d" % i, [128, D], F32) for i in range(3)]
        SG = [sb("SG_%d" % i, [128, 8, 128]) for i in range(3)]
        OA = two("OA", [128, 4, 128])
        Am = two("Am", [128, 4, 128]); AT = two("AT", [128, 4, 128])
        KW = two("KW", [128, 4, 128]); KD = two("KD", [128, 4, 128]); VB = two("VB", [128, 4, 128])
        QKT = two("QKT", [128, 4, 128]); QDEC = two("QDEC", [128, 4, 128])
        EX = two("EX", [128, 12], F32)
        hT = sb("hT", [128, 8, 128])
        FM = two("FM", [128, 4, 128])
        U = [sb("U%d" % i, [128, 12, 131]) for i in range(2)]
        lraug = two("lraug", [32, 128])
        KVt = two("KVt", [128, 768])
        spt = sb("spt", [128, 256], F32)
        E1 = sb("E1", [128, 2, 128]); E2 = sb("E2", [128, 2, 128])
        ER = sb("ER", [128, 2, 2], F32)
        QT = sb("QT", [128, 2, 128]); KT = sb("KT", [128, 2, 128])
        EK = sb("EK", [128, 256]); KP = sb("KP", [128, 256])
        ATT = sb("ATT", [128, 4, 128])
        SPb = sb("SPb", [128, 4, 128])
        KTz = sb("KTz", [128, 4, 128])
        Sgla = [sb("Sgla%d" % i, [128, 2, 128], F32) for i in range(2)]
        QKV = sb("QKV", [128, 12, 128])
        SQ = sb("SQ", [128, 8, 128])
        RNb = SQ
        QKN = sb("QKN", [128, 8, 128])
        T4 = sb("T4", [128, 4], F32); G4 = sb("G4", [128, 4], F32); AB = two("AB", [128, 8], F32)
        TB = sb("TB", [128, 4], F32); BETA = sb("BETA", [128, 4], F32); LNT = sb("LNT", [128, 4], F32)
        D4 = sb("D4", [128, 4], F32); ND4 = sb("ND4", [128, 4], F32); D4B = sb("D4B", [128, 4], F32); KWs = sb("KWs", [128, 4], F32)
        DG = RN[:, 4:8, :]; DG2 = RN[:, 0:4, :]
        ED = sb("ED", [128, 4, 128]); E1g = sb("E1g", [128, 4, 128]); E2g = sb("E2g", [128, 4, 128])
        INV = sb("INV", [128, 8, 4, 128])
        Pp = [INV[:, 0], INV[:, 1]]; PTp = [INV[:, 2], INV[:, 3]]; Yp = [INV[:, 4], INV[:, 5]]; Zp = [INV[:, 6], INV[:, 7]]
        X1 = sb("X1", [128, 4, 128]); X2 = sb("X2", [128, 4, 128])
        YTA = sb("YTA", [128, 4, 128])
        WT = X2; VN = X1; OB = sb("OB", [128, 4, 128])
        Sgb = [sb("Sgb%d" % i, [128, 4, 128]) for i in range(2)]
        v8 = lambda a, b: INV[:, a:b].rearrange("p a h c -> p (a h) c")
        SQO, SQOk = v8(0, 2), ["Pp0", "Pp1"]
        TO, TOk = v8(2, 4), ["PTp0", "PTp1"]
        YT, YTk = v8(4, 6), ["Yp0", "Yp1"]
        RNOb, RNObk = v8(6, 8), ["Zp0", "Zp1"]
        RNO = R[:].rearrange("p (a b) -> p a b", a=8)
        JUNK = X1
        ST = sb("ST", [128, 8], F32)
        DBG = None
        if taps:
            try:
                DBG = sb("DBG", [128, 512], F32)
            except AssertionError:
                DBG = None

        for i in range(2):
            s.op("gpsimd", lambda e, i=i: e.memset(Sgla[i][:], 0.0), writes=["Sgla%d" % i])
            s.op("gpsimd", lambda e, i=i: e.memset(Sg[i][:], 0.0), reads=["modp", "g1b0", "g1b1"], writes=["Sg%d" % i])
            s.op("gpsimd", lambda e, i=i: e.memset(Sgb[i][:], 0.0), writes=["Sgb%d" % i])
            s.op("gpsimd", lambda e, i=i: e.memset(U[i][:], 0.0), writes=["U%d" % i])
        for i in range(2):
            s.op("gpsimd", lambda e, i=i: e.memset(lraug[i][:], 1.0), writes=["lraug%d" % i])

        f3 = lambda ap: ap.rearrange("p (h c) -> p h c", h=4)
        bc = lambda ap: ap.unsqueeze(2).to_broadcast([128, 4, 128])

        def tt(eng, out, a, b, op, reads, writes):
            s.op(eng, lambda e: e.tensor_tensor(out=out, in0=a, in1=b, op=op), reads=reads, writes=writes)

        def act(out, in_, func, reads, writes, **kw):
            s.op("scalar", lambda e: e.activation(out=out, in_=in_, func=func, **kw), reads=reads, writes=writes)

        units = [(sq, m) for m in range(n_units) for sq in range(2)]
        dbg_d = {}

        def dbg(sq, m, name, ap, keys):
            if not taps or DBG is None:
                return
            shp = list(ap.shape)
            P = shp[0]
            w = int(np.prod(shp[1:]))
            if name not in dbg_d:
                dbg_d[name] = nc.dram_tensor("dbg_" + name, [2, n_units, 128, 512], F32, kind="ExternalOutput").ap()
            dst = DBG[0:P, 0:w]
            if len(shp) == 3:
                dst = dst.rearrange("p (h c) -> p h c", h=shp[1])
            s.op("vector", lambda e: e.tensor_copy(out=dst, in_=ap), reads=list(keys), writes=["DBG"])
            s.dma("sync", lambda e: e.dma_start(out=dbg_d[name][sq, m, 0:P, 0:w], in_=DBG[0:P, 0:w]), reads=["DBG"], group="tap")

        def load_x(u):
            sq_, m_ = units[u]
            p3_ = u % 3
            s.dma("sync", lambda e: e.dma_start(out=xt[p3_][:], in_=x_d[sq_, m_ * 128:(m_ + 1) * 128, :]), writes=["xt%d" % p3_], group="xt%d" % p3_)

        def pstage(sq, m, u):
            p, p3 = u % 2, u % 3
            xs, xk = xt[p3], "xt%d" % p3
            pk = lambda n: "%s%d" % (n, p)
            Uk = "U%d" % sq
            for _ in range(PDELAY):
                yield
            for g in range(2):
                bk, bkey = nb_p()
                for c in range(4):
                    kt = g * 4 + c
                    s.op("tensor", lambda e, bk=bk, c=c, kt=kt: e.transpose(out=bk[:, c * 128:(c + 1) * 128], in_=xs[:, kt * 128:(kt + 1) * 128], identity=ident_f),
                         reads=[xk, "cf"], writes=[bkey])
                for c in range(4):
                    kt = g * 4 + c
                    s.op("vector", lambda e, bk=bk, c=c, kt=kt: e.tensor_scalar(out=hT[:, kt, :], in0=bk[:, c * 128:(c + 1) * 128],
                                                                             scalar1=modp[:, 16 + kt * 2 + sq:17 + kt * 2 + sq], scalar2=modp[:, kt * 2 + sq:kt * 2 + sq + 1],
                                                                             op0=ALU.mult, op1=ALU.add), reads=[bkey, "modp"], writes=["hT"])
                yield
            fm_cols = [0, 128, 256, 384] + [1040 + 128 * i for i in range(4)] + [3096 + 128 * i for i in range(4)] + [1552 + 128 * i for i in range(12)]
            for g in range(6):
                bk, bkey = nb_p()
                for c in range(4):
                    c0 = fm_cols[g * 4 + c]
                    for kt in range(8):
                        s.op("tensor", lambda e, bk=bk, c=c, c0=c0, kt=kt: e.matmul(out=bk[:, c * 128:(c + 1) * 128], lhsT=Wb[:, kt, c0:c0 + 128], rhs=hT[:, kt, :],
                                                                                  start=(kt == 0), stop=(kt == 7)), reads=["Wb", "hT"], writes=[bkey])
                    if (c + 1) % PYIELD == 0:
                        yield
                if g == 0:
                    act(FM[p][:], f3(bk[:, :]), AF.Copy, [bkey], [pk("FM")])
                elif g in (1, 2):
                    act(SG[p3][:, (g - 1) * 4:g * 4, :], f3(bk[:, :]), AF.Silu, [bkey], ["SG%d" % p3])
                else:
                    s.op("vector", lambda e, bk=bk, g=g: e.tensor_copy(out=U[sq][:, (g - 3) * 4:(g - 2) * 4, 3:131], in_=f3(bk[:, :])), reads=[bkey], writes=[Uk])
            abk, abkey = nb_p()
            for kt in range(8):
                s.op("tensor", lambda e, kt=kt: e.matmul(out=abk[0:16, 0:128], lhsT=Wb[:, kt, 1024:1040], rhs=hT[:, kt, :], start=(kt == 0), stop=(kt == 7)),
                     reads=["Wb", "hT"], writes=[abkey])
            for kt in range(8):
                s.op("tensor", lambda e, kt=kt: e.matmul(out=abk[:, 128:136], lhsT=hT[:, kt, :], rhs=Wb[:, kt, 3088:3096], start=(kt == 0), stop=(kt == 7)),
                     reads=["Wb", "hT"], writes=[abkey])
            s.op("vector", lambda e: e.tensor_copy(out=lraug[p][0:16, :], in_=abk[0:16, 0:128]), reads=[abkey], writes=[pk("lraug")])
            s.op("vector", lambda e: e.tensor_copy(out=AB[p][:], in_=abk[:, 128:136]), reads=[abkey], writes=[pk("AB")])
            yield
            for part, (c0, w) in enumerate(((256, 512), (768, 256))):
                bk, bkey = nb_p()
                for kt in range(8):
                    s.op("tensor", lambda e, bk=bk, kt=kt, c0=c0, w=w: e.matmul(out=bk[:, 0:w], lhsT=hT[:, kt, :], rhs=Wb[:, kt, c0:c0 + w], start=(kt == 0), stop=(kt == 7)),
                         reads=["Wb", "hT"], writes=[bkey])
                s.op("vector", lambda e, bk=bk, c0=c0, w=w: e.tensor_copy(out=KVt[p][:, c0 - 256:c0 - 256 + w], in_=bk[:, 0:w]), reads=[bkey], writes=[pk("KVt")])
                yield

        def mid(sq, m, u):
            yield from interleave([gla(sq, m, u % 2), gdnpre(sq, m, u % 2)])

        def gla(sq, m, p):
            pk = lambda n: "%s%d" % (n, p)
            Slk = "Sgla%d" % sq
            bk, bkey = nb_gla()
            s.op("tensor", lambda e, bk=bk: e.matmul(out=bk[:, 0:256], lhsT=lraug[p][0:17, :], rhs=wgb[0:17, :], start=True, stop=True), reads=[pk("lraug"), "wgb"], writes=[bkey])
            act(spt[:], bk[:, 0:256], AF.Exp, [bkey], ["spt"], scale=-1.0)
            act(spt[:], spt[:], AF.Ln, ["spt"], ["spt"], bias=1.0)
            yield
            pbk, pbkey = nb_gla()
            for hp in range(2):
                s.op("tensor", lambda e, hp=hp: e.matmul(out=pbk[:, hp * 130:(hp + 1) * 130], lhsT=spt[:, hp * 128:(hp + 1) * 128], rhs=uref[:, :], start=True, stop=True),
                     reads=["spt", "uref"], writes=[pbkey])
            pv = pbk[:, 0:260].rearrange("p (h c) -> p h c", h=2)
            act(E1[:], pv[:, :, 0:128], AF.Exp, [pbkey], ["E1"], bias=float(np.log(0.125)))
            act(E2[:], pv[:, :, 0:128], AF.Exp, [pbkey], ["E2"], scale=-1.0)
            act(ER[:], pv[:, :, 128:130], AF.Exp, [pbkey], ["ER"])
            yield
            tt("vector", QT[:], FM[p][:, 0:2, :], E1[:], ALU.mult, [pk("FM"), "E1"], ["QT"])
            tt("vector", KT[:], FM[p][:, 2:4, :], E2[:], ALU.mult, [pk("FM"), "E2"], ["KT"])
            bk, bkey = nb_gla()
            s.op("tensor", lambda e, bk=bk: e.matmul(out=bk[:, 0:256], lhsT=mgt16, rhs=spt[:, :], start=True, stop=True), reads=["spt", "cf"], writes=[bkey])
            act(EK[:], bk[:, 0:256], AF.Exp, [bkey], ["EK"])
            tt("gpsimd", KP[:], KVt[p][:, 0:256], EK[:], ALU.mult, [pk("KVt"), "EK"], ["KP"])
            yield
            for h in range(4):
                hp, par = h // 2, h % 2
                s.op("vector", lambda e, h=h, hp=hp, par=par: e.tensor_scalar(out=KTz[:, h, :], in0=KT[:, hp, :], scalar1=cf[:, 2, par:par + 1], scalar2=None, op0=ALU.mult),
                     reads=["KT", "cf"], writes=["KTz"])
            yield
            bk, bkey = nb_gla()
            for h in range(4):
                hp, par = h // 2, h % 2
                s.op("tensor", lambda e, bk=bk, h=h, hp=hp, par=par: e.matmul(out=bk[:, h * 128:(h + 1) * 128], lhsT=KTz[:, h, :], rhs=QT[:, hp, :],
                                                                           start=True, stop=True), reads=["KTz", "QT"], writes=[bkey])
            tt("vector", ATT[:], f3(bk[:, :]), masku, ALU.mult, [bkey, "bmr"], ["ATT"])
            for h in range(4):
                hp, par = h // 2, h % 2
                s.op("vector", lambda e, h=h, hp=hp, par=par: e.tensor_scalar(out=SPb[:, h, :], in0=Sgla[sq][:, hp, :], scalar1=ER[:, hp, 0:1], scalar2=cf[:, 2, par:par + 1],
                                                                           op0=ALU.mult, op1=ALU.mult), reads=[Slk, "ER", "cf"], writes=["SPb"])
            yield
            oak, oakey = nb_gla()
            for h in range(4):
                hp, par = h // 2, h % 2
                s.op("tensor", lambda e, h=h, hp=hp, par=par: e.matmul(out=oak[:, h * 128:(h + 1) * 128], lhsT=SPb[:, h, :], rhs=QT[:, hp, :],
                                                                      start=True, stop=False), reads=["SPb", "QT"], writes=[oakey])
                s.op("tensor", lambda e, h=h: e.matmul(out=oak[:, h * 128:(h + 1) * 128], lhsT=KVt[p][:, 256 + h * 128:384 + h * 128], rhs=ATT[:, h, :], start=False, stop=True),
                     reads=[pk("KVt"), "ATT"], writes=[oakey])
            act(OA[p][:], f3(oak[:, :]), AF.Copy, [oakey], [pk("OA")])
            dbg(sq, m, "oa", OA[p][:], [pk("OA")])
            yield
            bk, bkey = nb_gla()
            for h in range(4):
                hp, par = h // 2, h % 2
                s.op("tensor", lambda e, bk=bk, h=h, hp=hp, par=par: e.matmul(out=bk[:, h * 128:(h + 1) * 128], lhsT=KP[:, hp * 128:(hp + 1) * 128],
                                                                           rhs=KVt[p][:, 256 + h * 128:384 + h * 128], start=True, stop=True), reads=["KP", pk("KVt")], writes=[bkey])
            for h in range(4):
                hp, par = h // 2, h % 2
                ps_ = slice(par * 64, (par + 1) * 64)
                s.op("vector", lambda e, bk=bk, h=h, hp=hp, ps_=ps_: e.scalar_tensor_tensor(out=Sgla[sq][ps_, hp, :], in0=Sgla[sq][ps_, hp, :], scalar=ER[ps_, hp, 1:2], in1=bk[ps_, h * 128:(h + 1) * 128],
                                                                                     op0=ALU.mult, op1=ALU.add), reads=[Slk, "ER", bkey], writes=[Slk])
            yield

        def gdnpre(sq, m, p):
            pk = lambda n: "%s%d" % (n, p)
            Uk = "U%d" % sq
            for g3 in range(3):
                bk, bkey = nb_gdn()
                for c in range(4):
                    t = g3 * 4 + c
                    for k in range(4):
                        s.op("tensor", lambda e, bk=bk, c=c, t=t, k=k: e.matmul(out=bk[:, c * 128:(c + 1) * 128], lhsT=diag[:, t * 4 + k, :], rhs=U[sq][:, t, k:k + 128],
                                                                              start=(k == 0), stop=(k == 3)), reads=["diag", Uk], writes=[bkey])
                act(QKV[:, g3 * 4:(g3 + 1) * 4, :], f3(bk[:, :]), AF.Silu, [bkey], ["QKV"])
                yield
            s.op("gpsimd", lambda e: e.tensor_copy(out=U[sq][:, :, 0:3], in_=U[sq][:, :, 128:131]), reads=[Uk], writes=[Uk])
            tt("gpsimd", SQ[:], QKV[:, 0:8, :], QKV[:, 0:8, :], ALU.mult, ["QKV"], ["SQ"])
            tt("vector", T4[:], AB[p][:, 0:4], small["dtb"][:], ALU.add, [pk("AB"), "dtb"], ["T4"])
            act(T4[:], T4[:], AF.Exp, ["T4"], ["T4"])
            act(T4[:], T4[:], AF.Ln, ["T4"], ["T4"], bias=1.0)
            tt("vector", G4[:], T4[:], nega[:], ALU.mult, ["T4", "nega"], ["G4"])
            act(TB[:], AB[p][:, 4:8], AF.Exp, [pk("AB")], ["TB"], scale=-1.0)
            s.op("vector", lambda e: e.tensor_scalar(out=TB[:], in0=TB[:], scalar1=1.0, scalar2=None, op0=ALU.add), reads=["TB"], writes=["TB"])
            s.op("vector", lambda e: e.reciprocal(out=BETA[:], in_=TB[:]), reads=["TB"], writes=["BETA"])
            act(LNT[:], TB[:], AF.Ln, ["TB"], ["LNT"])
            yield
            for g in range(2):
                bk, bkey = nb_gdn()
                for c in range(4):
                    s.op("tensor", lambda e, bk=bk, c=c, g=g: e.matmul(out=bk[:, c * 128:(c + 1) * 128], lhsT=ones_b[:, :], rhs=SQ[:, g * 4 + c, :], start=True, stop=True),
                         reads=["ones_b", "SQ"], writes=[bkey])
                act(RN[:, g * 4:(g + 1) * 4, :], f3(bk[:, :]), AF.Ln, [bkey], ["RN"], bias=1e-6)
                act(RNb[:, g * 4:(g + 1) * 4, :], RN[:, g * 4:(g + 1) * 4, :], AF.Exp, ["RN"], ["SQ"], scale=-0.5, bias=(float(np.log(128.0 ** -0.5)) if g == 0 else 0.0))
                yield
            tt("vector", QKN[:], QKV[:, 0:8, :], RNb[:], ALU.mult, ["QKV", "SQ"], ["QKN"])
            dbk, dbkey = nb_gdn()
            for n, lm in enumerate((uincl, mgt, ones_f)):
                s.op("tensor", lambda e, n=n, lm=lm: e.matmul(out=dbk[:, n * 4:(n + 1) * 4], lhsT=lm, rhs=G4[:, :], start=True, stop=True), reads=["cf", "G4"], writes=[dbkey])
            act(EX[p][:], dbk[:, 0:12], AF.Exp, [dbkey], [pk("EX")])
            s.op("vector", lambda e: e.tensor_copy(out=D4[:], in_=dbk[:, 0:4]), reads=[dbkey], writes=["D4"])
            s.op("vector", lambda e: e.tensor_scalar(out=ND4[:], in0=dbk[:, 0:4], scalar1=-1.0, scalar2=None, op0=ALU.mult), reads=[dbkey], writes=["ND4"])
            tt("vector", D4B[:], D4[:], LNT[:], ALU.subtract, ["D4", "LNT"], ["D4B"])
            tt("vector", KWs[:], BETA[:], EX[p][:, 0:4], ALU.mult, ["BETA", pk("EX")], ["KWs"])
            yield
            for t in range(4):
                s.op("tensor", lambda e, t=t: e.transpose(out=psb[:, t * 128:(t + 1) * 128], in_=QKN[:, 4 + t, :], identity=ident_b), reads=["QKN", "bmr"], writes=["psb"])
            for t in range(4):
                s.op("tensor", lambda e, t=t: e.transpose(out=psb[:, 512 + t * 128:512 + (t + 1) * 128], in_=QKV[:, 8 + t, :], identity=ident_b), reads=["QKV", "bmr"], writes=["psb"])
            tt("vector", KW[p][:], f3(psb[:, 0:512]), bc(KWs[:]), ALU.mult, ["psb", "KWs"], [pk("KW")])
            tt("vector", KD[p][:], f3(psb[:, 0:512]), bc(EX[p][:, 4:8]), ALU.mult, ["psb", pk("EX")], [pk("KD")])
            tt("vector", VB[p][:], f3(psb[:, 512:1024]), bc(BETA[:]), ALU.mult, ["psb", "BETA"], [pk("VB")])
            yield
            tt("gpsimd", DG, identr_b, bc(D4[:]), ALU.mult, ["bmr", "D4"], ["RN"])
            tt("gpsimd", DG2, identr_b, bc(D4B[:]), ALU.mult, ["bmr", "D4B"], ["RN"])
            dgf = DG.rearrange("p h c -> p (h c)")
            dg2f = DG2.rearrange("p h c -> p (h c)")
            bk, bkey = nb_gdn()
            s.op("tensor", lambda e, bk=bk: e.matmul(out=bk[:, :], lhsT=ones_f, rhs=dgf, start=True, stop=True), reads=["cf", "RN"], writes=[bkey])
            act(ED[:], f3(bk[:, :]), AF.Exp, [bkey], ["ED"])
            yield
            for (src, srck, msk, dst, dstk) in ((dgf, "RN", mt_incl, E1g, "E1g"), (dg2f, "RN", mt_strict, E2g, "E2g")):
                bk, bkey = nb_gdn()
                for h in range(4):
                    s.op("tensor", lambda e, bk=bk, src=src, h=h: e.matmul(out=bk[:, h * 128:(h + 1) * 128], lhsT=ones_f, rhs=src[:, h * 128:(h + 1) * 128], start=True, stop=False),
                         reads=["cf", srck], writes=[bkey])
                    s.op("tensor", lambda e, bk=bk, h=h, msk=msk: e.matmul(out=bk[:, h * 128:(h + 1) * 128], lhsT=ident_f, rhs=msk, start=False, stop=False), reads=["cf"], writes=[bkey])
                    s.op("tensor", lambda e, bk=bk, h=h: e.matmul(out=bk[:, h * 128:(h + 1) * 128], lhsT=DG[:, h, :], rhs=cf[:, 3, :], start=False, stop=True), reads=["cf", "RN"], writes=[bkey])
                act(dst[:], f3(bk[:, :]), AF.Exp, [bkey], [dstk])
                yield
            tt("gpsimd", QDEC[p][:], QKN[:, 0:4, :], ED[:], ALU.mult, ["QKN", "ED"], [pk("QDEC")])
            gk, gkey = nb_gdn()
            for h in range(4):
                s.op("tensor", lambda e, h=h: e.matmul(out=gk[:, h * 128:(h + 1) * 128], lhsT=QKN[:, 4 + h, :], rhs=QKN[:, 4 + h, :], start=True, stop=True), reads=["QKN"], writes=[gkey])
            tt("vector", AT[p][:], f3(gk[:, :]), E2g[:], ALU.mult, [gkey, "E2g"], [pk("AT")])
            qk, qkey = nb_gdn()
            for h in range(4):
                s.op("tensor", lambda e, h=h: e.matmul(out=qk[:, h * 128:(h + 1) * 128], lhsT=QKN[:, 4 + h, :], rhs=QKN[:, h, :], start=True, stop=True), reads=["QKN"], writes=[qkey])
            tt("vector", QKT[p][:], f3(qk[:, :]), E1g[:], ALU.mult, [qkey, "E1g"], [pk("QKT")])
            yield
            for h in range(4):
                s.op("tensor", lambda e, h=h: e.transpose(out=psb[:, h * 128:(h + 1) * 128], in_=AT[p][:, h, :], identity=ident_b), reads=[pk("AT"), "bmr"], writes=["psb"])
            act(Am[p][:], f3(psb[:, 0:512]), AF.Copy, ["psb"], [pk("Am")])
            yield

        def back(sq, m, u):
            p, p3 = u % 2, u % 3
            xs, xk = xt[p3], "xt%d" % p3
            pk = lambda n: "%s%d" % (n, p)
            Sgk, Sgbk = "Sg%d" % sq, "Sgb%d" % sq
            A_, Ak, AT_, ATk = Am[p], pk("Am"), AT[p], pk("AT")

            def mm4(L, Lk, Rt, Rk):
                bk, bkey = nb_back()
                for h in range(4):
                    s.op("tensor", lambda e, bk=bk, h=h: e.matmul(out=bk[:, h * 128:(h + 1) * 128], lhsT=L[:, h, :], rhs=Rt[:, h, :], start=True, stop=True),
                         reads=[Lk, Rk], writes=[bkey])
                return bk, bkey
            def invchain():
                tt("vector", Pp[0], A_[:], bd16, ALU.mult, [Ak, "bmr"], ["Pp0"])
                tt("gpsimd", PTp[0], AT_[:], bd16, ALU.mult, [ATk, "bmr"], ["PTp0"])
                tt("vector", Yp[0], identr_b, Pp[0], ALU.subtract, ["bmr", "Pp0"], ["Yp0"])
                tt("gpsimd", Zp[0], identr_b, PTp[0], ALU.subtract, ["bmr", "PTp0"], ["Zp0"])
                yield
                act(R[:], xs[:], AF.Copy, [xk], ["R"], scale=ALPHA)
                if u + 3 < len(units):
                    load_x(u + 3)
                cur = 0
                yz = 0
                for lvl in range(3):
                    nx = 1 - cur
                    b1, k1 = mm4(PTp[cur], "PTp%d" % cur, Pp[cur], "Pp%d" % cur)
                    b2, k2 = mm4(Pp[cur], "Pp%d" % cur, PTp[cur], "PTp%d" % cur)
                    act(Pp[nx], f3(b1[:, :]), AF.Copy, [k1], ["Pp%d" % nx])
                    s.op("vector", lambda e, b2=b2, nx=nx: e.tensor_copy(out=PTp[nx], in_=f3(b2[:, :])), reads=[k2], writes=["PTp%d" % nx])
                    yield
                    b3, k3 = mm4(PTp[nx], "PTp%d" % nx, Yp[yz], "Yp%d" % yz)
                    b4, k4 = mm4(Pp[nx], "Pp%d" % nx, Zp[yz], "Zp%d" % yz)
                    tt("vector", Yp[1 - yz], f3(b3[:, :]), Yp[yz], ALU.add, [k3, "Yp%d" % yz], ["Yp%d" % (1 - yz)])
                    tt("vector", Zp[1 - yz], f3(b4[:, :]), Zp[yz], ALU.add, [k4, "Zp%d" % yz], ["Zp%d" % (1 - yz)])
                    yield
                    cur = nx
                    yz = 1 - yz
                for (mi, both) in ((3, True), (5, True), (7, False)):
                    Yc, Zc, Yk, Zk = Yp[yz], Zp[yz], "Yp%d" % yz, "Zp%d" % yz
                    Yn, Zn, Ynk, Znk = Yp[1 - yz], Zp[1 - yz], "Yp%d" % (1 - yz), "Zp%d" % (1 - yz)
                    mT = bcm(mi + 1) if both else bcm(mi)
                    b1, k1 = mm4(A_, Ak, Zc, Zk)
                    if both:
                        b3, k3 = mm4(AT_, ATk, Yc, Yk)
                    tt("vector", X1[:], f3(b1[:, :]), mT, ALU.mult, [k1, "bmr"], ["X1"])
                    if both:
                        tt("vector", X2[:], f3(b3[:, :]), bcm(mi), ALU.mult, [k3, "bmr"], ["X2"])
                    yield
                    b2, k2 = mm4(Yc, Yk, X1, "X1")
                    if both:
                        b4, k4 = mm4(Zc, Zk, X2, "X2")
                    tt("vector", Zn, Zc, f3(b2[:, :]), ALU.subtract, [Zk, k2], [Znk])
                    if both:
                        tt("vector", Yn, Yc, f3(b4[:, :]), ALU.subtract, [Yk, k4], [Ynk])
                    yield
                    yz = 1 - yz
                TT, TTk = Zp[yz], "Zp%d" % yz
                dbg(sq, m, "TT", TT, [TTk])
                b1, k1 = mm4(TT, TTk, VB[p], pk("VB"))
                b2, k2 = mm4(KW[p], pk("KW"), TT, TTk)
                act(UU[:], f3(b1[:, :]), AF.Copy, [k1], ["UU"])
                act(WT[:], f3(b2[:, :]), AF.Copy, [k2], ["X2"])
                yield
                b3, k3 = mm4(WT, "X2", Sgb[sq], Sgbk)
                tt("vector", VN[:], UU[:], f3(b3[:, :]), ALU.subtract, ["UU", k3], ["X1"])
                yield
                obk, obkey = nb_back()
                for h in range(4):
                    s.op("tensor", lambda e, h=h: e.matmul(out=obk[:, h * 128:(h + 1) * 128], lhsT=Sgb[sq][:, h, :], rhs=QDEC[p][:, h, :], start=True, stop=False), reads=[Sgbk, pk("QDEC")], writes=[obkey])
                    s.op("tensor", lambda e, h=h: e.matmul(out=obk[:, h * 128:(h + 1) * 128], lhsT=VN[:, h, :], rhs=QKT[p][:, h, :], start=False, stop=True), reads=["X1", pk("QKT")], writes=[obkey])
                act(OB[:], f3(obk[:, :]), AF.Copy, [obkey], ["OB"])
                b4, k4 = mm4(KD[p], pk("KD"), VN, "X1")
                for h in range(4):
                    s.op("vector", lambda e, h=h: e.scalar_tensor_tensor(out=Sg[sq][:, h, :], in0=Sg[sq][:, h, :], scalar=EX[p][:, 8 + h:9 + h], in1=b4[:, h * 128:(h + 1) * 128],
                                                                         op0=ALU.mult, op1=ALU.add), reads=[Sgk, pk("EX"), k4], writes=[Sgk])
                s.op("gpsimd", lambda e: e.tensor_copy(out=Sgb[sq][:], in_=Sg[sq][:]), reads=[Sgk], writes=[Sgbk])
                dbg(sq, m, "ob", OB[:], ["OB"])
                yield

            def early():
                tt("gpsimd", X1[:], OA[p][:], OA[p][:], ALU.mult, [pk("OA")], ["X1"])
                yield
                bk, bkey = nb_back()
                for c in range(4):
                    s.op("tensor", lambda e, bk=bk, c=c: e.matmul(out=bk[:, c * 128:(c + 1) * 128], lhsT=ones_b[:, :], rhs=X1[:, c, :], start=True, stop=True),
                         reads=["ones_b", "X1"], writes=[bkey])
                act(UU[:], f3(bk[:, :]), AF.Ln, [bkey], ["UU"], bias=128e-6)
                yield
                act(X2[:], UU[:], AF.Exp, ["UU"], ["X2"], scale=-0.5, bias=float(0.5 * np.log(128.0)))
                tt("gpsimd", X1[:], OA[p][:], X2[:], ALU.mult, [pk("OA"), "X2"], ["X1"])
                yield
                s.op("vector", lambda e: e.scalar_tensor_tensor(out=YTA[:], in0=X1[:], scalar=small["glanw"][:, 0:1], in1=SG[p3][:, 0:4, :],
                                                                op0=ALU.mult, op1=ALU.mult), reads=["X1", "SG%d" % p3, "glanw"], writes=["YTA"])
                yield

            yield from interleave([invchain(), early()])
            for g, (osb, okey) in ((1, (OB, "OB")),):
                sl = slice(g * 4, (g + 1) * 4)
                tt("vector", SQO[:, sl, :], osb[:], osb[:], ALU.mult, [okey], SQOk)
                bk, bkey = nb_back()
                for c in range(4):
                    s.op("tensor", lambda e, bk=bk, c=c, g=g: e.matmul(out=bk[:, c * 128:(c + 1) * 128], lhsT=ones_b[:, :], rhs=SQO[:, g * 4 + c, :], start=True, stop=True),
                         reads=["ones_b"] + SQOk, writes=[bkey])
                act(UU[:], f3(bk[:, :]), AF.Ln, [bkey], ["UU"], bias=128e-6)
                act(RNOb[:, sl, :], UU[:], AF.Exp, ["UU"], RNObk, scale=-0.5, bias=float(0.5 * np.log(128.0)))
                tt("vector", TO[:, sl, :], osb[:], RNOb[:, sl, :], ALU.mult, [okey] + RNObk, TOk)
                nw = small["glanw"] if g == 0 else small["gdnnw"]
                s.op("vector", lambda e, sl=sl, nw=nw: e.scalar_tensor_tensor(out=YT[:, sl, :], in0=TO[:, sl, :], scalar=nw[:, 0:1], in1=SG[p3][:, sl, :],
                                                                           op0=ALU.mult, op1=ALU.mult), reads=TOk + ["SG%d" % p3, "glanw", "gdnnw"], writes=YTk)
                yield
            for half in range(2):
                bk, bkey = nb_back()
                for kt in range(8):
                    ylhs = YTA[:, kt, :] if kt < 4 else YT[:, kt, :]
                    s.op("tensor", lambda e, bk=bk, kt=kt, half=half, ylhs=ylhs: e.matmul(out=bk[:, :], lhsT=ylhs, rhs=Woutb[:, kt, half * 512:(half + 1) * 512], start=(kt == 0), stop=(kt == 7)),
                         reads=YTk + ["YTA", "Woutb"], writes=[bkey])
                uuf = UU[:].rearrange("p h c -> p (h c)")
                tt("vector", uuf, bk[:, :], g1b[sq][:, half * 512:(half + 1) * 512], ALU.mult, [bkey, "g1b%d" % sq], ["UU"])
                tt("vector", R[:, half * 512:(half + 1) * 512], R[:, half * 512:(half + 1) * 512], uuf, ALU.add, ["R", "UU"], ["R"])
                yield
            s.op("gpsimd", lambda e: e.memset(ST[:], 0.0), writes=["ST"])
            jv = lambda t: t[:].rearrange("p h c -> p (h c)")
            for hf in range(2):
                act(jv(X1), R[:, hf * 512:(hf + 1) * 512], AF.Copy, ["R", "ST"], ["X1", "ST"], accum_out=ST[:, hf:hf + 1])
                act(jv(X1), R[:, hf * 512:(hf + 1) * 512], AF.Square, ["R", "ST"], ["X1", "ST"], accum_out=ST[:, 2 + hf:3 + hf])
            yield
            tt("vector", ST[:, 4:5], ST[:, 0:1], ST[:, 1:2], ALU.add, ["ST"], ["ST"])
            tt("vector", ST[:, 5:6], ST[:, 2:3], ST[:, 3:4], ALU.add, ["ST"], ["ST"])
            s.op("vector", lambda e: e.tensor_scalar(out=ST[:, 4:5], in0=ST[:, 4:5], scalar1=1.0 / D, scalar2=None, op0=ALU.mult), reads=["ST"], writes=["ST"])
            tt("vector", ST[:, 6:7], ST[:, 4:5], ST[:, 4:5], ALU.mult, ["ST"], ["ST"])
            s.op("vector", lambda e: e.scalar_tensor_tensor(out=ST[:, 7:8], in0=ST[:, 5:6], scalar=1.0 / D, in1=ST[:, 6:7], op0=ALU.mult, op1=ALU.subtract), reads=["ST"], writes=["ST"])
            act(ST[:, 7:8], ST[:, 7:8], AF.Ln, ["ST"], ["ST"], bias=1e-5)
            act(ST[:, 7:8], ST[:, 7:8], AF.Exp, ["ST"], ["ST"], scale=-0.5)
            s.op("vector", lambda e: e.scalar_tensor_tensor(out=ST[:, 6:7], in0=ST[:, 4:5], scalar=-1.0, in1=ST[:, 7:8], op0=ALU.mult, op1=ALU.mult), reads=["ST"], writes=["ST"])
            act(R[:], R[:], AF.Identity, ["R", "ST"], ["R"], scale=ST[:, 7:8], bias=ST[:, 6:7])
            tt("vector", R[:], R[:], lnw_b[:], ALU.mult, ["R", "lnw_b"], ["R"])
            tt("gpsimd", R[:], R[:], lnb_b[:], ALU.add, ["R", "lnb_b"], ["R"])
            s.dma("sync", lambda e: e.dma_start(out=out_d[sq, m * 128:(m + 1) * 128, :], in_=R[:]), reads=["R"], group="out")
            yield

        NUu = len(units)
        for u0 in range(min(3, NUu)):
            load_x(u0)
        for it in range(NUu + 2):
            gd = {}
            if 0 <= it - 2 < NUu:
                gd["b"] = back(units[it - 2][0], units[it - 2][1], it - 2)
            if 0 <= it - 1 < NUu:
                gd["m"] = mid(units[it - 1][0], units[it - 1][1], it - 1)
            if it < NUu:
                gd["p"] = pstage(units[it][0], units[it][1], it)
            run(interleave([gd[k] for k in ORDER if k in gd]))
        fw = ["out"] + (["tap"] if (taps and "tap" in s.dma_groups) else [])
        cnt = s.emit(final_wait_groups=fw)
        print("ops", len(s.ops), "signals", cnt)
    return nc


def make_inputs_for_core(core, inputs, consts, n_units):
    T = n_units * 128
    cfa, urefa, bma, sela = consts
    f = lambda a: np.ascontiguousarray(np.asarray(a, dtype=np.float32))
    xs = f(inputs["x"][2 * core:2 * core + 2, :T])
    c2 = f(inputs["c"][2 * core:2 * core + 2])
    cT = np.ascontiguousarray(c2.reshape(2, 8, 128).transpose(2, 1, 0))
    convw = f(inputs["gdn_conv_w"][0])
    convw_l = np.ascontiguousarray(convw.reshape(4, 12, 128).transpose(2, 1, 0).reshape(128, 48))
    return dict(
        x=xs, cT=cT, w_ada=f(inputs["w_ada"][0]), b_ada=f(inputs["b_ada"]), w_in=f(inputs["w_in"][0]),
        wg_aug=np.ascontiguousarray(np.concatenate([f(inputs["gla_w_gate_up"][0]), f(inputs["gla_b_gate"])], axis=0)),
        gla_nw=f(inputs["gla_norm_w"][0]).reshape(128, 1), gdn_nw=f(inputs["gdn_norm_w"][0]).reshape(128, 1),
        convw=convw_l,
        alog_b=np.ascontiguousarray(np.broadcast_to(f(inputs["gdn_a_log"]), (128, 4))),
        dtb_b=np.ascontiguousarray(np.broadcast_to(f(inputs["gdn_dt_bias"]), (128, 4))),
        w_out=f(inputs["w_out"][0]), ln_w=f(inputs["ln_w"]), ln_b=f(inputs["ln_b"]),
        cf=cfa, uref=urefa, bm=bma, sel=sela)


def kernel(**inputs):
    n_units = 32
    consts = host_consts()
    nc = build(n_units)
    in_maps = [make_inputs_for_core(c, inputs, consts, n_units) for c in range(8)]
    res = run_bass_kernel_spmd(nc, in_maps, core_ids=list(range(8)))
    out = np.concatenate([r["out"] for r in res.results], axis=0)
    return out.astype(np.float32)
```

```python
import bisect
import contextlib
import os
import numpy as np
import concourse.bass as bass
import concourse.mybir as mybir
from concourse.bass_utils import run_bass_kernel_spmd

F32 = mybir.dt.float32
BF16 = mybir.dt.bfloat16
AF = mybir.ActivationFunctionType
ALU = mybir.AluOpType

ENGINES = ("tensor", "vector", "scalar", "gpsimd", "sync")
D = 1024
NCOL = 3608
ALPHA = 2.0 ** 0.25
NEG = -30000.0


class Sched:
    def __init__(self, nc):
        self.nc = nc
        self.ops = []
        self.last_writer = {}
        self.readers = {}
        self.dma_groups = {}

    def _add(self, eng, fn, reads, writes, dma_group=None):
        idx = len(self.ops)
        deps = set()
        for r in reads:
            w = self.last_writer.get(r)
            if w is not None:
                deps.add((w, "raw"))
        for r in writes:
            w = self.last_writer.get(r)
            if w is not None:
                deps.add((w, "waw"))
            for rd in self.readers.get(r, ()):
                deps.add((rd, "war"))
        for r in reads:
            self.readers.setdefault(r, []).append(idx)
        for r in writes:
            self.last_writer[r] = idx
            self.readers[r] = []
        op = dict(eng=eng, fn=fn, deps=deps, dma_group=dma_group, signal=False, seq=None)
        if dma_group is not None:
            c = self.dma_groups.get(dma_group, 0) + 1
            self.dma_groups[dma_group] = c
            op["seq"] = c
            op["signal"] = True
        self.ops.append(op)
        return idx

    def op(self, eng, fn, reads=(), writes=()):
        return self._add(eng, fn, tuple(reads), tuple(writes))

    def dma(self, eng, fn, reads=(), writes=(), group=None):
        return self._add(eng, fn, tuple(reads), tuple(writes), dma_group=group)

    def emit(self, final_wait_groups=()):
        nc = self.nc
        ops = self.ops
        for i, op in enumerate(ops):
            nd = set()
            for (p, kind) in op["deps"]:
                pop = ops[p]
                if pop["dma_group"] is None and op["dma_group"] is None and pop["eng"] == op["eng"]:
                    if op["eng"] == "tensor" or kind != "raw":
                        continue
                nd.add(p)
            best = {}
            keep = set()
            for p in nd:
                pop = ops[p]
                if pop["dma_group"] is not None:
                    keep.add(p)
                elif p > best.get(pop["eng"], -1):
                    best[pop["eng"]] = p
            keep.update(best.values())
            op["ndeps"] = keep
            for p in keep:
                ops[p]["signal"] = True
        cnt = {e: 0 for e in ENGINES}
        for op in ops:
            if op["dma_group"] is None and op["signal"]:
                cnt[op["eng"]] += 1
                op["seq"] = cnt[op["eng"]]
        stack = contextlib.ExitStack()
        esem = {e: stack.enter_context(nc.semaphore("s_" + e)) for e in ENGINES if cnt[e] > 0}
        dsem = {g: stack.enter_context(nc.semaphore("d_%d" % k)) for k, g in enumerate(self.dma_groups)}
        per_eng = {e: [] for e in ENGINES}
        grp_idx = {}
        for i, op in enumerate(ops):
            per_eng[op["eng"]].append(i)
            if op["dma_group"] is not None:
                grp_idx.setdefault(op["dma_group"], []).append(i)

        def run_engine(ename, eobj):
            waited = {}
            for i in per_eng[ename]:
                op = ops[i]
                need = {}
                for p in op["ndeps"]:
                    pop = ops[p]
                    if pop["dma_group"] is not None:
                        key = ("d", pop["dma_group"])
                        val = 16 * bisect.bisect_left(grp_idx[pop["dma_group"]], i)
                    else:
                        key = ("e", pop["eng"])
                        val = pop["seq"]
                    if val > need.get(key, 0):
                        need[key] = val
                for key, val in need.items():
                    if waited.get(key, 0) >= val:
                        continue
                    waited[key] = val
                    sem = dsem[key[1]] if key[0] == "d" else esem[key[1]]
                    eobj.wait_ge(sem, val)
                ins = op["fn"](eobj)
                if op["dma_group"] is not None:
                    ins.then_inc(dsem[op["dma_group"]], 16)
                elif op["signal"]:
                    ins.then_inc(esem[ename], 1)
            if ename == "sync":
                for g in final_wait_groups:
                    eobj.wait_ge(dsem[g], 16 * self.dma_groups[g])

        with stack:
            with nc.Block() as block:
                for ename in ENGINES:
                    getattr(block, ename)(lambda eobj, ename=ename: run_engine(ename, eobj))
        return cnt


def host_consts():
    j = np.arange(128)[:, None]
    i = np.arange(128)[None, :]
    f = lambda m: m.astype(np.float32)
    cf = np.zeros((128, 9, 128), np.float32)
    cf[:, 0] = np.eye(128)
    cf[:, 1] = 1.0
    cf[:, 2] = 0.0
    cf[:64, 2, 0] = 1.0
    cf[64:, 2, 1] = 1.0
    cf[:, 3] = -1.0
    cf[:, 4] = -(1.0 / 16.0) * f(j > i)
    cf[:, 5] = f(j <= i)
    cf[:, 6] = f(j > i)
    cf[:, 7] = np.where(i >= j, 0.0, NEG)
    cf[:, 8] = np.where(i > j, 0.0, NEG)
    uref = np.zeros((128, 130), np.float32)
    uref[:, :128] = -(1.0 / 16.0) * (f(j <= i) - f(j <= 63))
    uref[:, 128] = -(1.0 / 16.0) * f(j[:, 0] <= 63)
    uref[:, 129] = -(1.0 / 16.0)
    bm = np.zeros((128, 8, 128), np.float32)
    bm[:, 0] = np.eye(128)
    bm[:, 1] = f(j <= i)
    bm[:, 2] = f((j // 16) == (i // 16))
    for n, s in ((3, 16), (5, 32), (7, 64)):
        low = f(((j // s) % 2 == 1) & ((i // s) == (j // s) - 1))
        if n < 7:
            bm[:, n] = low
            bm[:, n + 1] = low.T
        else:
            bm[:, n] = low.T
    sel = np.zeros((2, 2, 128), np.float32)
    sel[0, 0] = 1.0
    sel[1, 1] = 1.0
    return cf, uref, bm, sel


def interleave(gens):
    gens = list(gens)
    while gens:
        for g in list(gens):
            if g not in gens:
                continue
            try:
                next(g)
            except StopIteration:
                gens = [x for x in gens if x is not g]
        yield


def run(gen):
    for _ in gen:
        pass


PYIELD = int(os.environ.get("K_PYIELD", "1"))
ORDER = os.environ.get("K_ORDER", "bpbmp")
PDELAY = int(os.environ.get("K_PDELAY", "0"))
BANKSETS = {"a": ([0], [1], [2, 3], [4, 5, 6]), "b": ([0, 1], [2], [3, 4], [5, 6]), "c": ([0, 1], [2], [3], [4, 5, 6]), "d": ([0, 1, 2], [3], [4], [5, 6]), "e": ([0, 1], [2, 3], [4], [5, 6])}[os.environ.get("K_BANKS", "b")]


def build(n_units, taps=False, stage=99):
    T = n_units * 128
    nc = bass.Bass("TRN2", target_bir_lowering=False)
    din = lambda name, shape: nc.dram_tensor(name, shape, F32, kind="ExternalInput").ap()
    x_d = din("x", [2, T, D])
    cT_d = din("cT", [128, 8, 2])
    wada_d = din("w_ada", [D, 3 * D])
    bada_d = din("b_ada", [1, 3 * D])
    win_d = din("w_in", [D, NCOL])
    wg_d = din("wg_aug", [17, 256])
    glanw_d = din("gla_nw", [128, 1])
    gdnnw_d = din("gdn_nw", [128, 1])
    convw_d = din("convw", [128, 48])
    alog_d = din("alog_b", [128, 4])
    dtb_d = din("dtb_b", [128, 4])
    wout_d = din("w_out", [D, D])
    lnw_d = din("ln_w", [1, D])
    lnb_d = din("ln_b", [1, D])
    cf_d = din("cf", [128, 9, 128])
    uref_d = din("uref", [128, 130])
    bm_d = din("bm", [128, 8, 128])
    sel_d = din("sel", [2, 2, 128])
    out_d = nc.dram_tensor("out", [2, T, D], F32, kind="ExternalOutput").ap()

    es = contextlib.ExitStack()
    with es:
        def sb(name, shape, dt=BF16):
            return es.enter_context(nc.sbuf_tensor("s_" + name, shape, dt))
        s = Sched(nc)
        banks = [es.enter_context(nc.psum_tensor("ps%d" % i, [128, 512], F32)) for i in range(7)]
        psb = es.enter_context(nc.psum_tensor("psb", [128, 1024], BF16))

        def mkbanks(ids):
            ctr = [0]

            def nb():
                b = ids[ctr[0] % len(ids)]
                ctr[0] += 1
                return banks[b], "ps%d" % b
            return nb
        nb_setup = mkbanks([0, 1, 2, 3, 4, 5, 6])
        _bs = BANKSETS
        nb_p = mkbanks(_bs[0])
        nb_gla = mkbanks(_bs[1])
        nb_gdn = mkbanks(_bs[2])
        nb_back = mkbanks(_bs[3])

        cf = sb("cf", [128, 9, 128], F32)
        uref = sb("uref", [128, 130], F32)
        R = sb("R", [128, D], F32)
        RN = sb("RN", [128, 8, 128], F32)
        bmf = R[:].rearrange("p (a b) -> p a b", a=8)
        UU = sb("UU", [128, 4, 128], F32)
        sel = UU[0:2, 0:2, :]
        s.dma("sync", lambda e: e.dma_start(out=cf[:], in_=cf_d), writes=["cf"], group="c0")
        s.dma("sync", lambda e: e.dma_start(out=uref[:], in_=uref_d), writes=["uref"], group="c0")
        s.dma("sync", lambda e: e.dma_start(out=bmf, in_=bm_d), writes=["R"], group="c0")
        s.dma("sync", lambda e: e.dma_start(out=sel, in_=sel_d), writes=["UU"], group="c0")
        ident_f, ones_f = cf[:, 0, :], cf[:, 1, :]
        mgt16, uincl, mgt, mt_incl, mt_strict = cf[:, 4, :], cf[:, 5, :], cf[:, 6, :], cf[:, 7, :], cf[:, 8, :]
        bmrep = sb("bmrep", [128, 2, 4, 128])
        bm1 = sb("bm1", [128, 8, 128])
        for n, m in enumerate((0, 2)):
            s.op("vector", lambda e, m=m, n=n: e.tensor_copy(out=bmrep[:, n, :, :], in_=bmf[:, m, :].unsqueeze(1).to_broadcast([128, 4, 128])),
                 reads=["R"], writes=["bmr"])
        s.op("vector", lambda e: e.tensor_copy(out=bm1[:], in_=bmf), reads=["R"], writes=["bmr"])
        bcm = lambda m: bm1[:, m, :].unsqueeze(1).to_broadcast([128, 4, 128])
        identr_b, bd16 = bmrep[:, 0], bmrep[:, 1]
        masku = bcm(1)
        ident_b = bm1[:, 0, :]
        ones_b = sb("ones_b", [128, 128])
        s.op("vector", lambda e: e.tensor_copy(out=ones_b[:], in_=cf[:, 1, :]), reads=["cf"], writes=["ones_b"])

        small = {}
        for name, d, shape in (("glanw", glanw_d, [128, 1]), ("gdnnw", gdnnw_d, [128, 1]), ("convw", convw_d, [128, 48]),
                               ("alog", alog_d, [128, 4]), ("dtb", dtb_d, [128, 4]), ("cT", cT_d, [128, 8, 2])):
            t = sb(name, shape, F32)
            small[name] = t
            s.dma("sync", lambda e, t=t, d=d: e.dma_start(out=t[:], in_=d), writes=[name], group="c1")
        lnw_b = sb("lnw_b", [128, D], F32)
        lnb_b = sb("lnb_b", [128, D], F32)
        s.dma("sync", lambda e: e.dma_start(out=lnw_b[:], in_=lnw_d.partition_broadcast(128)), writes=["lnw_b"], group="c1")
        s.dma("sync", lambda e: e.dma_start(out=lnb_b[:], in_=lnb_d.partition_broadcast(128)), writes=["lnb_b"], group="c1")
        wgb = sb("wgb", [17, 256])
        s.dma("sync", lambda e: e.dma_start(out=RN[0:17, 0:2, :].rearrange("p a b -> p (a b)"), in_=wg_d), writes=["RN"], group="RN")
        s.op("vector", lambda e: e.tensor_copy(out=wgb[:], in_=RN[0:17, 0:2, :].rearrange("p a b -> p (a b)")), reads=["RN"], writes=["wgb"])
        nega = sb("nega", [128, 4], F32)
        s.op("scalar", lambda e: e.activation(out=nega[:], in_=small["alog"][:], func=AF.Exp), reads=["alog"], writes=["nega"])
        s.op("vector", lambda e: e.tensor_scalar(out=nega[:], in0=nega[:], scalar1=-1.0, scalar2=None, op0=ALU.mult), reads=["nega"], writes=["nega"])
        diag = sb("diag", [128, 48, 128])
        for t in range(48):
            s.op("vector" if t % 2 else "gpsimd",
                 lambda e, t=t: e.tensor_scalar(out=diag[:, t, :], in0=ident_f, scalar1=small["convw"][:, t:t + 1], scalar2=None, op0=ALU.mult),
                 reads=["cf", "convw"], writes=["diag"])

        xt = [sb("xt_%d" % i, [128, D], F32) for i in range(3)]
        stgs = [(R[:, :], "R"), (RN[:].rearrange("p a b -> p (a b)"), "RN")] + [(xt[i][:, :], "xt%d" % i) for i in range(3)]
        NST = len(stgs)
        Wb = sb("Wb", [128, 8, NCOL])
        Woutb = sb("Woutb", [128, 8, D])
        Sg = [sb("Sg%d" % i, [128, 4, 128], F32) for i in range(2)]
        modp = sb("modp", [128, 32], F32)
        g1b = [sb("g1b%d" % i, [128, D], F32) for i in range(2)]
        mch = Sg[0][0:2, :, :].rearrange("p h c -> p (h c)")
        bch = Sg[1][0:2, :, :].rearrange("p h c -> p (h c)")
        modbanks = [(banks[i], "ps%d" % i) for i in range(6)]
        mpb, mpkey = banks[6], "ps6"
        nst = 0
        for kt in range(8):
            for ch in range(3):
                st, key = stgs[nst % NST]
                nst += 1
                s.dma("sync", lambda e, st=st, kt=kt, ch=ch: e.dma_start(out=st[:, 0:1024], in_=wada_d[kt * 128:(kt + 1) * 128, ch * 1024:(ch + 1) * 1024]), writes=[key], group=key)
                for cgl in range(2):
                    bk, bkey = modbanks[ch * 2 + cgl]
                    s.op("tensor", lambda e, bk=bk, st=st, kt=kt, cgl=cgl: e.matmul(out=bk[0:2, 0:512], lhsT=small["cT"][:, kt, :], rhs=st[:, cgl * 512:(cgl + 1) * 512],
                                                                                 start=(kt == 0), stop=(kt == 7)), reads=[key, "cT"], writes=[bkey])
        for cg in range(6):
            bk, bkey = modbanks[cg]
            s.dma("sync", lambda e, cg=cg: e.dma_start(out=bch, in_=bada_d[:, cg * 512:(cg + 1) * 512].partition_broadcast(2)), writes=["Sg1"], group="bch")
            s.op("vector", lambda e, bk=bk: e.tensor_tensor(out=mch, in0=bk[0:2, 0:512], in1=bch, op=ALU.add), reads=[bkey, "Sg1"], writes=["Sg0"])
            if cg >= 2:
                s.op("vector", lambda e: e.tensor_scalar(out=mch, in0=mch, scalar1=1.0, scalar2=None, op0=ALU.add), reads=["Sg0"], writes=["Sg0"])
            if cg < 4:
                for j in range(4):
                    jn = cg * 4 + j
                    s.op("tensor", lambda e, jn=jn, j=j: e.matmul(out=mpb[:, jn * 2:(jn + 1) * 2], lhsT=mch[:, j * 128:(j + 1) * 128], rhs=cf[0:2, 0, 0:2], start=True, stop=True),
                         reads=["Sg0", "cf"], writes=[mpkey])
            else:
                half = cg - 4
                for sq in range(2):
                    b2, b2key = modbanks[sq]
                    s.op("tensor", lambda e, b2=b2, sq=sq: e.matmul(out=b2[:, 0:512], lhsT=sel[:, sq, :], rhs=mch, start=True, stop=True), reads=["Sg0", "UU"], writes=[b2key])
                    s.op("vector", lambda e, b2=b2, sq=sq, half=half: e.tensor_copy(out=g1b[sq][:, half * 512:(half + 1) * 512], in_=b2[:, 0:512]), reads=[b2key], writes=["g1b%d" % sq])
        s.op("vector", lambda e: e.tensor_copy(out=modp[:], in_=mpb[:, 0:32]), reads=[mpkey], writes=["modp"])
        for kt in range(8):
            for ch in range(4):
                c0 = ch * 1024
                w = min(1024, NCOL - c0)
                st, key = stgs[nst % NST]
                nst += 1
                s.dma("sync", lambda e, st=st, kt=kt, c0=c0, w=w: e.dma_start(out=st[:, 0:w], in_=win_d[kt * 128:(kt + 1) * 128, c0:c0 + w]), writes=[key], group=key)
                s.op("vector" if nst % 2 else "gpsimd", lambda e, st=st, kt=kt, c0=c0, w=w: e.tensor_copy(out=Wb[:, kt, c0:c0 + w], in_=st[:, 0:w]), reads=[key], writes=["Wb"])
        for kt in range(8):
            st, key = stgs[nst % NST]
            nst += 1
            s.dma("sync", lambda e, st=st, kt=kt: e.dma_start(out=st[:, 0:D], in_=wout_d[kt * 128:(kt + 1) * 128, :]), writes=[key], group=key)
            s.op("vector" if nst % 2 else "gpsimd", lambda e, st=st, kt=kt: e.tensor_copy(out=Woutb[:, kt, :], in_=st[:, 0:D]), reads=[key], writes=["Woutb"])

        def two(name, shape, dt=BF16):
            return [sb("%s_%d" % (name, i), shape, dt) for i in range(2)]
        SG = [sb("SG_%d" % i, [128, 8, 128]) for i in range(3)]
        OA = two("OA", [128, 4, 128])
        Am = two("Am", [128, 4, 128]); AT = two("AT", [128, 4, 128])
        KW = two("KW", [128, 4, 128]); KD = two("KD", [128, 4, 128]); VB = two("VB", [128, 4, 128])
        QKT = two("QKT", [128, 4, 128]); QDEC = two("QDEC", [128, 4, 128])
        EX = two("EX", [128, 12], F32)
        hT = sb("hT", [128, 8, 128])
        FM = two("FM", [128, 4, 128])
        U = [sb("U%d" % i, [128, 12, 131]) for i in range(2)]
        lraug = two("lraug", [32, 128])
        KVt = two("KVt", [128, 768])
        spt = sb("spt", [128, 256], F32)
        E1 = sb("E1", [128, 2, 128]); E2 = sb("E2", [128, 2, 128])
        ER = sb("ER", [128, 2, 2], F32)
        QT = sb("QT", [128, 2, 128]); KT = sb("KT", [128, 2, 128])
        EK = sb("EK", [128, 256]); KP = sb("KP", [128, 256])
        ATT = sb("ATT", [128, 4, 128])
        SPb = sb("SPb", [128, 4, 128])
        KTz = sb("KTz", [128, 4, 128])
        Sgla = [sb("Sgla%d" % i, [128, 2, 128], F32) for i in range(2)]
        QKV = sb("QKV", [128, 12, 128])
        SQ = sb("SQ", [128, 8, 128])
        RNb = SQ
        QKN = sb("QKN", [128, 8, 128])
        T4 = sb("T4", [128, 4], F32); G4 = sb("G4", [128, 4], F32); AB = two("AB", [128, 8], F32)
        TB = sb("TB", [128, 4], F32); BETA = sb("BETA", [128, 4], F32); LNT = sb("LNT", [128, 4], F32)
        D4 = sb("D4", [128, 4], F32); ND4 = sb("ND4", [128, 4], F32); D4B = sb("D4B", [128, 4], F32); KWs = sb("KWs", [128, 4], F32)
        DG = RN[:, 4:8, :]; DG2 = RN[:, 0:4, :]
        ED = sb("ED", [128, 4, 128]); E1g = sb("E1g", [128, 4, 128]); E2g = sb("E2g", [128, 4, 128])
        INV = sb("INV", [128, 8, 4, 128])
        Pp = [INV[:, 0], INV[:, 1]]; PTp = [INV[:, 2], INV[:, 3]]; Yp = [INV[:, 4], INV[:, 5]]; Zp = [INV[:, 6], INV[:, 7]]
        X1 = sb("X1", [128, 4, 128]); X2 = sb("X2", [128, 4, 128])
        YTA = sb("YTA", [128, 4, 128])
        WT = X2; VN = X1; OB = sb("OB", [128, 4, 128])
        Sgb = [sb("Sgb%d" % i, [128, 4, 128]) for i in range(2)]
        v8 = lambda a, b: INV[:, a:b].rearrange("p a h c -> p (a h) c")
        SQO, SQOk = v8(0, 2), ["Pp0", "Pp1"]
        TO, TOk = v8(2, 4), ["PTp0", "PTp1"]
        YT, YTk = v8(4, 6), ["Yp0", "Yp1"]
        RNOb, RNObk = v8(6, 8), ["Zp0", "Zp1"]
        RNO = R[:].rearrange("p (a b) -> p a b", a=8)
        JUNK = X1
        ST = sb("ST", [128, 8], F32)
        DBG = None
        if taps:
            try:
                DBG = sb("DBG", [128, 512], F32)
            except AssertionError:
                DBG = None

        for i in range(2):
            s.op("gpsimd", lambda e, i=i: e.memset(Sgla[i][:], 0.0), writes=["Sgla%d" % i])
            s.op("gpsimd", lambda e, i=i: e.memset(Sg[i][:], 0.0), reads=["modp", "g1b0", "g1b1"], writes=["Sg%d" % i])
            s.op("gpsimd", lambda e, i=i: e.memset(Sgb[i][:], 0.0), writes=["Sgb%d" % i])
            s.op("gpsimd", lambda e, i=i: e.memset(U[i][:], 0.0), writes=["U%d" % i])
        for i in range(2):
            s.op("gpsimd", lambda e, i=i: e.memset(lraug[i][:], 1.0), writes=["lraug%d" % i])

        f3 = lambda ap: ap.rearrange("p (h c) -> p h c", h=4)
        bc = lambda ap: ap.unsqueeze(2).to_broadcast([128, 4, 128])

        def tt(eng, out, a, b, op, reads, writes):
            s.op(eng, lambda e: e.tensor_tensor(out=out, in0=a, in1=b, op=op), reads=reads, writes=writes)

        def act(out, in_, func, reads, writes, **kw):
            s.op("scalar", lambda e: e.activation(out=out, in_=in_, func=func, **kw), reads=reads, writes=writes)

        units = [(sq, m) for m in range(n_units) for sq in range(2)]
        dbg_d = {}

        def dbg(sq, m, name, ap, keys):
            if not taps or DBG is None:
                return
            shp = list(ap.shape)
            P = shp[0]
            w = int(np.prod(shp[1:]))
            if name not in dbg_d:
                dbg_d[name] = nc.dram_tensor("dbg_" + name, [2, n_units, 128, 512], F32, kind="ExternalOutput").ap()
            dst = DBG[0:P, 0:w]
            if len(shp) == 3:
                dst = dst.rearrange("p (h c) -> p h c", h=shp[1])
            s.op("vector", lambda e: e.tensor_copy(out=dst, in_=ap), reads=list(keys), writes=["DBG"])
            s.dma("sync", lambda e: e.dma_start(out=dbg_d[name][sq, m, 0:P, 0:w], in_=DBG[0:P, 0:w]), reads=["DBG"], group="tap")

        def load_x(u):
            sq_, m_ = units[u]
            p3_ = u % 3
            s.dma("sync", lambda e: e.dma_start(out=xt[p3_][:], in_=x_d[sq_, m_ * 128:(m_ + 1) * 128, :]), writes=["xt%d" % p3_], group="xt%d" % p3_)

        def pstage(sq, m, u):
            p, p3 = u % 2, u % 3
            xs, xk = xt[p3], "xt%d" % p3
            pk = lambda n: "%s%d" % (n, p)
            Uk = "U%d" % sq
            for _ in range(PDELAY):
                yield
            for g in range(2):
                bk, bkey = nb_p()
                for c in range(4):
                    kt = g * 4 + c
                    s.op("tensor", lambda e, bk=bk, c=c, kt=kt: e.transpose(out=bk[:, c * 128:(c + 1) * 128], in_=xs[:, kt * 128:(kt + 1) * 128], identity=ident_f),
                         reads=[xk, "cf"], writes=[bkey])
                for c in range(4):
                    kt = g * 4 + c
                    s.op("vector", lambda e, bk=bk, c=c, kt=kt: e.tensor_scalar(out=hT[:, kt, :], in0=bk[:, c * 128:(c + 1) * 128],
                                                                             scalar1=modp[:, 16 + kt * 2 + sq:17 + kt * 2 + sq], scalar2=modp[:, kt * 2 + sq:kt * 2 + sq + 1],
                                                                             op0=ALU.mult, op1=ALU.add), reads=[bkey, "modp"], writes=["hT"])
                yield
            fm_cols = [0, 128, 256, 384] + [1040 + 128 * i for i in range(4)] + [3096 + 128 * i for i in range(4)] + [1552 + 128 * i for i in range(12)]
            for g in range(6):
                bk, bkey = nb_p()
                for c in range(4):
                    c0 = fm_cols[g * 4 + c]
                    for kt in range(8):
                        s.op("tensor", lambda e, bk=bk, c=c, c0=c0, kt=kt: e.matmul(out=bk[:, c * 128:(c + 1) * 128], lhsT=Wb[:, kt, c0:c0 + 128], rhs=hT[:, kt, :],
                                                                                  start=(kt == 0), stop=(kt == 7)), reads=["Wb", "hT"], writes=[bkey])
                    if (c + 1) % PYIELD == 0:
                        yield
                if g == 0:
                    act(FM[p][:], f3(bk[:, :]), AF.Copy, [bkey], [pk("FM")])
                elif g in (1, 2):
                    act(SG[p3][:, (g - 1) * 4:g * 4, :], f3(bk[:, :]), AF.Silu, [bkey], ["SG%d" % p3])
                else:
                    s.op("vector", lambda e, bk=bk, g=g: e.tensor_copy(out=U[sq][:, (g - 3) * 4:(g - 2) * 4, 3:131], in_=f3(bk[:, :])), reads=[bkey], writes=[Uk])
            abk, abkey = nb_p()
            for kt in range(8):
                s.op("tensor", lambda e, kt=kt: e.matmul(out=abk[0:16, 0:128], lhsT=Wb[:, kt, 1024:1040], rhs=hT[:, kt, :], start=(kt == 0), stop=(kt == 7)),
                     reads=["Wb", "hT"], writes=[abkey])
            for kt in range(8):
                s.op("tensor", lambda e, kt=kt: e.matmul(out=abk[:, 128:136], lhsT=hT[:, kt, :], rhs=Wb[:, kt, 3088:3096], start=(kt == 0), stop=(kt == 7)),
                     reads=["Wb", "hT"], writes=[abkey])
            s.op("vector", lambda e: e.tensor_copy(out=lraug[p][0:16, :], in_=abk[0:16, 0:128]), reads=[abkey], writes=[pk("lraug")])
            s.op("vector", lambda e: e.tensor_copy(out=AB[p][:], in_=abk[:, 128:136]), reads=[abkey], writes=[pk("AB")])
            yield
            for part, (c0, w) in enumerate(((256, 512), (768, 256))):
                bk, bkey = nb_p()
                for kt in range(8):
                    s.op("tensor", lambda e, bk=bk, kt=kt, c0=c0, w=w: e.matmul(out=bk[:, 0:w], lhsT=hT[:, kt, :], rhs=Wb[:, kt, c0:c0 + w], start=(kt == 0), stop=(kt == 7)),
                         reads=["Wb", "hT"], writes=[bkey])
                s.op("vector", lambda e, bk=bk, c0=c0, w=w: e.tensor_copy(out=KVt[p][:, c0 - 256:c0 - 256 + w], in_=bk[:, 0:w]), reads=[bkey], writes=[pk("KVt")])
                yield

        def mid(sq, m, u):
            yield from interleave([gla(sq, m, u % 2), gdnpre(sq, m, u % 2)])

        def gla(sq, m, p):
            pk = lambda n: "%s%d" % (n, p)
            Slk = "Sgla%d" % sq
            bk, bkey = nb_gla()
            s.op("tensor", lambda e, bk=bk: e.matmul(out=bk[:, 0:256], lhsT=lraug[p][0:17, :], rhs=wgb[0:17, :], start=True, stop=True), reads=[pk("lraug"), "wgb"], writes=[bkey])
            act(spt[:], bk[:, 0:256], AF.Exp, [bkey], ["spt"], scale=-1.0)
            act(spt[:], spt[:], AF.Ln, ["spt"], ["spt"], bias=1.0)
            yield
            pbk, pbkey = nb_gla()
            for hp in range(2):
                s.op("tensor", lambda e, hp=hp: e.matmul(out=pbk[:, hp * 130:(hp + 1) * 130], lhsT=spt[:, hp * 128:(hp + 1) * 128], rhs=uref[:, :], start=True, stop=True),
                     reads=["spt", "uref"], writes=[pbkey])
            pv = pbk[:, 0:260].rearrange("p (h c) -> p h c", h=2)
            act(E1[:], pv[:, :, 0:128], AF.Exp, [pbkey], ["E1"], bias=float(np.log(0.125)))
            act(E2[:], pv[:, :, 0:128], AF.Exp, [pbkey], ["E2"], scale=-1.0)
            act(ER[:], pv[:, :, 128:130], AF.Exp, [pbkey], ["ER"])
            yield
            tt("vector", QT[:], FM[p][:, 0:2, :], E1[:], ALU.mult, [pk("FM"), "E1"], ["QT"])
            tt("vector", KT[:], FM[p][:, 2:4, :], E2[:], ALU.mult, [pk("FM"), "E2"], ["KT"])
            bk, bkey = nb_gla()
            s.op("tensor", lambda e, bk=bk: e.matmul(out=bk[:, 0:256], lhsT=mgt16, rhs=spt[:, :], start=True, stop=True), reads=["spt", "cf"], writes=[bkey])
            act(EK[:], bk[:, 0:256], AF.Exp, [bkey], ["EK"])
            tt("gpsimd", KP[:], KVt[p][:, 0:256], EK[:], ALU.mult, [pk("KVt"), "EK"], ["KP"])
            yield
            for h in range(4):
                hp, par = h // 2, h % 2
                s.op("vector", lambda e, h=h, hp=hp, par=par: e.tensor_scalar(out=KTz[:, h, :], in0=KT[:, hp, :], scalar1=cf[:, 2, par:par + 1], scalar2=None, op0=ALU.mult),
                     reads=["KT", "cf"], writes=["KTz"])
            yield
            bk, bkey = nb_gla()
            for h in range(4):
                hp, par = h // 2, h % 2
                s.op("tensor", lambda e, bk=bk, h=h, hp=hp, par=par: e.matmul(out=bk[:, h * 128:(h + 1) * 128], lhsT=KTz[:, h, :], rhs=QT[:, hp, :],
                                                                           start=True, stop=True), reads=["KTz", "QT"], writes=[bkey])
            tt("vector", ATT[:], f3(bk[:, :]), masku, ALU.mult, [bkey, "bmr"], ["ATT"])
            for h in range(4):
                hp, par = h // 2, h % 2
                s.op("vector", lambda e, h=h, hp=hp, par=par: e.tensor_scalar(out=SPb[:, h, :], in0=Sgla[sq][:, hp, :], scalar1=ER[:, hp, 0:1], scalar2=cf[:, 2, par:par + 1],
                                                                           op0=ALU.mult, op1=ALU.mult), reads=[Slk, "ER", "cf"], writes=["SPb"])
            yield
            oak, oakey = nb_gla()
            for h in range(4):
                hp, par = h // 2, h % 2
                s.op("tensor", lambda e, h=h, hp=hp, par=par: e.matmul(out=oak[:, h * 128:(h + 1) * 128], lhsT=SPb[:, h, :], rhs=QT[:, hp, :],
                                                                      start=True, stop=False), reads=["SPb", "QT"], writes=[oakey])
                s.op("tensor", lambda e, h=h: e.matmul(out=oak[:, h * 128:(h + 1) * 128], lhsT=KVt[p][:, 256 + h * 128:384 + h * 128], rhs=ATT[:, h, :], start=False, stop=True),
                     reads=[pk("KVt"), "ATT"], writes=[oakey])
            act(OA[p][:], f3(oak[:, :]), AF.Copy, [oakey], [pk("OA")])
            dbg(sq, m, "oa", OA[p][:], [pk("OA")])
            yield
            bk, bkey = nb_gla()
            for h in range(4):
                hp, par = h // 2, h % 2
                s.op("tensor", lambda e, bk=bk, h=h, hp=hp, par=par: e.matmul(out=bk[:, h * 128:(h + 1) * 128], lhsT=KP[:, hp * 128:(hp + 1) * 128],
                                                                           rhs=KVt[p][:, 256 + h * 128:384 + h * 128], start=True, stop=True), reads=["KP", pk("KVt")], writes=[bkey])
            for h in range(4):
                hp, par = h // 2, h % 2
                ps_ = slice(par * 64, (par + 1) * 64)
                s.op("vector", lambda e, bk=bk, h=h, hp=hp, ps_=ps_: e.scalar_tensor_tensor(out=Sgla[sq][ps_, hp, :], in0=Sgla[sq][ps_, hp, :], scalar=ER[ps_, hp, 1:2], in1=bk[ps_, h * 128:(h + 1) * 128],
                                                                                     op0=ALU.mult, op1=ALU.add), reads=[Slk, "ER", bkey], writes=[Slk])
            yield

        def gdnpre(sq, m, p):
            pk = lambda n: "%s%d" % (n, p)
            Uk = "U%d" % sq
            for g3 in range(3):
                bk, bkey = nb_gdn()
                for c in range(4):
                    t = g3 * 4 + c
                    for k in range(4):
                        s.op("tensor", lambda e, bk=bk, c=c, t=t, k=k: e.matmul(out=bk[:, c * 128:(c + 1) * 128], lhsT=diag[:, t * 4 + k, :], rhs=U[sq][:, t, k:k + 128],
                                                                              start=(k == 0), stop=(k == 3)), reads=["diag", Uk], writes=[bkey])
                act(QKV[:, g3 * 4:(g3 + 1) * 4, :], f3(bk[:, :]), AF.Silu, [bkey], ["QKV"])
                yield
            s.op("gpsimd", lambda e: e.tensor_copy(out=U[sq][:, :, 0:3], in_=U[sq][:, :, 128:131]), reads=[Uk], writes=[Uk])
            tt("gpsimd", SQ[:], QKV[:, 0:8, :], QKV[:, 0:8, :], ALU.mult, ["QKV"], ["SQ"])
            tt("vector", T4[:], AB[p][:, 0:4], small["dtb"][:], ALU.add, [pk("AB"), "dtb"], ["T4"])
            act(T4[:], T4[:], AF.Exp, ["T4"], ["T4"])
            act(T4[:], T4[:], AF.Ln, ["T4"], ["T4"], bias=1.0)
            tt("vector", G4[:], T4[:], nega[:], ALU.mult, ["T4", "nega"], ["G4"])
            act(TB[:], AB[p][:, 4:8], AF.Exp, [pk("AB")], ["TB"], scale=-1.0)
            s.op("vector", lambda e: e.tensor_scalar(out=TB[:], in0=TB[:], scalar1=1.0, scalar2=None, op0=ALU.add), reads=["TB"], writes=["TB"])
            s.op("vector", lambda e: e.reciprocal(out=BETA[:], in_=TB[:]), reads=["TB"], writes=["BETA"])
            act(LNT[:], TB[:], AF.Ln, ["TB"], ["LNT"])
            yield
            for g in range(2):
                bk, bkey = nb_gdn()
                for c in range(4):
                    s.op("tensor", lambda e, bk=bk, c=c, g=g: e.matmul(out=bk[:, c * 128:(c + 1) * 128], lhsT=ones_b[:, :], rhs=SQ[:, g * 4 + c, :], start=True, stop=True),
                         reads=["ones_b", "SQ"], writes=[bkey])
                act(RN[:, g * 4:(g + 1) * 4, :], f3(bk[:, :]), AF.Ln, [bkey], ["RN"], bias=1e-6)
                act(RNb[:, g * 4:(g + 1) * 4, :], RN[:, g * 4:(g + 1) * 4, :], AF.Exp, ["RN"], ["SQ"], scale=-0.5, bias=(float(np.log(128.0 ** -0.5)) if g == 0 else 0.0))
                yield
            tt("vector", QKN[:], QKV[:, 0:8, :], RNb[:], ALU.mult, ["QKV", "SQ"], ["QKN"])
            dbk, dbkey = nb_gdn()
            for n, lm in enumerate((uincl, mgt, ones_f)):
                s.op("tensor", lambda e, n=n, lm=lm: e.matmul(out=dbk[:, n * 4:(n + 1) * 4], lhsT=lm, rhs=G4[:, :], start=True, stop=True), reads=["cf", "G4"], writes=[dbkey])
            act(EX[p][:], dbk[:, 0:12], AF.Exp, [dbkey], [pk("EX")])
            s.op("vector", lambda e: e.tensor_copy(out=D4[:], in_=dbk[:, 0:4]), reads=[dbkey], writes=["D4"])
            s.op("vector", lambda e: e.tensor_scalar(out=ND4[:], in0=dbk[:, 0:4], scalar1=-1.0, scalar2=None, op0=ALU.mult), reads=[dbkey], writes=["ND4"])
            tt("vector", D4B[:], D4[:], LNT[:], ALU.subtract, ["D4", "LNT"], ["D4B"])
            tt("vector", KWs[:], BETA[:], EX[p][:, 0:4], ALU.mult, ["BETA", pk("EX")], ["KWs"])
            yield
            for t in range(4):
                s.op("tensor", lambda e, t=t: e.transpose(out=psb[:, t * 128:(t + 1) * 128], in_=QKN[:, 4 + t, :], identity=ident_b), reads=["QKN", "bmr"], writes=["psb"])
            for t in range(4):
                s.op("tensor", lambda e, t=t: e.transpose(out=psb[:, 512 + t * 128:512 + (t + 1) * 128], in_=QKV[:, 8 + t, :], identity=ident_b), reads=["QKV", "bmr"], writes=["psb"])
            tt("vector", KW[p][:], f3(psb[:, 0:512]), bc(KWs[:]), ALU.mult, ["psb", "KWs"], [pk("KW")])
            tt("vector", KD[p][:], f3(psb[:, 0:512]), bc(EX[p][:, 4:8]), ALU.mult, ["psb", pk("EX")], [pk("KD")])
            tt("vector", VB[p][:], f3(psb[:, 512:1024]), bc(BETA[:]), ALU.mult, ["psb", "BETA"], [pk("VB")])
            yield
            tt("gpsimd", DG, identr_b, bc(D4[:]), ALU.mult, ["bmr", "D4"], ["RN"])
            tt("gpsimd", DG2, identr_b, bc(D4B[:]), ALU.mult, ["bmr", "D4B"], ["RN"])
            dgf = DG.rearrange("p h c -> p (h c)")
            dg2f = DG2.rearrange("p h c -> p (h c)")
            bk, bkey = nb_gdn()
            s.op("tensor", lambda e, bk=bk: e.matmul(out=bk[:, :], lhsT=ones_f, rhs=dgf, start=True, stop=True), reads=["cf", "RN"], writes=[bkey])
            act(ED[:], f3(bk[:, :]), AF.Exp, [bkey], ["ED"])
            yield
            for (src, srck, msk, dst, dstk) in ((dgf, "RN", mt_incl, E1g, "E1g"), (dg2f, "RN", mt_strict, E2g, "E2g")):
                bk, bkey = nb_gdn()
                for h in range(4):
                    s.op("tensor", lambda e, bk=bk, src=src, h=h: e.matmul(out=bk[:, h * 128:(h + 1) * 128], lhsT=ones_f, rhs=src[:, h * 128:(h + 1) * 128], start=True, stop=False),
                         reads=["cf", srck], writes=[bkey])
                    s.op("tensor", lambda e, bk=bk, h=h, msk=msk: e.matmul(out=bk[:, h * 128:(h + 1) * 128], lhsT=ident_f, rhs=msk, start=False, stop=False), reads=["cf"], writes=[bkey])
                    s.op("tensor", lambda e, bk=bk, h=h: e.matmul(out=bk[:, h * 128:(h + 1) * 128], lhsT=DG[:, h, :], rhs=cf[:, 3, :], start=False, stop=True), reads=["cf", "RN"], writes=[bkey])
                act(dst[:], f3(bk[:, :]), AF.Exp, [bkey], [dstk])
                yield
            tt("gpsimd", QDEC[p][:], QKN[:, 0:4, :], ED[:], ALU.mult, ["QKN", "ED"], [pk("QDEC")])
            gk, gkey = nb_gdn()
            for h in range(4):
                s.op("tensor", lambda e, h=h: e.matmul(out=gk[:, h * 128:(h + 1) * 128], lhsT=QKN[:, 4 + h, :], rhs=QKN[:, 4 + h, :], start=True, stop=True), reads=["QKN"], writes=[gkey])
            tt("vector", AT[p][:], f3(gk[:, :]), E2g[:], ALU.mult, [gkey, "E2g"], [pk("AT")])
            qk, qkey = nb_gdn()
            for h in range(4):
                s.op("tensor", lambda e, h=h: e.matmul(out=qk[:, h * 128:(h + 1) * 128], lhsT=QKN[:, 4 + h, :], rhs=QKN[:, h, :], start=True, stop=True), reads=["QKN"], writes=[qkey])
            tt("vector", QKT[p][:], f3(qk[:, :]), E1g[:], ALU.mult, [qkey, "E1g"], [pk("QKT")])
            yield
            for h in range(4):
                s.op("tensor", lambda e, h=h: e.transpose(out=psb[:, h * 128:(h + 1) * 128], in_=AT[p][:, h, :], identity=ident_b), reads=[pk("AT"), "bmr"], writes=["psb"])
            act(Am[p][:], f3(psb[:, 0:512]), AF.Copy, ["psb"], [pk("Am")])
            yield

        def back(sq, m, u):
            p, p3 = u % 2, u % 3
            xs, xk = xt[p3], "xt%d" % p3
            pk = lambda n: "%s%d" % (n, p)
            Sgk, Sgbk = "Sg%d" % sq, "Sgb%d" % sq
            A_, Ak, AT_, ATk = Am[p], pk("Am"), AT[p], pk("AT")

            def mm4(L, Lk, Rt, Rk):
                bk, bkey = nb_back()
                for h in range(4):
                    s.op("tensor", lambda e, bk=bk, h=h: e.matmul(out=bk[:, h * 128:(h + 1) * 128], lhsT=L[:, h, :], rhs=Rt[:, h, :], start=True, stop=True),
                         reads=[Lk, Rk], writes=[bkey])
                return bk, bkey
            def invchain():
                tt("vector", Pp[0], A_[:], bd16, ALU.mult, [Ak, "bmr"], ["Pp0"])
                tt("gpsimd", PTp[0], AT_[:], bd16, ALU.mult, [ATk, "bmr"], ["PTp0"])
                tt("vector", Yp[0], identr_b, Pp[0], ALU.subtract, ["bmr", "Pp0"], ["Yp0"])
                tt("gpsimd", Zp[0], identr_b, PTp[0], ALU.subtract, ["bmr", "PTp0"], ["Zp0"])
                yield
                act(R[:], xs[:], AF.Copy, [xk], ["R"], scale=ALPHA)
                if u + 3 < len(units):
                    load_x(u + 3)
                cur = 0
                yz = 0
                for lvl in range(3):
                    nx = 1 - cur
                    b1, k1 = mm4(PTp[cur], "PTp%d" % cur, Pp[cur], "Pp%d" % cur)
                    b2, k2 = mm4(Pp[cur], "Pp%d" % cur, PTp[cur], "PTp%d" % cur)
                    act(Pp[nx], f3(b1[:, :]), AF.Copy, [k1], ["Pp%d" % nx])
                    s.op("vector", lambda e, b2=b2, nx=nx: e.tensor_copy(out=PTp[nx], in_=f3(b2[:, :])), reads=[k2], writes=["PTp%d" % nx])
                    yield
                    b3, k3 = mm4(PTp[nx], "PTp%d" % nx, Yp[yz], "Yp%d" % yz)
                    b4, k4 = mm4(Pp[nx], "Pp%d" % nx, Zp[yz], "Zp%d" % yz)
                    tt("vector", Yp[1 - yz], f3(b3[:, :]), Yp[yz], ALU.add, [k3, "Yp%d" % yz], ["Yp%d" % (1 - yz)])
                    tt("vector", Zp[1 - yz], f3(b4[:, :]), Zp[yz], ALU.add, [k4, "Zp%d" % yz], ["Zp%d" % (1 - yz)])
                    yield
                    cur = nx
                    yz = 1 - yz
                for (mi, both) in ((3, True), (5, True), (7, False)):
                    Yc, Zc, Yk, Zk = Yp[yz], Zp[yz], "Yp%d" % yz, "Zp%d" % yz
                    Yn, Zn, Ynk, Znk = Yp[1 - yz], Zp[1 - yz], "Yp%d" % (1 - yz), "Zp%d" % (1 - yz)
                    mT = bcm(mi + 1) if both else bcm(mi)
                    b1, k1 = mm4(A_, Ak, Zc, Zk)
                    if both:
                        b3, k3 = mm4(AT_, ATk, Yc, Yk)
                    tt("vector", X1[:], f3(b1[:, :]), mT, ALU.mult, [k1, "bmr"], ["X1"])
                    if both:
                        tt("vector", X2[:], f3(b3[:, :]), bcm(mi), ALU.mult, [k3, "bmr"], ["X2"])
                    yield
                    b2, k2 = mm4(Yc, Yk, X1, "X1")
                    if both:
                        b4, k4 = mm4(Zc, Zk, X2, "X2")
                    tt("vector", Zn, Zc, f3(b2[:, :]), ALU.subtract, [Zk, k2], [Znk])
                    if both:
                        tt("vector", Yn, Yc, f3(b4[:, :]), ALU.subtract, [Yk, k4], [Ynk])
                    yield
                    yz = 1 - yz
                TT, TTk = Zp[yz], "Zp%d" % yz
                dbg(sq, m, "TT", TT, [TTk])
                b1, k1 = mm4(TT, TTk, VB[p], pk("VB"))
                b2, k2 = mm4(KW[p], pk("KW"), TT, TTk)
                act(UU[:], f3(b1[:, :]), AF.Copy, [k1], ["UU"])
                act(WT[:], f3(b2[:, :]), AF.Copy, [k2], ["X2"])
                yield
                b3, k3 = mm4(WT, "X2", Sgb[sq], Sgbk)
                tt("vector", VN[:], UU[:], f3(b3[:, :]), ALU.subtract, ["UU", k3], ["X1"])
                yield
                obk, obkey = nb_back()
                for h in range(4):
                    s.op("tensor", lambda e, h=h: e.matmul(out=obk[:, h * 128:(h + 1) * 128], lhsT=Sgb[sq][:, h, :], rhs=QDEC[p][:, h, :], start=True, stop=False), reads=[Sgbk, pk("QDEC")], writes=[obkey])
                    s.op("tensor", lambda e, h=h: e.matmul(out=obk[:, h * 128:(h + 1) * 128], lhsT=VN[:, h, :], rhs=QKT[p][:, h, :], start=False, stop=True), reads=["X1", pk("QKT")], writes=[obkey])
                act(OB[:], f3(obk[:, :]), AF.Copy, [obkey], ["OB"])
                b4, k4 = mm4(KD[p], pk("KD"), VN, "X1")
                for h in range(4):
                    s.op("vector", lambda e, h=h: e.scalar_tensor_tensor(out=Sg[sq][:, h, :], in0=Sg[sq][:, h, :], scalar=EX[p][:, 8 + h:9 + h], in1=b4[:, h * 128:(h + 1) * 128],
                                                                         op0=ALU.mult, op1=ALU.add), reads=[Sgk, pk("EX"), k4], writes=[Sgk])
                s.op("gpsimd", lambda e: e.tensor_copy(out=Sgb[sq][:], in_=Sg[sq][:]), reads=[Sgk], writes=[Sgbk])
                dbg(sq, m, "ob", OB[:], ["OB"])
                yield

            def early():
                tt("gpsimd", X1[:], OA[p][:], OA[p][:], ALU.mult, [pk("OA")], ["X1"])
                yield
                bk, bkey = nb_back()
                for c in range(4):
                    s.op("tensor", lambda e, bk=bk, c=c: e.matmul(out=bk[:, c * 128:(c + 1) * 128], lhsT=ones_b[:, :], rhs=X1[:, c, :], start=True, stop=True),
                         reads=["ones_b", "X1"], writes=[bkey])
                act(UU[:], f3(bk[:, :]), AF.Ln, [bkey], ["UU"], bias=128e-6)
                yield
                act(X2[:], UU[:], AF.Exp, ["UU"], ["X2"], scale=-0.5, bias=float(0.5 * np.log(128.0)))
                tt("gpsimd", X1[:], OA[p][:], X2[:], ALU.mult, [pk("OA"), "X2"], ["X1"])
                yield
                s.op("vector", lambda e: e.scalar_tensor_tensor(out=YTA[:], in0=X1[:], scalar=small["glanw"][:, 0:1], in1=SG[p3][:, 0:4, :],
                                                                op0=ALU.mult, op1=ALU.mult), reads=["X1", "SG%d" % p3, "glanw"], writes=["YTA"])
                yield

            yield from interleave([invchain(), early()])
            for g, (osb, okey) in ((1, (OB, "OB")),):
                sl = slice(g * 4, (g + 1) * 4)
                tt("vector", SQO[:, sl, :], osb[:], osb[:], ALU.mult, [okey], SQOk)
                bk, bkey = nb_back()
                for c in range(4):
                    s.op("tensor", lambda e, bk=bk, c=c, g=g: e.matmul(out=bk[:, c * 128:(c + 1) * 128], lhsT=ones_b[:, :], rhs=SQO[:, g * 4 + c, :], start=True, stop=True),
                         reads=["ones_b"] + SQOk, writes=[bkey])
                act(UU[:], f3(bk[:, :]), AF.Ln, [bkey], ["UU"], bias=128e-6)
                act(RNOb[:, sl, :], UU[:], AF.Exp, ["UU"], RNObk, scale=-0.5, bias=float(0.5 * np.log(128.0)))
                tt("vector", TO[:, sl, :], osb[:], RNOb[:, sl, :], ALU.mult, [okey] + RNObk, TOk)
                nw = small["glanw"] if g == 0 else small["gdnnw"]
                s.op("vector", lambda e, sl=sl, nw=nw: e.scalar_tensor_tensor(out=YT[:, sl, :], in0=TO[:, sl, :], scalar=nw[:, 0:1], in1=SG[p3][:, sl, :],
                                                                           op0=ALU.mult, op1=ALU.mult), reads=TOk + ["SG%d" % p3, "glanw", "gdnnw"], writes=YTk)
                yield
            for half in range(2):
                bk, bkey = nb_back()
                for kt in range(8):
                    ylhs = YTA[:, kt, :] if kt < 4 else YT[:, kt, :]
                    s.op("tensor", lambda e, bk=bk, kt=kt, half=half, ylhs=ylhs: e.matmul(out=bk[:, :], lhsT=ylhs, rhs=Woutb[:, kt, half * 512:(half + 1) * 512], start=(kt == 0), stop=(kt == 7)),
                         reads=YTk + ["YTA", "Woutb"], writes=[bkey])
                uuf = UU[:].rearrange("p h c -> p (h c)")
                tt("vector", uuf, bk[:, :], g1b[sq][:, half * 512:(half + 1) * 512], ALU.mult, [bkey, "g1b%d" % sq], ["UU"])
                tt("vector", R[:, half * 512:(half + 1) * 512], R[:, half * 512:(half + 1) * 512], uuf, ALU.add, ["R", "UU"], ["R"])
                yield
            s.op("gpsimd", lambda e: e.memset(ST[:], 0.0), writes=["ST"])
            jv = lambda t: t[:].rearrange("p h c -> p (h c)")
            for hf in range(2):
                act(jv(X1), R[:, hf * 512:(hf + 1) * 512], AF.Copy, ["R", "ST"], ["X1", "ST"], accum_out=ST[:, hf:hf + 1])
                act(jv(X1), R[:, hf * 512:(hf + 1) * 512], AF.Square, ["R", "ST"], ["X1", "ST"], accum_out=ST[:, 2 + hf:3 + hf])
            yield
            tt("vector", ST[:, 4:5], ST[:, 0:1], ST[:, 1:2], ALU.add, ["ST"], ["ST"])
            tt("vector", ST[:, 5:6], ST[:, 2:3], ST[:, 3:4], ALU.add, ["ST"], ["ST"])
            s.op("vector", lambda e: e.tensor_scalar(out=ST[:, 4:5], in0=ST[:, 4:5], scalar1=1.0 / D, scalar2=None, op0=ALU.mult), reads=["ST"], writes=["ST"])
            tt("vector", ST[:, 6:7], ST[:, 4:5], ST[:, 4:5], ALU.mult, ["ST"], ["ST"])
            s.op("vector", lambda e: e.scalar_tensor_tensor(out=ST[:, 7:8], in0=ST[:, 5:6], scalar=1.0 / D, in1=ST[:, 6:7], op0=ALU.mult, op1=ALU.subtract), reads=["ST"], writes=["ST"])
            act(ST[:, 7:8], ST[:, 7:8], AF.Ln, ["ST"], ["ST"], bias=1e-5)
            act(ST[:, 7:8], ST[:, 7:8], AF.Exp, ["ST"], ["ST"], scale=-0.5)
            s.op("vector", lambda e: e.scalar_tensor_tensor(out=ST[:, 6:7], in0=ST[:, 4:5], scalar=-1.0, in1=ST[:, 7:8], op0=ALU.mult, op1=ALU.mult), reads=["ST"], writes=["ST"])
            act(R[:], R[:], AF.Identity, ["R", "ST"], ["R"], scale=ST[:, 7:8], bias=ST[:, 6:7])
            tt("vector", R[:], R[:], lnw_b[:], ALU.mult, ["R", "lnw_b"], ["R"])
            tt("gpsimd", R[:], R[:], lnb_b[:], ALU.add, ["R", "lnb_b"], ["R"])
            s.dma("sync", lambda e: e.dma_start(out=out_d[sq, m * 128:(m + 1) * 128, :], in_=R[:]), reads=["R"], group="out")
            yield

        NUu = len(units)
        for u0 in range(min(3, NUu)):
            load_x(u0)
        for it in range(NUu + 2):
            gd = {}
            if 0 <= it - 2 < NUu:
                gd["b"] = back(units[it - 2][0], units[it - 2][1], it - 2)
            if 0 <= it - 1 < NUu:
                gd["m"] = mid(units[it - 1][0], units[it - 1][1], it - 1)
            if it < NUu:
                gd["p"] = pstage(units[it][0], units[it][1], it)
            run(interleave([gd[k] for k in ORDER if k in gd]))
        fw = ["out"] + (["tap"] if (taps and "tap" in s.dma_groups) else [])
        cnt = s.emit(final_wait_groups=fw)
        print("ops", len(s.ops), "signals", cnt)
    return nc


def make_inputs_for_core(core, inputs, consts, n_units):
    T = n_units * 128
    cfa, urefa, bma, sela = consts
    f = lambda a: np.ascontiguousarray(np.asarray(a, dtype=np.float32))
    xs = f(inputs["x"][2 * core:2 * core + 2, :T])
    c2 = f(inputs["c"][2 * core:2 * core + 2])
    cT = np.ascontiguousarray(c2.reshape(2, 8, 128).transpose(2, 1, 0))
    convw = f(inputs["gdn_conv_w"][0])
    convw_l = np.ascontiguousarray(convw.reshape(4, 12, 128).transpose(2, 1, 0).reshape(128, 48))
    return dict(
        x=xs, cT=cT, w_ada=f(inputs["w_ada"][0]), b_ada=f(inputs["b_ada"]), w_in=f(inputs["w_in"][0]),
        wg_aug=np.ascontiguousarray(np.concatenate([f(inputs["gla_w_gate_up"][0]), f(inputs["gla_b_gate"])], axis=0)),
        gla_nw=f(inputs["gla_norm_w"][0]).reshape(128, 1), gdn_nw=f(inputs["gdn_norm_w"][0]).reshape(128, 1),
        convw=convw_l,
        alog_b=np.ascontiguousarray(np.broadcast_to(f(inputs["gdn_a_log"]), (128, 4))),
        dtb_b=np.ascontiguousarray(np.broadcast_to(f(inputs["gdn_dt_bias"]), (128, 4))),
        w_out=f(inputs["w_out"][0]), ln_w=f(inputs["ln_w"]), ln_b=f(inputs["ln_b"]),
        cf=cfa, uref=urefa, bm=bma, sel=sela)


def kernel(**inputs):
    n_units = 32
    consts = host_consts()
    nc = build(n_units)
    in_maps = [make_inputs_for_core(c, inputs, consts, n_units) for c in range(8)]
    res = run_bass_kernel_spmd(nc, in_maps, core_ids=list(range(8)))
    out = np.concatenate([r["out"] for r in res.results], axis=0)
    return out.astype(np.float32)
```

```python
import bisect
import contextlib
import os
import numpy as np
import concourse.bass as bass
import concourse.mybir as mybir
from concourse.bass_utils import run_bass_kernel_spmd

F32 = mybir.dt.float32
BF16 = mybir.dt.bfloat16
AF = mybir.ActivationFunctionType
ALU = mybir.AluOpType

ENGINES = ("tensor", "vector", "scalar", "gpsimd", "sync")
D = 1024
NCOL = 3608
ALPHA = 2.0 ** 0.25
NEG = -30000.0


class Sched:
    def __init__(self, nc):
        self.nc = nc
        self.ops = []
        self.last_writer = {}
        self.readers = {}
        self.dma_groups = {}

    def _add(self, eng, fn, reads, writes, dma_group=None):
        idx = len(self.ops)
        deps = set()
        for r in reads:
            w = self.last_writer.get(r)
            if w is not None:
                deps.add((w, "raw"))
        for r in writes:
            w = self.last_writer.get(r)
            if w is not None:
                deps.add((w, "waw"))
            for rd in self.readers.get(r, ()):
                deps.add((rd, "war"))
        for r in reads:
            self.readers.setdefault(r, []).append(idx)
        for r in writes:
            self.last_writer[r] = idx
            self.readers[r] = []
        op = dict(eng=eng, fn=fn, deps=deps, dma_group=dma_group, signal=False, seq=None)
        if dma_group is not None:
            c = self.dma_groups.get(dma_group, 0) + 1
            self.dma_groups[dma_group] = c
            op["seq"] = c
            op["signal"] = True
        self.ops.append(op)
        return idx

    def op(self, eng, fn, reads=(), writes=()):
        return self._add(eng, fn, tuple(reads), tuple(writes))

    def dma(self, eng, fn, reads=(), writes=(), group=None):
        return self._add(eng, fn, tuple(reads), tuple(writes), dma_group=group)

    def emit(self, final_wait_groups=()):
        nc = self.nc
        ops = self.ops
        for i, op in enumerate(ops):
            nd = set()
            for (p, kind) in op["deps"]:
                pop = ops[p]
                if pop["dma_group"] is None and op["dma_group"] is None and pop["eng"] == op["eng"]:
                    if op["eng"] == "tensor" or kind != "raw":
                        continue
                nd.add(p)
            best = {}
            keep = set()
            for p in nd:
                pop = ops[p]
                if pop["dma_group"] is not None:
                    keep.add(p)
                elif p > best.get(pop["eng"], -1):
                    best[pop["eng"]] = p
            keep.update(best.values())
            op["ndeps"] = keep
            for p in keep:
                ops[p]["signal"] = True
        cnt = {e: 0 for e in ENGINES}
        for op in ops:
            if op["dma_group"] is None and op["signal"]:
                cnt[op["eng"]] += 1
                op["seq"] = cnt[op["eng"]]
        stack = contextlib.ExitStack()
        esem = {e: stack.enter_context(nc.semaphore("s_" + e)) for e in ENGINES if cnt[e] > 0}
        dsem = {g: stack.enter_context(nc.semaphore("d_%d" % k)) for k, g in enumerate(self.dma_groups)}
        per_eng = {e: [] for e in ENGINES}
        grp_idx = {}
        for i, op in enumerate(ops):
            per_eng[op["eng"]].append(i)
            if op["dma_group"] is not None:
                grp_idx.setdefault(op["dma_group"], []).append(i)

        def run_engine(ename, eobj):
            waited = {}
            for i in per_eng[ename]:
                op = ops[i]
                need = {}
                for p in op["ndeps"]:
                    pop = ops[p]
                    if pop["dma_group"] is not None:
                        key = ("d", pop["dma_group"])
                        val = 16 * bisect.bisect_left(grp_idx[pop["dma_group"]], i)
                    else:
                        key = ("e", pop["eng"])
                        val = pop["seq"]
                    if val > need.get(key, 0):
                        need[key] = val
                for key, val in need.items():
                    if waited.get(key, 0) >= val:
                        continue
                    waited[key] = val
                    sem = dsem[key[1]] if key[0] == "d" else esem[key[1]]
                    eobj.wait_ge(sem, val)
                ins = op["fn"](eobj)
                if op["dma_group"] is not None:
                    ins.then_inc(dsem[op["dma_group"]], 16)
                elif op["signal"]:
                    ins.then_inc(esem[ename], 1)
            if ename == "sync":
                for g in final_wait_groups:
                    eobj.wait_ge(dsem[g], 16 * self.dma_groups[g])

        with stack:
            with nc.Block() as block:
                for ename in ENGINES:
                    getattr(block, ename)(lambda eobj, ename=ename: run_engine(ename, eobj))
        return cnt


def host_consts():
    j = np.arange(128)[:, None]
    i = np.arange(128)[None, :]
    f = lambda m: m.astype(np.float32)
    cf = np.zeros((128, 9, 128), np.float32)
    cf[:, 0] = np.eye(128)
    cf[:, 1] = 1.0
    cf[:, 2] = 0.0
    cf[:64, 2, 0] = 1.0
    cf[64:, 2, 1] = 1.0
    cf[:, 3] = -1.0
    cf[:, 4] = -(1.0 / 16.0) * f(j > i)
    cf[:, 5] = f(j <= i)
    cf[:, 6] = f(j > i)
    cf[:, 7] = np.where(i >= j, 0.0, NEG)
    cf[:, 8] = np.where(i > j, 0.0, NEG)
    uref = np.zeros((128, 130), np.float32)
    uref[:, :128] = -(1.0 / 16.0) * (f(j <= i) - f(j <= 63))
    uref[:, 128] = -(1.0 / 16.0) * f(j[:, 0] <= 63)
    uref[:, 129] = -(1.0 / 16.0)
    bm = np.zeros((128, 8, 128), np.float32)
    bm[:, 0] = np.eye(128)
    bm[:, 1] = f(j <= i)
    bm[:, 2] = f((j // 16) == (i // 16))
    for n, s in ((3, 16), (5, 32), (7, 64)):
        low = f(((j // s) % 2 == 1) & ((i // s) == (j // s) - 1))
        if n < 7:
            bm[:, n] = low
            bm[:, n + 1] = low.T
        else:
            bm[:, n] = low.T
    sel = np.zeros((2, 2, 128), np.float32)
    sel[0, 0] = 1.0
    sel[1, 1] = 1.0
    return cf, uref, bm, sel


def interleave(gens):
    gens = list(gens)
    while gens:
        for g in list(gens):
            if g not in gens:
                continue
            try:
                next(g)
            except StopIteration:
                gens = [x for x in gens if x is not g]
        yield


def run(gen):
    for _ in gen:
        pass


PYIELD = int(os.environ.get("K_PYIELD", "1"))
ORDER = os.environ.get("K_ORDER", "bpbmp")
PDELAY = int(os.environ.get("K_PDELAY", "0"))
BANKSETS = {"a": ([0], [1], [2, 3], [4, 5, 6]), "b": ([0, 1], [2], [3, 4], [5, 6]), "c": ([0, 1], [2], [3], [4, 5, 6]), "d": ([0, 1, 2], [3], [4], [5, 6]), "e": ([0, 1], [2, 3], [4], [5, 6])}[os.environ.get("K_BANKS", "b")]


def build(n_units, taps=False, stage=99):
    T = n_units * 128
    nc = bass.Bass("TRN2", target_bir_lowering=False)
    din = lambda name, shape: nc.dram_tensor(name, shape, F32, kind="ExternalInput").ap()
    x_d = din("x", [2, T, D])
    cT_d = din("cT", [128, 8, 2])
    wada_d = din("w_ada", [D, 3 * D])
    bada_d = din("b_ada", [1, 3 * D])
    win_d = din("w_in", [D, NCOL])
    wg_d = din("wg_aug", [17, 256])
    glanw_d = din("gla_nw", [128, 1])
    gdnnw_d = din("gdn_nw", [128, 1])
    convw_d = din("convw", [128, 48])
    alog_d = din("alog_b", [128, 4])
    dtb_d = din("dtb_b", [128, 4])
    wout_d = din("w_out", [D, D])
    lnw_d = din("ln_w", [1, D])
    lnb_d = din("ln_b", [1, D])
    cf_d = din("cf", [128, 9, 128])
    uref_d = din("uref", [128, 130])
    bm_d = din("bm", [128, 8, 128])
    sel_d = din("sel", [2, 2, 128])
    out_d = nc.dram_tensor("out", [2, T, D], F32, kind="ExternalOutput").ap()

    es = contextlib.ExitStack()
    with es:
        def sb(name, shape, dt=BF16):
            return es.enter_context(nc.sbuf_tensor("s_" + name, shape, dt))
        s = Sched(nc)
        banks = [es.enter_context(nc.psum_tensor("ps%d" % i, [128, 512], F32)) for i in range(7)]
        psb = es.enter_context(nc.psum_tensor("psb", [128, 1024], BF16))

        def mkbanks(ids):
            ctr = [0]

            def nb():
                b = ids[ctr[0] % len(ids)]
                ctr[0] += 1
                return banks[b], "ps%d" % b
            return nb
        nb_setup = mkbanks([0, 1, 2, 3, 4, 5, 6])
        _bs = BANKSETS
        nb_p = mkbanks(_bs[0])
        nb_gla = mkbanks(_bs[1])
        nb_gdn = mkbanks(_bs[2])
        nb_back = mkbanks(_bs[3])

        cf = sb("cf", [128, 9, 128], F32)
        uref = sb("uref", [128, 130], F32)
        R = sb("R", [128, D], F32)
        RN = sb("RN", [128, 8, 128], F32)
        bmf = R[:].rearrange("p (a b) -> p a b", a=8)
        UU = sb("UU", [128, 4, 128], F32)
        sel = UU[0:2, 0:2, :]
        s.dma("sync", lambda e: e.dma_start(out=cf[:], in_=cf_d), writes=["cf"], group="c0")
        s.dma("sync", lambda e: e.dma_start(out=uref[:], in_=uref_d), writes=["uref"], group="c0")
        s.dma("sync", lambda e: e.dma_start(out=bmf, in_=bm_d), writes=["R"], group="c0")
        s.dma("sync", lambda e: e.dma_start(out=sel, in_=sel_d), writes=["UU"], group="c0")
        ident_f, ones_f = cf[:, 0, :], cf[:, 1, :]
        mgt16, uincl, mgt, mt_incl, mt_strict = cf[:, 4, :], cf[:, 5, :], cf[:, 6, :], cf[:, 7, :], cf[:, 8, :]
        bmrep = sb("bmrep", [128, 2, 4, 128])
        bm1 = sb("bm1", [128, 8, 128])
        for n, m in enumerate((0, 2)):
            s.op("vector", lambda e, m=m, n=n: e.tensor_copy(out=bmrep[:, n, :, :], in_=bmf[:, m, :].unsqueeze(1).to_broadcast([128, 4, 128])),
                 reads=["R"], writes=["bmr"])
        s.op("vector", lambda e: e.tensor_copy(out=bm1[:], in_=bmf), reads=["R"], writes=["bmr"])
        bcm = lambda m: bm1[:, m, :].unsqueeze(1).to_broadcast([128, 4, 128])
        identr_b, bd16 = bmrep[:, 0], bmrep[:, 1]
        masku = bcm(1)
        ident_b = bm1[:, 0, :]
        ones_b = sb("ones_b", [128, 128])
        s.op("vector", lambda e: e.tensor_copy(out=ones_b[:], in_=cf[:, 1, :]), reads=["cf"], writes=["ones_b"])

        small = {}
        for name, d, shape in (("glanw", glanw_d, [128, 1]), ("gdnnw", gdnnw_d, [128, 1]), ("convw", convw_d, [128, 48]),
                               ("alog", alog_d, [128, 4]), ("dtb", dtb_d, [128, 4]), ("cT", cT_d, [128, 8, 2])):
            t = sb(name, shape, F32)
            small[name] = t
            s.dma("sync", lambda e, t=t, d=d: e.dma_start(out=t[:], in_=d), writes=[name], group="c1")
        lnw_b = sb("lnw_b", [128, D], F32)
        lnb_b = sb("lnb_b", [128, D], F32)
        s.dma("sync", lambda e: e.dma_start(out=lnw_b[:], in_=lnw_d.partition_broadcast(128)), writes=["lnw_b"], group="c1")
        s.dma("sync", lambda e: e.dma_start(out=lnb_b[:], in_=lnb_d.partition_broadcast(128)), writes=["lnb_b"], group="c1")
        wgb = sb("wgb", [17, 256])
        s.dma("sync", lambda e: e.dma_start(out=RN[0:17, 0:2, :].rearrange("p a b -> p (a b)"), in_=wg_d), writes=["RN"], group="RN")
        s.op("vector", lambda e: e.tensor_copy(out=wgb[:], in_=RN[0:17, 0:2, :].rearrange("p a b -> p (a b)")), reads=["RN"], writes=["wgb"])
        nega = sb("nega", [128, 4], F32)
        s.op("scalar", lambda e: e.activation(out=nega[:], in_=small["alog"][:], func=AF.Exp), reads=["alog"], writes=["nega"])
        s.op("vector", lambda e: e.tensor_scalar(out=nega[:], in0=nega[:], scalar1=-1.0, scalar2=None, op0=ALU.mult), reads=["nega"], writes=["nega"])
        diag = sb("diag", [128, 48, 128])
        for t in range(48):
            if t % 2:
                s.op("vector", lambda e, t=t: e.tensor_scalar(out=diag[:, t, :], in0=ident_f, scalar1=small["convw"][:, t:t + 1], scalar2=None, op0=ALU.mult),
                     reads=["cf", "convw"], writes=["diag"])
            else:
                s.op("scalar", lambda e, t=t: e.activation(out=diag[:, t, :], in_=ident_f, func=AF.Copy, scale=small["convw"][:, t:t + 1]),
                     reads=["cf", "convw"], writes=["diag"])

        xt = [sb("xt_%d" % i, [128, D], F32) for i in range(3)]
        stgs = [(R[:, :], "R"), (RN[:].rearrange("p a b -> p (a b)"), "RN")] + [(xt[i][:, :], "xt%d" % i) for i in range(3)]
        NST = len(stgs)
        Wb = sb("Wb", [128, 8, NCOL])
        Woutb = sb("Woutb", [128, 8, D])
        Sg = [sb("Sg%d" % i, [128, 4, 128], F32) for i in range(2)]
        modp = sb("modp", [128, 32], F32)
        g1b = [sb("g1b%d" % i, [128, D], F32) for i in range(2)]
        mch = Sg[0][0:2, :, :].rearrange("p h c -> p (h c)")
        bch = Sg[1][0:2, :, :].rearrange("p h c -> p (h c)")
        modbanks = [(banks[i], "ps%d" % i) for i in range(6)]
        mpb, mpkey = banks[6], "ps6"
        nst = 0
        for kt in range(8):
            for ch in range(3):
                st, key = stgs[nst % NST]
                nst += 1
                s.dma("sync", lambda e, st=st, kt=kt, ch=ch: e.dma_start(out=st[:, 0:1024], in_=wada_d[kt * 128:(kt + 1) * 128, ch * 1024:(ch + 1) * 1024]), writes=[key], group=key)
                for cgl in range(2):
                    bk, bkey = modbanks[ch * 2 + cgl]
                    s.op("tensor", lambda e, bk=bk, st=st, kt=kt, cgl=cgl: e.matmul(out=bk[0:2, 0:512], lhsT=small["cT"][:, kt, :], rhs=st[:, cgl * 512:(cgl + 1) * 512],
                                                                                 start=(kt == 0), stop=(kt == 7)), reads=[key, "cT"], writes=[bkey])
        for cg in range(6):
            bk, bkey = modbanks[cg]
            s.dma("sync", lambda e, cg=cg: e.dma_start(out=bch, in_=bada_d[:, cg * 512:(cg + 1) * 512].partition_broadcast(2)), writes=["Sg1"], group="bch")
            s.op("vector", lambda e, bk=bk: e.tensor_tensor(out=mch, in0=bk[0:2, 0:512], in1=bch, op=ALU.add), reads=[bkey, "Sg1"], writes=["Sg0"])
            if cg >= 2:
                s.op("vector", lambda e: e.tensor_scalar(out=mch, in0=mch, scalar1=1.0, scalar2=None, op0=ALU.add), reads=["Sg0"], writes=["Sg0"])
            if cg < 4:
                for j in range(4):
                    jn = cg * 4 + j
                    s.op("tensor", lambda e, jn=jn, j=j: e.matmul(out=mpb[:, jn * 2:(jn + 1) * 2], lhsT=mch[:, j * 128:(j + 1) * 128], rhs=cf[0:2, 0, 0:2], start=True, stop=True),
                         reads=["Sg0", "cf"], writes=[mpkey])
            else:
                half = cg - 4
                for sq in range(2):
                    b2, b2key = modbanks[sq]
                    s.op("tensor", lambda e, b2=b2, sq=sq: e.matmul(out=b2[:, 0:512], lhsT=sel[:, sq, :], rhs=mch, start=True, stop=True), reads=["Sg0", "UU"], writes=[b2key])
                    s.op("vector", lambda e, b2=b2, sq=sq, half=half: e.tensor_copy(out=g1b[sq][:, half * 512:(half + 1) * 512], in_=b2[:, 0:512]), reads=[b2key], writes=["g1b%d" % sq])
        s.op("vector", lambda e: e.tensor_copy(out=modp[:], in_=mpb[:, 0:32]), reads=[mpkey], writes=["modp"])
        for kt in range(8):
            for ch in range(4):
                c0 = ch * 1024
                w = min(1024, NCOL - c0)
                st, key = stgs[nst % NST]
                nst += 1
                s.dma("sync", lambda e, st=st, kt=kt, c0=c0, w=w: e.dma_start(out=st[:, 0:w], in_=win_d[kt * 128:(kt + 1) * 128, c0:c0 + w]), writes=[key], group=key)
                if nst % 2:
                    s.op("vector", lambda e, st=st, kt=kt, c0=c0, w=w: e.tensor_copy(out=Wb[:, kt, c0:c0 + w], in_=st[:, 0:w]), reads=[key], writes=["Wb"])
                else:
                    s.op("scalar", lambda e, st=st, kt=kt, c0=c0, w=w: e.activation(out=Wb[:, kt, c0:c0 + w], in_=st[:, 0:w], func=AF.Copy), reads=[key], writes=["Wb"])
        for kt in range(8):
            st, key = stgs[nst % NST]
            nst += 1
            s.dma("sync", lambda e, st=st, kt=kt: e.dma_start(out=st[:, 0:D], in_=wout_d[kt * 128:(kt + 1) * 128, :]), writes=[key], group=key)
            if nst % 2:
                s.op("vector", lambda e, st=st, kt=kt: e.tensor_copy(out=Woutb[:, kt, :], in_=st[:, 0:D]), reads=[key], writes=["Woutb"])
            else:
                s.op("scalar", lambda e, st=st, kt=kt: e.activation(out=Woutb[:, kt, :], in_=st[:, 0:D], func=AF.Copy), reads=[key], writes=["Woutb"])

        def two(name, shape, dt=BF16):
            return [sb("%s_%d" % (name, i), shape, dt) for i in range(2)]
        SG = [sb("SG_%d" % i, [128, 8, 128]) for i in range(3)]
        OA = two("OA", [128, 4, 128])
        Am = two("Am", [128, 4, 128]); AT = two("AT", [128, 4, 128])
        KW = two("KW", [128, 4, 128]); KD = two("KD", [128, 4, 128]); VB = two("VB", [128, 4, 128])
        QKT = two("QKT", [128, 4, 128]); QDEC = two("QDEC", [128, 4, 128])
        EX = two("EX", [128, 12], F32)
        hT = sb("hT", [128, 8, 128])
        FM = two("FM", [128, 4, 128])
        U = [sb("U%d" % i, [128, 12, 131]) for i in range(2)]
        lraug = two("lraug", [32, 128])
        KVt = two("KVt", [128, 768])
        spt = sb("spt", [128, 256], F32)
        E1 = sb("E1", [128, 2, 128]); E2 = sb("E2", [128, 2, 128])
        ER = sb("ER", [128, 2, 2], F32)
        QT = sb("QT", [128, 2, 128]); KT = sb("KT", [128, 2, 128])
        EK = sb("EK", [128, 256]); KP = sb("KP", [128, 256])
        ATT = sb("ATT", [128, 4, 128])
        SPb = sb("SPb", [128, 4, 128])
        KTz = sb("KTz", [128, 4, 128])
        Sgla = [sb("Sgla%d" % i, [128, 2, 128], F32) for i in range(2)]
        QKV = sb("QKV", [128, 12, 128])
        SQ = sb("SQ", [128, 8, 128])
        RNb = SQ
        QKN = sb("QKN", [128, 8, 128])
        T4 = sb("T4", [128, 4], F32); G4 = sb("G4", [128, 4], F32); AB = two("AB", [128, 8], F32)
        TB = sb("TB", [128, 4], F32); BETA = sb("BETA", [128, 4], F32); LNT = sb("LNT", [128, 4], F32)
        D4 = sb("D4", [128, 4], F32); ND4 = sb("ND4", [128, 4], F32); D4B = sb("D4B", [128, 4], F32); KWs = sb("KWs", [128, 4], F32)
        DG = RN[:, 4:8, :]; DG2 = RN[:, 0:4, :]
        ED = sb("ED", [128, 4, 128]); E1g = sb("E1g", [128, 4, 128]); E2g = sb("E2g", [128, 4, 128])
        INV = sb("INV", [128, 8, 4, 128])
        Pp = [INV[:, 0], INV[:, 1]]; PTp = [INV[:, 2], INV[:, 3]]; Yp = [INV[:, 4], INV[:, 5]]; Zp = [INV[:, 6], INV[:, 7]]
        X1 = sb("X1", [128, 4, 128]); X2 = sb("X2", [128, 4, 128])
        YTA = sb("YTA", [128, 4, 128])
        WT = X2; VN = X1; OB = sb("OB", [128, 4, 128])
        Sgb = [sb("Sgb%d" % i, [128, 4, 128]) for i in range(2)]
        v8 = lambda a, b: INV[:, a:b].rearrange("p a h c -> p (a h) c")
        SQO, SQOk = v8(0, 2), ["Pp0", "Pp1"]
        TO, TOk = v8(2, 4), ["PTp0", "PTp1"]
        YT, YTk = v8(4, 6), ["Yp0", "Yp1"]
        RNOb, RNObk = v8(6, 8), ["Zp0", "Zp1"]
        RNO = R[:].rearrange("p (a b) -> p a b", a=8)
        JUNK = X1
        ST = sb("ST", [128, 8], F32)
        DBG = None
        if taps:
            try:
                DBG = sb("DBG", [128, 512], F32)
            except AssertionError:
                DBG = None

        for i in range(2):
            s.op("gpsimd", lambda e, i=i: e.memset(Sgla[i][:], 0.0), writes=["Sgla%d" % i])
            s.op("gpsimd", lambda e, i=i: e.memset(Sg[i][:], 0.0), reads=["modp", "g1b0", "g1b1"], writes=["Sg%d" % i])
            s.op("gpsimd", lambda e, i=i: e.memset(Sgb[i][:], 0.0), writes=["Sgb%d" % i])
            s.op("gpsimd", lambda e, i=i: e.memset(U[i][:], 0.0), writes=["U%d" % i])
        for i in range(2):
            s.op("gpsimd", lambda e, i=i: e.memset(lraug[i][:], 1.0), writes=["lraug%d" % i])

        f3 = lambda ap: ap.rearrange("p (h c) -> p h c", h=4)
        bc = lambda ap: ap.unsqueeze(2).to_broadcast([128, 4, 128])

        def tt(eng, out, a, b, op, reads, writes):
            s.op(eng, lambda e: e.tensor_tensor(out=out, in0=a, in1=b, op=op), reads=reads, writes=writes)

        def act(out, in_, func, reads, writes, **kw):
            s.op("scalar", lambda e: e.activation(out=out, in_=in_, func=func, **kw), reads=reads, writes=writes)

        units = [(sq, m) for m in range(n_units) for sq in range(2)]
        dbg_d = {}

        def dbg(sq, m, name, ap, keys):
            if not taps or DBG is None:
                return
            shp = list(ap.shape)
            P = shp[0]
            w = int(np.prod(shp[1:]))
            if name not in dbg_d:
                dbg_d[name] = nc.dram_tensor("dbg_" + name, [2, n_units, 128, 512], F32, kind="ExternalOutput").ap()
            dst = DBG[0:P, 0:w]
            if len(shp) == 3:
                dst = dst.rearrange("p (h c) -> p h c", h=shp[1])
            s.op("vector", lambda e: e.tensor_copy(out=dst, in_=ap), reads=list(keys), writes=["DBG"])
            s.dma("sync", lambda e: e.dma_start(out=dbg_d[name][sq, m, 0:P, 0:w], in_=DBG[0:P, 0:w]), reads=["DBG"], group="tap")

        def load_x(u):
            sq_, m_ = units[u]
            p3_ = u % 3
            s.dma("sync", lambda e: e.dma_start(out=xt[p3_][:], in_=x_d[sq_, m_ * 128:(m_ + 1) * 128, :]), writes=["xt%d" % p3_], group="xt%d" % p3_)

        def pstage(sq, m, u):
            p, p3 = u % 2, u % 3
            xs, xk = xt[p3], "xt%d" % p3
            pk = lambda n: "%s%d" % (n, p)
            Uk = "U%d" % sq
            for _ in range(PDELAY):
                yield
            for g in range(2):
                bk, bkey = nb_p()
                for c in range(4):
                    kt = g * 4 + c
                    s.op("tensor", lambda e, bk=bk, c=c, kt=kt: e.transpose(out=bk[:, c * 128:(c + 1) * 128], in_=xs[:, kt * 128:(kt + 1) * 128], identity=ident_f),
                         reads=[xk, "cf"], writes=[bkey])
                for c in range(4):
                    kt = g * 4 + c
                    s.op("vector", lambda e, bk=bk, c=c, kt=kt: e.tensor_scalar(out=hT[:, kt, :], in0=bk[:, c * 128:(c + 1) * 128],
                                                                             scalar1=modp[:, 16 + kt * 2 + sq:17 + kt * 2 + sq], scalar2=modp[:, kt * 2 + sq:kt * 2 + sq + 1],
                                                                             op0=ALU.mult, op1=ALU.add), reads=[bkey, "modp"], writes=["hT"])
                yield
            fm_cols = [0, 128, 256, 384] + [1040 + 128 * i for i in range(4)] + [3096 + 128 * i for i in range(4)] + [1552 + 128 * i for i in range(12)]
            for g in range(6):
                bk, bkey = nb_p()
                for c in range(4):
                    c0 = fm_cols[g * 4 + c]
                    for kt in range(8):
                        s.op("tensor", lambda e, bk=bk, c=c, c0=c0, kt=kt: e.matmul(out=bk[:, c * 128:(c + 1) * 128], lhsT=Wb[:, kt, c0:c0 + 128], rhs=hT[:, kt, :],
                                                                                  start=(kt == 0), stop=(kt == 7)), reads=["Wb", "hT"], writes=[bkey])
                    if (c + 1) % PYIELD == 0:
                        yield
                if g == 0:
                    act(FM[p][:], f3(bk[:, :]), AF.Copy, [bkey], [pk("FM")])
                elif g in (1, 2):
                    act(SG[p3][:, (g - 1) * 4:g * 4, :], f3(bk[:, :]), AF.Silu, [bkey], ["SG%d" % p3])
                else:
                    s.op("vector", lambda e, bk=bk, g=g: e.tensor_copy(out=U[sq][:, (g - 3) * 4:(g - 2) * 4, 3:131], in_=f3(bk[:, :])), reads=[bkey], writes=[Uk])
            abk, abkey = nb_p()
            for kt in range(8):
                s.op("tensor", lambda e, kt=kt: e.matmul(out=abk[0:16, 0:128], lhsT=Wb[:, kt, 1024:1040], rhs=hT[:, kt, :], start=(kt == 0), stop=(kt == 7)),
                     reads=["Wb", "hT"], writes=[abkey])
            for kt in range(8):
                s.op("tensor", lambda e, kt=kt: e.matmul(out=abk[:, 128:136], lhsT=hT[:, kt, :], rhs=Wb[:, kt, 3088:3096], start=(kt == 0), stop=(kt == 7)),
                     reads=["Wb", "hT"], writes=[abkey])
            s.op("vector", lambda e: e.tensor_copy(out=lraug[p][0:16, :], in_=abk[0:16, 0:128]), reads=[abkey], writes=[pk("lraug")])
            s.op("vector", lambda e: e.tensor_copy(out=AB[p][:], in_=abk[:, 128:136]), reads=[abkey], writes=[pk("AB")])
            yield
            for part, (c0, w) in enumerate(((256, 512), (768, 256))):
                bk, bkey = nb_p()
                for kt in range(8):
                    s.op("tensor", lambda e, bk=bk, kt=kt, c0=c0, w=w: e.matmul(out=bk[:, 0:w], lhsT=hT[:, kt, :], rhs=Wb[:, kt, c0:c0 + w], start=(kt == 0), stop=(kt == 7)),
                         reads=["Wb", "hT"], writes=[bkey])
                s.op("vector", lambda e, bk=bk, c0=c0, w=w: e.tensor_copy(out=KVt[p][:, c0 - 256:c0 - 256 + w], in_=bk[:, 0:w]), reads=[bkey], writes=[pk("KVt")])
                yield

        def mid(sq, m, u):
            yield from interleave([gla(sq, m, u % 2), gdnpre(sq, m, u % 2)])

        def gla(sq, m, p):
            pk = lambda n: "%s%d" % (n, p)
            Slk = "Sgla%d" % sq
            bk, bkey = nb_gla()
            s.op("tensor", lambda e, bk=bk: e.matmul(out=bk[:, 0:256], lhsT=lraug[p][0:17, :], rhs=wgb[0:17, :], start=True, stop=True), reads=[pk("lraug"), "wgb"], writes=[bkey])
            act(spt[:], bk[:, 0:256], AF.Exp, [bkey], ["spt"], scale=-1.0)
            act(spt[:], spt[:], AF.Ln, ["spt"], ["spt"], bias=1.0)
            yield
            pbk, pbkey = nb_gla()
            for hp in range(2):
                s.op("tensor", lambda e, hp=hp: e.matmul(out=pbk[:, hp * 130:(hp + 1) * 130], lhsT=spt[:, hp * 128:(hp + 1) * 128], rhs=uref[:, :], start=True, stop=True),
                     reads=["spt", "uref"], writes=[pbkey])
            pv = pbk[:, 0:260].rearrange("p (h c) -> p h c", h=2)
            act(E1[:], pv[:, :, 0:128], AF.Exp, [pbkey], ["E1"], bias=float(np.log(0.125)))
            act(E2[:], pv[:, :, 0:128], AF.Exp, [pbkey], ["E2"], scale=-1.0)
            act(ER[:], pv[:, :, 128:130], AF.Exp, [pbkey], ["ER"])
            yield
            tt("vector", QT[:], FM[p][:, 0:2, :], E1[:], ALU.mult, [pk("FM"), "E1"], ["QT"])
            tt("vector", KT[:], FM[p][:, 2:4, :], E2[:], ALU.mult, [pk("FM"), "E2"], ["KT"])
            bk, bkey = nb_gla()
            s.op("tensor", lambda e, bk=bk: e.matmul(out=bk[:, 0:256], lhsT=mgt16, rhs=spt[:, :], start=True, stop=True), reads=["spt", "cf"], writes=[bkey])
            act(EK[:], bk[:, 0:256], AF.Exp, [bkey], ["EK"])
            tt("gpsimd", KP[:], KVt[p][:, 0:256], EK[:], ALU.mult, [pk("KVt"), "EK"], ["KP"])
            yield
            for h in range(4):
                hp, par = h // 2, h % 2
                s.op("vector", lambda e, h=h, hp=hp, par=par: e.tensor_scalar(out=KTz[:, h, :], in0=KT[:, hp, :], scalar1=cf[:, 2, par:par + 1], scalar2=None, op0=ALU.mult),
                     reads=["KT", "cf"], writes=["KTz"])
            yield
            bk, bkey = nb_gla()
            for h in range(4):
                hp, par = h // 2, h % 2
                s.op("tensor", lambda e, bk=bk, h=h, hp=hp, par=par: e.matmul(out=bk[:, h * 128:(h + 1) * 128], lhsT=KTz[:, h, :], rhs=QT[:, hp, :],
                                                                           start=True, stop=True), reads=["KTz", "QT"], writes=[bkey])
            tt("vector", ATT[:], f3(bk[:, :]), masku, ALU.mult, [bkey, "bmr"], ["ATT"])
            for h in range(4):
                hp, par = h // 2, h % 2
                s.op("vector", lambda e, h=h, hp=hp, par=par: e.tensor_scalar(out=SPb[:, h, :], in0=Sgla[sq][:, hp, :], scalar1=ER[:, hp, 0:1], scalar2=cf[:, 2, par:par + 1],
                                                                           op0=ALU.mult, op1=ALU.mult), reads=[Slk, "ER", "cf"], writes=["SPb"])
            yield
            oak, oakey = nb_gla()
            for h in range(4):
                hp, par = h // 2, h % 2
                s.op("tensor", lambda e, h=h, hp=hp, par=par: e.matmul(out=oak[:, h * 128:(h + 1) * 128], lhsT=SPb[:, h, :], rhs=QT[:, hp, :],
                                                                      start=True, stop=False), reads=["SPb", "QT"], writes=[oakey])
                s.op("tensor", lambda e, h=h: e.matmul(out=oak[:, h * 128:(h + 1) * 128], lhsT=KVt[p][:, 256 + h * 128:384 + h * 128], rhs=ATT[:, h, :], start=False, stop=True),
                     reads=[pk("KVt"), "ATT"], writes=[oakey])
            act(OA[p][:], f3(oak[:, :]), AF.Copy, [oakey], [pk("OA")])
            dbg(sq, m, "oa", OA[p][:], [pk("OA")])
            yield
            bk, bkey = nb_gla()
            for h in range(4):
                hp, par = h // 2, h % 2
                s.op("tensor", lambda e, bk=bk, h=h, hp=hp, par=par: e.matmul(out=bk[:, h * 128:(h + 1) * 128], lhsT=KP[:, hp * 128:(hp + 1) * 128],
                                                                           rhs=KVt[p][:, 256 + h * 128:384 + h * 128], start=True, stop=True), reads=["KP", pk("KVt")], writes=[bkey])
            for h in range(4):
                hp, par = h // 2, h % 2
                ps_ = slice(par * 64, (par + 1) * 64)
                s.op("vector", lambda e, bk=bk, h=h, hp=hp, ps_=ps_: e.scalar_tensor_tensor(out=Sgla[sq][ps_, hp, :], in0=Sgla[sq][ps_, hp, :], scalar=ER[ps_, hp, 1:2], in1=bk[ps_, h * 128:(h + 1) * 128],
                                                                                     op0=ALU.mult, op1=ALU.add), reads=[Slk, "ER", bkey], writes=[Slk])
            yield

        def gdnpre(sq, m, p):
            pk = lambda n: "%s%d" % (n, p)
            Uk = "U%d" % sq
            for g3 in range(3):
                bk, bkey = nb_gdn()
                for c in range(4):
                    t = g3 * 4 + c
                    for k in range(4):
                        s.op("tensor", lambda e, bk=bk, c=c, t=t, k=k: e.matmul(out=bk[:, c * 128:(c + 1) * 128], lhsT=diag[:, t * 4 + k, :], rhs=U[sq][:, t, k:k + 128],
                                                                              start=(k == 0), stop=(k == 3)), reads=["diag", Uk], writes=[bkey])
                act(QKV[:, g3 * 4:(g3 + 1) * 4, :], f3(bk[:, :]), AF.Silu, [bkey], ["QKV"])
                yield
            s.op("gpsimd", lambda e: e.tensor_copy(out=U[sq][:, :, 0:3], in_=U[sq][:, :, 128:131]), reads=[Uk], writes=[Uk])
            tt("gpsimd", SQ[:], QKV[:, 0:8, :], QKV[:, 0:8, :], ALU.mult, ["QKV"], ["SQ"])
            tt("vector", T4[:], AB[p][:, 0:4], small["dtb"][:], ALU.add, [pk("AB"), "dtb"], ["T4"])
            act(T4[:], T4[:], AF.Exp, ["T4"], ["T4"])
            act(T4[:], T4[:], AF.Ln, ["T4"], ["T4"], bias=1.0)
            tt("vector", G4[:], T4[:], nega[:], ALU.mult, ["T4", "nega"], ["G4"])
            act(TB[:], AB[p][:, 4:8], AF.Exp, [pk("AB")], ["TB"], scale=-1.0)
            s.op("vector", lambda e: e.tensor_scalar(out=TB[:], in0=TB[:], scalar1=1.0, scalar2=None, op0=ALU.add), reads=["TB"], writes=["TB"])
            s.op("vector", lambda e: e.reciprocal(out=BETA[:], in_=TB[:]), reads=["TB"], writes=["BETA"])
            act(LNT[:], TB[:], AF.Ln, ["TB"], ["LNT"])
            yield
            for g in range(2):
                bk, bkey = nb_gdn()
                for c in range(4):
                    s.op("tensor", lambda e, bk=bk, c=c, g=g: e.matmul(out=bk[:, c * 128:(c + 1) * 128], lhsT=ones_b[:, :], rhs=SQ[:, g * 4 + c, :], start=True, stop=True),
                         reads=["ones_b", "SQ"], writes=[bkey])
                act(RN[:, g * 4:(g + 1) * 4, :], f3(bk[:, :]), AF.Ln, [bkey], ["RN"], bias=1e-6)
                act(RNb[:, g * 4:(g + 1) * 4, :], RN[:, g * 4:(g + 1) * 4, :], AF.Exp, ["RN"], ["SQ"], scale=-0.5, bias=(float(np.log(128.0 ** -0.5)) if g == 0 else 0.0))
                yield
            tt("vector", QKN[:], QKV[:, 0:8, :], RNb[:], ALU.mult, ["QKV", "SQ"], ["QKN"])
            dbk, dbkey = nb_gdn()
            for n, lm in enumerate((uincl, mgt, ones_f)):
                s.op("tensor", lambda e, n=n, lm=lm: e.matmul(out=dbk[:, n * 4:(n + 1) * 4], lhsT=lm, rhs=G4[:, :], start=True, stop=True), reads=["cf", "G4"], writes=[dbkey])
            act(EX[p][:], dbk[:, 0:12], AF.Exp, [dbkey], [pk("EX")])
            s.op("vector", lambda e: e.tensor_copy(out=D4[:], in_=dbk[:, 0:4]), reads=[dbkey], writes=["D4"])
            s.op("vector", lambda e: e.tensor_scalar(out=ND4[:], in0=dbk[:, 0:4], scalar1=-1.0, scalar2=None, op0=ALU.mult), reads=[dbkey], writes=["ND4"])
            tt("vector", D4B[:], D4[:], LNT[:], ALU.subtract, ["D4", "LNT"], ["D4B"])
            tt("vector", KWs[:], BETA[:], EX[p][:, 0:4], ALU.mult, ["BETA", pk("EX")], ["KWs"])
            yield
            for t in range(4):
                s.op("tensor", lambda e, t=t: e.transpose(out=psb[:, t * 128:(t + 1) * 128], in_=QKN[:, 4 + t, :], identity=ident_b), reads=["QKN", "bmr"], writes=["psb"])
            for t in range(4):
                s.op("tensor", lambda e, t=t: e.transpose(out=psb[:, 512 + t * 128:512 + (t + 1) * 128], in_=QKV[:, 8 + t, :], identity=ident_b), reads=["QKV", "bmr"], writes=["psb"])
            tt("vector", KW[p][:], f3(psb[:, 0:512]), bc(KWs[:]), ALU.mult, ["psb", "KWs"], [pk("KW")])
            tt("vector", KD[p][:], f3(psb[:, 0:512]), bc(EX[p][:, 4:8]), ALU.mult, ["psb", pk("EX")], [pk("KD")])
            tt("vector", VB[p][:], f3(psb[:, 512:1024]), bc(BETA[:]), ALU.mult, ["psb", "BETA"], [pk("VB")])
            yield
            tt("gpsimd", DG, identr_b, bc(D4[:]), ALU.mult, ["bmr", "D4"], ["RN"])
            tt("gpsimd", DG2, identr_b, bc(D4B[:]), ALU.mult, ["bmr", "D4B"], ["RN"])
            dgf = DG.rearrange("p h c -> p (h c)")
            dg2f = DG2.rearrange("p h c -> p (h c)")
            bk, bkey = nb_gdn()
            s.op("tensor", lambda e, bk=bk: e.matmul(out=bk[:, :], lhsT=ones_f, rhs=dgf, start=True, stop=True), reads=["cf", "RN"], writes=[bkey])
            act(ED[:], f3(bk[:, :]), AF.Exp, [bkey], ["ED"])
            yield
            for (src, srck, msk, dst, dstk) in ((dgf, "RN", mt_incl, E1g, "E1g"), (dg2f, "RN", mt_strict, E2g, "E2g")):
                bk, bkey = nb_gdn()
                for h in range(4):
                    s.op("tensor", lambda e, bk=bk, src=src, h=h: e.matmul(out=bk[:, h * 128:(h + 1) * 128], lhsT=ones_f, rhs=src[:, h * 128:(h + 1) * 128], start=True, stop=False),
                         reads=["cf", srck], writes=[bkey])
                    s.op("tensor", lambda e, bk=bk, h=h, msk=msk: e.matmul(out=bk[:, h * 128:(h + 1) * 128], lhsT=ident_f, rhs=msk, start=False, stop=False), reads=["cf"], writes=[bkey])
                    s.op("tensor", lambda e, bk=bk, h=h: e.matmul(out=bk[:, h * 128:(h + 1) * 128], lhsT=DG[:, h, :], rhs=cf[:, 3, :], start=False, stop=True), reads=["cf", "RN"], writes=[bkey])
                act(dst[:], f3(bk[:, :]), AF.Exp, [bkey], [dstk])
                yield
            tt("gpsimd", QDEC[p][:], QKN[:, 0:4, :], ED[:], ALU.mult, ["QKN", "ED"], [pk("QDEC")])
            gk, gkey = nb_gdn()
            for h in range(4):
                s.op("tensor", lambda e, h=h: e.matmul(out=gk[:, h * 128:(h + 1) * 128], lhsT=QKN[:, 4 + h, :], rhs=QKN[:, 4 + h, :], start=True, stop=True), reads=["QKN"], writes=[gkey])
            tt("vector", AT[p][:], f3(gk[:, :]), E2g[:], ALU.mult, [gkey, "E2g"], [pk("AT")])
            qk, qkey = nb_gdn()
            for h in range(4):
                s.op("tensor", lambda e, h=h: e.matmul(out=qk[:, h * 128:(h + 1) * 128], lhsT=QKN[:, 4 + h, :], rhs=QKN[:, h, :], start=True, stop=True), reads=["QKN"], writes=[qkey])
            tt("vector", QKT[p][:], f3(qk[:, :]), E1g[:], ALU.mult, [qkey, "E1g"], [pk("QKT")])
            yield
            for h in range(4):
                s.op("tensor", lambda e, h=h: e.transpose(out=psb[:, h * 128:(h + 1) * 128], in_=AT[p][:, h, :], identity=ident_b), reads=[pk("AT"), "bmr"], writes=["psb"])
            act(Am[p][:], f3(psb[:, 0:512]), AF.Copy, ["psb"], [pk("Am")])
            yield

        def back(sq, m, u):
            p, p3 = u % 2, u % 3
            xs, xk = xt[p3], "xt%d" % p3
            pk = lambda n: "%s%d" % (n, p)
            Sgk, Sgbk = "Sg%d" % sq, "Sgb%d" % sq
            A_, Ak, AT_, ATk = Am[p], pk("Am"), AT[p], pk("AT")

            def mm4(L, Lk, Rt, Rk):
                bk, bkey = nb_back()
                for h in range(4):
                    s.op("tensor", lambda e, bk=bk, h=h: e.matmul(out=bk[:, h * 128:(h + 1) * 128], lhsT=L[:, h, :], rhs=Rt[:, h, :], start=True, stop=True),
                         reads=[Lk, Rk], writes=[bkey])
                return bk, bkey
            def invchain():
                tt("vector", Pp[0], A_[:], bd16, ALU.mult, [Ak, "bmr"], ["Pp0"])
                tt("gpsimd", PTp[0], AT_[:], bd16, ALU.mult, [ATk, "bmr"], ["PTp0"])
                tt("vector", Yp[0], identr_b, Pp[0], ALU.subtract, ["bmr", "Pp0"], ["Yp0"])
                tt("gpsimd", Zp[0], identr_b, PTp[0], ALU.subtract, ["bmr", "PTp0"], ["Zp0"])
                yield
                act(R[:], xs[:], AF.Copy, [xk], ["R"], scale=ALPHA)
                if u + 3 < len(units):
                    load_x(u + 3)
                cur = 0
                yz = 0
                for lvl in range(3):
                    nx = 1 - cur
                    b1, k1 = mm4(PTp[cur], "PTp%d" % cur, Pp[cur], "Pp%d" % cur)
                    b2, k2 = mm4(Pp[cur], "Pp%d" % cur, PTp[cur], "PTp%d" % cur)
                    act(Pp[nx], f3(b1[:, :]), AF.Copy, [k1], ["Pp%d" % nx])
                    s.op("vector", lambda e, b2=b2, nx=nx: e.tensor_copy(out=PTp[nx], in_=f3(b2[:, :])), reads=[k2], writes=["PTp%d" % nx])
                    yield
                    b3, k3 = mm4(PTp[nx], "PTp%d" % nx, Yp[yz], "Yp%d" % yz)
                    b4, k4 = mm4(Pp[nx], "Pp%d" % nx, Zp[yz], "Zp%d" % yz)
                    tt("vector", Yp[1 - yz], f3(b3[:, :]), Yp[yz], ALU.add, [k3, "Yp%d" % yz], ["Yp%d" % (1 - yz)])
                    tt("vector", Zp[1 - yz], f3(b4[:, :]), Zp[yz], ALU.add, [k4, "Zp%d" % yz], ["Zp%d" % (1 - yz)])
                    yield
                    cur = nx
                    yz = 1 - yz
                for (mi, both) in ((3, True), (5, True), (7, False)):
                    Yc, Zc, Yk, Zk = Yp[yz], Zp[yz], "Yp%d" % yz, "Zp%d" % yz
                    Yn, Zn, Ynk, Znk = Yp[1 - yz], Zp[1 - yz], "Yp%d" % (1 - yz), "Zp%d" % (1 - yz)
                    mT = bcm(mi + 1) if both else bcm(mi)
                    b1, k1 = mm4(A_, Ak, Zc, Zk)
                    if both:
                        b3, k3 = mm4(AT_, ATk, Yc, Yk)
                    tt("vector", X1[:], f3(b1[:, :]), mT, ALU.mult, [k1, "bmr"], ["X1"])
                    if both:
                        tt("vector", X2[:], f3(b3[:, :]), bcm(mi), ALU.mult, [k3, "bmr"], ["X2"])
                    yield
                    b2, k2 = mm4(Yc, Yk, X1, "X1")
                    if both:
                        b4, k4 = mm4(Zc, Zk, X2, "X2")
                    tt("vector", Zn, Zc, f3(b2[:, :]), ALU.subtract, [Zk, k2], [Znk])
                    if both:
                        tt("vector", Yn, Yc, f3(b4[:, :]), ALU.subtract, [Yk, k4], [Ynk])
                    yield
                    yz = 1 - yz
                TT, TTk = Zp[yz], "Zp%d" % yz
                dbg(sq, m, "TT", TT, [TTk])
                b1, k1 = mm4(TT, TTk, VB[p], pk("VB"))
                b2, k2 = mm4(KW[p], pk("KW"), TT, TTk)
                act(UU[:], f3(b1[:, :]), AF.Copy, [k1], ["UU"])
                act(WT[:], f3(b2[:, :]), AF.Copy, [k2], ["X2"])
                yield
                b3, k3 = mm4(WT, "X2", Sgb[sq], Sgbk)
                tt("vector", VN[:], UU[:], f3(b3[:, :]), ALU.subtract, ["UU", k3], ["X1"])
                yield
                obk, obkey = nb_back()
                for h in range(4):
                    s.op("tensor", lambda e, h=h: e.matmul(out=obk[:, h * 128:(h + 1) * 128], lhsT=Sgb[sq][:, h, :], rhs=QDEC[p][:, h, :], start=True, stop=False), reads=[Sgbk, pk("QDEC")], writes=[obkey])
                    s.op("tensor", lambda e, h=h: e.matmul(out=obk[:, h * 128:(h + 1) * 128], lhsT=VN[:, h, :], rhs=QKT[p][:, h, :], start=False, stop=True), reads=["X1", pk("QKT")], writes=[obkey])
                act(OB[:], f3(obk[:, :]), AF.Copy, [obkey], ["OB"])
                b4, k4 = mm4(KD[p], pk("KD"), VN, "X1")
                for h in range(4):
                    s.op("vector", lambda e, h=h: e.scalar_tensor_tensor(out=Sg[sq][:, h, :], in0=Sg[sq][:, h, :], scalar=EX[p][:, 8 + h:9 + h], in1=b4[:, h * 128:(h + 1) * 128],
                                                                         op0=ALU.mult, op1=ALU.add), reads=[Sgk, pk("EX"), k4], writes=[Sgk])
                s.op("gpsimd", lambda e: e.tensor_copy(out=Sgb[sq][:], in_=Sg[sq][:]), reads=[Sgk], writes=[Sgbk])
                dbg(sq, m, "ob", OB[:], ["OB"])
                yield

            def early():
                tt("gpsimd", X1[:], OA[p][:], OA[p][:], ALU.mult, [pk("OA")], ["X1"])
                yield
                bk, bkey = nb_back()
                for c in range(4):
                    s.op("tensor", lambda e, bk=bk, c=c: e.matmul(out=bk[:, c * 128:(c + 1) * 128], lhsT=ones_b[:, :], rhs=X1[:, c, :], start=True, stop=True),
                         reads=["ones_b", "X1"], writes=[bkey])
                act(UU[:], f3(bk[:, :]), AF.Ln, [bkey], ["UU"], bias=128e-6)
                yield
                act(X2[:], UU[:], AF.Exp, ["UU"], ["X2"], scale=-0.5, bias=float(0.5 * np.log(128.0)))
                tt("gpsimd", X1[:], OA[p][:], X2[:], ALU.mult, [pk("OA"), "X2"], ["X1"])
                yield
                s.op("vector", lambda e: e.scalar_tensor_tensor(out=YTA[:], in0=X1[:], scalar=small["glanw"][:, 0:1], in1=SG[p3][:, 0:4, :],
                                                                op0=ALU.mult, op1=ALU.mult), reads=["X1", "SG%d" % p3, "glanw"], writes=["YTA"])
                yield

            yield from interleave([invchain(), early()])
            for g, (osb, okey) in ((1, (OB, "OB")),):
                sl = slice(g * 4, (g + 1) * 4)
                tt("vector", SQO[:, sl, :], osb[:], osb[:], ALU.mult, [okey], SQOk)
                bk, bkey = nb_back()
                for c in range(4):
                    s.op("tensor", lambda e, bk=bk, c=c, g=g: e.matmul(out=bk[:, c * 128:(c + 1) * 128], lhsT=ones_b[:, :], rhs=SQO[:, g * 4 + c, :], start=True, stop=True),
                         reads=["ones_b"] + SQOk, writes=[bkey])
                act(UU[:], f3(bk[:, :]), AF.Ln, [bkey], ["UU"], bias=128e-6)
                act(RNOb[:, sl, :], UU[:], AF.Exp, ["UU"], RNObk, scale=-0.5, bias=float(0.5 * np.log(128.0)))
                tt("vector", TO[:, sl, :], osb[:], RNOb[:, sl, :], ALU.mult, [okey] + RNObk, TOk)
                nw = small["glanw"] if g == 0 else small["gdnnw"]
                s.op("vector", lambda e, sl=sl, nw=nw: e.scalar_tensor_tensor(out=YT[:, sl, :], in0=TO[:, sl, :], scalar=nw[:, 0:1], in1=SG[p3][:, sl, :],
                                                                           op0=ALU.mult, op1=ALU.mult), reads=TOk + ["SG%d" % p3, "glanw", "gdnnw"], writes=YTk)
                yield
            for half in range(2):
                bk, bkey = nb_back()
                for kt in range(8):
                    ylhs = YTA[:, kt, :] if kt < 4 else YT[:, kt, :]
                    s.op("tensor", lambda e, bk=bk, kt=kt, half=half, ylhs=ylhs: e.matmul(out=bk[:, :], lhsT=ylhs, rhs=Woutb[:, kt, half * 512:(half + 1) * 512], start=(kt == 0), stop=(kt == 7)),
                         reads=YTk + ["YTA", "Woutb"], writes=[bkey])
                uuf = UU[:].rearrange("p h c -> p (h c)")
                tt("vector", uuf, bk[:, :], g1b[sq][:, half * 512:(half + 1) * 512], ALU.mult, [bkey, "g1b%d" % sq], ["UU"])
                tt("vector", R[:, half * 512:(half + 1) * 512], R[:, half * 512:(half + 1) * 512], uuf, ALU.add, ["R", "UU"], ["R"])
                yield
            s.op("gpsimd", lambda e: e.memset(ST[:], 0.0), writes=["ST"])
            jv = lambda t: t[:].rearrange("p h c -> p (h c)")
            for hf in range(2):
                act(jv(X1), R[:, hf * 512:(hf + 1) * 512], AF.Copy, ["R", "ST"], ["X1", "ST"], accum_out=ST[:, hf:hf + 1])
                act(jv(X1), R[:, hf * 512:(hf + 1) * 512], AF.Square, ["R", "ST"], ["X1", "ST"], accum_out=ST[:, 2 + hf:3 + hf])
            yield
            tt("vector", ST[:, 4:5], ST[:, 0:1], ST[:, 1:2], ALU.add, ["ST"], ["ST"])
            tt("vector", ST[:, 5:6], ST[:, 2:3], ST[:, 3:4], ALU.add, ["ST"], ["ST"])
            s.op("vector", lambda e: e.tensor_scalar(out=ST[:, 4:5], in0=ST[:, 4:5], scalar1=1.0 / D, scalar2=None, op0=ALU.mult), reads=["ST"], writes=["ST"])
            tt("vector", ST[:, 6:7], ST[:, 4:5], ST[:, 4:5], ALU.mult, ["ST"], ["ST"])
            s.op("vector", lambda e: e.scalar_tensor_tensor(out=ST[:, 7:8], in0=ST[:, 5:6], scalar=1.0 / D, in1=ST[:, 6:7], op0=ALU.mult, op1=ALU.subtract), reads=["ST"], writes=["ST"])
            act(ST[:, 7:8], ST[:, 7:8], AF.Ln, ["ST"], ["ST"], bias=1e-5)
            act(ST[:, 7:8], ST[:, 7:8], AF.Exp, ["ST"], ["ST"], scale=-0.5)
            s.op("vector", lambda e: e.scalar_tensor_tensor(out=ST[:, 6:7], in0=ST[:, 4:5], scalar=-1.0, in1=ST[:, 7:8], op0=ALU.mult, op1=ALU.mult), reads=["ST"], writes=["ST"])
            act(R[:], R[:], AF.Identity, ["R", "ST"], ["R"], scale=ST[:, 7:8], bias=ST[:, 6:7])
            tt("vector", R[:], R[:], lnw_b[:], ALU.mult, ["R", "lnw_b"], ["R"])
            tt("gpsimd", R[:], R[:], lnb_b[:], ALU.add, ["R", "lnb_b"], ["R"])
            s.dma("sync", lambda e: e.dma_start(out=out_d[sq, m * 128:(m + 1) * 128, :], in_=R[:]), reads=["R"], group="out")
            yield

        NUu = len(units)
        for u0 in range(min(3, NUu)):
            load_x(u0)
        for it in range(NUu + 2):
            gd = {}
            if 0 <= it - 2 < NUu:
                gd["b"] = back(units[it - 2][0], units[it - 2][1], it - 2)
            if 0 <= it - 1 < NUu:
                gd["m"] = mid(units[it - 1][0], units[it - 1][1], it - 1)
            if it < NUu:
                gd["p"] = pstage(units[it][0], units[it][1], it)
            run(interleave([gd[k] for k in ORDER if k in gd]))
        fw = ["out"] + (["tap"] if (taps and "tap" in s.dma_groups) else [])
        cnt = s.emit(final_wait_groups=fw)
        print("ops", len(s.ops), "signals", cnt)
    return nc


def make_inputs_for_core(core, inputs, consts, n_units):
    T = n_units * 128
    cfa, urefa, bma, sela = consts
    f = lambda a: np.ascontiguousarray(np.asarray(a, dtype=np.float32))
    xs = f(inputs["x"][2 * core:2 * core + 2, :T])
    c2 = f(inputs["c"][2 * core:2 * core + 2])
    cT = np.ascontiguousarray(c2.reshape(2, 8, 128).transpose(2, 1, 0))
    convw = f(inputs["gdn_conv_w"][0])
    convw_l = np.ascontiguousarray(convw.reshape(4, 12, 128).transpose(2, 1, 0).reshape(128, 48))
    return dict(
        x=xs, cT=cT, w_ada=f(inputs["w_ada"][0]), b_ada=f(inputs["b_ada"]), w_in=f(inputs["w_in"][0]),
        wg_aug=np.ascontiguousarray(np.concatenate([f(inputs["gla_w_gate_up"][0]), f(inputs["gla_b_gate"])], axis=0)),
        gla_nw=f(inputs["gla_norm_w"][0]).reshape(128, 1), gdn_nw=f(inputs["gdn_norm_w"][0]).reshape(128, 1),
        convw=convw_l,
        alog_b=np.ascontiguousarray(np.broadcast_to(f(inputs["gdn_a_log"]), (128, 4))),
        dtb_b=np.ascontiguousarray(np.broadcast_to(f(inputs["gdn_dt_bias"]), (128, 4))),
        w_out=f(inputs["w_out"][0]), ln_w=f(inputs["ln_w"]), ln_b=f(inputs["ln_b"]),
        cf=cfa, uref=urefa, bm=bma, sel=sela)


def kernel(**inputs):
    n_units = 32
    consts = host_consts()
    nc = build(n_units)
    in_maps = [make_inputs_for_core(c, inputs, consts, n_units) for c in range(8)]
    res = run_bass_kernel_spmd(nc, in_maps, core_ids=list(range(8)))
    out = np.concatenate([r["out"] for r in res.results], axis=0)
    return out.astype(np.float32)
```

```python
import bisect
import contextlib
import os
import numpy as np
import concourse.bass as bass
import concourse.mybir as mybir
from concourse.bass_utils import run_bass_kernel_spmd

F32 = mybir.dt.float32
BF16 = mybir.dt.bfloat16
AF = mybir.ActivationFunctionType
ALU = mybir.AluOpType

ENGINES = ("tensor", "vector", "scalar", "gpsimd", "sync")
D = 1024
NCOL = 3608
ALPHA = 2.0 ** 0.25
NEG = -30000.0


class Sched:
    def __init__(self, nc):
        self.nc = nc
        self.ops = []
        self.last_writer = {}
        self.readers = {}
        self.dma_groups = {}

    def _add(self, eng, fn, reads, writes, dma_group=None):
        idx = len(self.ops)
        deps = set()
        for r in reads:
            w = self.last_writer.get(r)
            if w is not None:
                deps.add((w, "raw"))
        for r in writes:
            w = self.last_writer.get(r)
            if w is not None:
                deps.add((w, "waw"))
            for rd in self.readers.get(r, ()):
                deps.add((rd, "war"))
        for r in reads:
            self.readers.setdefault(r, []).append(idx)
        for r in writes:
            self.last_writer[r] = idx
            self.readers[r] = []
        op = dict(eng=eng, fn=fn, deps=deps, dma_group=dma_group, signal=False, seq=None)
        if dma_group is not None:
            c = self.dma_groups.get(dma_group, 0) + 1
            self.dma_groups[dma_group] = c
            op["seq"] = c
            op["signal"] = True
        self.ops.append(op)
        return idx

    def op(self, eng, fn, reads=(), writes=()):
        return self._add(eng, fn, tuple(reads), tuple(writes))

    def dma(self, eng, fn, reads=(), writes=(), group=None):
        return self._add(eng, fn, tuple(reads), tuple(writes), dma_group=group)

    def emit(self, final_wait_groups=()):
        nc = self.nc
        ops = self.ops
        for i, op in enumerate(ops):
            nd = set()
            for (p, kind) in op["deps"]:
                pop = ops[p]
                if pop["dma_group"] is None and op["dma_group"] is None and pop["eng"] == op["eng"]:
                    if op["eng"] == "tensor" or kind != "raw":
                        continue
                nd.add(p)
            best = {}
            keep = set()
            for p in nd:
                pop = ops[p]
                if pop["dma_group"] is not None:
                    keep.add(p)
                elif p > best.get(pop["eng"], -1):
                    best[pop["eng"]] = p
            keep.update(best.values())
            op["ndeps"] = keep
            for p in keep:
                ops[p]["signal"] = True
        cnt = {e: 0 for e in ENGINES}
        for op in ops:
            if op["dma_group"] is None and op["signal"]:
                cnt[op["eng"]] += 1
                op["seq"] = cnt[op["eng"]]
        stack = contextlib.ExitStack()
        esem = {e: stack.enter_context(nc.semaphore("s_" + e)) for e in ENGINES if cnt[e] > 0}
        dsem = {g: stack.enter_context(nc.semaphore("d_%d" % k)) for k, g in enumerate(self.dma_groups)}
        per_eng = {e: [] for e in ENGINES}
        grp_idx = {}
        for i, op in enumerate(ops):
            per_eng[op["eng"]].append(i)
            if op["dma_group"] is not None:
                grp_idx.setdefault(op["dma_group"], []).append(i)

        def run_engine(ename, eobj):
            waited = {}
            for i in per_eng[ename]:
                op = ops[i]
                need = {}
                for p in op["ndeps"]:
                    pop = ops[p]
                    if pop["dma_group"] is not None:
                        key = ("d", pop["dma_group"])
                        val = 16 * bisect.bisect_left(grp_idx[pop["dma_group"]], i)
                    else:
                        key = ("e", pop["eng"])
                        val = pop["seq"]
                    if val > need.get(key, 0):
                        need[key] = val
                for key, val in need.items():
                    if waited.get(key, 0) >= val:
                        continue
                    waited[key] = val
                    sem = dsem[key[1]] if key[0] == "d" else esem[key[1]]
                    eobj.wait_ge(sem, val)
                ins = op["fn"](eobj)
                if op["dma_group"] is not None:
                    ins.then_inc(dsem[op["dma_group"]], 16)
                elif op["signal"]:
                    ins.then_inc(esem[ename], 1)
            if ename == "sync":
                for g in final_wait_groups:
                    eobj.wait_ge(dsem[g], 16 * self.dma_groups[g])

        with stack:
            with nc.Block() as block:
                for ename in ENGINES:
                    getattr(block, ename)(lambda eobj, ename=ename: run_engine(ename, eobj))
        return cnt


def host_consts():
    j = np.arange(128)[:, None]
    i = np.arange(128)[None, :]
    f = lambda m: m.astype(np.float32)
    cf = np.zeros((128, 9, 128), np.float32)
    cf[:, 0] = np.eye(128)
    cf[:, 1] = 1.0
    cf[:, 2] = 0.0
    cf[:64, 2, 0] = 1.0
    cf[64:, 2, 1] = 1.0
    cf[:, 3] = -1.0
    cf[:, 4] = -(1.0 / 16.0) * f(j > i)
    cf[:, 5] = f(j <= i)
    cf[:, 6] = f(j > i)
    cf[:, 7] = np.where(i >= j, 0.0, NEG)
    cf[:, 8] = np.where(i > j, 0.0, NEG)
    uref = np.zeros((128, 130), np.float32)
    uref[:, :128] = -(1.0 / 16.0) * (f(j <= i) - f(j <= 63))
    uref[:, 128] = -(1.0 / 16.0) * f(j[:, 0] <= 63)
    uref[:, 129] = -(1.0 / 16.0)
    bm = np.zeros((128, 8, 128), np.float32)
    bm[:, 0] = np.eye(128)
    bm[:, 1] = f(j <= i)
    bm[:, 2] = f((j // 16) == (i // 16))
    for n, s in ((3, 16), (5, 32), (7, 64)):
        low = f(((j // s) % 2 == 1) & ((i // s) == (j // s) - 1))
        if n < 7:
            bm[:, n] = low
            bm[:, n + 1] = low.T
        else:
            bm[:, n] = low.T
    sel = np.zeros((2, 2, 128), np.float32)
    sel[0, 0] = 1.0
    sel[1, 1] = 1.0
    return cf, uref, bm, sel


def interleave(gens):
    gens = list(gens)
    while gens:
        for g in list(gens):
            if g not in gens:
                continue
            try:
                next(g)
            except StopIteration:
                gens = [x for x in gens if x is not g]
        yield


def run(gen):
    for _ in gen:
        pass


PYIELD = int(os.environ.get("K_PYIELD", "1"))
ORDER = os.environ.get("K_ORDER", "bpbmp")
PDELAY = int(os.environ.get("K_PDELAY", "0"))
BANKSETS = {"a": ([0], [1], [2, 3], [4, 5, 6]), "b": ([0, 1], [2], [3, 4], [5, 6]), "c": ([0, 1], [2], [3], [4, 5, 6]), "d": ([0, 1, 2], [3], [4], [5, 6]), "e": ([0, 1], [2, 3], [4], [5, 6])}[os.environ.get("K_BANKS", "b")]


def build(n_units, taps=False, stage=99):
    T = n_units * 128
    nc = bass.Bass("TRN2", target_bir_lowering=False)
    din = lambda name, shape: nc.dram_tensor(name, shape, F32, kind="ExternalInput").ap()
    x_d = din("x", [2, T, D])
    cT_d = din("cT", [128, 8, 2])
    wada_d = din("w_ada", [D, 3 * D])
    bada_d = din("b_ada", [1, 3 * D])
    win_d = din("w_in", [D, NCOL])
    wg_d = din("wg_aug", [17, 256])
    glanw_d = din("gla_nw", [128, 1])
    gdnnw_d = din("gdn_nw", [128, 1])
    convw_d = din("convw", [128, 48])
    alog_d = din("alog_b", [128, 4])
    dtb_d = din("dtb_b", [128, 4])
    wout_d = din("w_out", [D, D])
    lnw_d = din("ln_w", [1, D])
    lnb_d = din("ln_b", [1, D])
    cf_d = din("cf", [128, 9, 128])
    uref_d = din("uref", [128, 130])
    bm_d = din("bm", [128, 8, 128])
    sel_d = din("sel", [2, 2, 128])
    out_d = nc.dram_tensor("out", [2, T, D], F32, kind="ExternalOutput").ap()

    es = contextlib.ExitStack()
    with es:
        def sb(name, shape, dt=BF16):
            return es.enter_context(nc.sbuf_tensor("s_" + name, shape, dt))
        s = Sched(nc)
        banks = [es.enter_context(nc.psum_tensor("ps%d" % i, [128, 512], F32)) for i in range(7)]
        psb = es.enter_context(nc.psum_tensor("psb", [128, 1024], BF16))

        def mkbanks(ids):
            ctr = [0]

            def nb():
                b = ids[ctr[0] % len(ids)]
                ctr[0] += 1
                return banks[b], "ps%d" % b
            return nb
        nb_setup = mkbanks([0, 1, 2, 3, 4, 5, 6])
        _bs = BANKSETS
        nb_p = mkbanks(_bs[0])
        nb_gla = mkbanks(_bs[1])
        nb_gdn = mkbanks(_bs[2])
        nb_back = mkbanks(_bs[3])

        cf = sb("cf", [128, 9, 128], F32)
        uref = sb("uref", [128, 130], F32)
        R = sb("R", [128, D], F32)
        RN = sb("RN", [128, 8, 128], F32)
        bmf = R[:].rearrange("p (a b) -> p a b", a=8)
        UU = sb("UU", [128, 4, 128], F32)
        sel = UU[0:2, 0:2, :]
        s.dma("sync", lambda e: e.dma_start(out=cf[:], in_=cf_d), writes=["cf"], group="c0")
        s.dma("sync", lambda e: e.dma_start(out=uref[:], in_=uref_d), writes=["uref"], group="c0")
        s.dma("sync", lambda e: e.dma_start(out=bmf, in_=bm_d), writes=["R"], group="c0")
        s.dma("sync", lambda e: e.dma_start(out=sel, in_=sel_d), writes=["UU"], group="c0")
        ident_f, ones_f = cf[:, 0, :], cf[:, 1, :]
        mgt16, uincl, mgt, mt_incl, mt_strict = cf[:, 4, :], cf[:, 5, :], cf[:, 6, :], cf[:, 7, :], cf[:, 8, :]
        bmrep = sb("bmrep", [128, 2, 4, 128])
        bm1 = sb("bm1", [128, 8, 128])
        for n, m in enumerate((0, 2)):
            s.op("vector", lambda e, m=m, n=n: e.tensor_copy(out=bmrep[:, n, :, :], in_=bmf[:, m, :].unsqueeze(1).to_broadcast([128, 4, 128])),
                 reads=["R"], writes=["bmr"])
        s.op("vector", lambda e: e.tensor_copy(out=bm1[:], in_=bmf), reads=["R"], writes=["bmr"])
        bcm = lambda m: bm1[:, m, :].unsqueeze(1).to_broadcast([128, 4, 128])
        identr_b, bd16 = bmrep[:, 0], bmrep[:, 1]
        masku = bcm(1)
        ident_b = bm1[:, 0, :]
        ones_b = sb("ones_b", [128, 128])
        s.op("vector", lambda e: e.tensor_copy(out=ones_b[:], in_=cf[:, 1, :]), reads=["cf"], writes=["ones_b"])

        small = {}
        for name, d, shape in (("glanw", glanw_d, [128, 1]), ("gdnnw", gdnnw_d, [128, 1]), ("convw", convw_d, [128, 48]),
                               ("alog", alog_d, [128, 4]), ("dtb", dtb_d, [128, 4]), ("cT", cT_d, [128, 8, 2])):
            t = sb(name, shape, F32)
            small[name] = t
            s.dma("sync", lambda e, t=t, d=d: e.dma_start(out=t[:], in_=d), writes=[name], group="c1")
        lnw_b = sb("lnw_b", [128, D], F32)
        lnb_b = sb("lnb_b", [128, D], F32)
        s.dma("sync", lambda e: e.dma_start(out=lnw_b[:], in_=lnw_d.partition_broadcast(128)), writes=["lnw_b"], group="c1")
        s.dma("sync", lambda e: e.dma_start(out=lnb_b[:], in_=lnb_d.partition_broadcast(128)), writes=["lnb_b"], group="c1")
        glanw_s = sb("glanw_s", [128, 1], F32)
        s.op("vector", lambda e: e.tensor_scalar(out=glanw_s[:], in0=small["glanw"][:], scalar1=float(np.sqrt(128.0)), scalar2=None, op0=ALU.mult), reads=["glanw"], writes=["glanw_s"])
        wgb = sb("wgb", [17, 256])
        s.dma("sync", lambda e: e.dma_start(out=RN[0:17, 0:2, :].rearrange("p a b -> p (a b)"), in_=wg_d), writes=["RN"], group="RN")
        s.op("vector", lambda e: e.tensor_copy(out=wgb[:], in_=RN[0:17, 0:2, :].rearrange("p a b -> p (a b)")), reads=["RN"], writes=["wgb"])
        nega = sb("nega", [128, 4], F32)
        s.op("scalar", lambda e: e.activation(out=nega[:], in_=small["alog"][:], func=AF.Exp), reads=["alog"], writes=["nega"])
        s.op("vector", lambda e: e.tensor_scalar(out=nega[:], in0=nega[:], scalar1=-1.0, scalar2=None, op0=ALU.mult), reads=["nega"], writes=["nega"])
        diag = sb("diag", [128, 48, 128])
        for t in range(48):
            if t % 2:
                s.op("vector", lambda e, t=t: e.tensor_scalar(out=diag[:, t, :], in0=ident_f, scalar1=small["convw"][:, t:t + 1], scalar2=None, op0=ALU.mult),
                     reads=["cf", "convw"], writes=["diag"])
            else:
                s.op("scalar", lambda e, t=t: e.activation(out=diag[:, t, :], in_=ident_f, func=AF.Copy, scale=small["convw"][:, t:t + 1]),
                     reads=["cf", "convw"], writes=["diag"])

        xt = [sb("xt_%d" % i, [128, D], F32) for i in range(3)]
        stgs = [(R[:, :], "R"), (RN[:].rearrange("p a b -> p (a b)"), "RN")] + [(xt[i][:, :], "xt%d" % i) for i in range(3)]
        NST = len(stgs)
        Wb = sb("Wb", [128, 8, NCOL])
        Woutb = sb("Woutb", [128, 8, D])
        Sg = [sb("Sg%d" % i, [128, 4, 128], F32) for i in range(2)]
        modp = sb("modp", [128, 32], F32)
        g1b = [sb("g1b%d" % i, [128, D], F32) for i in range(2)]
        mch = Sg[0][0:2, :, :].rearrange("p h c -> p (h c)")
        bch = Sg[1][0:2, :, :].rearrange("p h c -> p (h c)")
        modbanks = [(banks[i], "ps%d" % i) for i in range(6)]
        mpb, mpkey = banks[6], "ps6"
        nst = 0
        for kt in range(8):
            for ch in range(3):
                st, key = stgs[nst % NST]
                nst += 1
                s.dma("sync", lambda e, st=st, kt=kt, ch=ch: e.dma_start(out=st[:, 0:1024], in_=wada_d[kt * 128:(kt + 1) * 128, ch * 1024:(ch + 1) * 1024]), writes=[key], group=key)
                for cgl in range(2):
                    bk, bkey = modbanks[ch * 2 + cgl]
                    s.op("tensor", lambda e, bk=bk, st=st, kt=kt, cgl=cgl: e.matmul(out=bk[0:2, 0:512], lhsT=small["cT"][:, kt, :], rhs=st[:, cgl * 512:(cgl + 1) * 512],
                                                                                 start=(kt == 0), stop=(kt == 7)), reads=[key, "cT"], writes=[bkey])
        for cg in range(6):
            bk, bkey = modbanks[cg]
            s.dma("sync", lambda e, cg=cg: e.dma_start(out=bch, in_=bada_d[:, cg * 512:(cg + 1) * 512].partition_broadcast(2)), writes=["Sg1"], group="bch")
            s.op("vector", lambda e, bk=bk: e.tensor_tensor(out=mch, in0=bk[0:2, 0:512], in1=bch, op=ALU.add), reads=[bkey, "Sg1"], writes=["Sg0"])
            if cg >= 2:
                s.op("vector", lambda e: e.tensor_scalar(out=mch, in0=mch, scalar1=1.0, scalar2=None, op0=ALU.add), reads=["Sg0"], writes=["Sg0"])
            if cg < 4:
                for j in range(4):
                    jn = cg * 4 + j
                    s.op("tensor", lambda e, jn=jn, j=j: e.matmul(out=mpb[:, jn * 2:(jn + 1) * 2], lhsT=mch[:, j * 128:(j + 1) * 128], rhs=cf[0:2, 0, 0:2], start=True, stop=True),
                         reads=["Sg0", "cf"], writes=[mpkey])
            else:
                half = cg - 4
                for sq in range(2):
                    b2, b2key = modbanks[sq]
                    s.op("tensor", lambda e, b2=b2, sq=sq: e.matmul(out=b2[:, 0:512], lhsT=sel[:, sq, :], rhs=mch, start=True, stop=True), reads=["Sg0", "UU"], writes=[b2key])
                    s.op("vector", lambda e, b2=b2, sq=sq, half=half: e.tensor_copy(out=g1b[sq][:, half * 512:(half + 1) * 512], in_=b2[:, 0:512]), reads=[b2key], writes=["g1b%d" % sq])
        s.op("vector", lambda e: e.tensor_copy(out=modp[:], in_=mpb[:, 0:32]), reads=[mpkey], writes=["modp"])
        for kt in range(8):
            for ch in range(4):
                c0 = ch * 1024
                w = min(1024, NCOL - c0)
                st, key = stgs[nst % NST]
                nst += 1
                s.dma("sync", lambda e, st=st, kt=kt, c0=c0, w=w: e.dma_start(out=st[:, 0:w], in_=win_d[kt * 128:(kt + 1) * 128, c0:c0 + w]), writes=[key], group=key)
                if nst % 2:
                    s.op("vector", lambda e, st=st, kt=kt, c0=c0, w=w: e.tensor_copy(out=Wb[:, kt, c0:c0 + w], in_=st[:, 0:w]), reads=[key], writes=["Wb"])
                else:
                    s.op("scalar", lambda e, st=st, kt=kt, c0=c0, w=w: e.activation(out=Wb[:, kt, c0:c0 + w], in_=st[:, 0:w], func=AF.Copy), reads=[key], writes=["Wb"])
        for kt in range(8):
            st, key = stgs[nst % NST]
            nst += 1
            s.dma("sync", lambda e, st=st, kt=kt: e.dma_start(out=st[:, 0:D], in_=wout_d[kt * 128:(kt + 1) * 128, :]), writes=[key], group=key)
            if nst % 2:
                s.op("vector", lambda e, st=st, kt=kt: e.tensor_copy(out=Woutb[:, kt, :], in_=st[:, 0:D]), reads=[key], writes=["Woutb"])
            else:
                s.op("scalar", lambda e, st=st, kt=kt: e.activation(out=Woutb[:, kt, :], in_=st[:, 0:D], func=AF.Copy), reads=[key], writes=["Woutb"])

        def two(name, shape, dt=BF16):
            return [sb("%s_%d" % (name, i), shape, dt) for i in range(2)]
        SG = [sb("SG_%d" % i, [128, 8, 128]) for i in range(3)]
        OA1 = sb("OA1", [128, 4, 128])
        YTA = two("YTA", [128, 4, 128])
        Am = two("Am", [128, 4, 128]); AT = two("AT", [128, 4, 128])
        KW = two("KW", [128, 4, 128]); KD = two("KD", [128, 4, 128]); VB = two("VB", [128, 4, 128])
        QKT = two("QKT", [128, 4, 128]); QDEC = two("QDEC", [128, 4, 128])
        EX = two("EX", [128, 12], F32)
        hT = sb("hT", [128, 8, 128])
        FM = two("FM", [128, 4, 128])
        U = [sb("U%d" % i, [128, 12, 131]) for i in range(2)]
        lraug = two("lraug", [32, 128])
        KVt = two("KVt", [128, 768])
        spt = sb("spt", [128, 256], F32)
        E1 = sb("E1", [128, 2, 128]); E2 = sb("E2", [128, 2, 128])
        ER = sb("ER", [128, 2, 2], F32)
        QT = sb("QT", [128, 2, 128]); KT = sb("KT", [128, 2, 128])
        EK = sb("EK", [128, 256]); KP = sb("KP", [128, 256])
        ATT = sb("ATT", [128, 4, 128])
        SPb = sb("SPb", [128, 4, 128])
        KTz = sb("KTz", [128, 4, 128])
        Sgla = [sb("Sgla%d" % i, [128, 2, 128], F32) for i in range(2)]
        QKV = sb("QKV", [128, 12, 128])
        SQ = sb("SQ", [128, 8, 128])
        RNb = SQ
        QKN = sb("QKN", [128, 8, 128])
        T4 = sb("T4", [128, 4], F32); G4 = sb("G4", [128, 4], F32); AB = two("AB", [128, 8], F32)
        TB = sb("TB", [128, 4], F32); BETA = sb("BETA", [128, 4], F32); LNT = sb("LNT", [128, 4], F32)
        D4 = sb("D4", [128, 4], F32); ND4 = sb("ND4", [128, 4], F32); D4B = sb("D4B", [128, 4], F32); KWs = sb("KWs", [128, 4], F32)
        DG = RN[:, 4:8, :]; DG2 = RN[:, 0:4, :]
        ED = sb("ED", [128, 4, 128]); E1g = sb("E1g", [128, 4, 128]); E2g = sb("E2g", [128, 4, 128])
        INV = sb("INV", [128, 8, 4, 128])
        Pp = [INV[:, 0], INV[:, 1]]; PTp = [INV[:, 2], INV[:, 3]]; Yp = [INV[:, 4], INV[:, 5]]; Zp = [INV[:, 6], INV[:, 7]]
        X1 = sb("X1", [128, 4, 128]); X2 = sb("X2", [128, 4, 128])
        WT = X2; VN = X1; OB = sb("OB", [128, 4, 128])
        Sgb = [sb("Sgb%d" % i, [128, 4, 128]) for i in range(2)]
        v8 = lambda a, b: INV[:, a:b].rearrange("p a h c -> p (a h) c")
        SQO, SQOk = v8(0, 2), ["Pp0", "Pp1"]
        TO, TOk = v8(2, 4), ["PTp0", "PTp1"]
        YT, YTk = v8(4, 6), ["Yp0", "Yp1"]
        RNOb, RNObk = v8(6, 8), ["Zp0", "Zp1"]
        RNO = R[:].rearrange("p (a b) -> p a b", a=8)
        JUNK = X1
        ST = sb("ST", [128, 8], F32)
        DBG = None
        if taps:
            try:
                DBG = sb("DBG", [128, 512], F32)
            except AssertionError:
                DBG = None

        for i in range(2):
            s.op("gpsimd", lambda e, i=i: e.memset(Sgla[i][:], 0.0), writes=["Sgla%d" % i])
            s.op("gpsimd", lambda e, i=i: e.memset(Sg[i][:], 0.0), reads=["modp", "g1b0", "g1b1"], writes=["Sg%d" % i])
            s.op("gpsimd", lambda e, i=i: e.memset(Sgb[i][:], 0.0), writes=["Sgb%d" % i])
            s.op("gpsimd", lambda e, i=i: e.memset(U[i][:], 0.0), writes=["U%d" % i])
        for i in range(2):
            s.op("gpsimd", lambda e, i=i: e.memset(lraug[i][:], 1.0), writes=["lraug%d" % i])

        f3 = lambda ap: ap.rearrange("p (h c) -> p h c", h=4)
        bc = lambda ap: ap.unsqueeze(2).to_broadcast([128, 4, 128])

        def tt(eng, out, a, b, op, reads, writes):
            s.op(eng, lambda e: e.tensor_tensor(out=out, in0=a, in1=b, op=op), reads=reads, writes=writes)

        def act(out, in_, func, reads, writes, **kw):
            s.op("scalar", lambda e: e.activation(out=out, in_=in_, func=func, **kw), reads=reads, writes=writes)

        units = [(sq, m) for m in range(n_units) for sq in range(2)]
        u_of = {um: i for i, um in enumerate(units)}
        dbg_d = {}

        def dbg(sq, m, name, ap, keys):
            if not taps or DBG is None:
                return
            shp = list(ap.shape)
            P = shp[0]
            w = int(np.prod(shp[1:]))
            if name not in dbg_d:
                dbg_d[name] = nc.dram_tensor("dbg_" + name, [2, n_units, 128, 512], F32, kind="ExternalOutput").ap()
            dst = DBG[0:P, 0:w]
            if len(shp) == 3:
                dst = dst.rearrange("p (h c) -> p h c", h=shp[1])
            s.op("vector", lambda e: e.tensor_copy(out=dst, in_=ap), reads=list(keys), writes=["DBG"])
            s.dma("sync", lambda e: e.dma_start(out=dbg_d[name][sq, m, 0:P, 0:w], in_=DBG[0:P, 0:w]), reads=["DBG"], group="tap")

        def load_x(u):
            sq_, m_ = units[u]
            p3_ = u % 3
            s.dma("sync", lambda e: e.dma_start(out=xt[p3_][:], in_=x_d[sq_, m_ * 128:(m_ + 1) * 128, :]), writes=["xt%d" % p3_], group="xt%d" % p3_)

        def pstage(sq, m, u):
            p, p3 = u % 2, u % 3
            xs, xk = xt[p3], "xt%d" % p3
            pk = lambda n: "%s%d" % (n, p)
            Uk = "U%d" % sq
            for _ in range(PDELAY):
                yield
            for g in range(2):
                bk, bkey = nb_p()
                for c in range(4):
                    kt = g * 4 + c
                    s.op("tensor", lambda e, bk=bk, c=c, kt=kt: e.transpose(out=bk[:, c * 128:(c + 1) * 128], in_=xs[:, kt * 128:(kt + 1) * 128], identity=ident_f),
                         reads=[xk, "cf"], writes=[bkey])
                for c in range(4):
                    kt = g * 4 + c
                    s.op("vector", lambda e, bk=bk, c=c, kt=kt: e.tensor_scalar(out=hT[:, kt, :], in0=bk[:, c * 128:(c + 1) * 128],
                                                                             scalar1=modp[:, 16 + kt * 2 + sq:17 + kt * 2 + sq], scalar2=modp[:, kt * 2 + sq:kt * 2 + sq + 1],
                                                                             op0=ALU.mult, op1=ALU.add), reads=[bkey, "modp"], writes=["hT"])
                yield
            fm_cols = [0, 128, 256, 384] + [1040 + 128 * i for i in range(4)] + [3096 + 128 * i for i in range(4)] + [1552 + 128 * i for i in range(12)]
            for g in range(6):
                bk, bkey = nb_p()
                for c in range(4):
                    c0 = fm_cols[g * 4 + c]
                    for kt in range(8):
                        s.op("tensor", lambda e, bk=bk, c=c, c0=c0, kt=kt: e.matmul(out=bk[:, c * 128:(c + 1) * 128], lhsT=Wb[:, kt, c0:c0 + 128], rhs=hT[:, kt, :],
                                                                                  start=(kt == 0), stop=(kt == 7)), reads=["Wb", "hT"], writes=[bkey])
                    if (c + 1) % PYIELD == 0:
                        yield
                if g == 0:
                    act(FM[p][:], f3(bk[:, :]), AF.Copy, [bkey], [pk("FM")])
                elif g in (1, 2):
                    act(SG[p3][:, (g - 1) * 4:g * 4, :], f3(bk[:, :]), AF.Silu, [bkey], ["SG%d" % p3])
                else:
                    s.op("vector", lambda e, bk=bk, g=g: e.tensor_copy(out=U[sq][:, (g - 3) * 4:(g - 2) * 4, 3:131], in_=f3(bk[:, :])), reads=[bkey], writes=[Uk])
            abk, abkey = nb_p()
            for kt in range(8):
                s.op("tensor", lambda e, kt=kt: e.matmul(out=abk[0:16, 0:128], lhsT=Wb[:, kt, 1024:1040], rhs=hT[:, kt, :], start=(kt == 0), stop=(kt == 7)),
                     reads=["Wb", "hT"], writes=[abkey])
            for kt in range(8):
                s.op("tensor", lambda e, kt=kt: e.matmul(out=abk[:, 128:136], lhsT=hT[:, kt, :], rhs=Wb[:, kt, 3088:3096], start=(kt == 0), stop=(kt == 7)),
                     reads=["Wb", "hT"], writes=[abkey])
            s.op("vector", lambda e: e.tensor_copy(out=lraug[p][0:16, :], in_=abk[0:16, 0:128]), reads=[abkey], writes=[pk("lraug")])
            s.op("vector", lambda e: e.tensor_copy(out=AB[p][:], in_=abk[:, 128:136]), reads=[abkey], writes=[pk("AB")])
            yield
            for part, (c0, w) in enumerate(((256, 512), (768, 256))):
                bk, bkey = nb_p()
                for kt in range(8):
                    s.op("tensor", lambda e, bk=bk, kt=kt, c0=c0, w=w: e.matmul(out=bk[:, 0:w], lhsT=hT[:, kt, :], rhs=Wb[:, kt, c0:c0 + w], start=(kt == 0), stop=(kt == 7)),
                         reads=["Wb", "hT"], writes=[bkey])
                s.op("vector", lambda e, bk=bk, c0=c0, w=w: e.tensor_copy(out=KVt[p][:, c0 - 256:c0 - 256 + w], in_=bk[:, 0:w]), reads=[bkey], writes=[pk("KVt")])
                yield

        def mid(sq, m, u):
            yield from interleave([gla(sq, m, u % 2), gdnpre(sq, m, u % 2)])

        def gla(sq, m, p):
            pk = lambda n: "%s%d" % (n, p)
            Slk = "Sgla%d" % sq
            bk, bkey = nb_gla()
            s.op("tensor", lambda e, bk=bk: e.matmul(out=bk[:, 0:256], lhsT=lraug[p][0:17, :], rhs=wgb[0:17, :], start=True, stop=True), reads=[pk("lraug"), "wgb"], writes=[bkey])
            act(spt[:], bk[:, 0:256], AF.Exp, [bkey], ["spt"], scale=-1.0)
            act(spt[:], spt[:], AF.Ln, ["spt"], ["spt"], bias=1.0)
            yield
            pbk, pbkey = nb_gla()
            for hp in range(2):
                s.op("tensor", lambda e, hp=hp: e.matmul(out=pbk[:, hp * 130:(hp + 1) * 130], lhsT=spt[:, hp * 128:(hp + 1) * 128], rhs=uref[:, :], start=True, stop=True),
                     reads=["spt", "uref"], writes=[pbkey])
            pv = pbk[:, 0:260].rearrange("p (h c) -> p h c", h=2)
            act(E1[:], pv[:, :, 0:128], AF.Exp, [pbkey], ["E1"], bias=float(np.log(0.125)))
            act(E2[:], pv[:, :, 0:128], AF.Exp, [pbkey], ["E2"], scale=-1.0)
            act(ER[:], pv[:, :, 128:130], AF.Exp, [pbkey], ["ER"])
            yield
            tt("vector", QT[:], FM[p][:, 0:2, :], E1[:], ALU.mult, [pk("FM"), "E1"], ["QT"])
            tt("vector", KT[:], FM[p][:, 2:4, :], E2[:], ALU.mult, [pk("FM"), "E2"], ["KT"])
            bk, bkey = nb_gla()
            s.op("tensor", lambda e, bk=bk: e.matmul(out=bk[:, 0:256], lhsT=mgt16, rhs=spt[:, :], start=True, stop=True), reads=["spt", "cf"], writes=[bkey])
            act(EK[:], bk[:, 0:256], AF.Exp, [bkey], ["EK"])
            tt("gpsimd", KP[:], KVt[p][:, 0:256], EK[:], ALU.mult, [pk("KVt"), "EK"], ["KP"])
            yield
            for h in range(4):
                hp, par = h // 2, h % 2
                s.op("vector", lambda e, h=h, hp=hp, par=par: e.tensor_scalar(out=KTz[:, h, :], in0=KT[:, hp, :], scalar1=cf[:, 2, par:par + 1], scalar2=None, op0=ALU.mult),
                     reads=["KT", "cf"], writes=["KTz"])
            yield
            bk, bkey = nb_gla()
            for h in range(4):
                hp, par = h // 2, h % 2
                s.op("tensor", lambda e, bk=bk, h=h, hp=hp, par=par: e.matmul(out=bk[:, h * 128:(h + 1) * 128], lhsT=KTz[:, h, :], rhs=QT[:, hp, :],
                                                                           start=True, stop=True), reads=["KTz", "QT"], writes=[bkey])
            tt("vector", ATT[:], f3(bk[:, :]), masku, ALU.mult, [bkey, "bmr"], ["ATT"])
            for h in range(4):
                hp, par = h // 2, h % 2
                s.op("vector", lambda e, h=h, hp=hp, par=par: e.tensor_scalar(out=SPb[:, h, :], in0=Sgla[sq][:, hp, :], scalar1=ER[:, hp, 0:1], scalar2=cf[:, 2, par:par + 1],
                                                                           op0=ALU.mult, op1=ALU.mult), reads=[Slk, "ER", "cf"], writes=["SPb"])
            yield
            oak, oakey = nb_gla()
            for h in range(4):
                hp, par = h // 2, h % 2
                s.op("tensor", lambda e, h=h, hp=hp, par=par: e.matmul(out=oak[:, h * 128:(h + 1) * 128], lhsT=SPb[:, h, :], rhs=QT[:, hp, :],
                                                                      start=True, stop=False), reads=["SPb", "QT"], writes=[oakey])
                s.op("tensor", lambda e, h=h: e.matmul(out=oak[:, h * 128:(h + 1) * 128], lhsT=KVt[p][:, 256 + h * 128:384 + h * 128], rhs=ATT[:, h, :], start=False, stop=True),
                     reads=[pk("KVt"), "ATT"], writes=[oakey])
            act(OA1[:], f3(oak[:, :]), AF.Copy, [oakey], ["OA1"])
            yield
            bk, bkey = nb_gla()
            for h in range(4):
                hp, par = h // 2, h % 2
                s.op("tensor", lambda e, bk=bk, h=h, hp=hp, par=par: e.matmul(out=bk[:, h * 128:(h + 1) * 128], lhsT=KP[:, hp * 128:(hp + 1) * 128],
                                                                           rhs=KVt[p][:, 256 + h * 128:384 + h * 128], start=True, stop=True), reads=["KP", pk("KVt")], writes=[bkey])
            for h in range(4):
                hp, par = h // 2, h % 2
                ps_ = slice(par * 64, (par + 1) * 64)
                s.op("vector", lambda e, bk=bk, h=h, hp=hp, ps_=ps_: e.scalar_tensor_tensor(out=Sgla[sq][ps_, hp, :], in0=Sgla[sq][ps_, hp, :], scalar=ER[ps_, hp, 1:2], in1=bk[ps_, h * 128:(h + 1) * 128],
                                                                                     op0=ALU.mult, op1=ALU.add), reads=[Slk, "ER", bkey], writes=[Slk])
            yield
            p3g = (u_of[(sq, m)]) % 3
            tt("gpsimd", ATT[:], OA1[:], OA1[:], ALU.mult, ["OA1"], ["ATT"])
            yield
            bk, bkey = nb_gla()
            for c in range(4):
                s.op("tensor", lambda e, bk=bk, c=c: e.matmul(out=bk[:, c * 128:(c + 1) * 128], lhsT=ones_b[:, :], rhs=ATT[:, c, :], start=True, stop=True),
                     reads=["ones_b", "ATT"], writes=[bkey])
            for hf in range(2):
                act(spt[:], bk[:, hf * 256:(hf + 1) * 256], AF.Ln, [bkey, "spt"], ["spt"], bias=128e-6)
                act(SPb[:, 2 * hf:2 * hf + 2, :], spt[:].rearrange("p (h c) -> p h c", h=2), AF.Exp, ["spt"], ["SPb"], scale=-0.5)
            yield
            tt("gpsimd", ATT[:], OA1[:], SPb[:], ALU.mult, ["OA1", "SPb"], ["ATT"])
            yield
            s.op("vector", lambda e: e.scalar_tensor_tensor(out=YTA[p][:], in0=ATT[:], scalar=glanw_s[:, 0:1], in1=SG[p3g][:, 0:4, :],
                                                            op0=ALU.mult, op1=ALU.mult), reads=["ATT", "SG%d" % p3g, "glanw_s"], writes=[pk("YTA")])
            yield

        def gdnpre(sq, m, p):
            pk = lambda n: "%s%d" % (n, p)
            Uk = "U%d" % sq
            for g3 in range(3):
                bk, bkey = nb_gdn()
                for c in range(4):
                    t = g3 * 4 + c
                    for k in range(4):
                        s.op("tensor", lambda e, bk=bk, c=c, t=t, k=k: e.matmul(out=bk[:, c * 128:(c + 1) * 128], lhsT=diag[:, t * 4 + k, :], rhs=U[sq][:, t, k:k + 128],
                                                                              start=(k == 0), stop=(k == 3)), reads=["diag", Uk], writes=[bkey])
                act(QKV[:, g3 * 4:(g3 + 1) * 4, :], f3(bk[:, :]), AF.Silu, [bkey], ["QKV"])
                yield
            s.op("gpsimd", lambda e: e.tensor_copy(out=U[sq][:, :, 0:3], in_=U[sq][:, :, 128:131]), reads=[Uk], writes=[Uk])
            tt("gpsimd", SQ[:], QKV[:, 0:8, :], QKV[:, 0:8, :], ALU.mult, ["QKV"], ["SQ"])
            tt("vector", T4[:], AB[p][:, 0:4], small["dtb"][:], ALU.add, [pk("AB"), "dtb"], ["T4"])
            act(T4[:], T4[:], AF.Exp, ["T4"], ["T4"])
            act(T4[:], T4[:], AF.Ln, ["T4"], ["T4"], bias=1.0)
            tt("vector", G4[:], T4[:], nega[:], ALU.mult, ["T4", "nega"], ["G4"])
            act(TB[:], AB[p][:, 4:8], AF.Exp, [pk("AB")], ["TB"], scale=-1.0)
            s.op("vector", lambda e: e.tensor_scalar(out=TB[:], in0=TB[:], scalar1=1.0, scalar2=None, op0=ALU.add), reads=["TB"], writes=["TB"])
            s.op("vector", lambda e: e.reciprocal(out=BETA[:], in_=TB[:]), reads=["TB"], writes=["BETA"])
            act(LNT[:], TB[:], AF.Ln, ["TB"], ["LNT"])
            yield
            for g in range(2):
                bk, bkey = nb_gdn()
                for c in range(4):
                    s.op("tensor", lambda e, bk=bk, c=c, g=g: e.matmul(out=bk[:, c * 128:(c + 1) * 128], lhsT=ones_b[:, :], rhs=SQ[:, g * 4 + c, :], start=True, stop=True),
                         reads=["ones_b", "SQ"], writes=[bkey])
                act(RN[:, g * 4:(g + 1) * 4, :], f3(bk[:, :]), AF.Ln, [bkey], ["RN"], bias=1e-6)
                act(RNb[:, g * 4:(g + 1) * 4, :], RN[:, g * 4:(g + 1) * 4, :], AF.Exp, ["RN"], ["SQ"], scale=-0.5, bias=(float(np.log(128.0 ** -0.5)) if g == 0 else 0.0))
                yield
            tt("vector", QKN[:], QKV[:, 0:8, :], RNb[:], ALU.mult, ["QKV", "SQ"], ["QKN"])
            dbk, dbkey = nb_gdn()
            for n, lm in enumerate((uincl, mgt, ones_f)):
                s.op("tensor", lambda e, n=n, lm=lm: e.matmul(out=dbk[:, n * 4:(n + 1) * 4], lhsT=lm, rhs=G4[:, :], start=True, stop=True), reads=["cf", "G4"], writes=[dbkey])
            act(EX[p][:], dbk[:, 0:12], AF.Exp, [dbkey], [pk("EX")])
            s.op("vector", lambda e: e.tensor_copy(out=D4[:], in_=dbk[:, 0:4]), reads=[dbkey], writes=["D4"])
            s.op("vector", lambda e: e.tensor_scalar(out=ND4[:], in0=dbk[:, 0:4], scalar1=-1.0, scalar2=None, op0=ALU.mult), reads=[dbkey], writes=["ND4"])
            tt("vector", D4B[:], D4[:], LNT[:], ALU.subtract, ["D4", "LNT"], ["D4B"])
            tt("vector", KWs[:], BETA[:], EX[p][:, 0:4], ALU.mult, ["BETA", pk("EX")], ["KWs"])
            yield
            for t in range(4):
                s.op("tensor", lambda e, t=t: e.transpose(out=psb[:, t * 128:(t + 1) * 128], in_=QKN[:, 4 + t, :], identity=ident_b), reads=["QKN", "bmr"], writes=["psb"])
            for t in range(4):
                s.op("tensor", lambda e, t=t: e.transpose(out=psb[:, 512 + t * 128:512 + (t + 1) * 128], in_=QKV[:, 8 + t, :], identity=ident_b), reads=["QKV", "bmr"], writes=["psb"])
            tt("vector", KW[p][:], f3(psb[:, 0:512]), bc(KWs[:]), ALU.mult, ["psb", "KWs"], [pk("KW")])
            tt("vector", KD[p][:], f3(psb[:, 0:512]), bc(EX[p][:, 4:8]), ALU.mult, ["psb", pk("EX")], [pk("KD")])
            tt("vector", VB[p][:], f3(psb[:, 512:1024]), bc(BETA[:]), ALU.mult, ["psb", "BETA"], [pk("VB")])
            yield
            tt("gpsimd", DG, identr_b, bc(D4[:]), ALU.mult, ["bmr", "D4"], ["RN"])
            tt("gpsimd", DG2, identr_b, bc(D4B[:]), ALU.mult, ["bmr", "D4B"], ["RN"])
            dgf = DG.rearrange("p h c -> p (h c)")
            dg2f = DG2.rearrange("p h c -> p (h c)")
            bk, bkey = nb_gdn()
            s.op("tensor", lambda e, bk=bk: e.matmul(out=bk[:, :], lhsT=ones_f, rhs=dgf, start=True, stop=True), reads=["cf", "RN"], writes=[bkey])
            act(ED[:], f3(bk[:, :]), AF.Exp, [bkey], ["ED"])
            yield
            for (src, srck, msk, dst, dstk) in ((dgf, "RN", mt_incl, E1g, "E1g"), (dg2f, "RN", mt_strict, E2g, "E2g")):
                bk, bkey = nb_gdn()
                for h in range(4):
                    s.op("tensor", lambda e, bk=bk, src=src, h=h: e.matmul(out=bk[:, h * 128:(h + 1) * 128], lhsT=ones_f, rhs=src[:, h * 128:(h + 1) * 128], start=True, stop=False),
                         reads=["cf", srck], writes=[bkey])
                    s.op("tensor", lambda e, bk=bk, h=h, msk=msk: e.matmul(out=bk[:, h * 128:(h + 1) * 128], lhsT=ident_f, rhs=msk, start=False, stop=False), reads=["cf"], writes=[bkey])
                    s.op("tensor", lambda e, bk=bk, h=h: e.matmul(out=bk[:, h * 128:(h + 1) * 128], lhsT=DG[:, h, :], rhs=cf[:, 3, :], start=False, stop=True), reads=["cf", "RN"], writes=[bkey])
                act(dst[:], f3(bk[:, :]), AF.Exp, [bkey], [dstk])
                yield
            tt("gpsimd", QDEC[p][:], QKN[:, 0:4, :], ED[:], ALU.mult, ["QKN", "ED"], [pk("QDEC")])
            gk, gkey = nb_gdn()
            for h in range(4):
                s.op("tensor", lambda e, h=h: e.matmul(out=gk[:, h * 128:(h + 1) * 128], lhsT=QKN[:, 4 + h, :], rhs=QKN[:, 4 + h, :], start=True, stop=True), reads=["QKN"], writes=[gkey])
            tt("vector", AT[p][:], f3(gk[:, :]), E2g[:], ALU.mult, [gkey, "E2g"], [pk("AT")])
            qk, qkey = nb_gdn()
            for h in range(4):
                s.op("tensor", lambda e, h=h: e.matmul(out=qk[:, h * 128:(h + 1) * 128], lhsT=QKN[:, 4 + h, :], rhs=QKN[:, h, :], start=True, stop=True), reads=["QKN"], writes=[qkey])
            tt("vector", QKT[p][:], f3(qk[:, :]), E1g[:], ALU.mult, [qkey, "E1g"], [pk("QKT")])
            yield
            for h in range(4):
                s.op("tensor", lambda e, h=h: e.transpose(out=psb[:, h * 128:(h + 1) * 128], in_=AT[p][:, h, :], identity=ident_b), reads=[pk("AT"), "bmr"], writes=["psb"])
            act(Am[p][:], f3(psb[:, 0:512]), AF.Copy, ["psb"], [pk("Am")])
            yield

        def back(sq, m, u):
            p, p3 = u % 2, u % 3
            xs, xk = xt[p3], "xt%d" % p3
            pk = lambda n: "%s%d" % (n, p)
            Sgk, Sgbk = "Sg%d" % sq, "Sgb%d" % sq
            A_, Ak, AT_, ATk = Am[p], pk("Am"), AT[p], pk("AT")

            def mm4(L, Lk, Rt, Rk):
                bk, bkey = nb_back()
                for h in range(4):
                    s.op("tensor", lambda e, bk=bk, h=h: e.matmul(out=bk[:, h * 128:(h + 1) * 128], lhsT=L[:, h, :], rhs=Rt[:, h, :], start=True, stop=True),
                         reads=[Lk, Rk], writes=[bkey])
                return bk, bkey
            def invchain():
                tt("vector", Pp[0], A_[:], bd16, ALU.mult, [Ak, "bmr"], ["Pp0"])
                tt("gpsimd", PTp[0], AT_[:], bd16, ALU.mult, [ATk, "bmr"], ["PTp0"])
                tt("vector", Yp[0], identr_b, Pp[0], ALU.subtract, ["bmr", "Pp0"], ["Yp0"])
                tt("gpsimd", Zp[0], identr_b, PTp[0], ALU.subtract, ["bmr", "PTp0"], ["Zp0"])
                yield
                act(R[:], xs[:], AF.Copy, [xk], ["R"], scale=ALPHA)
                if u + 3 < len(units):
                    load_x(u + 3)
                cur = 0
                yz = 0
                for lvl in range(3):
                    nx = 1 - cur
                    b1, k1 = mm4(PTp[cur], "PTp%d" % cur, Pp[cur], "Pp%d" % cur)
                    b2, k2 = mm4(Pp[cur], "Pp%d" % cur, PTp[cur], "PTp%d" % cur)
                    act(Pp[nx], f3(b1[:, :]), AF.Copy, [k1], ["Pp%d" % nx])
                    s.op("vector", lambda e, b2=b2, nx=nx: e.tensor_copy(out=PTp[nx], in_=f3(b2[:, :])), reads=[k2], writes=["PTp%d" % nx])
                    yield
                    b3, k3 = mm4(PTp[nx], "PTp%d" % nx, Yp[yz], "Yp%d" % yz)
                    b4, k4 = mm4(Pp[nx], "Pp%d" % nx, Zp[yz], "Zp%d" % yz)
                    tt("vector", Yp[1 - yz], f3(b3[:, :]), Yp[yz], ALU.add, [k3, "Yp%d" % yz], ["Yp%d" % (1 - yz)])
                    tt("vector", Zp[1 - yz], f3(b4[:, :]), Zp[yz], ALU.add, [k4, "Zp%d" % yz], ["Zp%d" % (1 - yz)])
                    yield
                    cur = nx
                    yz = 1 - yz
                for (mi, both) in ((3, True), (5, True), (7, False)):
                    Yc, Zc, Yk, Zk = Yp[yz], Zp[yz], "Yp%d" % yz, "Zp%d" % yz
                    Yn, Zn, Ynk, Znk = Yp[1 - yz], Zp[1 - yz], "Yp%d" % (1 - yz), "Zp%d" % (1 - yz)
                    mT = bcm(mi + 1) if both else bcm(mi)
                    b1, k1 = mm4(A_, Ak, Zc, Zk)
                    if both:
                        b3, k3 = mm4(AT_, ATk, Yc, Yk)
                    tt("vector", X1[:], f3(b1[:, :]), mT, ALU.mult, [k1, "bmr"], ["X1"])
                    if both:
                        tt("vector", X2[:], f3(b3[:, :]), bcm(mi), ALU.mult, [k3, "bmr"], ["X2"])
                    yield
                    b2, k2 = mm4(Yc, Yk, X1, "X1")
                    if both:
                        b4, k4 = mm4(Zc, Zk, X2, "X2")
                    tt("vector", Zn, Zc, f3(b2[:, :]), ALU.subtract, [Zk, k2], [Znk])
                    if both:
                        tt("vector", Yn, Yc, f3(b4[:, :]), ALU.subtract, [Yk, k4], [Ynk])
                    yield
                    yz = 1 - yz
                TT, TTk = Zp[yz], "Zp%d" % yz
                dbg(sq, m, "TT", TT, [TTk])
                b1, k1 = mm4(TT, TTk, VB[p], pk("VB"))
                b2, k2 = mm4(KW[p], pk("KW"), TT, TTk)
                act(UU[:], f3(b1[:, :]), AF.Copy, [k1], ["UU"])
                act(WT[:], f3(b2[:, :]), AF.Copy, [k2], ["X2"])
                yield
                b3, k3 = mm4(WT, "X2", Sgb[sq], Sgbk)
                tt("vector", VN[:], UU[:], f3(b3[:, :]), ALU.subtract, ["UU", k3], ["X1"])
                yield
                obk, obkey = nb_back()
                for h in range(4):
                    s.op("tensor", lambda e, h=h: e.matmul(out=obk[:, h * 128:(h + 1) * 128], lhsT=Sgb[sq][:, h, :], rhs=QDEC[p][:, h, :], start=True, stop=False), reads=[Sgbk, pk("QDEC")], writes=[obkey])
                    s.op("tensor", lambda e, h=h: e.matmul(out=obk[:, h * 128:(h + 1) * 128], lhsT=VN[:, h, :], rhs=QKT[p][:, h, :], start=False, stop=True), reads=["X1", pk("QKT")], writes=[obkey])
                act(OB[:], f3(obk[:, :]), AF.Copy, [obkey], ["OB"])
                b4, k4 = mm4(KD[p], pk("KD"), VN, "X1")
                for h in range(4):
                    s.op("vector", lambda e, h=h: e.scalar_tensor_tensor(out=Sg[sq][:, h, :], in0=Sg[sq][:, h, :], scalar=EX[p][:, 8 + h:9 + h], in1=b4[:, h * 128:(h + 1) * 128],
                                                                         op0=ALU.mult, op1=ALU.add), reads=[Sgk, pk("EX"), k4], writes=[Sgk])
                s.op("gpsimd", lambda e: e.tensor_copy(out=Sgb[sq][:], in_=Sg[sq][:]), reads=[Sgk], writes=[Sgbk])
                dbg(sq, m, "ob", OB[:], ["OB"])
                yield

            yield from invchain()
            for g, (osb, okey) in ((1, (OB, "OB")),):
                sl = slice(g * 4, (g + 1) * 4)
                tt("vector", SQO[:, sl, :], osb[:], osb[:], ALU.mult, [okey], SQOk)
                bk, bkey = nb_back()
                for c in range(4):
                    s.op("tensor", lambda e, bk=bk, c=c, g=g: e.matmul(out=bk[:, c * 128:(c + 1) * 128], lhsT=ones_b[:, :], rhs=SQO[:, g * 4 + c, :], start=True, stop=True),
                         reads=["ones_b"] + SQOk, writes=[bkey])
                act(UU[:], f3(bk[:, :]), AF.Ln, [bkey], ["UU"], bias=128e-6)
                act(RNOb[:, sl, :], UU[:], AF.Exp, ["UU"], RNObk, scale=-0.5, bias=float(0.5 * np.log(128.0)))
                tt("vector", TO[:, sl, :], osb[:], RNOb[:, sl, :], ALU.mult, [okey] + RNObk, TOk)
                nw = small["glanw"] if g == 0 else small["gdnnw"]
                s.op("vector", lambda e, sl=sl, nw=nw: e.scalar_tensor_tensor(out=YT[:, sl, :], in0=TO[:, sl, :], scalar=nw[:, 0:1], in1=SG[p3][:, sl, :],
                                                                           op0=ALU.mult, op1=ALU.mult), reads=TOk + ["SG%d" % p3, "glanw", "gdnnw"], writes=YTk)
                yield
            for half in range(2):
                bk, bkey = nb_back()
                for kt in range(8):
                    ylhs = YTA[p][:, kt, :] if kt < 4 else YT[:, kt, :]
                    s.op("tensor", lambda e, bk=bk, kt=kt, half=half, ylhs=ylhs: e.matmul(out=bk[:, :], lhsT=ylhs, rhs=Woutb[:, kt, half * 512:(half + 1) * 512], start=(kt == 0), stop=(kt == 7)),
                         reads=YTk + [pk("YTA"), "Woutb"], writes=[bkey])
                uuf = UU[:].rearrange("p h c -> p (h c)")
                tt("vector", uuf, bk[:, :], g1b[sq][:, half * 512:(half + 1) * 512], ALU.mult, [bkey, "g1b%d" % sq], ["UU"])
                tt("vector", R[:, half * 512:(half + 1) * 512], R[:, half * 512:(half + 1) * 512], uuf, ALU.add, ["R", "UU"], ["R"])
                yield
            s.op("gpsimd", lambda e: e.memset(ST[:], 0.0), writes=["ST"])
            jv = lambda t: t[:].rearrange("p h c -> p (h c)")
            for hf in range(2):
                act(jv(X1), R[:, hf * 512:(hf + 1) * 512], AF.Copy, ["R", "ST"], ["X1", "ST"], accum_out=ST[:, hf:hf + 1])
                act(jv(X1), R[:, hf * 512:(hf + 1) * 512], AF.Square, ["R", "ST"], ["X1", "ST"], accum_out=ST[:, 2 + hf:3 + hf])
            yield
            tt("vector", ST[:, 4:5], ST[:, 0:1], ST[:, 1:2], ALU.add, ["ST"], ["ST"])
            tt("vector", ST[:, 5:6], ST[:, 2:3], ST[:, 3:4], ALU.add, ["ST"], ["ST"])
            s.op("vector", lambda e: e.tensor_scalar(out=ST[:, 4:5], in0=ST[:, 4:5], scalar1=1.0 / D, scalar2=None, op0=ALU.mult), reads=["ST"], writes=["ST"])
            tt("vector", ST[:, 6:7], ST[:, 4:5], ST[:, 4:5], ALU.mult, ["ST"], ["ST"])
            s.op("vector", lambda e: e.scalar_tensor_tensor(out=ST[:, 7:8], in0=ST[:, 5:6], scalar=1.0 / D, in1=ST[:, 6:7], op0=ALU.mult, op1=ALU.subtract), reads=["ST"], writes=["ST"])
            act(ST[:, 7:8], ST[:, 7:8], AF.Ln, ["ST"], ["ST"], bias=1e-5)
            act(ST[:, 7:8], ST[:, 7:8], AF.Exp, ["ST"], ["ST"], scale=-0.5)
            s.op("vector", lambda e: e.scalar_tensor_tensor(out=ST[:, 6:7], in0=ST[:, 4:5], scalar=-1.0, in1=ST[:, 7:8], op0=ALU.mult, op1=ALU.mult), reads=["ST"], writes=["ST"])
            act(R[:], R[:], AF.Identity, ["R", "ST"], ["R"], scale=ST[:, 7:8], bias=ST[:, 6:7])
            tt("vector", R[:], R[:], lnw_b[:], ALU.mult, ["R", "lnw_b"], ["R"])
            tt("gpsimd", R[:], R[:], lnb_b[:], ALU.add, ["R", "lnb_b"], ["R"])
            s.dma("sync", lambda e: e.dma_start(out=out_d[sq, m * 128:(m + 1) * 128, :], in_=R[:]), reads=["R"], group="out")
            yield

        NUu = len(units)
        for u0 in range(min(3, NUu)):
            load_x(u0)
        for it in range(NUu + 2):
            gd = {}
            if 0 <= it - 2 < NUu:
                gd["b"] = back(units[it - 2][0], units[it - 2][1], it - 2)
            if 0 <= it - 1 < NUu:
                gd["m"] = mid(units[it - 1][0], units[it - 1][1], it - 1)
            if it < NUu:
                gd["p"] = pstage(units[it][0], units[it][1], it)
            run(interleave([gd[k] for k in ORDER if k in gd]))
        fw = ["out"] + (["tap"] if (taps and "tap" in s.dma_groups) else [])
        cnt = s.emit(final_wait_groups=fw)
        print("ops", len(s.ops), "signals", cnt)
    return nc


def make_inputs_for_core(core, inputs, consts, n_units):
    T = n_units * 128
    cfa, urefa, bma, sela = consts
    f = lambda a: np.ascontiguousarray(np.asarray(a, dtype=np.float32))
    xs = f(inputs["x"][2 * core:2 * core + 2, :T])
    c2 = f(inputs["c"][2 * core:2 * core + 2])
    cT = np.ascontiguousarray(c2.reshape(2, 8, 128).transpose(2, 1, 0))
    convw = f(inputs["gdn_conv_w"][0])
    convw_l = np.ascontiguousarray(convw.reshape(4, 12, 128).transpose(2, 1, 0).reshape(128, 48))
    return dict(
        x=xs, cT=cT, w_ada=f(inputs["w_ada"][0]), b_ada=f(inputs["b_ada"]), w_in=f(inputs["w_in"][0]),
        wg_aug=np.ascontiguousarray(np.concatenate([f(inputs["gla_w_gate_up"][0]), f(inputs["gla_b_gate"])], axis=0)),
        gla_nw=f(inputs["gla_norm_w"][0]).reshape(128, 1), gdn_nw=f(inputs["gdn_norm_w"][0]).reshape(128, 1),
        convw=convw_l,
        alog_b=np.ascontiguousarray(np.broadcast_to(f(inputs["gdn_a_log"]), (128, 4))),
        dtb_b=np.ascontiguousarray(np.broadcast_to(f(inputs["gdn_dt_bias"]), (128, 4))),
        w_out=f(inputs["w_out"][0]), ln_w=f(inputs["ln_w"]), ln_b=f(inputs["ln_b"]),
        cf=cfa, uref=urefa, bm=bma, sel=sela)


def kernel(**inputs):
    n_units = 32
    consts = host_consts()
    nc = build(n_units)
    in_maps = [make_inputs_for_core(c, inputs, consts, n_units) for c in range(8)]
    res = run_bass_kernel_spmd(nc, in_maps, core_ids=list(range(8)))
    out = np.concatenate([r["out"] for r in res.results], axis=0)
    return out.astype(np.float32)
```
